# Optimizing a Trainium2 kernel written in Bass

```python
import math
import jax, jax.numpy as jnp
from jax import lax
import numpy as np

D_MODEL = 1024
BATCH = 4
SEQ = 4096
DEPTH = 4

N_MIXERS = 3
CHUNK = 64
LN_EPS = 1e-5
RMS_EPS = 1e-6
L2_EPS = 1e-6
DN_ALPHA = (2 * DEPTH) ** 0.25
DN_BETA = (8 * DEPTH) ** -0.25
D_FF = ((8 * D_MODEL + 3 * 256 - 1) // (3 * 256)) * 256

GDN_DK = 128
GDN_DV = 128
GDN_HEADS = D_MODEL // GDN_DK
GDN_KEY = GDN_HEADS * GDN_DK
GDN_VAL = GDN_HEADS * GDN_DV
GDN_CONV = 4
GDN_QKV = 2 * GDN_KEY + GDN_VAL
GDN_IN = GDN_QKV + GDN_VAL + 2 * GDN_HEADS

HGRN_DK = 128
HGRN_HEADS = D_MODEL // HGRN_DK
HGRN_DV = D_MODEL // HGRN_HEADS
HGRN_IN = 4 * D_MODEL

GLA_HEADS = 4
GLA_KEY = D_MODEL // 2
GLA_DK = GLA_KEY // GLA_HEADS
GLA_VAL = D_MODEL
GLA_DV = GLA_VAL // GLA_HEADS
GLA_RANK = 16
GLA_TAU = 16.0
GLA_IN = 2 * GLA_KEY + 2 * GLA_VAL + GLA_RANK

N_GDN = (DEPTH + 2) // 3
N_HGRN = (DEPTH + 1) // 3
N_GLA = DEPTH // 3

kernel_name = "hybrid_gdn_hgrn2_gla_deepnorm"

F32 = jnp.float32


def layer_norm(x, g, b):
    xf = x.astype(F32)
    mu = jnp.mean(xf, -1, keepdims=True)
    var = jnp.mean(jnp.square(xf - mu), -1, keepdims=True)
    return ((xf - mu) * lax.rsqrt(var + LN_EPS) * g + b).astype(x.dtype)


def gated_rms_norm(o, gate, w):
    of = o.astype(F32)
    y = of * lax.rsqrt(jnp.mean(of * of, -1, keepdims=True) + RMS_EPS) * w
    return y * jax.nn.silu(gate.astype(F32))


def l2_normalize(t):
    t = t.astype(F32)
    return t * lax.rsqrt(jnp.sum(t * t, -1, keepdims=True) + L2_EPS)


def causal_depthwise_conv(x, w):
    K, C = w.shape
    return lax.conv_general_dilated(
        x, w[:, None, :].astype(x.dtype), window_strides=(1,), padding=[(K - 1, 0)],
        dimension_numbers=('NWC', 'WIO', 'NWC'), feature_group_count=C)


def to_chunks(t):
    B, T, H = t.shape[:3]
    return jnp.moveaxis(t.reshape(B, T // CHUNK, CHUNK, H, *t.shape[3:]), 3, 1)


def from_chunks(t):
    B, H, N, C, d = t.shape
    return jnp.moveaxis(t, 1, 3).reshape(B, N * C, H, d)


def lead(t):
    return jnp.moveaxis(t, 2, 0)


def chunk_gated_delta_rule(q, k, v, beta, g):
    B, T, H, dk = q.shape
    dv = v.shape[-1]
    q, k, v = to_chunks(q.astype(F32)), to_chunks(k.astype(F32)), to_chunks(v.astype(F32))
    beta, g = to_chunks(beta.astype(F32)), to_chunks(g.astype(F32))
    G = jnp.cumsum(g, axis=-1)
    incl = jnp.tril(jnp.ones((CHUNK, CHUNK), bool))
    strict = jnp.tril(jnp.ones((CHUNK, CHUNK), bool), -1)
    decay = jnp.exp(jnp.where(incl, G[..., :, None] - G[..., None, :], -jnp.inf))
    kb = k * beta[..., None]
    L = jnp.where(strict, jnp.einsum('bhncd,bhnsd->bhncs', kb, k) * decay, 0.0)
    rhs = jnp.concatenate([v * beta[..., None], kb * jnp.exp(G)[..., None]], axis=-1)
    sol = lax.linalg.triangular_solve(L + jnp.eye(CHUNK, dtype=F32), rhs, left_side=True, lower=True)
    u, w = sol[..., :dv], sol[..., dv:]
    a_qk = jnp.einsum('bhncd,bhnsd->bhncs', q, k) * decay
    q_dec = q * jnp.exp(G)[..., None]
    G_last = G[..., -1:]
    k_dec = k * jnp.exp(G_last - G)[..., None]
    g_last = jnp.exp(G_last[..., 0])

    def step(S, inp):
        u_c, w_c, a_c, q_c, k_c, gl_c = inp
        v_new = u_c - jnp.einsum('bhcd,bhde->bhce', w_c, S)
        o_c = jnp.einsum('bhcd,bhde->bhce', q_c, S) + jnp.einsum('bhcs,bhse->bhce', a_c, v_new)
        S = S * gl_c[..., None, None] + jnp.einsum('bhcd,bhce->bhde', k_c, v_new)
        return S, o_c

    S0 = jnp.zeros((B, H, dk, dv), F32)
    _, o = lax.scan(step, S0, (lead(u), lead(w), lead(a_qk), lead(q_dec), lead(k_dec), lead(g_last)))
    return from_chunks(jnp.moveaxis(o, 0, 2))


def chunk_gla(q, k, v, g):
    B, T, H, dk = q.shape
    dv = v.shape[-1]
    q, k, v, g = (to_chunks(t.astype(F32)) for t in (q, k, v, g))
    G = jnp.cumsum(g, axis=-2)
    G_last = G[..., -1:, :]
    q_dec = q * jnp.exp(G)
    k_dec = k * jnp.exp(G_last - G)
    g_last = jnp.exp(G_last)
    incl = jnp.tril(jnp.ones((CHUNK, CHUNK), bool))[..., None]

    def step(S, inp):
        q_c, k_c, v_c, G_c, qd_c, kd_c, gl_c = inp
        rel = jnp.exp(jnp.where(incl, G_c[..., :, None, :] - G_c[..., None, :, :], -jnp.inf))
        a = jnp.einsum('bhcd,bhsd,bhcsd->bhcs', q_c, k_c, rel)
        o_c = jnp.einsum('bhcd,bhde->bhce', qd_c, S) + jnp.einsum('bhcs,bhse->bhce', a, v_c)
        S = S * gl_c[..., 0, :, None] + jnp.einsum('bhcd,bhce->bhde', kd_c, v_c)
        return S, o_c

    S0 = jnp.zeros((B, H, dk, dv), F32)
    xs = (lead(q), lead(k), lead(v), lead(G), lead(q_dec), lead(k_dec), lead(g_last))
    _, o = lax.scan(step, S0, xs)
    return from_chunks(jnp.moveaxis(o, 0, 2))


def gated_deltanet(x, w_in, conv_w, a_log, dt_bias, norm_w, w_out):
    B, T, _ = x.shape
    p = x @ w_in
    qkv, z, b, a = jnp.split(p, [GDN_QKV, GDN_QKV + GDN_VAL, GDN_QKV + GDN_VAL + GDN_HEADS], axis=-1)
    qkv = jax.nn.silu(causal_depthwise_conv(qkv, conv_w))
    q, k, v = jnp.split(qkv, [GDN_KEY, 2 * GDN_KEY], axis=-1)
    q = l2_normalize(q.reshape(B, T, GDN_HEADS, GDN_DK)) * (GDN_DK ** -0.5)
    k = l2_normalize(k.reshape(B, T, GDN_HEADS, GDN_DK))
    v = v.reshape(B, T, GDN_HEADS, GDN_DV)
    beta = jax.nn.sigmoid(b.astype(F32))
    g = -jnp.exp(a_log.astype(F32)) * jax.nn.softplus(a.astype(F32) + dt_bias.astype(F32))
    o = chunk_gated_delta_rule(q, k, v, beta, g)
    o = gated_rms_norm(o, z.reshape(B, T, GDN_HEADS, GDN_DV), norm_w.astype(F32))
    return o.reshape(B, T, GDN_VAL).astype(x.dtype) @ w_out


def hgrn2(x, w_in, lb, norm_w, w_out):
    B, T, _ = x.shape
    q, f, i, r = jnp.split(x @ w_in, 4, axis=-1)
    q = jax.nn.silu(q.astype(F32)).reshape(B, T, HGRN_HEADS, HGRN_DK) * (HGRN_DK ** -0.5)
    log_f = jnp.logaddexp(jnp.log(lb), jnp.log1p(-lb) + jax.nn.log_sigmoid(f.astype(F32)))
    k = -jnp.expm1(log_f)
    shp_k = (B, T, HGRN_HEADS, HGRN_DK)
    o = chunk_gla(q, k.reshape(shp_k), i.reshape(B, T, HGRN_HEADS, HGRN_DV), log_f.reshape(shp_k))
    o = gated_rms_norm(o, r.reshape(B, T, HGRN_HEADS, HGRN_DV), norm_w.astype(F32))
    return o.reshape(B, T, D_MODEL).astype(x.dtype) @ w_out


def gla(x, w_in, w_gk2, b_gk, norm_w, w_out):
    B, T, _ = x.shape
    q, k, v, r, gk = jnp.split(x @ w_in, [GLA_KEY, 2 * GLA_KEY, 2 * GLA_KEY + GLA_VAL, 2 * GLA_KEY + 2 * GLA_VAL], axis=-1)
    g = jax.nn.log_sigmoid((gk @ w_gk2 + b_gk).astype(F32)) / GLA_TAU
    shp_k = (B, T, GLA_HEADS, GLA_DK)
    q = q.astype(F32).reshape(shp_k) * (GLA_DK ** -0.5)
    o = chunk_gla(q, k.reshape(shp_k), v.reshape(B, T, GLA_HEADS, GLA_DV), g.reshape(shp_k))
    o = gated_rms_norm(o, r.reshape(B, T, GLA_HEADS, GLA_DV), norm_w.astype(F32))
    return o.reshape(B, T, GLA_VAL).astype(x.dtype) @ w_out


def swiglu(x, w_in, w_down):
    gate, up = jnp.split(x @ w_in, 2, axis=-1)
    return (jax.nn.silu(gate) * up) @ w_down


def setup_inputs(seed: int = 0) -> dict:
    key = jax.random.key(seed)
    ks = jax.random.split(key, 22)
    nrm = lambda k, shape, s: jax.random.normal(k, shape, F32) * s
    x = jax.random.normal(ks[0], (BATCH, SEQ, D_MODEL), F32)
    gdn_w_in = nrm(ks[1], (N_GDN, D_MODEL, GDN_IN), D_MODEL ** -0.5)
    gdn_conv_w = nrm(ks[2], (N_GDN, GDN_CONV, GDN_QKV), GDN_CONV ** -0.5)
    gdn_a_log = jnp.log(jax.random.uniform(ks[3], (N_GDN, GDN_HEADS), F32, 1.0, 16.0))
    dt = jnp.exp(jax.random.uniform(ks[4], (N_GDN, GDN_HEADS), F32, math.log(1e-3), math.log(1e-1)))
    gdn_dt_bias = dt + jnp.log(-jnp.expm1(-dt))
    gdn_norm_w = 1.0 + nrm(ks[5], (N_GDN, GDN_DV), 0.02)
    gdn_w_out = nrm(ks[6], (N_GDN, GDN_VAL, D_MODEL), GDN_VAL ** -0.5 * DN_BETA)
    hgrn_w_in = nrm(ks[7], (N_HGRN, D_MODEL, HGRN_IN), D_MODEL ** -0.5)
    hgrn_lb_logits = nrm(ks[8], (DEPTH, HGRN_HEADS * HGRN_DK), 0.5)
    hgrn_norm_w = 1.0 + nrm(ks[9], (N_HGRN, HGRN_DV), 0.02)
    hgrn_w_out = nrm(ks[10], (N_HGRN, D_MODEL, D_MODEL), D_MODEL ** -0.5 * DN_BETA)
    gla_w_in = nrm(ks[11], (N_GLA, D_MODEL, GLA_IN), D_MODEL ** -0.5)
    gla_w_gk2 = nrm(ks[12], (N_GLA, GLA_RANK, GLA_KEY), GLA_RANK ** -0.5)
    gla_b_gk = nrm(ks[13], (N_GLA, GLA_KEY), 0.1)
    gla_norm_w = 1.0 + nrm(ks[14], (N_GLA, GLA_DV), 0.02)
    gla_w_out = nrm(ks[15], (N_GLA, GLA_VAL, D_MODEL), GLA_VAL ** -0.5 * DN_BETA)
    mix_ln_g = 1.0 + nrm(ks[16], (DEPTH, D_MODEL), 0.02)
    mix_ln_b = nrm(ks[17], (DEPTH, D_MODEL), 0.02)
    ffn_w_in = nrm(ks[18], (DEPTH, D_MODEL, 2 * D_FF), D_MODEL ** -0.5)
    ffn_w_down = nrm(ks[19], (DEPTH, D_FF, D_MODEL), D_FF ** -0.5 * DN_BETA)
    ffn_ln_g = 1.0 + nrm(ks[20], (DEPTH, D_MODEL), 0.02)
    ffn_ln_b = nrm(ks[21], (DEPTH, D_MODEL), 0.02)
    return {"x": x, "gdn_w_in": gdn_w_in, "gdn_conv_w": gdn_conv_w, "gdn_a_log": gdn_a_log,
            "gdn_dt_bias": gdn_dt_bias, "gdn_norm_w": gdn_norm_w, "gdn_w_out": gdn_w_out,
            "hgrn_w_in": hgrn_w_in, "hgrn_lb_logits": hgrn_lb_logits, "hgrn_norm_w": hgrn_norm_w,
            "hgrn_w_out": hgrn_w_out, "gla_w_in": gla_w_in, "gla_w_gk2": gla_w_gk2, "gla_b_gk": gla_b_gk,
            "gla_norm_w": gla_norm_w, "gla_w_out": gla_w_out, "mix_ln_g": mix_ln_g, "mix_ln_b": mix_ln_b,
            "ffn_w_in": ffn_w_in, "ffn_w_down": ffn_w_down, "ffn_ln_g": ffn_ln_g, "ffn_ln_b": ffn_ln_b}


def reference(x, gdn_w_in, gdn_conv_w, gdn_a_log, gdn_dt_bias, gdn_norm_w, gdn_w_out,
              hgrn_w_in, hgrn_lb_logits, hgrn_norm_w, hgrn_w_out,
              gla_w_in, gla_w_gk2, gla_b_gk, gla_norm_w, gla_w_out,
              mix_ln_g, mix_ln_b, ffn_w_in, ffn_w_down, ffn_ln_g, ffn_ln_b):
    lb_soft = jax.nn.softmax(hgrn_lb_logits.astype(F32), axis=0)
    lb_all = jnp.cumsum(lb_soft, axis=0) - lb_soft[0]
    for i in range(DEPTH):
        kind, j = i % N_MIXERS, i // N_MIXERS
        if kind == 0:
            m = gated_deltanet(x, gdn_w_in[j], gdn_conv_w[j], gdn_a_log[j], gdn_dt_bias[j], gdn_norm_w[j], gdn_w_out[j])
        elif kind == 1:
            m = hgrn2(x, hgrn_w_in[j], lb_all[i], hgrn_norm_w[j], hgrn_w_out[j])
        else:
            m = gla(x, gla_w_in[j], gla_w_gk2[j], gla_b_gk[j], gla_norm_w[j], gla_w_out[j])
        x = layer_norm(DN_ALPHA * x + m, mix_ln_g[i], mix_ln_b[i])
        x = layer_norm(DN_ALPHA * x + swiglu(x, ffn_w_in[i], ffn_w_down[i]), ffn_ln_g[i], ffn_ln_b[i])
    return x
```

```python
import numpy as np
from contextlib import ExitStack
import concourse.bass as bass
import concourse.mybir as mybir
from concourse.bass_utils import run_bass_kernel_spmd

F32 = mybir.dt.float32
BF16 = mybir.dt.bfloat16
AF = mybir.ActivationFunctionType
ALU = mybir.AluOpType
AX = mybir.AxisListType

D = 1024
DFF = 2816
TT = 512
C = 64
NCH = TT // C
DEPTH = 4
ALPHA = float((2 * DEPTH) ** 0.25)
LN_EPS, RMS_EPS, L2_EPS = 1e-5, 1e-6, 1e-6
GDN_IN, HGRN_IN, GLA_IN = 4112, 4096, 3088
SW = 1024

C_ID, C_ONE, C_MU, C_SL, C_SC, C_I8, C_N = 0, 128, 256, 768, 1280, 1792, 2304


def make_consts():
    c = np.zeros((128, C_N), np.float32)
    c[:, C_ID:C_ID + 128] = np.eye(128)
    c[:, C_ONE:C_ONE + 128] = 1.0
    i = np.arange(64)
    mu = (i[:, None] <= i[None, :]).astype(np.float32)
    sl = (i[:, None] > i[None, :]).astype(np.float32)
    c[:64, C_MU:C_MU + 512] = np.tile(mu, (1, 8))
    c[:64, C_SL:C_SL + 512] = np.tile(sl, (1, 8))
    c[:64, C_I8:C_I8 + 512] = np.tile(np.eye(64, dtype=np.float32), (1, 8))
    sc = np.ones((128, 512), np.float32)
    sc[:, ::64] = 0.0
    c[:, C_SC:C_SC + 512] = sc
    return c


def pv_layout():
    off = {}
    n = 0
    for i in range(DEPTH):
        for nm in ("g1", "b1", "g2", "b2"):
            off[(nm, i)] = n
            n += 8
    for j in range(2):
        off[("conv", j)] = n
        n += 96
        off[("gnw", j)] = n
        n += 1
    off["hnw"] = n
    n += 1
    off["lbl"] = n
    n += 32
    off["lnw"] = n
    n += 2
    off["bgk"] = n
    n += 4
    off["n"] = n
    return off


PV = pv_layout()


def fm(v):
    return np.ascontiguousarray(v.reshape(-1, 128).T)


def make_pv(inp):
    pv = np.zeros((128, PV["n"]), np.float32)
    for i in range(DEPTH):
        pv[:, PV[("g1", i)]:PV[("g1", i)] + 8] = fm(inp["mix_ln_g"][i])
        pv[:, PV[("b1", i)]:PV[("b1", i)] + 8] = fm(inp["mix_ln_b"][i])
        pv[:, PV[("g2", i)]:PV[("g2", i)] + 8] = fm(inp["ffn_ln_g"][i])
        pv[:, PV[("b2", i)]:PV[("b2", i)] + 8] = fm(inp["ffn_ln_b"][i])
    for j in range(2):
        cw = inp["gdn_conv_w"][j]
        pv[:, PV[("conv", j)]:PV[("conv", j)] + 96] = cw.T.reshape(24, 128, 4).transpose(1, 0, 2).reshape(128, 96)
        pv[:, PV[("gnw", j)]] = inp["gdn_norm_w"][j]
    pv[:, PV["hnw"]] = inp["hgrn_norm_w"][0]
    lb = inp["hgrn_lb_logits"]
    pv[:, PV["lbl"]:PV["lbl"] + 32] = lb.T.reshape(8, 128, 4).transpose(1, 0, 2).reshape(128, 32)
    pv[:, PV["lnw"]:PV["lnw"] + 2] = fm(inp["gla_norm_w"][0])
    pv[:, PV["bgk"]:PV["bgk"] + 4] = fm(inp["gla_b_gk"][0])
    return pv


def make_pr(inp):
    pr = np.zeros((128, 32), np.float32)
    for j in range(2):
        pr[:, j * 16:j * 16 + 8] = inp["gdn_a_log"][j][None, :]
        pr[:, j * 16 + 8:j * 16 + 16] = inp["gdn_dt_bias"][j][None, :]
    return pr


class Rot:
    def __init__(self, items):
        self.items = items
        self.i = 0

    def next(self):
        it = self.items[self.i % len(self.items)]
        self.i += 1
        return it


class Prog:
    ENGS = ("pe", "act", "dve", "pool", "sp")
    NSLOT = {"sp": 24, "pool": 8, "act": 4}

    def __init__(self, nc):
        self.nc = nc
        self.ops = []

    def op(self, eng, fn, r=(), w=(), dma=False):
        r = list(r)
        for k in list(r) + list(w):
            if isinstance(k, tuple) and k[0] in ("WA", "WB"):
                ek = ("epoch", k[0])
                if ek not in r and ek not in w:
                    r.append(ek)
        self.ops.append({"eng": eng, "fn": fn, "r": tuple(r), "w": tuple(w), "dma": dma})

    def dma(self, out, in_, r, w, q="sp"):
        self.op(q, lambda e: e.dma_start(out=out, in_=in_), r, w, dma=True)

    def emit(self, stack):
        nc = self.nc
        ops = self.ops
        last_w, readers = {}, {}
        eng_pos = {e: 0 for e in self.ENGS}
        for i, o in enumerate(ops):
            deps = set()
            for k in o["r"]:
                if k in last_w:
                    deps.add(last_w[k])
                if isinstance(k, str) and k[:2] in ("pa", "pd"):
                    for rd in readers.get(k, ()):
                        if ops[rd]["eng"] != o["eng"]:
                            deps.add(rd)
            for k in o["w"]:
                if k in last_w:
                    deps.add(last_w[k])
                deps.update(readers.get(k, ()))
            deps.discard(i)
            o["deps"] = deps
            for k in o["r"]:
                readers.setdefault(k, []).append(i)
            for k in o["w"]:
                last_w[k] = i
                readers[k] = []
            o["pos"] = eng_pos[o["eng"]]
            eng_pos[o["eng"]] += 1
            o["signal"] = False
            o["waits_eng"] = {}
            o["waits_dma"] = {}
        seen = {e: {p: -1 for p in self.ENGS} for e in self.ENGS}
        seen_dma = {e: {} for e in self.ENGS}
        slot_cnt, slot_last = {}, {}
        dma_n = {e: 0 for e in self.ENGS}
        for i, o in enumerate(ops):
            e = o["eng"]
            if o["dma"]:
                s = dma_n[e] % self.NSLOT[e]
                dma_n[e] += 1
                key = (e, s)
                if key in slot_last:
                    o["deps"].add(slot_last[key])
                slot_last[key] = i
                slot_cnt[key] = slot_cnt.get(key, 0) + 1
                o["slot"] = key
                o["slot_val"] = 16 * slot_cnt[key]
            for d in sorted(o["deps"]):
                od = ops[d]
                if od["dma"]:
                    if seen_dma[e].get(od["slot"], 0) < od["slot_val"]:
                        seen_dma[e][od["slot"]] = od["slot_val"]
                        o["waits_dma"][od["slot"]] = od["slot_val"]
                    continue
                pe = od["eng"]
                if pe == e:
                    if e in ("pe", "sp"):
                        continue
                    if o["pos"] - od["pos"] > 2:
                        continue
                if seen[e][pe] >= od["pos"]:
                    continue
                seen[e][pe] = od["pos"]
                od["signal"] = True
                cur = o["waits_eng"].get(pe)
                if cur is None or ops[cur]["pos"] < od["pos"]:
                    o["waits_eng"][pe] = d
        tick = {e: 0 for e in self.ENGS}
        for o in ops:
            if o["signal"]:
                tick[o["eng"]] += 1
                o["tick"] = tick[o["eng"]]
        esem = {e: stack.enter_context(nc.semaphore("s_" + e)) for e in self.ENGS}
        ssem = {k: stack.enter_context(nc.semaphore("d_%s%d" % k)) for k in slot_cnt}
        per_eng = {e: [o for o in ops if o["eng"] == e] for e in self.ENGS}

        def run(ename, eng):
            for o in per_eng[ename]:
                for pe, d in o["waits_eng"].items():
                    eng.wait_ge(esem[pe], ops[d]["tick"])
                for sk, sv in o["waits_dma"].items():
                    eng.wait_ge(ssem[sk], sv)
                inst = o["fn"](eng)
                if o["dma"]:
                    inst.then_inc(ssem[o["slot"]], 16)
                elif o["signal"]:
                    inst.then_inc(esem[ename], 1)
            if ename == "sp":
                for k, c in slot_cnt.items():
                    eng.wait_ge(ssem[k], 16 * c)
                for e2 in self.ENGS:
                    if tick[e2]:
                        eng.wait_ge(esem[e2], tick[e2])

        with nc.Block() as block:
            @block.tensor
            def _(eng):
                run("pe", eng)

            @block.scalar
            def _(eng):
                run("act", eng)

            @block.vector
            def _(eng):
                run("dve", eng)

            @block.gpsimd
            def _(eng):
                run("pool", eng)

            @block.sync
            def _(eng):
                run("sp", eng)


class Builder:
    def __init__(self, TL, layers, dbg=(), upto=9):
        self.upto = upto
        self.TL = TL
        self.NT = TL // TT
        self.layers = layers
        self.dbg = set(dbg)
        self.nc = bass.Bass("TRN2", target_bir_lowering=False)
        self.P = Prog(self.nc)
        self.uid = 0

    def mm(self, out, lhsT, rhs, start, stop, r, w):
        self.P.op("pe", lambda e: e.matmul(out, lhsT, rhs, start=start, stop=stop), r, w)

    def tr(self, out, in_, ident, r, w):
        self.P.op("pe", lambda e: e.transpose(out, in_, ident), r, w)

    def act(self, out, in_, func, r, w, bias=None, scale=None):
        kw = {}
        if bias is not None:
            kw["bias"] = bias
        if scale is not None:
            kw["scale"] = scale
        self.P.op("act", lambda e: e.activation(out=out, in_=in_, func=func, **kw), r, w)

    def tt(self, eng, out, in0, in1, op, r, w):
        self.P.op(eng, lambda e: e.tensor_tensor(out=out, in0=in0, in1=in1, op=op), r, w)

    def ts(self, eng, out, in0, s1, op0, r, w, s2=None, op1=None):
        if op1 is None:
            self.P.op(eng, lambda e: e.tensor_scalar(out=out, in0=in0, scalar1=s1, scalar2=None, op0=op0), r, w)
        else:
            self.P.op(eng, lambda e: e.tensor_scalar(out=out, in0=in0, scalar1=s1, scalar2=s2, op0=op0, op1=op1), r, w)

    def stt(self, eng, out, in0, scalar, in1, op0, op1, r, w):
        self.P.op(eng, lambda e: e.scalar_tensor_tensor(out=out, in0=in0, scalar=scalar, in1=in1, op0=op0, op1=op1), r, w)

    def run_pipe(self, gens, depth):
        active, gens, more = [], iter(gens), True
        while True:
            if more and len(active) < depth:
                try:
                    active.insert(0, next(gens))
                except StopIteration:
                    more = False
            if not active:
                break
            for g in list(active):
                try:
                    next(g)
                except StopIteration:
                    active.remove(g)

    def rsqrt(self, out, in_, scale, eps, r, wk):
        self.act(out, in_, AF.Ln, r, [wk], bias=eps, scale=scale)
        self.act(out, out, AF.Exp, [wk], [wk], scale=-0.5)

    def cp(self, eng, out, in_, r, w):
        if eng == "act":
            self.act(out, in_, AF.Copy, r, w)
        else:
            self.P.op(eng, lambda e: e.tensor_copy(out=out, in_=in_), r, w)

    def red(self, eng, out, in_, r, w):
        self.P.op(eng, lambda e: e.tensor_reduce(out=out, in_=in_, axis=AX.X, op=ALU.add), r, w)

    def scan(self, out, d0, d1, r, w):
        self.P.op("dve", lambda e: e.tensor_tensor_scan(out=out, data0=d0, data1=d1, initial=0.0,
                                                        op0=ALU.mult, op1=ALU.add), r, w)

    def dram(self, name, shape, dt):
        kind = "ExternalOutput" if name in self.dbg else "Internal"
        return self.nc.dram_tensor(name, shape, dt, kind=kind).ap()

    def build(self):
        nc, P, TL, NT = self.nc, self.P, self.TL, self.NT
        st = ExitStack()
        self.st = st
        self.in_names = []

        def ein(n, s):
            self.in_names.append(n)
            return nc.dram_tensor(n, s, F32, kind="ExternalInput").ap()
        self.x_in = ein("x", [TL, D])
        self.cst_d = ein("cst", [128, C_N])
        self.pv_d = ein("pv", [128, PV["n"]])
        self.pr_d = ein("pr", [128, 32])
        self.wgk2_d = ein("wgk2", [16, 512])
        self.w = {}
        for kind, li in self.layers:
            nin = (GDN_IN, HGRN_IN, GLA_IN)[kind]
            self.w[li] = (ein("win%d" % li, [D, nin]) if self.upto >= 1 else None,
                          ein("wout%d" % li, [D, D]) if self.upto >= 1 else None,
                          ein("wf1_%d" % li, [D, 2 * DFF]) if self.upto >= 3 else None,
                          ein("wf2_%d" % li, [DFF, D]) if self.upto >= 3 else None)
        self.out_d = nc.dram_tensor("out", [TL, D], F32, kind="ExternalOutput").ap()
        self.X32 = self.dram("X32", [D, TL], F32)
        self.Xb = self.dram("Xb", [D, TL], BF16)
        self.Hd = self.dram("Hd", [DFF, TL], BF16)
        self.Qd = self.dram("Qd", [D, TL], F32)
        self.Kd = self.dram("Kd", [D, TL], F32)
        self.Gd = self.dram("Gd", [D, TL], F32)
        self.Rd = self.dram("Rd", [D, TL], F32)
        self.Vt = self.dram("Vt", [TL, D], F32)
        self.Kt = self.dram("Kt", [TL, D], F32)
        self.BGd = self.dram("BGd", [TL, 16], F32)

        def sb(name, shape, dt=F32):
            return st.enter_context(nc.sbuf_tensor(name, shape, dt))

        def ps(name, shape, dt=F32):
            return st.enter_context(nc.psum_tensor(name, shape, dt))

        self.cst = sb("cst_s", [128, C_N])
        self.pv = sb("pv_s", [128, PV["n"]])
        self.pr = sb("pr_s", [128, 32])
        self.stg = Rot([(sb("stg%d" % i, [128, SW]), "stg%d" % i) for i in range(2)])
        self.xb = Rot([(sb("xb%d" % i, [128, 8, TT], BF16), "xb%d" % i) for i in range(2)])
        self.x32 = sb("x32", [128, 8, TT])
        self.t512 = Rot([(sb("t512_%d" % i, [128, TT]), "t512_%d" % i) for i in range(4)])
        self.b512 = Rot([(sb("b512_%d" % i, [128, TT], BF16), "b512_%d" % i) for i in range(2)])
        self.s1 = sb("ln_s1", [128, TT])
        self.s2 = sb("ln_s2", [128, TT])
        self.mean = sb("ln_mean", [128, TT])
        self.rstd = sb("ln_rstd", [128, TT])
        self.S32 = sb("S32", [128, 1024])
        self.sm = sb("small", [128, 216])
        self.carry = sb("carry", [128, 24, 4])
        self.dummy = sb("dummy_t", [128, 4])
        self.WA = sb("WA", [128, 23040])
        self.WB = sb("WB", [128, 11264])
        self.pa = Rot([(ps("pa%d" % i, [128, TT]), "pa%d" % i) for i in range(4)])
        self.pd = Rot([(ps("pd%d" % i, [128, 2 * TT]), "pd%d" % i) for i in range(2)])
        self.ident = self.cst[:, C_ID:C_ID + 128]
        self.ones = self.cst[:, C_ONE:C_ONE + 128]

        P.dma(self.cst[:], self.cst_d, [], ["cst"])
        P.dma(self.pv[:], self.pv_d, [], ["pv"])
        P.dma(self.pr[:], self.pr_d, [], ["pr"])
        P.op("pool", lambda e: e.memset(self.S32[:], 0.0), [], ["S32"])
        P.op("pool", lambda e: e.memset(self.carry[:], 0.0), [], ["carry"])

        self.phase0()
        nl = len(self.layers)
        for n, (kind, li) in enumerate(self.layers):
            if self.upto < 1:
                break
            self.wb_load(self.w[li][1], D, D, ("wO", li))
            [self.gdn_A, self.hgrn_A, self.gla_A][kind](li)
            if self.upto < 2:
                break
            if kind == 0:
                self.gdn_B(li)
            elif kind == 1:
                self.gla_B(li, 8, 128, PV["hnw"])
            else:
                self.gla_B(li, 4, 256, PV["lnw"])
            if self.upto < 3:
                break
            self.ffn1(li)
            if self.upto < 4:
                break
            self.ffn2(li, last=(n == nl - 1))
        P.emit(st)
        st.close()
        return nc

    def wa_view(self, kc_n, ncols):
        return self.WA[:, 0:kc_n * ncols // 2].bitcast(BF16).rearrange("p (k n) -> p k n", n=ncols)

    def wb_view(self, kc_n, ncols):
        return self.WB[:, 0:kc_n * ncols // 2].bitcast(BF16).rearrange("p (k n) -> p k n", n=ncols)

    def waf(self, off, shape):
        n = int(np.prod(shape[1:]))
        ap = self.WA[0:shape[0], off:off + n]
        if len(shape) == 3:
            ap = ap.rearrange("p (a b) -> p a b", b=shape[2])
        return ap

    def fence(self, arena):
        d = self.dummy
        self.P.op("pool", lambda e: e.memset(d[:, 0:1], 0.0), [], [("epoch", arena)])

    def load_w(self, dst, w_dram, K, N, arena, name):
        for kc in range(K // 128):
            for n0 in range(0, N, SW):
                nn = min(SW, N - n0)
                s, sk = self.stg.next()
                self.P.dma(s[:, :nn], w_dram[kc * 128:(kc + 1) * 128, n0:n0 + nn], [], [sk])
                self.cp("pool", dst[:, kc, n0:n0 + nn], s[:, :nn], [sk], [(arena, name, kc, n0 // SW)])

    def wb_load(self, w_dram, K, N, key):
        self.fence("WB")
        self.load_w(self.wb_view(K // 128, N), w_dram, K, N, "WB", key[0])

    def tile_fm(self, dram, ti, r0=0, nrows=D):
        t0 = ti * TT
        return dram[r0:r0 + nrows, t0:t0 + TT].rearrange("(c p) t -> p c t", p=128)

    @property
    def x32k(self):
        return [("x32", fc) for fc in range(8)]

    def store_x(self, ti, xb, xbk):
        self.P.dma(self.tile_fm(self.X32, ti), self.x32[:], self.x32k, [("X32", ti)], q="sp")
        self.P.dma(self.tile_fm(self.Xb, ti), xb[:], [(xbk, fc) for fc in range(8)], [("Xb", ti)], q="sp")

    def load_xb(self, ti):
        xb, xbk = self.xb.next()
        self.P.dma(xb[:], self.tile_fm(self.Xb, ti), [("Xb", ti)], [(xbk, fc) for fc in range(8)])
        return xb, xbk

    def phase0(self):
        P = self.P
        self.fence("WA")
        xt = self.waf(0, [128, 4, D])
        for ti in range(self.NT):
            t0 = ti * TT
            P.dma(xt, self.x_in[t0:t0 + TT, :].rearrange("(b p) f -> p b f", p=128), [], [("WA", "xt")])
            xb, xbk = self.xb.next()
            for fc in range(8):
                pa, pk = self.pa.next()
                for b in range(4):
                    self.tr(pa[:, b * 128:(b + 1) * 128], xt[:, b, fc * 128:(fc + 1) * 128], self.ident,
                            [("WA", "xt"), "cst"], [pk])
                self.cp("act", self.x32[:, fc, :], pa[:], [pk], [("x32", fc)])
                self.cp("dve", xb[:, fc, :], pa[:], [pk], [(xbk, fc)])
            self.store_x(ti, xb, xbk)

    def ln_acc(self, fc, y_fc, yk):
        if fc == 0:
            self.cp("act", self.s1[:], y_fc, [yk], ["s1"])
            self.act(self.s2[:], y_fc, AF.Square, [yk], ["s2"])
        else:
            self.tt("dve", self.s1[:], self.s1[:], y_fc, ALU.add, [yk, "s1"], ["s1"])
            t, tk = self.t512.next()
            self.act(t[:], y_fc, AF.Square, [yk], [tk])
            self.tt("dve", self.s2[:], self.s2[:], t[:], ALU.add, [tk, "s2"], ["s2"])

    def ln_finish(self, y, yks, gcol, bcol, xb, xbk):
        pa, pk = self.pa.next()
        self.mm(pa[:], self.ones, self.s1[:], True, True, ["cst", "s1"], [pk])
        pb, pbk = self.pa.next()
        self.mm(pb[:], self.ones, self.s2[:], True, True, ["cst", "s2"], [pbk])
        self.ts("dve", self.mean[:], pa[:], 1.0 / D, ALU.mult, [pk], ["mean"])
        t, tk = self.t512.next()
        self.tt("dve", t[:], self.mean[:], self.mean[:], ALU.mult, ["mean"], [tk])
        self.stt("dve", t[:], pb[:], 1.0 / D, t[:], ALU.mult, ALU.subtract, [pbk, tk], [tk])
        self.rsqrt(self.rstd[:], t[:], 1.0, LN_EPS, [tk], "rstd")
        for fc in range(8):
            self.tt("dve", y[:, fc, :], y[:, fc, :], self.mean[:], ALU.subtract, [yks[fc], "mean"], [yks[fc]])
            self.tt("dve", y[:, fc, :], y[:, fc, :], self.rstd[:], ALU.mult, [yks[fc], "rstd"], [yks[fc]])
            self.act(self.x32[:, fc, :], y[:, fc, :], AF.Identity, [yks[fc], "pv"], [("x32", fc)],
                     bias=self.pv[:, bcol + fc:bcol + fc + 1], scale=self.pv[:, gcol + fc:gcol + fc + 1])
            self.cp("pool", xb[:, fc, :], self.x32[:, fc, :], [("x32", fc)], [(xbk, fc)])

    def proj_fm(self, wv, arena, name, col0, xb, xbk, M=128):
        pa, pk = self.pa.next()
        for kc in range(8):
            self.mm(pa[0:M, :], wv[:, kc, col0:col0 + M], xb[:, kc, :], kc == 0, kc == 7,
                    [(arena, name, kc, col0 // SW), (xbk, kc)], [pk])
        return pa, pk

    def proj_tok(self, wv, arena, name, col0, ncol, xb, xbk, b):
        pa, pk = self.pa.next()
        for kc in range(8):
            self.mm(pa[:, 0:ncol], xb[:, kc, b * 128:(b + 1) * 128], wv[:, kc, col0:col0 + ncol], kc == 0, kc == 7,
                    [(arena, name, kc, col0 // SW), (xbk, kc)], [pk])
        return pa, pk

    def out_fm(self, dram, dname, ti, fc, src, sk):
        t0 = ti * TT
        self.P.dma(dram[fc * 128:(fc + 1) * 128, t0:t0 + TT], src, [sk], [(dname, ti, fc)], q="sp")

    def v_tok(self, wv, name, col0, xb, xbk, ti):
        for b in range(4):
            for half in range(2):
                pa, pk = self.proj_tok(wv, "WA", name, col0 + half * 512, 512, xb, xbk, b)
                t, tk = self.t512.next()
                self.cp("act", t[:], pa[:], [pk], [tk])
                r0 = ti * TT + b * 128
                self.P.dma(self.Vt[r0:r0 + 128, half * 512:(half + 1) * 512], t[:], [tk], [("Vt", ti, b, half)], q="sp")

    def ffn1(self, li):
        P = self.P
        self.fence("WA")
        w1 = self.wa_view(8, 2 * DFF)
        self.load_w(w1, self.w[li][2], D, 2 * DFF, "WA", "wF1")
        self.fence("WB")
        self.load_w(self.wb_view(22, D), self.w[li][3], DFF, D, "WB", "wF2")
        for ti in range(self.NT):
            xb, xbk = self.load_xb(ti)
            for j in range(22):
                pg, pgk = self.proj_fm(w1, "WA", "wF1", j * 128, xb, xbk)
                pu, puk = self.proj_fm(w1, "WA", "wF1", DFF + j * 128, xb, xbk)
                t, tk = self.t512.next()
                self.act(t[:], pg[:], AF.Silu, [pgk], [tk])
                hb, hbk = self.b512.next()
                self.tt("dve", hb[:], t[:], pu[:], ALU.mult, [tk, puk], [hbk])
                self.out_fm(self.Hd, "Hd", ti, j, hb[:], hbk)

    def ffn2(self, li, last):
        P = self.P
        self.fence("WA")
        w2 = self.wb_view(22, D)
        h = self.WA[:, 0:5632].bitcast(BF16).rearrange("p (j t) -> p j t", t=TT)
        ys = [self.waf(5632 + i * 4096, [128, 8, TT]) for i in range(2)]
        g2, b2 = PV[("g2", li)], PV[("b2", li)]
        for ti in range(self.NT):
            t0 = ti * TT
            P.dma(h, self.Hd[:, t0:t0 + TT].rearrange("(j p) t -> p j t", p=128),
                  [("Hd", ti, j) for j in range(22)], [("WA", "h")])
            P.dma(self.x32[:], self.tile_fm(self.X32, ti), [("X32", ti)], self.x32k)
            y = ys[ti % 2]
            yks = [("WA", "y", ti % 2, fc) for fc in range(8)]
            for fc in range(8):
                pa, pk = self.pa.next()
                for j in range(22):
                    self.mm(pa[:], w2[:, j, fc * 128:(fc + 1) * 128], h[:, j, :], j == 0, j == 21,
                            [("WB", "wF2", j, 0), ("WA", "h")], [pk])
                self.stt("dve", y[:, fc, :], self.x32[:, fc, :], ALPHA, pa[:], ALU.mult, ALU.add,
                         [("x32", fc), pk], [yks[fc]])
                self.ln_acc(fc, y[:, fc, :], yks[fc])
            xb, xbk = self.xb.next()
            self.ln_finish(y, yks, g2, b2, xb, xbk)
            if last:
                for b in range(4):
                    for half in range(2):
                        pa, pk = self.pa.next()
                        for q in range(4):
                            self.tr(pa[:, q * 128:(q + 1) * 128], self.x32[:, half * 4 + q, b * 128:(b + 1) * 128],
                                    self.ident, [("x32", half * 4 + q), "cst"], [pk])
                        t, tk = self.t512.next()
                        self.cp("act", t[:], pa[:], [pk], [tk])
                        P.dma(self.out_d[t0 + b * 128:t0 + (b + 1) * 128, half * 512:(half + 1) * 512], t[:],
                              [tk], [("out", ti, b, half)], q="sp")
            else:
                self.store_x(ti, xb, xbk)

    def epilogue(self, li, ti, dv, nwcol, yoff):
        P = self.P
        import os
        if os.environ.get("K_SKIP_EPI"):
            xb, xbk = self.xb.next()
            self.store_x(ti, xb, xbk)
            return
        wO = self.wb_view(8, D)
        nfc = dv // 128
        ob, obk = self.xb.next()
        t0 = ti * TT
        for hd in range(8 // nfc):
            pa, pk = self.pa.next()
            for q in range(nfc):
                fc = hd * nfc + q
                t, tk = self.t512.next()
                self.act(t[:], self.x32[:, fc, :], AF.Square, [("x32", fc)], [tk])
                self.mm(pa[:], self.ones, t[:], q == 0, q == nfc - 1, ["cst", tk], [pk])
            rs, rsk = self.t512.next()
            self.rsqrt(rs[:], pa[:], 1.0 / dv, RMS_EPS, [pk], rsk)
            for q in range(nfc):
                fc = hd * nfc + q
                r, rk = self.t512.next()
                P.dma(r[:], self.Rd[fc * 128:(fc + 1) * 128, t0:t0 + TT], [("Rd", ti, fc)], [rk])
                t, tk = self.t512.next()
                self.tt("dve", t[:], self.x32[:, fc, :], rs[:], ALU.mult, [("x32", fc), rsk], [tk])
                self.stt("dve", ob[:, fc, :], t[:], self.pv[:, nwcol + q:nwcol + q + 1], r[:], ALU.mult, ALU.mult,
                         [tk, rk, "pv"], [(obk, fc)])
        y = self.waf(yoff, [128, 8, TT])
        yks = [("WA", "ye", fc) for fc in range(8)]
        for fo in range(8):
            pa, pk = self.pa.next()
            for kc in range(8):
                self.mm(pa[:], wO[:, kc, fo * 128:(fo + 1) * 128], ob[:, kc, :], kc == 0, kc == 7,
                        [("WB", "wO", kc, 0), (obk, kc)], [pk])
            xr, xrk = self.t512.next()
            P.dma(xr[:], self.X32[fo * 128:(fo + 1) * 128, t0:t0 + TT], [("X32", ti)], [xrk])
            self.stt("dve", y[:, fo, :], xr[:], ALPHA, pa[:], ALU.mult, ALU.add, [xrk, pk], [yks[fo]])
            self.ln_acc(fo, y[:, fo, :], yks[fo])
        xb, xbk = self.xb.next()
        self.ln_finish(y, yks, PV[("g1", li)], PV[("b1", li)], xb, xbk)
        self.store_x(ti, xb, xbk)

    def hgrn_A(self, li):
        P = self.P
        self.fence("WA")
        wA = self.wa_view(8, HGRN_IN)
        self.load_w(wA, self.w[li][0], D, HGRN_IN, "WA", "wA")
        sm = self.sm[:, 128:216]
        lbl = self.pv[:, PV["lbl"]:PV["lbl"] + 32]
        self.act(sm[:, 0:32], lbl, AF.Exp, ["pv"], ["sm"])
        ev = sm[:, 0:32].rearrange("p (c d) -> p c d", d=4)
        self.red("dve", sm[:, 32:40], ev, ["sm"], ["sm1"])
        self.red("dve", sm[:, 40:48], ev[:, :, 1:li + 1], ["sm"], ["sm2"])
        P.op("dve", lambda e: e.reciprocal(out=sm[:, 48:56], in_=sm[:, 32:40]), ["sm1"], ["sm3"])
        self.tt("dve", sm[:, 56:64], sm[:, 40:48], sm[:, 48:56], ALU.mult, ["sm2", "sm3"], ["lb"])
        self.ts("dve", sm[:, 64:72], sm[:, 56:64], -1.0, ALU.mult, ["lb"], ["oml"], s2=1.0, op1=ALU.add)
        for ti in range(self.NT):
            xb, xbk = self.load_xb(ti)
            for fc in range(8):
                pa, pk = self.proj_fm(wA, "WA", "wA", fc * 128, xb, xbk)
                t, tk = self.t512.next()
                self.act(t[:], pa[:], AF.Silu, [pk], [tk])
                self.out_fm(self.Qd, "Qd", ti, fc, t[:], tk)
            for fc in range(8):
                pa, pk = self.proj_fm(wA, "WA", "wA", 1024 + fc * 128, xb, xbk)
                t, tk = self.t512.next()
                self.act(t[:], pa[:], AF.Sigmoid, [pk], [tk])
                self.ts("dve", t[:], t[:], sm[:, 64 + fc:65 + fc], ALU.mult, [tk, "oml", "lb"], [tk],
                        s2=sm[:, 56 + fc:57 + fc], op1=ALU.add)
                g, gk = self.t512.next()
                self.act(g[:], t[:], AF.Ln, [tk], [gk])
                self.out_fm(self.Gd, "Gd", ti, fc, g[:], gk)
                k, kk = self.t512.next()
                self.ts("dve", k[:], t[:], -1.0, ALU.mult, [tk], [kk], s2=1.0, op1=ALU.add)
                self.out_fm(self.Kd, "Kd", ti, fc, k[:], kk)
            self.v_tok(wA, "wA", 2048, xb, xbk, ti)
            for fc in range(8):
                pa, pk = self.proj_fm(wA, "WA", "wA", 3072 + fc * 128, xb, xbk)
                t, tk = self.t512.next()
                self.act(t[:], pa[:], AF.Silu, [pk], [tk])
                self.out_fm(self.Rd, "Rd", ti, fc, t[:], tk)

    def gla_A(self, li):
        P = self.P
        self.fence("WA")
        wA = self.wa_view(8, GLA_IN)
        self.load_w(wA, self.w[li][0], D, GLA_IN, "WA", "wA")
        wg = self.waf(13000, [16, 512])
        P.dma(wg, self.wgk2_d, [], [("WA", "wgk2")])
        for ti in range(self.NT):
            xb, xbk = self.load_xb(ti)
            for fc in range(4):
                pa, pk = self.proj_fm(wA, "WA", "wA", fc * 128, xb, xbk)
                t, tk = self.t512.next()
                self.cp("act", t[:], pa[:], [pk], [tk])
                self.out_fm(self.Qd, "Qd", ti, fc, t[:], tk)
            for fc in range(4):
                pa, pk = self.proj_fm(wA, "WA", "wA", 512 + fc * 128, xb, xbk)
                t, tk = self.t512.next()
                self.cp("act", t[:], pa[:], [pk], [tk])
                self.out_fm(self.Kd, "Kd", ti, fc, t[:], tk)
            self.v_tok(wA, "wA", 1024, xb, xbk, ti)
            for fc in range(8):
                pa, pk = self.proj_fm(wA, "WA", "wA", 2048 + fc * 128, xb, xbk)
                t, tk = self.t512.next()
                self.act(t[:], pa[:], AF.Silu, [pk], [tk])
                self.out_fm(self.Rd, "Rd", ti, fc, t[:], tk)
            pa, pk = self.proj_fm(wA, "WA", "wA", 3072, xb, xbk, M=16)
            gk_, gkk = self.waf(13512, [16, 512]), ("WA", "gk")
            self.cp("act", gk_[0:16, :], pa[0:16, :], [pk], [gkk])
            for fc in range(4):
                pb, pbk = self.pa.next()
                self.mm(pb[:], wg[0:16, fc * 128:(fc + 1) * 128], gk_[0:16, :], True, True, [("WA", "wgk2"), gkk], [pbk])
                t, tk = self.t512.next()
                self.act(t[:], pb[:], AF.Sigmoid, [pbk, "pv"], [tk], bias=self.pv[:, PV["bgk"] + fc:PV["bgk"] + fc + 1])
                self.act(t[:], t[:], AF.Ln, [tk], [tk])
                g, gk2 = self.t512.next()
                self.ts("dve", g[:], t[:], 1.0 / 16.0, ALU.mult, [tk], [gk2])
                self.out_fm(self.Gd, "Gd", ti, fc, g[:], gk2)

    def gla_B(self, li, H, dv, nwcol):
        P = self.P
        self.fence("WA")
        S32 = self.S32
        P.op("pool", lambda e: e.memset(S32[:], 0.0), [], ["S32"])
        q32 = self.waf(0, [128, 4, TT])
        k32 = self.waf(2048, [128, 4, TT])
        g32 = self.waf(4096, [128, 4, TT])
        gm = self.waf(6144, [128, 4, TT])
        AT = self.waf(8192, [64, 4, TT])
        kT = self.WA[0:64, 10240:14336].rearrange("p (n h d) -> p n h d", h=4, d=128)
        Vc = [self.waf(14336 + i * 1024, [64, 1024]) for i in range(2)]
        yoff = 16384
        Gs = self.waf(20480, [128, 4, TT])
        sm = self.sm
        egm = sm[:, 0:32].rearrange("p (h n) -> p h n", n=8)
        egl = sm[:, 32:64].rearrange("p (h n) -> p h n", n=8)
        gam = sm[:, 64:96].rearrange("p (h n) -> p h n", n=8)
        nfc = dv // 128
        scm = self.cst[:, C_SC:C_SC + 512]
        mu8 = self.cst[0:64, C_MU:C_MU + 512]
        scale = 128 ** -0.5
        vi = 0
        import os
        lvl = int(os.environ.get("K_REC_LEVEL", "9"))
        for ti in range(self.NT):
            t0 = ti * TT
            for gi in range(H // 4 if lvl >= 1 else 0):
                h0 = gi * 4
                for (dst, dr, nm) in ((q32, self.Qd, "Qd"), (k32, self.Kd, "Kd"), (g32, self.Gd, "Gd")):
                    P.dma(dst, self.tile_fm(dr, ti, h0 * 128, 512), [(nm, ti, h0 + i) for i in range(4)],
                          [("WA", nm + "t", i) for i in range(4)])
                for hh in range(4 if lvl >= 2 else 0):
                    kq, kk_, kg, kgm = ("WA", "Qdt", hh), ("WA", "Kdt", hh), ("WA", "Gdt", hh), ("WA", "gm", hh)
                    if os.environ.get("K_NOSCAN"):
                        self.cp("dve", Gs[:, hh, :], g32[:, hh, :], [kg], [("WA", "Gs", hh)])
                    else:
                        self.scan(Gs[:, hh, :], scm, g32[:, hh, :], [kg, "cst"], [("WA", "Gs", hh)])
                    kg = ("WA", "Gs", hh)
                    Gv = Gs[:, hh, :].rearrange("p (n c) -> p n c", c=64)
                    gmv = gm[:, hh, :].rearrange("p (n c) -> p n c", c=64)
                    self.tt("dve", gmv, Gv, Gv[:, :, 31:32].broadcast_to([128, 8, 64]), ALU.subtract, [kg], [kgm])
                    e1, e1k = self.t512.next()
                    self.act(e1[:], gm[:, hh, :], AF.Exp, [kgm], [e1k])
                    e2, e2k = self.t512.next()
                    self.act(e2[:], gm[:, hh, :], AF.Exp, [kgm], [e2k], scale=-1.0)
                    self.stt("dve", q32[:, hh, :], q32[:, hh, :], scale, e1[:], ALU.mult, ALU.mult, [kq, e1k], [kq])
                    self.tt("dve", k32[:, hh, :], k32[:, hh, :], e2[:], ALU.mult, [kk_, e2k], [kk_])
                    self.act(egm[:, hh, :], Gv[:, :, 31], AF.Exp, [kg], [("egm", hh)])
                    self.act(egl[:, hh, :], gmv[:, :, 63], AF.Exp, [kgm], [("egl", hh)])
                    self.act(gam[:, hh, :], Gv[:, :, 63], AF.Exp, [kg], [("gam", hh)])
                    if lvl < 3:
                        continue
                    pa, pk = self.pa.next()
                    for n in range(8):
                        self.mm(pa[0:64, n * 64:(n + 1) * 64], k32[:, hh, n * 64:(n + 1) * 64],
                                q32[:, hh, n * 64:(n + 1) * 64], True, True, [kk_, kq], [pk])
                    self.tt("dve", AT[:, hh, :], pa[0:64, :], mu8, ALU.mult, [pk, "cst"], [("WA", "AT", hh)])
                    if lvl < 4:
                        continue
                    qv = q32[:, hh, :].rearrange("p (n c) -> p n c", c=64)
                    kv = k32[:, hh, :].rearrange("p (n c) -> p n c", c=64)
                    self.tt("dve", qv, qv, egm[:, hh, :].unsqueeze(2).broadcast_to([128, 8, 64]), ALU.mult,
                            [kq, ("egm", hh)], [kq])
                    self.tt("dve", kv, kv, egl[:, hh, :].unsqueeze(2).broadcast_to([128, 8, 64]), ALU.mult,
                            [kk_, ("egl", hh)], [kk_])
                    pd, pdk = self.pd.next()
                    for n in range(8):
                        self.tr(pd[0:64, n * 128:(n + 1) * 128], k32[:, hh, n * 64:(n + 1) * 64], self.ident,
                                [kk_, "cst"], [pdk])
                    for hf in range(2):
                        self.cp("act", kT[:, hf * 4:(hf + 1) * 4, hh, :],
                                pd[0:64, hf * 512:(hf + 1) * 512].rearrange("p (n d) -> p n d", d=128), [pdk],
                                [("WA", "kT", hh)])
                for n in range(8 if lvl >= 5 else 0):
                    V = Vc[vi % 2]
                    vk = ("WA", "Vc", vi % 2)
                    vi += 1
                    r0 = t0 + n * 64
                    P.dma(V, self.Vt[r0:r0 + 64, :], [("Vt", ti, n // 2, hf) for hf in range(2)], [vk])
                    po, pok = self.pa.next()
                    for hh in range(4):
                        h = h0 + hh
                        for eh in range(nfc):
                            slot = hh * nfc + eh
                            col = h * dv + eh * 128
                            self.mm(po[:, slot * 64:(slot + 1) * 64], S32[:, col:col + 128],
                                    q32[:, hh, n * 64:(n + 1) * 64], True, False, ["S32", ("WA", "Qdt", hh)], [pok])
                            self.mm(po[:, slot * 64:(slot + 1) * 64], V[:, col:col + 128],
                                    AT[:, hh, n * 64:(n + 1) * 64], False, True, [vk, ("WA", "AT", hh)], [pok])
                    fc0 = h0 * nfc
                    ns = 4 * nfc
                    self.cp("act", self.x32[:, fc0:fc0 + ns, n * 64:(n + 1) * 64],
                            po[:, 0:ns * 64].rearrange("p (s c) -> p s c", c=64), [pok],
                            [("x32", fc0 + s) for s in range(ns)])
                    if lvl < 6:
                        continue
                    pS, pSk = self.pd.next()
                    for hh in range(4):
                        h = h0 + hh
                        self.mm(pS[:, hh * dv:(hh + 1) * dv], kT[:, n, hh, :], V[:, h * dv:(h + 1) * dv], True, True,
                                [("WA", "kT", hh), vk], [pSk])
                    Sv = S32[:, h0 * dv:(h0 + 4) * dv].rearrange("p (h e) -> p h e", e=dv)
                    self.tt("dve", Sv, Sv, gam[:, :, n:n + 1].broadcast_to([128, 4, dv]), ALU.mult,
                            ["S32"] + [("gam", hh) for hh in range(4)], ["S32"])
                    for c0 in range(0, 4 * dv, 512):
                        self.tt("dve", S32[:, h0 * dv + c0:h0 * dv + c0 + 512], S32[:, h0 * dv + c0:h0 * dv + c0 + 512],
                                pS[:, c0:c0 + 512], ALU.add, ["S32", pSk], ["S32"])
            self.epilogue(li, ti, dv, nwcol, yoff)

    def gdn_A(self, li):
        P = self.P
        j = li // 3
        self.fence("WA")
        wA = self.wa_view(8, GDN_IN)
        self.load_w(wA, self.w[li][0], D, GDN_IN, "WA", "wA")
        pres = Rot([(self.waf(16448 + i * 520, [128, 516]), ("WA", "pre", i)) for i in range(3)])
        tl = Rot([(self.waf(18008 + i * 512, [128, 512]), ("WA", "tl", i)) for i in range(7)])
        tkR = Rot([(self.waf(18008 + i * 512, [128, 512]), ("WA", "tl", i)) for i in range(7, 9)])
        sqR = self.t512
        carry = self.carry
        P.op("pool", lambda e: e.memset(carry[:], 0.0), [], [("carry", fc) for fc in range(24)])
        sm = self.sm
        negA = sm[:, 208:216]
        self.act(negA, self.pr[:, j * 16:j * 16 + 8], AF.Exp, ["pr"], ["negA"])
        self.ts("dve", negA, negA, -1.0, ALU.mult, ["negA"], ["negA"])
        cw0 = PV[("conv", j)]
        for ti in range(self.NT):
            t0 = ti * TT
            xb, xbk = self.load_xb(ti)
            def fc_s1(fc, xb=xb, xbk=xbk):
                pa, pk = self.proj_fm(wA, "WA", "wA", fc * 128, xb, xbk)
                pre, prk = pres.next()
                self.cp("pool", pre[:, 0:3], carry[:, fc, 0:3], [("carry", fc)], [prk])
                self.cp("act", pre[:, 3:515], pa[:], [pk], [prk])
                self.cp("pool", carry[:, fc, 0:3], pre[:, 512:515], [prk], [("carry", fc)])
                acc, ak = tl.next()
                wc = lambda k: self.pv[:, cw0 + fc * 4 + k:cw0 + fc * 4 + k + 1]
                self.ts("dve", acc[:], pre[:, 3:515], wc(3), ALU.mult, [prk, "pv"], [ak])
                for k in (2, 1, 0):
                    self.stt("dve", acc[:], pre[:, k:k + 512], wc(k), acc[:], ALU.mult, ALU.add, [prk, ak, "pv"], [ak])
                return acc, ak

            def fc_gen(fcs, ti=ti, t0=t0):
                st = [fc_s1(fc) for fc in fcs]
                sqs = []
                for fc, (s, sk) in zip(fcs, st):
                    self.act(s[:], s[:], AF.Silu, [sk], [sk])
                    if fc < 16:
                        sq, sqk = sqR.next()
                        self.tt("dve", sq[:], s[:], s[:], ALU.mult, [sk], [sqk])
                        sqs.append((sq, sqk))
                yield
                if fcs[0] < 16:
                    pbs = []
                    for (sq, sqk) in sqs:
                        pb, pbk = self.pa.next()
                        self.mm(pb[:], self.ones, sq[:], True, True, ["cst", sqk], [pbk])
                        pbs.append((pb, pbk))
                    for fc, (s, sk), (sq, sqk), (pb, pbk) in zip(fcs, st, sqs, pbs):
                        self.rsqrt(sq[:], pb[:], 1.0, L2_EPS, [pbk, sqk], sqk)
                        if fc < 8:
                            self.stt("dve", s[:], s[:], 128 ** -0.5, sq[:], ALU.mult, ALU.mult, [sk, sqk], [sk])
                            self.out_fm(self.Qd, "Qd", ti, fc, s[:], sk)
                        else:
                            self.tt("dve", s[:], s[:], sq[:], ALU.mult, [sk, sqk], [sk])
                            self.out_fm(self.Kd, "Kd", ti, fc - 8, s[:], sk)
                    if fcs[0] < 8:
                        return
                    yield
                for fc, (s, sk) in zip(fcs, st):
                    dt_, dn, hcol = (self.Kt, "Kt", (fc - 8) * 128) if fc < 16 else (self.Vt, "Vt", (fc - 16) * 128)
                    pt, ptk = self.pa.next()
                    for b in range(4):
                        self.tr(pt[:, b * 128:(b + 1) * 128], s[:, b * 128:(b + 1) * 128], self.ident, [sk, "cst"], [ptk])
                    tk_, tkk = tkR.next()
                    self.cp("act", tk_[:], pt[:], [ptk], [tkk])
                    P.dma(dt_[t0:t0 + TT, hcol:hcol + 128].rearrange("(b p) d -> p b d", p=128),
                          tk_[:].rearrange("p (b d) -> p b d", d=128), [tkk], [(dn, ti, hcol // 128)], q="sp")

            self.run_pipe((fc_gen([fc, fc + 1]) for fc in range(0, 24, 2)), 3)
            for fc in range(8):
                pa, pk = self.proj_fm(wA, "WA", "wA", 3072 + fc * 128, xb, xbk)
                t, tk = tl.next()
                self.act(t[:], pa[:], AF.Silu, [pk], [tk])
                self.out_fm(self.Rd, "Rd", ti, fc, t[:], tk)
            for b in range(4):
                pa, pk = self.proj_tok(wA, "WA", "wA", 4096, 16, xb, xbk, b)
                bg, bgk = tl.next()
                self.act(bg[:, 0:8], pa[:, 0:8], AF.Sigmoid, [pk], [bgk])
                self.tt("dve", bg[:, 16:24], pa[:, 8:16], self.pr[:, j * 16 + 8:j * 16 + 16], ALU.add, [pk, "pr"], [bgk])
                self.act(bg[:, 16:24], bg[:, 16:24], AF.Exp, [bgk], [bgk])
                self.act(bg[:, 16:24], bg[:, 16:24], AF.Ln, [bgk], [bgk], bias=1.0)
                self.tt("dve", bg[:, 8:16], bg[:, 16:24], negA, ALU.mult, [bgk, "negA"], [bgk])
                r0 = t0 + b * 128
                P.dma(self.BGd[r0:r0 + 128, :], bg[:, 0:16], [bgk], [("BGd", ti, b)], q="sp")

    def gdn_B(self, li):
        P = self.P
        j = li // 3
        self.fence("WA")
        S32 = self.S32
        P.op("pool", lambda e: e.memset(S32[:], 0.0), [], ["S32"])
        W = lambda off, shape: self.waf(off, shape)
        qc = W(0, [128, 8, 64])
        kc = W(512, [128, 8, 64])
        ktok = W(1024, [64, 1024])
        vtok = W(2048, [64, 1024])
        bg = W(3072, [64, 16])
        o = 3136
        names512 = ["rhs1", "DL", "DU", "N", "NT", "Pa", "Pb", "PTa", "PTb", "Ta", "Tb", "AT"]
        t5 = {}
        for nm in names512:
            t5[nm] = W(o, [64, 8, 64])
            o += 512
        t10 = {}
        for nm in ["ru", "rw", "u", "vnew", "khat"]:
            t10[nm] = W(o, [64, 1024])
            o += 1024
        qh = W(o, [128, 8, 64])
        o += 512
        wT = W(o, [128, 512])
        o += 512
        eGr = W(o, [128, 512])
        o += 512
        yoff = o
        assert yoff + 4096 <= 23040, yoff
        K = lambda nm: ("WA", nm)
        sg = self.sm
        Gcol, eGc, beg, egl, nbeta, tmp8 = (sg[0:64, 0:8], sg[0:64, 8:16], sg[0:64, 16:24], sg[0:64, 24:32],
                                            sg[0:64, 32:40], sg[0:64, 40:48])
        gam = sg[:, 48:56]
        cst = self.cst
        triU = cst[0:64, C_MU:C_MU + 64]
        mu8 = cst[0:64, C_MU:C_MU + 512].rearrange("p (h c) -> p h c", c=64)
        sl8 = cst[0:64, C_SL:C_SL + 512].rearrange("p (h c) -> p h c", c=64)
        i8 = cst[0:64, C_I8:C_I8 + 512].rearrange("p (h c) -> p h c", c=64)
        id64 = cst[0:64, C_ID:C_ID + 64]
        ones64 = cst[0:64, C_ONE:C_ONE + 128]
        v3 = lambda ap: ap.rearrange("p (h c) -> p h c", c=64)
        b8 = lambda ap, n: ap.unsqueeze(2).broadcast_to([64, 8, n])

        def mm8(ps_ap, pk, lhs_of, rhs_of, r, width, npart):
            for h in range(8):
                self.mm(ps_ap[0:npart, h * width:(h + 1) * width], lhs_of(h), rhs_of(h), True, True, r, [pk])

        for ti in range(self.NT):
            for n in range(NCH):
                t0c = ti * TT + n * C
                P.dma(qc, self.Qd[:, t0c:t0c + C].rearrange("(h p) t -> p h t", p=128),
                      [("Qd", ti, f) for f in range(8)], [K("qc")])
                P.dma(kc, self.Kd[:, t0c:t0c + C].rearrange("(h p) t -> p h t", p=128),
                      [("Kd", ti, f) for f in range(8)], [K("kc")])
                P.dma(ktok, self.Kt[t0c:t0c + C, :], [("Kt", ti, f) for f in range(8)], [K("ktok")])
                P.dma(vtok, self.Vt[t0c:t0c + C, :], [("Vt", ti, f) for f in range(8)], [K("vtok")])
                P.dma(bg, self.BGd[t0c:t0c + C, :], [("BGd", ti, n // 2)], [K("bg")])
                beta, g = bg[:, 0:8], bg[:, 8:16]
                pA, pAk = self.pa.next()
                self.mm(pA[0:64, 0:8], triU, g, True, True, ["cst", K("bg")], [pAk])
                self.mm(pA[:, 8:16], ones64, g, True, True, ["cst", K("bg")], [pAk])
                self.cp("dve", Gcol, pA[0:64, 0:8], [pAk], ["Gcol"])
                self.act(gam, pA[:, 8:16], AF.Exp, [pAk], ["gam"])
                self.tt("dve", tmp8, pA[0:64, 8:16], Gcol, ALU.subtract, [pAk, "Gcol"], ["tmp8"])
                self.act(egl, tmp8, AF.Exp, ["tmp8"], ["egl"])
                self.act(eGc, Gcol, AF.Exp, ["Gcol"], ["eGc"])
                self.tt("dve", beg, beta, eGc, ALU.mult, [K("bg"), "eGc"], ["beg"])
                self.ts("dve", nbeta, beta, -1.0, ALU.mult, [K("bg")], ["nbeta"])
                self.tt("dve", t5["rhs1"], b8(g, 64), mu8, ALU.mult, [K("bg"), "cst"], [K("rhs1")])
                pT1, pT1k = self.pa.next()
                self.mm(pT1[:, :], ones64, t5["rhs1"].rearrange("p h c -> p (h c)"), True, True,
                        ["cst", K("rhs1")], [pT1k])
                DL, DU = t5["DL"], t5["DU"]
                self.tt("dve", DL, b8(Gcol, 64), v3(pT1[0:64, :]), ALU.subtract, ["Gcol", pT1k], [K("DL")])
                self.ts("dve", DL, DL, 0.0, ALU.min, [K("DL")], [K("DL")])
                self.act(DL, DL, AF.Exp, [K("DL")], [K("DL")])
                self.tt("dve", DL, DL, sl8, ALU.mult, [K("DL"), "cst"], [K("DL")])
                self.tt("dve", DU, v3(pT1[0:64, :]), b8(Gcol, 64), ALU.subtract, ["Gcol", pT1k], [K("DU")])
                self.ts("dve", DU, DU, 0.0, ALU.min, [K("DU")], [K("DU")])
                self.act(DU, DU, AF.Exp, [K("DU")], [K("DU")])
                self.tt("dve", DU, DU, mu8, ALU.mult, [K("DU"), "cst"], [K("DU")])
                self.act(eGr[:], pT1[:, :], AF.Exp, [pT1k], [K("eGr")])
                self.tt("dve", qh, qc, eGr[:].rearrange("p (h c) -> p h c", c=64), ALU.mult, [K("qc"), K("eGr")], [K("qh")])
                pKK, pKKk = self.pa.next()
                mm8(pKK, pKKk, lambda h: kc[:, h, :], lambda h: kc[:, h, :], [K("kc")], 64, 64)
                pQK, pQKk = self.pa.next()
                mm8(pQK, pQKk, lambda h: kc[:, h, :], lambda h: qc[:, h, :], [K("kc"), K("qc")], 64, 64)
                N, NT, AT = t5["N"], t5["NT"], t5["AT"]
                self.tt("dve", N, v3(pKK[0:64, :]), DL, ALU.mult, [pKKk, K("DL")], [K("N")])
                self.tt("dve", N, N, b8(nbeta, 64), ALU.mult, [K("N"), "nbeta"], [K("N")])
                self.tt("dve", AT, v3(pQK[0:64, :]), DU, ALU.mult, [pQKk, K("DU")], [K("AT")])
                pN, pNk = self.pa.next()
                for h in range(8):
                    self.tr(pN[0:64, h * 64:(h + 1) * 64], N[:, h, :], id64, [K("N"), "cst"], [pNk])
                self.cp("act", NT, v3(pN[0:64, :]), [pNk], [K("NT")])
                Tc, Tk = t5["Ta"], K("Ta")
                self.tt("dve", Tc, NT, i8, ALU.add, [K("NT"), "cst"], [Tk])
                Pp, Ppk, PTp, PTpk = N, K("N"), NT, K("NT")
                for jj in range(1, 6):
                    Pn, Pnk = (t5["Pa"], K("Pa")) if jj % 2 else (t5["Pb"], K("Pb"))
                    PTn, PTnk = (t5["PTa"], K("PTa")) if jj % 2 else (t5["PTb"], K("PTb"))
                    Tn, Tnk = (t5["Tb"], K("Tb")) if jj % 2 else (t5["Ta"], K("Ta"))
                    p1, p1k = self.pa.next()
                    mm8(p1, p1k, lambda h: PTp[:, h, :], lambda h: Pp[:, h, :], [Ppk, PTpk], 64, 64)
                    self.cp("act", Pn, v3(p1[0:64, :]), [p1k], [Pnk])
                    if jj < 5:
                        p2, p2k = self.pa.next()
                        mm8(p2, p2k, lambda h: Pp[:, h, :], lambda h: PTp[:, h, :], [Ppk, PTpk], 64, 64)
                        self.cp("dve", PTn, v3(p2[0:64, :]), [p2k], [PTnk])
                    p3, p3k = self.pa.next()
                    mm8(p3, p3k, lambda h: Pn[:, h, :], lambda h: Tc[:, h, :], [Pnk, Tk], 64, 64)
                    self.tt("dve", Tn, v3(p3[0:64, :]), Tc, ALU.add, [p3k, Tk], [Tnk])
                    Pp, Ppk, PTp, PTpk, Tc, Tk = Pn, Pnk, PTn, PTnk, Tn, Tnk
                TT_ = Tc
                ru, rw, u, vnew, khat = t10["ru"], t10["rw"], t10["u"], t10["vnew"], t10["khat"]
                v4 = lambda ap: ap.rearrange("p (h d) -> p h d", d=128)
                self.tt("dve", v4(ru), v4(vtok), b8(beta, 128), ALU.mult, [K("vtok"), K("bg")], [K("ru")])
                self.tt("dve", v4(rw), v4(ktok), b8(beg, 128), ALU.mult, [K("ktok"), "beg"], [K("rw")])
                self.tt("dve", v4(khat), v4(ktok), b8(egl, 128), ALU.mult, [K("ktok"), "egl"], [K("khat")])
                pU, pUk = self.pd.next()
                mm8(pU, pUk, lambda h: TT_[:, h, :], lambda h: ru[:, h * 128:(h + 1) * 128], [Tk, K("ru")], 128, 64)
                self.cp("act", u[:, 0:512], pU[0:64, 0:512], [pUk], [K("u")])
                self.cp("act", u[:, 512:1024], pU[0:64, 512:1024], [pUk], [K("u")])
                pW, pWk = self.pa.next()
                mm8(pW, pWk, lambda h: rw[:, h * 128:(h + 1) * 128], lambda h: TT_[:, h, :], [Tk, K("rw")], 64, 128)
                self.cp("act", wT[:], pW[:, :], [pWk], [K("wT")])
                pWS, pWSk = self.pd.next()
                mm8(pWS, pWSk, lambda h: wT[:, h * 64:(h + 1) * 64], lambda h: S32[:, h * 128:(h + 1) * 128],
                    [K("wT"), "S32"], 128, 64)
                self.tt("dve", vnew[:, 0:512], u[:, 0:512], pWS[0:64, 0:512], ALU.subtract, [K("u"), pWSk], [K("vnew")])
                self.tt("dve", vnew[:, 512:1024], u[:, 512:1024], pWS[0:64, 512:1024], ALU.subtract, [K("u"), pWSk], [K("vnew")])
                pO, pOk = self.pa.next()
                for h in range(8):
                    self.mm(pO[:, h * 64:(h + 1) * 64], S32[:, h * 128:(h + 1) * 128], qh[:, h, :], True, False,
                            ["S32", K("qh")], [pOk])
                    self.mm(pO[:, h * 64:(h + 1) * 64], vnew[:, h * 128:(h + 1) * 128], AT[:, h, :], False, True,
                            [K("vnew"), K("AT")], [pOk])
                self.cp("act", self.x32[:, :, n * 64:(n + 1) * 64], pO[:, :].rearrange("p (s c) -> p s c", c=64), [pOk],
                        self.x32k)
                pS, pSk = self.pd.next()
                mm8(pS, pSk, lambda h: khat[:, h * 128:(h + 1) * 128], lambda h: vnew[:, h * 128:(h + 1) * 128],
                    [K("khat"), K("vnew")], 128, 128)
                Sv = S32[:, :].rearrange("p (h e) -> p h e", e=128)
                self.tt("dve", Sv, Sv, gam.unsqueeze(2).broadcast_to([128, 8, 128]), ALU.mult, ["S32", "gam"], ["S32"])
                self.tt("dve", S32[:, 0:512], S32[:, 0:512], pS[:, 0:512], ALU.add, ["S32", pSk], ["S32"])
                self.tt("dve", S32[:, 512:1024], S32[:, 512:1024], pS[:, 512:1024], ALU.add, ["S32", pSk], ["S32"])
            self.epilogue(li, ti, 128, PV[("gnw", j)], yoff)


_CACHE = {}


def get_nc(TL, layers, dbg=()):
    key = (TL, tuple(layers), tuple(dbg))
    if key not in _CACHE:
        _CACHE[key] = Builder(TL, list(layers), dbg).build()
    return _CACHE[key]


LAYERS = [(0, 0), (1, 1), (2, 2), (0, 3)]


def core_inputs(inp, xs, layers):
    m = {"x": np.ascontiguousarray(xs, dtype=np.float32), "cst": make_consts(), "pv": make_pv(inp),
         "pr": make_pr(inp), "wgk2": np.ascontiguousarray(inp["gla_w_gk2"][0], dtype=np.float32)}
    for kind, li in layers:
        j = li // 3
        if kind == 0:
            wi, wo = inp["gdn_w_in"][j], inp["gdn_w_out"][j]
        elif kind == 1:
            wi, wo = inp["hgrn_w_in"][0], inp["hgrn_w_out"][0]
        else:
            wi, wo = inp["gla_w_in"][0], inp["gla_w_out"][0]
        m["win%d" % li] = np.ascontiguousarray(wi, dtype=np.float32)
        m["wout%d" % li] = np.ascontiguousarray(wo, dtype=np.float32)
        m["wf1_%d" % li] = np.ascontiguousarray(inp["ffn_w_in"][li], dtype=np.float32)
        m["wf2_%d" % li] = np.ascontiguousarray(inp["ffn_w_down"][li], dtype=np.float32)
    return m


def kernel(**inputs):
    inp = {k: np.asarray(v) for k, v in inputs.items()}
    x = inp["x"]
    B, T, _ = x.shape
    nc = get_nc(T, LAYERS)
    maps = [core_inputs(inp, x[c % B], LAYERS) for c in range(8)]
    res = run_bass_kernel_spmd(nc, maps, core_ids=list(range(8)))
    return np.stack([np.asarray(res.results[b]["out"], dtype=np.float32) for b in range(B)], axis=0)
```

```python
import numpy as np
from contextlib import ExitStack
import concourse.bass as bass
import concourse.mybir as mybir
from concourse.bass_utils import run_bass_kernel_spmd

F32 = mybir.dt.float32
BF16 = mybir.dt.bfloat16
AF = mybir.ActivationFunctionType
ALU = mybir.AluOpType
AX = mybir.AxisListType

D = 1024
DFF = 2816
TT = 512
C = 64
NCH = TT // C
DEPTH = 4
ALPHA = float((2 * DEPTH) ** 0.25)
LN_EPS, RMS_EPS, L2_EPS = 1e-5, 1e-6, 1e-6
GDN_IN, HGRN_IN, GLA_IN = 4112, 4096, 3088
SW = 1024

C_ID, C_ONE, C_MU, C_SL, C_SC, C_I8, C_N = 0, 128, 256, 768, 1280, 1792, 2304


def make_consts():
    c = np.zeros((128, C_N), np.float32)
    c[:, C_ID:C_ID + 128] = np.eye(128)
    c[:, C_ONE:C_ONE + 128] = 1.0
    i = np.arange(64)
    mu = (i[:, None] <= i[None, :]).astype(np.float32)
    sl = (i[:, None] > i[None, :]).astype(np.float32)
    c[:64, C_MU:C_MU + 512] = np.tile(mu, (1, 8))
    c[:64, C_SL:C_SL + 512] = np.tile(sl, (1, 8))
    c[:64, C_I8:C_I8 + 512] = np.tile(np.eye(64, dtype=np.float32), (1, 8))
    sc = np.ones((128, 512), np.float32)
    sc[:, ::64] = 0.0
    c[:, C_SC:C_SC + 512] = sc
    return c


def pv_layout():
    off = {}
    n = 0
    for i in range(DEPTH):
        for nm in ("g1", "b1", "g2", "b2"):
            off[(nm, i)] = n
            n += 8
    for j in range(2):
        off[("conv", j)] = n
        n += 96
        off[("gnw", j)] = n
        n += 1
    off["hnw"] = n
    n += 1
    off["lbl"] = n
    n += 32
    off["lnw"] = n
    n += 2
    off["bgk"] = n
    n += 4
    off["n"] = n
    return off


PV = pv_layout()


def fm(v):
    return np.ascontiguousarray(v.reshape(-1, 128).T)


def make_pv(inp):
    pv = np.zeros((128, PV["n"]), np.float32)
    for i in range(DEPTH):
        pv[:, PV[("g1", i)]:PV[("g1", i)] + 8] = fm(inp["mix_ln_g"][i])
        pv[:, PV[("b1", i)]:PV[("b1", i)] + 8] = fm(inp["mix_ln_b"][i])
        pv[:, PV[("g2", i)]:PV[("g2", i)] + 8] = fm(inp["ffn_ln_g"][i])
        pv[:, PV[("b2", i)]:PV[("b2", i)] + 8] = fm(inp["ffn_ln_b"][i])
    for j in range(2):
        cw = inp["gdn_conv_w"][j]
        pv[:, PV[("conv", j)]:PV[("conv", j)] + 96] = cw.T.reshape(24, 128, 4).transpose(1, 0, 2).reshape(128, 96)
        pv[:, PV[("gnw", j)]] = inp["gdn_norm_w"][j]
    pv[:, PV["hnw"]] = inp["hgrn_norm_w"][0]
    lb = inp["hgrn_lb_logits"]
    pv[:, PV["lbl"]:PV["lbl"] + 32] = lb.T.reshape(8, 128, 4).transpose(1, 0, 2).reshape(128, 32)
    pv[:, PV["lnw"]:PV["lnw"] + 2] = fm(inp["gla_norm_w"][0])
    pv[:, PV["bgk"]:PV["bgk"] + 4] = fm(inp["gla_b_gk"][0])
    return pv


def make_pr(inp):
    pr = np.zeros((128, 32), np.float32)
    for j in range(2):
        pr[:, j * 16:j * 16 + 8] = inp["gdn_a_log"][j][None, :]
        pr[:, j * 16 + 8:j * 16 + 16] = inp["gdn_dt_bias"][j][None, :]
    return pr


class Rot:
    def __init__(self, items):
        self.items = items
        self.i = 0

    def next(self):
        it = self.items[self.i % len(self.items)]
        self.i += 1
        return it


class Prog:
    ENGS = ("pe", "act", "dve", "pool", "sp")
    NSLOT = {"sp": 24, "pool": 8, "act": 4}

    def __init__(self, nc):
        self.nc = nc
        self.ops = []

    def op(self, eng, fn, r=(), w=(), dma=False):
        r = list(r)
        for k in list(r) + list(w):
            if isinstance(k, tuple) and k[0] in ("WA", "WB"):
                ek = ("epoch", k[0])
                if ek not in r and ek not in w:
                    r.append(ek)
        self.ops.append({"eng": eng, "fn": fn, "r": tuple(r), "w": tuple(w), "dma": dma})

    def dma(self, out, in_, r, w, q="sp"):
        self.op(q, lambda e: e.dma_start(out=out, in_=in_), r, w, dma=True)

    def emit(self, stack):
        nc = self.nc
        ops = self.ops
        last_w, readers = {}, {}
        eng_pos = {e: 0 for e in self.ENGS}
        for i, o in enumerate(ops):
            deps = set()
            for k in o["r"]:
                if k in last_w:
                    deps.add(last_w[k])
                if isinstance(k, str) and k[:2] in ("pa", "pd"):
                    for rd in readers.get(k, ()):
                        if ops[rd]["eng"] != o["eng"]:
                            deps.add(rd)
            for k in o["w"]:
                if k in last_w:
                    deps.add(last_w[k])
                deps.update(readers.get(k, ()))
            deps.discard(i)
            o["deps"] = deps
            for k in o["r"]:
                readers.setdefault(k, []).append(i)
            for k in o["w"]:
                last_w[k] = i
                readers[k] = []
            o["pos"] = eng_pos[o["eng"]]
            eng_pos[o["eng"]] += 1
            o["signal"] = False
            o["waits_eng"] = {}
            o["waits_dma"] = {}
        seen = {e: {p: -1 for p in self.ENGS} for e in self.ENGS}
        seen_dma = {e: {} for e in self.ENGS}
        slot_cnt, slot_last = {}, {}
        dma_n = {e: 0 for e in self.ENGS}
        for i, o in enumerate(ops):
            e = o["eng"]
            if o["dma"]:
                s = dma_n[e] % self.NSLOT[e]
                dma_n[e] += 1
                key = (e, s)
                if key in slot_last:
                    o["deps"].add(slot_last[key])
                slot_last[key] = i
                slot_cnt[key] = slot_cnt.get(key, 0) + 1
                o["slot"] = key
                o["slot_val"] = 16 * slot_cnt[key]
            for d in sorted(o["deps"]):
                od = ops[d]
                if od["dma"]:
                    if seen_dma[e].get(od["slot"], 0) < od["slot_val"]:
                        seen_dma[e][od["slot"]] = od["slot_val"]
                        o["waits_dma"][od["slot"]] = od["slot_val"]
                    continue
                pe = od["eng"]
                if pe == e:
                    if e in ("pe", "sp"):
                        continue
                    if o["pos"] - od["pos"] > 2:
                        continue
                if seen[e][pe] >= od["pos"]:
                    continue
                seen[e][pe] = od["pos"]
                od["signal"] = True
                cur = o["waits_eng"].get(pe)
                if cur is None or ops[cur]["pos"] < od["pos"]:
                    o["waits_eng"][pe] = d
        tick = {e: 0 for e in self.ENGS}
        for o in ops:
            if o["signal"]:
                tick[o["eng"]] += 1
                o["tick"] = tick[o["eng"]]
        esem = {e: stack.enter_context(nc.semaphore("s_" + e)) for e in self.ENGS}
        ssem = {k: stack.enter_context(nc.semaphore("d_%s%d" % k)) for k in slot_cnt}
        per_eng = {e: [o for o in ops if o["eng"] == e] for e in self.ENGS}

        def run(ename, eng):
            for o in per_eng[ename]:
                for pe, d in o["waits_eng"].items():
                    eng.wait_ge(esem[pe], ops[d]["tick"])
                for sk, sv in o["waits_dma"].items():
                    eng.wait_ge(ssem[sk], sv)
                inst = o["fn"](eng)
                if o["dma"]:
                    inst.then_inc(ssem[o["slot"]], 16)
                elif o["signal"]:
                    inst.then_inc(esem[ename], 1)
            if ename == "sp":
                for k, c in slot_cnt.items():
                    eng.wait_ge(ssem[k], 16 * c)
                for e2 in self.ENGS:
                    if tick[e2]:
                        eng.wait_ge(esem[e2], tick[e2])

        with nc.Block() as block:
            @block.tensor
            def _(eng):
                run("pe", eng)

            @block.scalar
            def _(eng):
                run("act", eng)

            @block.vector
            def _(eng):
                run("dve", eng)

            @block.gpsimd
            def _(eng):
                run("pool", eng)

            @block.sync
            def _(eng):
                run("sp", eng)


class Builder:
    def __init__(self, TL, layers, dbg=(), upto=9):
        self.upto = upto
        self.TL = TL
        self.NT = TL // TT
        self.layers = layers
        self.dbg = set(dbg)
        self.nc = bass.Bass("TRN2", target_bir_lowering=False)
        self.P = Prog(self.nc)
        self.uid = 0

    def mm(self, out, lhsT, rhs, start, stop, r, w):
        self.P.op("pe", lambda e: e.matmul(out, lhsT, rhs, start=start, stop=stop), r, w)

    def tr(self, out, in_, ident, r, w):
        self.P.op("pe", lambda e: e.transpose(out, in_, ident), r, w)

    def act(self, out, in_, func, r, w, bias=None, scale=None):
        kw = {}
        if bias is not None:
            kw["bias"] = bias
        if scale is not None:
            kw["scale"] = scale
        self.P.op("act", lambda e: e.activation(out=out, in_=in_, func=func, **kw), r, w)

    def tt(self, eng, out, in0, in1, op, r, w):
        self.P.op(eng, lambda e: e.tensor_tensor(out=out, in0=in0, in1=in1, op=op), r, w)

    def ts(self, eng, out, in0, s1, op0, r, w, s2=None, op1=None):
        if op1 is None:
            self.P.op(eng, lambda e: e.tensor_scalar(out=out, in0=in0, scalar1=s1, scalar2=None, op0=op0), r, w)
        else:
            self.P.op(eng, lambda e: e.tensor_scalar(out=out, in0=in0, scalar1=s1, scalar2=s2, op0=op0, op1=op1), r, w)

    def stt(self, eng, out, in0, scalar, in1, op0, op1, r, w):
        self.P.op(eng, lambda e: e.scalar_tensor_tensor(out=out, in0=in0, scalar=scalar, in1=in1, op0=op0, op1=op1), r, w)

    def run_pipe(self, gens, depth):
        active, gens, more = [], iter(gens), True
        while True:
            if more and len(active) < depth:
                try:
                    active.insert(0, next(gens))
                except StopIteration:
                    more = False
            if not active:
                break
            for g in list(active):
                try:
                    next(g)
                except StopIteration:
                    active.remove(g)

    def rsqrt(self, out, in_, scale, eps, r, wk):
        self.act(out, in_, AF.Ln, r, [wk], bias=eps, scale=scale)
        self.act(out, out, AF.Exp, [wk], [wk], scale=-0.5)

    def cp(self, eng, out, in_, r, w):
        if eng == "act":
            self.act(out, in_, AF.Copy, r, w)
        else:
            self.P.op(eng, lambda e: e.tensor_copy(out=out, in_=in_), r, w)

    def red(self, eng, out, in_, r, w):
        self.P.op(eng, lambda e: e.tensor_reduce(out=out, in_=in_, axis=AX.X, op=ALU.add), r, w)

    def scan(self, out, d0, d1, r, w):
        self.P.op("dve", lambda e: e.tensor_tensor_scan(out=out, data0=d0, data1=d1, initial=0.0,
                                                        op0=ALU.mult, op1=ALU.add), r, w)

    def dram(self, name, shape, dt):
        kind = "ExternalOutput" if name in self.dbg else "Internal"
        return self.nc.dram_tensor(name, shape, dt, kind=kind).ap()

    def build(self):
        nc, P, TL, NT = self.nc, self.P, self.TL, self.NT
        st = ExitStack()
        self.st = st
        self.in_names = []

        def ein(n, s):
            self.in_names.append(n)
            return nc.dram_tensor(n, s, F32, kind="ExternalInput").ap()
        self.x_in = ein("x", [TL, D])
        self.cst_d = ein("cst", [128, C_N])
        self.pv_d = ein("pv", [128, PV["n"]])
        self.pr_d = ein("pr", [128, 32])
        self.wgk2_d = ein("wgk2", [16, 512])
        self.w = {}
        for kind, li in self.layers:
            nin = (GDN_IN, HGRN_IN, GLA_IN)[kind]
            self.w[li] = (ein("win%d" % li, [D, nin]) if self.upto >= 1 else None,
                          ein("wout%d" % li, [D, D]) if self.upto >= 1 else None,
                          ein("wf1_%d" % li, [D, 2 * DFF]) if self.upto >= 3 else None,
                          ein("wf2_%d" % li, [DFF, D]) if self.upto >= 3 else None)
        self.out_d = nc.dram_tensor("out", [TL, D], F32, kind="ExternalOutput").ap()
        self.X32 = self.dram("X32", [D, TL], F32)
        self.Xb = self.dram("Xb", [D, TL], BF16)
        self.Hd = self.dram("Hd", [DFF, TL], BF16)
        self.Qd = self.dram("Qd", [D, TL], F32)
        self.Kd = self.dram("Kd", [D, TL], F32)
        self.Gd = self.dram("Gd", [D, TL], F32)
        self.Rd = self.dram("Rd", [D, TL], F32)
        self.Vt = self.dram("Vt", [TL, D], F32)
        self.Kt = self.dram("Kt", [TL, D], F32)
        self.BGd = self.dram("BGd", [TL, 16], F32)

        def sb(name, shape, dt=F32):
            return st.enter_context(nc.sbuf_tensor(name, shape, dt))

        def ps(name, shape, dt=F32):
            return st.enter_context(nc.psum_tensor(name, shape, dt))

        self.cst = sb("cst_s", [128, C_N])
        self.pv = sb("pv_s", [128, PV["n"]])
        self.pr = sb("pr_s", [128, 32])
        self.stg = Rot([(sb("stg%d" % i, [128, SW]), "stg%d" % i) for i in range(2)])
        self.xb = Rot([(sb("xb%d" % i, [128, 8, TT], BF16), "xb%d" % i) for i in range(2)])
        self.x32 = sb("x32", [128, 8, TT])
        self.t512 = Rot([(sb("t512_%d" % i, [128, TT]), "t512_%d" % i) for i in range(4)])
        self.b512 = Rot([(sb("b512_%d" % i, [128, TT], BF16), "b512_%d" % i) for i in range(2)])
        self.s1 = sb("ln_s1", [128, TT])
        self.s2 = sb("ln_s2", [128, TT])
        self.mean = sb("ln_mean", [128, TT])
        self.rstd = sb("ln_rstd", [128, TT])
        self.S32 = sb("S32", [128, 1024])
        self.sm = sb("small", [128, 216])
        self.carry = sb("carry", [128, 24, 4])
        self.dummy = sb("dummy_t", [128, 4])
        self.WA = sb("WA", [128, 23040])
        self.WB = sb("WB", [128, 11264])
        self.pa = Rot([(ps("pa%d" % i, [128, TT]), "pa%d" % i) for i in range(4)])
        self.pd = Rot([(ps("pd%d" % i, [128, 2 * TT]), "pd%d" % i) for i in range(2)])
        self.ident = self.cst[:, C_ID:C_ID + 128]
        self.ones = self.cst[:, C_ONE:C_ONE + 128]

        P.dma(self.cst[:], self.cst_d, [], ["cst"])
        P.dma(self.pv[:], self.pv_d, [], ["pv"])
        P.dma(self.pr[:], self.pr_d, [], ["pr"])
        P.op("pool", lambda e: e.memset(self.S32[:], 0.0), [], ["S32"])
        P.op("pool", lambda e: e.memset(self.carry[:], 0.0), [], ["carry"])

        self.phase0()
        nl = len(self.layers)
        for n, (kind, li) in enumerate(self.layers):
            if self.upto < 1:
                break
            self.wb_load(self.w[li][1], D, D, ("wO", li))
            [self.gdn_A, self.hgrn_A, self.gla_A][kind](li)
            if self.upto < 2:
                break
            if kind == 0:
                self.gdn_B(li)
            elif kind == 1:
                self.gla_B(li, 8, 128, PV["hnw"])
            else:
                self.gla_B(li, 4, 256, PV["lnw"])
            if self.upto < 3:
                break
            self.ffn1(li)
            if self.upto < 4:
                break
            self.ffn2(li, last=(n == nl - 1))
        P.emit(st)
        st.close()
        return nc

    def wa_view(self, kc_n, ncols):
        return self.WA[:, 0:kc_n * ncols // 2].bitcast(BF16).rearrange("p (k n) -> p k n", n=ncols)

    def wb_view(self, kc_n, ncols):
        return self.WB[:, 0:kc_n * ncols // 2].bitcast(BF16).rearrange("p (k n) -> p k n", n=ncols)

    def waf(self, off, shape):
        n = int(np.prod(shape[1:]))
        ap = self.WA[0:shape[0], off:off + n]
        if len(shape) == 3:
            ap = ap.rearrange("p (a b) -> p a b", b=shape[2])
        return ap

    def fence(self, arena):
        d = self.dummy
        self.P.op("pool", lambda e: e.memset(d[:, 0:1], 0.0), [], [("epoch", arena)])

    def load_w(self, dst, w_dram, K, N, arena, name):
        i = 0
        for kc in range(K // 128):
            for n0 in range(0, N, SW):
                nn = min(SW, N - n0)
                s, sk = self.stg.next()
                self.P.dma(s[:, :nn], w_dram[kc * 128:(kc + 1) * 128, n0:n0 + nn], [], [sk])
                self.cp(("pool", "act", "dve")[i % 3], dst[:, kc, n0:n0 + nn], s[:, :nn], [sk],
                        [(arena, name, kc, n0 // SW)])
                i += 1

    def wb_load(self, w_dram, K, N, key):
        self.fence("WB")
        self.load_w(self.wb_view(K // 128, N), w_dram, K, N, "WB", key[0])

    def tile_fm(self, dram, ti, r0=0, nrows=D):
        t0 = ti * TT
        return dram[r0:r0 + nrows, t0:t0 + TT].rearrange("(c p) t -> p c t", p=128)

    @property
    def x32k(self):
        return [("x32", fc) for fc in range(8)]

    def store_x(self, ti, xb, xbk):
        self.P.dma(self.tile_fm(self.X32, ti), self.x32[:], self.x32k, [("X32", ti)], q="sp")
        self.P.dma(self.tile_fm(self.Xb, ti), xb[:], [(xbk, fc) for fc in range(8)], [("Xb", ti)], q="sp")

    def load_xb(self, ti):
        xb, xbk = self.xb.next()
        self.P.dma(xb[:], self.tile_fm(self.Xb, ti), [("Xb", ti)], [(xbk, fc) for fc in range(8)])
        return xb, xbk

    def phase0(self):
        P = self.P
        self.fence("WA")
        xt = self.waf(0, [128, 4, D])
        for ti in range(self.NT):
            t0 = ti * TT
            P.dma(xt, self.x_in[t0:t0 + TT, :].rearrange("(b p) f -> p b f", p=128), [], [("WA", "xt")])
            xb, xbk = self.xb.next()
            for fc in range(8):
                pa, pk = self.pa.next()
                for b in range(4):
                    self.tr(pa[:, b * 128:(b + 1) * 128], xt[:, b, fc * 128:(fc + 1) * 128], self.ident,
                            [("WA", "xt"), "cst"], [pk])
                self.cp("act", self.x32[:, fc, :], pa[:], [pk], [("x32", fc)])
                self.cp("dve", xb[:, fc, :], pa[:], [pk], [(xbk, fc)])
            self.store_x(ti, xb, xbk)

    def ln_acc(self, fc, y_fc, yk):
        if fc == 0:
            self.cp("act", self.s1[:], y_fc, [yk], ["s1"])
            self.act(self.s2[:], y_fc, AF.Square, [yk], ["s2"])
        else:
            self.tt("dve", self.s1[:], self.s1[:], y_fc, ALU.add, [yk, "s1"], ["s1"])
            t, tk = self.t512.next()
            self.act(t[:], y_fc, AF.Square, [yk], [tk])
            self.tt("dve", self.s2[:], self.s2[:], t[:], ALU.add, [tk, "s2"], ["s2"])

    def ln_finish(self, y, yks, gcol, bcol, xb, xbk):
        pa, pk = self.pa.next()
        self.mm(pa[:], self.ones, self.s1[:], True, True, ["cst", "s1"], [pk])
        pb, pbk = self.pa.next()
        self.mm(pb[:], self.ones, self.s2[:], True, True, ["cst", "s2"], [pbk])
        self.ts("dve", self.mean[:], pa[:], 1.0 / D, ALU.mult, [pk], ["mean"])
        t, tk = self.t512.next()
        self.tt("dve", t[:], self.mean[:], self.mean[:], ALU.mult, ["mean"], [tk])
        self.stt("dve", t[:], pb[:], 1.0 / D, t[:], ALU.mult, ALU.subtract, [pbk, tk], [tk])
        self.rsqrt(self.rstd[:], t[:], 1.0, LN_EPS, [tk], "rstd")
        for fc in range(8):
            self.tt("dve", y[:, fc, :], y[:, fc, :], self.mean[:], ALU.subtract, [yks[fc], "mean"], [yks[fc]])
            self.tt("dve", y[:, fc, :], y[:, fc, :], self.rstd[:], ALU.mult, [yks[fc], "rstd"], [yks[fc]])
            self.act(self.x32[:, fc, :], y[:, fc, :], AF.Identity, [yks[fc], "pv"], [("x32", fc)],
                     bias=self.pv[:, bcol + fc:bcol + fc + 1], scale=self.pv[:, gcol + fc:gcol + fc + 1])
            self.cp("pool", xb[:, fc, :], self.x32[:, fc, :], [("x32", fc)], [(xbk, fc)])

    def proj_fm(self, wv, arena, name, col0, xb, xbk, M=128):
        pa, pk = self.pa.next()
        for kc in range(8):
            self.mm(pa[0:M, :], wv[:, kc, col0:col0 + M], xb[:, kc, :], kc == 0, kc == 7,
                    [(arena, name, kc, col0 // SW), (xbk, kc)], [pk])
        return pa, pk

    def proj_tok(self, wv, arena, name, col0, ncol, xb, xbk, b):
        pa, pk = self.pa.next()
        for kc in range(8):
            self.mm(pa[:, 0:ncol], xb[:, kc, b * 128:(b + 1) * 128], wv[:, kc, col0:col0 + ncol], kc == 0, kc == 7,
                    [(arena, name, kc, col0 // SW), (xbk, kc)], [pk])
        return pa, pk

    def out_fm(self, dram, dname, ti, fc, src, sk):
        t0 = ti * TT
        self.P.dma(dram[fc * 128:(fc + 1) * 128, t0:t0 + TT], src, [sk], [(dname, ti, fc)], q="sp")

    def v_tok(self, wv, name, col0, xb, xbk, ti):
        for b in range(4):
            for half in range(2):
                pa, pk = self.proj_tok(wv, "WA", name, col0 + half * 512, 512, xb, xbk, b)
                t, tk = self.t512.next()
                self.cp("act", t[:], pa[:], [pk], [tk])
                r0 = ti * TT + b * 128
                self.P.dma(self.Vt[r0:r0 + 128, half * 512:(half + 1) * 512], t[:], [tk], [("Vt", ti, b, half)], q="sp")

    def ffn1(self, li):
        P = self.P
        self.fence("WA")
        w1 = self.wa_view(8, 2 * DFF)
        self.load_w(w1, self.w[li][2], D, 2 * DFF, "WA", "wF1")
        self.fence("WB")
        self.load_w(self.wb_view(22, D), self.w[li][3], DFF, D, "WB", "wF2")
        for ti in range(self.NT):
            xb, xbk = self.load_xb(ti)
            for j in range(22):
                pg, pgk = self.proj_fm(w1, "WA", "wF1", j * 128, xb, xbk)
                pu, puk = self.proj_fm(w1, "WA", "wF1", DFF + j * 128, xb, xbk)
                t, tk = self.t512.next()
                self.act(t[:], pg[:], AF.Silu, [pgk], [tk])
                hb, hbk = self.b512.next()
                self.tt("dve", hb[:], t[:], pu[:], ALU.mult, [tk, puk], [hbk])
                self.out_fm(self.Hd, "Hd", ti, j, hb[:], hbk)

    def ffn2(self, li, last):
        P = self.P
        self.fence("WA")
        w2 = self.wb_view(22, D)
        hs = [self.WA[:, i * 5632:(i + 1) * 5632].bitcast(BF16).rearrange("p (j t) -> p j t", t=TT) for i in range(2)]
        ys = [self.waf(11264 + i * 4096, [128, 8, TT]) for i in range(2)]
        g2, b2 = PV[("g2", li)], PV[("b2", li)]
        for ti in range(self.NT):
            t0 = ti * TT
            h, hk = hs[ti % 2], ("WA", "h", ti % 2)
            P.dma(h, self.Hd[:, t0:t0 + TT].rearrange("(j p) t -> p j t", p=128),
                  [("Hd", ti, j) for j in range(22)], [hk])
            P.dma(self.x32[:], self.tile_fm(self.X32, ti), [("X32", ti)], self.x32k)
            y = ys[ti % 2]
            yks = [("WA", "y", ti % 2, fc) for fc in range(8)]
            for fc in range(8):
                pa, pk = self.pa.next()
                for j in range(22):
                    self.mm(pa[:], w2[:, j, fc * 128:(fc + 1) * 128], h[:, j, :], j == 0, j == 21,
                            [("WB", "wF2", j, 0), hk], [pk])
                self.stt("dve", y[:, fc, :], self.x32[:, fc, :], ALPHA, pa[:], ALU.mult, ALU.add,
                         [("x32", fc), pk], [yks[fc]])
                self.ln_acc(fc, y[:, fc, :], yks[fc])
            xb, xbk = self.xb.next()
            self.ln_finish(y, yks, g2, b2, xb, xbk)
            if last:
                for b in range(4):
                    for half in range(2):
                        pa, pk = self.pa.next()
                        for q in range(4):
                            self.tr(pa[:, q * 128:(q + 1) * 128], self.x32[:, half * 4 + q, b * 128:(b + 1) * 128],
                                    self.ident, [("x32", half * 4 + q), "cst"], [pk])
                        t, tk = self.t512.next()
                        self.cp("act", t[:], pa[:], [pk], [tk])
                        P.dma(self.out_d[t0 + b * 128:t0 + (b + 1) * 128, half * 512:(half + 1) * 512], t[:],
                              [tk], [("out", ti, b, half)], q="sp")
            else:
                self.store_x(ti, xb, xbk)

    def epilogue(self, li, ti, dv, nwcol, yoff):
        P = self.P
        import os
        if os.environ.get("K_SKIP_EPI"):
            xb, xbk = self.xb.next()
            self.store_x(ti, xb, xbk)
            return
        wO = self.wb_view(8, D)
        nfc = dv // 128
        ob, obk = self.xb.next()
        t0 = ti * TT
        for hd in range(8 // nfc):
            pa, pk = self.pa.next()
            for q in range(nfc):
                fc = hd * nfc + q
                t, tk = self.t512.next()
                self.act(t[:], self.x32[:, fc, :], AF.Square, [("x32", fc)], [tk])
                self.mm(pa[:], self.ones, t[:], q == 0, q == nfc - 1, ["cst", tk], [pk])
            rs, rsk = self.t512.next()
            self.rsqrt(rs[:], pa[:], 1.0 / dv, RMS_EPS, [pk], rsk)
            for q in range(nfc):
                fc = hd * nfc + q
                r, rk = self.t512.next()
                P.dma(r[:], self.Rd[fc * 128:(fc + 1) * 128, t0:t0 + TT], [("Rd", ti, fc)], [rk])
                t, tk = self.t512.next()
                self.tt("dve", t[:], self.x32[:, fc, :], rs[:], ALU.mult, [("x32", fc), rsk], [tk])
                self.stt("dve", ob[:, fc, :], t[:], self.pv[:, nwcol + q:nwcol + q + 1], r[:], ALU.mult, ALU.mult,
                         [tk, rk, "pv"], [(obk, fc)])
        y = self.waf(yoff, [128, 8, TT])
        yks = [("WA", "ye", fc) for fc in range(8)]
        for fo in range(8):
            pa, pk = self.pa.next()
            for kc in range(8):
                self.mm(pa[:], wO[:, kc, fo * 128:(fo + 1) * 128], ob[:, kc, :], kc == 0, kc == 7,
                        [("WB", "wO", kc, 0), (obk, kc)], [pk])
            xr, xrk = self.t512.next()
            P.dma(xr[:], self.X32[fo * 128:(fo + 1) * 128, t0:t0 + TT], [("X32", ti)], [xrk])
            self.stt("dve", y[:, fo, :], xr[:], ALPHA, pa[:], ALU.mult, ALU.add, [xrk, pk], [yks[fo]])
            self.ln_acc(fo, y[:, fo, :], yks[fo])
        xb, xbk = self.xb.next()
        self.ln_finish(y, yks, PV[("g1", li)], PV[("b1", li)], xb, xbk)
        self.store_x(ti, xb, xbk)

    def hgrn_A(self, li):
        P = self.P
        self.fence("WA")
        wA = self.wa_view(8, HGRN_IN)
        self.load_w(wA, self.w[li][0], D, HGRN_IN, "WA", "wA")
        sm = self.sm[:, 128:216]
        lbl = self.pv[:, PV["lbl"]:PV["lbl"] + 32]
        self.act(sm[:, 0:32], lbl, AF.Exp, ["pv"], ["sm"])
        ev = sm[:, 0:32].rearrange("p (c d) -> p c d", d=4)
        self.red("dve", sm[:, 32:40], ev, ["sm"], ["sm1"])
        self.red("dve", sm[:, 40:48], ev[:, :, 1:li + 1], ["sm"], ["sm2"])
        P.op("dve", lambda e: e.reciprocal(out=sm[:, 48:56], in_=sm[:, 32:40]), ["sm1"], ["sm3"])
        self.tt("dve", sm[:, 56:64], sm[:, 40:48], sm[:, 48:56], ALU.mult, ["sm2", "sm3"], ["lb"])
        self.ts("dve", sm[:, 64:72], sm[:, 56:64], -1.0, ALU.mult, ["lb"], ["oml"], s2=1.0, op1=ALU.add)
        for ti in range(self.NT):
            xb, xbk = self.load_xb(ti)
            for fc in range(8):
                pa, pk = self.proj_fm(wA, "WA", "wA", fc * 128, xb, xbk)
                t, tk = self.t512.next()
                self.act(t[:], pa[:], AF.Silu, [pk], [tk])
                self.out_fm(self.Qd, "Qd", ti, fc, t[:], tk)
            for fc in range(8):
                pa, pk = self.proj_fm(wA, "WA", "wA", 1024 + fc * 128, xb, xbk)
                t, tk = self.t512.next()
                self.act(t[:], pa[:], AF.Sigmoid, [pk], [tk])
                self.ts("dve", t[:], t[:], sm[:, 64 + fc:65 + fc], ALU.mult, [tk, "oml", "lb"], [tk],
                        s2=sm[:, 56 + fc:57 + fc], op1=ALU.add)
                g, gk = self.t512.next()
                self.act(g[:], t[:], AF.Ln, [tk], [gk])
                self.out_fm(self.Gd, "Gd", ti, fc, g[:], gk)
                k, kk = self.t512.next()
                self.ts("dve", k[:], t[:], -1.0, ALU.mult, [tk], [kk], s2=1.0, op1=ALU.add)
                self.out_fm(self.Kd, "Kd", ti, fc, k[:], kk)
            self.v_tok(wA, "wA", 2048, xb, xbk, ti)
            for fc in range(8):
                pa, pk = self.proj_fm(wA, "WA", "wA", 3072 + fc * 128, xb, xbk)
                t, tk = self.t512.next()
                self.act(t[:], pa[:], AF.Silu, [pk], [tk])
                self.out_fm(self.Rd, "Rd", ti, fc, t[:], tk)

    def gla_A(self, li):
        P = self.P
        self.fence("WA")
        wA = self.wa_view(8, GLA_IN)
        self.load_w(wA, self.w[li][0], D, GLA_IN, "WA", "wA")
        wg = self.waf(13000, [16, 512])
        P.dma(wg, self.wgk2_d, [], [("WA", "wgk2")])
        for ti in range(self.NT):
            xb, xbk = self.load_xb(ti)
            for fc in range(4):
                pa, pk = self.proj_fm(wA, "WA", "wA", fc * 128, xb, xbk)
                t, tk = self.t512.next()
                self.cp("act", t[:], pa[:], [pk], [tk])
                self.out_fm(self.Qd, "Qd", ti, fc, t[:], tk)
            for fc in range(4):
                pa, pk = self.proj_fm(wA, "WA", "wA", 512 + fc * 128, xb, xbk)
                t, tk = self.t512.next()
                self.cp("act", t[:], pa[:], [pk], [tk])
                self.out_fm(self.Kd, "Kd", ti, fc, t[:], tk)
            self.v_tok(wA, "wA", 1024, xb, xbk, ti)
            for fc in range(8):
                pa, pk = self.proj_fm(wA, "WA", "wA", 2048 + fc * 128, xb, xbk)
                t, tk = self.t512.next()
                self.act(t[:], pa[:], AF.Silu, [pk], [tk])
                self.out_fm(self.Rd, "Rd", ti, fc, t[:], tk)
            pa, pk = self.proj_fm(wA, "WA", "wA", 3072, xb, xbk, M=16)
            gk_, gkk = self.waf(13512, [16, 512]), ("WA", "gk")
            self.cp("act", gk_[0:16, :], pa[0:16, :], [pk], [gkk])
            for fc in range(4):
                pb, pbk = self.pa.next()
                self.mm(pb[:], wg[0:16, fc * 128:(fc + 1) * 128], gk_[0:16, :], True, True, [("WA", "wgk2"), gkk], [pbk])
                t, tk = self.t512.next()
                self.act(t[:], pb[:], AF.Sigmoid, [pbk, "pv"], [tk], bias=self.pv[:, PV["bgk"] + fc:PV["bgk"] + fc + 1])
                self.act(t[:], t[:], AF.Ln, [tk], [tk])
                g, gk2 = self.t512.next()
                self.ts("dve", g[:], t[:], 1.0 / 16.0, ALU.mult, [tk], [gk2])
                self.out_fm(self.Gd, "Gd", ti, fc, g[:], gk2)

    def gla_B(self, li, H, dv, nwcol):
        P = self.P
        self.fence("WA")
        S32 = self.S32
        P.op("pool", lambda e: e.memset(S32[:], 0.0), [], ["S32"])
        q32 = self.waf(0, [128, 4, TT])
        k32 = self.waf(2048, [128, 4, TT])
        g32 = self.waf(4096, [128, 4, TT])
        gm = self.waf(6144, [128, 4, TT])
        AT = self.waf(8192, [64, 4, TT])
        kT = self.WA[0:64, 10240:14336].rearrange("p (n h d) -> p n h d", h=4, d=128)
        Vc = [self.waf(14336 + i * 1024, [64, 1024]) for i in range(2)]
        yoff = 16384
        Gs = self.waf(20480, [128, 4, TT])
        sm = self.sm
        egm = sm[:, 0:32].rearrange("p (h n) -> p h n", n=8)
        egl = sm[:, 32:64].rearrange("p (h n) -> p h n", n=8)
        gam = sm[:, 64:96].rearrange("p (h n) -> p h n", n=8)
        nfc = dv // 128
        scm = self.cst[:, C_SC:C_SC + 512]
        mu8 = self.cst[0:64, C_MU:C_MU + 512]
        scale = 128 ** -0.5
        vi = 0
        import os
        lvl = int(os.environ.get("K_REC_LEVEL", "9"))
        for ti in range(self.NT):
            t0 = ti * TT
            for gi in range(H // 4 if lvl >= 1 else 0):
                h0 = gi * 4
                for (dst, dr, nm) in ((q32, self.Qd, "Qd"), (k32, self.Kd, "Kd"), (g32, self.Gd, "Gd")):
                    P.dma(dst, self.tile_fm(dr, ti, h0 * 128, 512), [(nm, ti, h0 + i) for i in range(4)],
                          [("WA", nm + "t", i) for i in range(4)])
                for hh in range(4 if lvl >= 2 else 0):
                    kq, kk_, kg, kgm = ("WA", "Qdt", hh), ("WA", "Kdt", hh), ("WA", "Gdt", hh), ("WA", "gm", hh)
                    if os.environ.get("K_NOSCAN"):
                        self.cp("dve", Gs[:, hh, :], g32[:, hh, :], [kg], [("WA", "Gs", hh)])
                    else:
                        self.scan(Gs[:, hh, :], scm, g32[:, hh, :], [kg, "cst"], [("WA", "Gs", hh)])
                    kg = ("WA", "Gs", hh)
                    Gv = Gs[:, hh, :].rearrange("p (n c) -> p n c", c=64)
                    gmv = gm[:, hh, :].rearrange("p (n c) -> p n c", c=64)
                    self.tt("dve", gmv, Gv, Gv[:, :, 31:32].broadcast_to([128, 8, 64]), ALU.subtract, [kg], [kgm])
                    e1, e1k = self.t512.next()
                    self.act(e1[:], gm[:, hh, :], AF.Exp, [kgm], [e1k])
                    e2, e2k = self.t512.next()
                    self.act(e2[:], gm[:, hh, :], AF.Exp, [kgm], [e2k], scale=-1.0)
                    self.stt("dve", q32[:, hh, :], q32[:, hh, :], scale, e1[:], ALU.mult, ALU.mult, [kq, e1k], [kq])
                    self.tt("dve", k32[:, hh, :], k32[:, hh, :], e2[:], ALU.mult, [kk_, e2k], [kk_])
                    self.act(egm[:, hh, :], Gv[:, :, 31], AF.Exp, [kg], [("egm", hh)])
                    self.act(egl[:, hh, :], gmv[:, :, 63], AF.Exp, [kgm], [("egl", hh)])
                    self.act(gam[:, hh, :], Gv[:, :, 63], AF.Exp, [kg], [("gam", hh)])
                    if lvl < 3:
                        continue
                    pa, pk = self.pa.next()
                    for n in range(8):
                        self.mm(pa[0:64, n * 64:(n + 1) * 64], k32[:, hh, n * 64:(n + 1) * 64],
                                q32[:, hh, n * 64:(n + 1) * 64], True, True, [kk_, kq], [pk])
                    self.tt("dve", AT[:, hh, :], pa[0:64, :], mu8, ALU.mult, [pk, "cst"], [("WA", "AT", hh)])
                    if lvl < 4:
                        continue
                    qv = q32[:, hh, :].rearrange("p (n c) -> p n c", c=64)
                    kv = k32[:, hh, :].rearrange("p (n c) -> p n c", c=64)
                    self.tt("dve", qv, qv, egm[:, hh, :].unsqueeze(2).broadcast_to([128, 8, 64]), ALU.mult,
                            [kq, ("egm", hh)], [kq])
                    self.tt("dve", kv, kv, egl[:, hh, :].unsqueeze(2).broadcast_to([128, 8, 64]), ALU.mult,
                            [kk_, ("egl", hh)], [kk_])
                    pd, pdk = self.pd.next()
                    for n in range(8):
                        self.tr(pd[0:64, n * 128:(n + 1) * 128], k32[:, hh, n * 64:(n + 1) * 64], self.ident,
                                [kk_, "cst"], [pdk])
                    for hf in range(2):
                        self.cp("act", kT[:, hf * 4:(hf + 1) * 4, hh, :],
                                pd[0:64, hf * 512:(hf + 1) * 512].rearrange("p (n d) -> p n d", d=128), [pdk],
                                [("WA", "kT", hh)])
                for n in range(8 if lvl >= 5 else 0):
                    V = Vc[vi % 2]
                    vk = ("WA", "Vc", vi % 2)
                    vi += 1
                    r0 = t0 + n * 64
                    P.dma(V, self.Vt[r0:r0 + 64, :], [("Vt", ti, n // 2, hf) for hf in range(2)], [vk])
                    po, pok = self.pa.next()
                    for hh in range(4):
                        h = h0 + hh
                        for eh in range(nfc):
                            slot = hh * nfc + eh
                            col = h * dv + eh * 128
                            self.mm(po[:, slot * 64:(slot + 1) * 64], S32[:, col:col + 128],
                                    q32[:, hh, n * 64:(n + 1) * 64], True, False, ["S32", ("WA", "Qdt", hh)], [pok])
                            self.mm(po[:, slot * 64:(slot + 1) * 64], V[:, col:col + 128],
                                    AT[:, hh, n * 64:(n + 1) * 64], False, True, [vk, ("WA", "AT", hh)], [pok])
                    fc0 = h0 * nfc
                    ns = 4 * nfc
                    self.cp("act", self.x32[:, fc0:fc0 + ns, n * 64:(n + 1) * 64],
                            po[:, 0:ns * 64].rearrange("p (s c) -> p s c", c=64), [pok],
                            [("x32", fc0 + s) for s in range(ns)])
                    if lvl < 6:
                        continue
                    pS, pSk = self.pd.next()
                    for hh in range(4):
                        h = h0 + hh
                        self.mm(pS[:, hh * dv:(hh + 1) * dv], kT[:, n, hh, :], V[:, h * dv:(h + 1) * dv], True, True,
                                [("WA", "kT", hh), vk], [pSk])
                    Sv = S32[:, h0 * dv:(h0 + 4) * dv].rearrange("p (h e) -> p h e", e=dv)
                    self.tt("dve", Sv, Sv, gam[:, :, n:n + 1].broadcast_to([128, 4, dv]), ALU.mult,
                            ["S32"] + [("gam", hh) for hh in range(4)], ["S32"])
                    for c0 in range(0, 4 * dv, 512):
                        self.tt("dve", S32[:, h0 * dv + c0:h0 * dv + c0 + 512], S32[:, h0 * dv + c0:h0 * dv + c0 + 512],
                                pS[:, c0:c0 + 512], ALU.add, ["S32", pSk], ["S32"])
            self.epilogue(li, ti, dv, nwcol, yoff)

    def gdn_A(self, li):
        P = self.P
        j = li // 3
        self.fence("WA")
        wA = self.wa_view(8, GDN_IN)
        self.load_w(wA, self.w[li][0], D, GDN_IN, "WA", "wA")
        pres = Rot([(self.waf(16448 + i * 520, [128, 516]), ("WA", "pre", i)) for i in range(3)])
        tl = Rot([(self.waf(18008 + i * 512, [128, 512]), ("WA", "tl", i)) for i in range(7)])
        tkR = Rot([(self.waf(18008 + i * 512, [128, 512]), ("WA", "tl", i)) for i in range(7, 9)])
        sqR = self.t512
        carry = self.carry
        P.op("pool", lambda e: e.memset(carry[:], 0.0), [], [("carry", fc) for fc in range(24)])
        sm = self.sm
        negA = sm[:, 208:216]
        self.act(negA, self.pr[:, j * 16:j * 16 + 8], AF.Exp, ["pr"], ["negA"])
        self.ts("dve", negA, negA, -1.0, ALU.mult, ["negA"], ["negA"])
        cw0 = PV[("conv", j)]
        for ti in range(self.NT):
            t0 = ti * TT
            xb, xbk = self.load_xb(ti)
            def fc_s1(fc, xb=xb, xbk=xbk):
                pa, pk = self.proj_fm(wA, "WA", "wA", fc * 128, xb, xbk)
                pre, prk = pres.next()
                self.cp("pool", pre[:, 0:3], carry[:, fc, 0:3], [("carry", fc)], [prk])
                self.cp("act", pre[:, 3:515], pa[:], [pk], [prk])
                self.cp("pool", carry[:, fc, 0:3], pre[:, 512:515], [prk], [("carry", fc)])
                acc, ak = tl.next()
                wc = lambda k: self.pv[:, cw0 + fc * 4 + k:cw0 + fc * 4 + k + 1]
                self.ts("dve", acc[:], pre[:, 3:515], wc(3), ALU.mult, [prk, "pv"], [ak])
                for k in (2, 1, 0):
                    self.stt("dve", acc[:], pre[:, k:k + 512], wc(k), acc[:], ALU.mult, ALU.add, [prk, ak, "pv"], [ak])
                return acc, ak

            def fc_gen(fcs, ti=ti, t0=t0):
                st = [fc_s1(fc) for fc in fcs]
                sqs = []
                for fc, (s, sk) in zip(fcs, st):
                    self.act(s[:], s[:], AF.Silu, [sk], [sk])
                    if fc < 16:
                        sq, sqk = sqR.next()
                        self.tt("dve", sq[:], s[:], s[:], ALU.mult, [sk], [sqk])
                        sqs.append((sq, sqk))
                yield
                if fcs[0] < 16:
                    pbs = []
                    for (sq, sqk) in sqs:
                        pb, pbk = self.pa.next()
                        self.mm(pb[:], self.ones, sq[:], True, True, ["cst", sqk], [pbk])
                        pbs.append((pb, pbk))
                    for fc, (s, sk), (sq, sqk), (pb, pbk) in zip(fcs, st, sqs, pbs):
                        self.rsqrt(sq[:], pb[:], 1.0, L2_EPS, [pbk, sqk], sqk)
                        if fc < 8:
                            self.stt("dve", s[:], s[:], 128 ** -0.5, sq[:], ALU.mult, ALU.mult, [sk, sqk], [sk])
                            self.out_fm(self.Qd, "Qd", ti, fc, s[:], sk)
                        else:
                            self.tt("dve", s[:], s[:], sq[:], ALU.mult, [sk, sqk], [sk])
                            self.out_fm(self.Kd, "Kd", ti, fc - 8, s[:], sk)
                    if fcs[0] < 8:
                        return
                    yield
                for fc, (s, sk) in zip(fcs, st):
                    dt_, dn, hcol = (self.Kt, "Kt", (fc - 8) * 128) if fc < 16 else (self.Vt, "Vt", (fc - 16) * 128)
                    pt, ptk = self.pa.next()
                    for b in range(4):
                        self.tr(pt[:, b * 128:(b + 1) * 128], s[:, b * 128:(b + 1) * 128], self.ident, [sk, "cst"], [ptk])
                    tk_, tkk = tkR.next()
                    self.cp("act", tk_[:], pt[:], [ptk], [tkk])
                    P.dma(dt_[t0:t0 + TT, hcol:hcol + 128].rearrange("(b p) d -> p b d", p=128),
                          tk_[:].rearrange("p (b d) -> p b d", d=128), [tkk], [(dn, ti, hcol // 128)], q="sp")

            self.run_pipe((fc_gen([fc, fc + 1]) for fc in range(0, 24, 2)), 3)
            for fc in range(8):
                pa, pk = self.proj_fm(wA, "WA", "wA", 3072 + fc * 128, xb, xbk)
                t, tk = tl.next()
                self.act(t[:], pa[:], AF.Silu, [pk], [tk])
                self.out_fm(self.Rd, "Rd", ti, fc, t[:], tk)
            for b in range(4):
                pa, pk = self.proj_tok(wA, "WA", "wA", 4096, 16, xb, xbk, b)
                bg, bgk = tl.next()
                self.act(bg[:, 0:8], pa[:, 0:8], AF.Sigmoid, [pk], [bgk])
                self.tt("dve", bg[:, 16:24], pa[:, 8:16], self.pr[:, j * 16 + 8:j * 16 + 16], ALU.add, [pk, "pr"], [bgk])
                self.act(bg[:, 16:24], bg[:, 16:24], AF.Exp, [bgk], [bgk])
                self.act(bg[:, 16:24], bg[:, 16:24], AF.Ln, [bgk], [bgk], bias=1.0)
                self.tt("dve", bg[:, 8:16], bg[:, 16:24], negA, ALU.mult, [bgk, "negA"], [bgk])
                r0 = t0 + b * 128
                P.dma(self.BGd[r0:r0 + 128, :], bg[:, 0:16], [bgk], [("BGd", ti, b)], q="sp")

    def gdn_B(self, li):
        P = self.P
        j = li // 3
        self.fence("WA")
        S32 = self.S32
        P.op("pool", lambda e: e.memset(S32[:], 0.0), [], ["S32"])
        W = lambda off, shape: self.waf(off, shape)
        qc = W(0, [128, 8, 64])
        kc = W(512, [128, 8, 64])
        ktok = W(1024, [64, 1024])
        vtok = W(2048, [64, 1024])
        bg = W(3072, [64, 16])
        o = 3136
        names512 = ["rhs1", "DL", "DU", "N", "NT", "Pa", "Pb", "PTa", "PTb", "Ta", "Tb", "AT"]
        t5 = {}
        for nm in names512:
            t5[nm] = W(o, [64, 8, 64])
            o += 512
        t10 = {}
        for nm in ["ru", "rw", "u", "vnew", "khat"]:
            t10[nm] = W(o, [64, 1024])
            o += 1024
        qh = W(o, [128, 8, 64])
        o += 512
        wT = W(o, [128, 512])
        o += 512
        eGr = W(o, [128, 512])
        o += 512
        qcs, kcs, bgs = [qc], [kc], [bg]
        qcs.append(W(o, [128, 8, 64]))
        o += 512
        kcs.append(W(o, [128, 8, 64]))
        o += 512
        bgs.append(W(o, [64, 16]))
        o += 64
        yoff = o
        assert yoff + 4096 <= 23040, yoff
        K = lambda nm: ("WA", nm)
        sg = self.sm
        Gcol, eGc, beg, egl, nbeta, tmp8 = (sg[0:64, 0:8], sg[0:64, 8:16], sg[0:64, 16:24], sg[0:64, 24:32],
                                            sg[0:64, 32:40], sg[0:64, 40:48])
        gam = sg[:, 48:56]
        cst = self.cst
        triU = cst[0:64, C_MU:C_MU + 64]
        mu8 = cst[0:64, C_MU:C_MU + 512].rearrange("p (h c) -> p h c", c=64)
        sl8 = cst[0:64, C_SL:C_SL + 512].rearrange("p (h c) -> p h c", c=64)
        i8 = cst[0:64, C_I8:C_I8 + 512].rearrange("p (h c) -> p h c", c=64)
        id64 = cst[0:64, C_ID:C_ID + 64]
        ones64 = cst[0:64, C_ONE:C_ONE + 128]
        v3 = lambda ap: ap.rearrange("p (h c) -> p h c", c=64)
        b8 = lambda ap, n: ap.unsqueeze(2).broadcast_to([64, 8, n])

        def mm8(ps_ap, pk, lhs_of, rhs_of, r, width, npart):
            for h in range(8):
                self.mm(ps_ap[0:npart, h * width:(h + 1) * width], lhs_of(h), rhs_of(h), True, True, r, [pk])

        for ti in range(self.NT):
            for n in range(NCH):
                t0c = ti * TT + n * C
                par = n % 2
                qc, kc, bg = qcs[par], kcs[par], bgs[par]
                kqc, kkc, kbg = K("qc%d" % par), K("kc%d" % par), K("bg%d" % par)
                P.dma(bg, self.BGd[t0c:t0c + C, :], [("BGd", ti, n // 2)], [kbg])
                P.dma(kc, self.Kd[:, t0c:t0c + C].rearrange("(h p) t -> p h t", p=128),
                      [("Kd", ti, f) for f in range(8)], [kkc])
                P.dma(qc, self.Qd[:, t0c:t0c + C].rearrange("(h p) t -> p h t", p=128),
                      [("Qd", ti, f) for f in range(8)], [kqc])
                P.dma(ktok, self.Kt[t0c:t0c + C, :], [("Kt", ti, f) for f in range(8)], [K("ktok")])
                P.dma(vtok, self.Vt[t0c:t0c + C, :], [("Vt", ti, f) for f in range(8)], [K("vtok")])
                beta, g = bg[:, 0:8], bg[:, 8:16]
                pA, pAk = self.pa.next()
                self.mm(pA[0:64, 0:8], triU, g, True, True, ["cst", kbg], [pAk])
                self.mm(pA[:, 8:16], ones64, g, True, True, ["cst", kbg], [pAk])
                self.cp("dve", Gcol, pA[0:64, 0:8], [pAk], ["Gcol"])
                self.tt("dve", tmp8, pA[0:64, 8:16], Gcol, ALU.subtract, [pAk, "Gcol"], ["tmp8"])
                self.act(gam, pA[:, 8:16], AF.Exp, [pAk], ["gam"])
                self.ts("dve", nbeta, beta, -1.0, ALU.mult, [kbg], ["nbeta"])
                self.tt("dve", t5["rhs1"], b8(g, 64), mu8, ALU.mult, [kbg, "cst"], [K("rhs1")])
                pT1, pT1k = self.pa.next()
                self.mm(pT1[:, :], ones64, t5["rhs1"].rearrange("p h c -> p (h c)"), True, True,
                        ["cst", K("rhs1")], [pT1k])
                DL, DU = t5["DL"], t5["DU"]
                self.tt("dve", DL, b8(Gcol, 64), v3(pT1[0:64, :]), ALU.subtract, ["Gcol", pT1k], [K("DL")])
                self.tt("dve", DU, v3(pT1[0:64, :]), b8(Gcol, 64), ALU.subtract, ["Gcol", pT1k], [K("DU")])
                self.act(eGr[:], pT1[:, :], AF.Exp, [pT1k], [K("eGr")])
                self.ts("dve", DL, DL, 0.0, ALU.min, [K("DL")], [K("DL")])
                self.act(DL, DL, AF.Exp, [K("DL")], [K("DL")])
                self.tt("dve", DL, DL, sl8, ALU.mult, [K("DL"), "cst"], [K("DL")])
                pKK, pKKk = self.pa.next()
                mm8(pKK, pKKk, lambda h: kc[:, h, :], lambda h: kc[:, h, :], [kkc], 64, 64)
                N, NT, AT = t5["N"], t5["NT"], t5["AT"]
                self.tt("dve", N, v3(pKK[0:64, :]), DL, ALU.mult, [pKKk, K("DL")], [K("N")])
                self.tt("dve", N, N, b8(nbeta, 64), ALU.mult, [K("N"), "nbeta"], [K("N")])
                pN, pNk = self.pa.next()
                for h in range(8):
                    self.tr(pN[0:64, h * 64:(h + 1) * 64], N[:, h, :], id64, [K("N"), "cst"], [pNk])
                self.cp("act", NT, v3(pN[0:64, :]), [pNk], [K("NT")])
                ru, rw, u, vnew, khat = t10["ru"], t10["rw"], t10["u"], t10["vnew"], t10["khat"]
                v4 = lambda ap: ap.rearrange("p (h d) -> p h d", d=128)

                def d_qk():
                    pQK, pQKk = self.pa.next()
                    mm8(pQK, pQKk, lambda h: kc[:, h, :], lambda h: qc[:, h, :], [kkc, kqc], 64, 64)
                    self.tt("dve", AT, v3(pQK[0:64, :]), DU, ALU.mult, [pQKk, K("DU")], [K("AT")])

                deferred = [
                    lambda: self.ts("dve", DU, DU, 0.0, ALU.min, [K("DU")], [K("DU")]),
                    lambda: self.act(DU, DU, AF.Exp, [K("DU")], [K("DU")]),
                    lambda: self.tt("dve", DU, DU, mu8, ALU.mult, [K("DU"), "cst"], [K("DU")]),
                    d_qk,
                    lambda: self.act(egl, tmp8, AF.Exp, ["tmp8"], ["egl"]),
                    lambda: self.act(eGc, Gcol, AF.Exp, ["Gcol"], ["eGc"]),
                    lambda: self.tt("dve", beg, beta, eGc, ALU.mult, [kbg, "eGc"], ["beg"]),
                    lambda: self.tt("dve", qh, qc, eGr[:].rearrange("p (h c) -> p h c", c=64), ALU.mult,
                                    [kqc, K("eGr")], [K("qh")]),
                    lambda: self.tt("dve", v4(ru), v4(vtok), b8(beta, 128), ALU.mult, [K("vtok"), kbg], [K("ru")]),
                    lambda: self.tt("dve", v4(rw), v4(ktok), b8(beg, 128), ALU.mult, [K("ktok"), "beg"], [K("rw")]),
                    lambda: self.tt("dve", v4(khat), v4(ktok), b8(egl, 128), ALU.mult, [K("ktok"), "egl"], [K("khat")]),
                ]
                Tc, Tk = t5["Ta"], K("Ta")
                self.tt("dve", Tc, NT, i8, ALU.add, [K("NT"), "cst"], [Tk])
                Pp, Ppk, PTp, PTpk = N, K("N"), NT, K("NT")
                for jj in range(1, 6):
                    Pn, Pnk = (t5["Pa"], K("Pa")) if jj % 2 else (t5["Pb"], K("Pb"))
                    PTn, PTnk = (t5["PTa"], K("PTa")) if jj % 2 else (t5["PTb"], K("PTb"))
                    Tn, Tnk = (t5["Tb"], K("Tb")) if jj % 2 else (t5["Ta"], K("Ta"))
                    p1, p1k = self.pa.next()
                    mm8(p1, p1k, lambda h: PTp[:, h, :], lambda h: Pp[:, h, :], [Ppk, PTpk], 64, 64)
                    self.cp("act", Pn, v3(p1[0:64, :]), [p1k], [Pnk])
                    if jj < 5:
                        p2, p2k = self.pa.next()
                        mm8(p2, p2k, lambda h: Pp[:, h, :], lambda h: PTp[:, h, :], [Ppk, PTpk], 64, 64)
                        self.cp("dve", PTn, v3(p2[0:64, :]), [p2k], [PTnk])
                    for _ in range(3):
                        if deferred:
                            deferred.pop(0)()
                    p3, p3k = self.pa.next()
                    mm8(p3, p3k, lambda h: Pn[:, h, :], lambda h: Tc[:, h, :], [Pnk, Tk], 64, 64)
                    self.tt("dve", Tn, v3(p3[0:64, :]), Tc, ALU.add, [p3k, Tk], [Tnk])
                    Pp, Ppk, PTp, PTpk, Tc, Tk = Pn, Pnk, PTn, PTnk, Tn, Tnk
                while deferred:
                    deferred.pop(0)()
                TT_ = Tc
                pU, pUk = self.pd.next()
                mm8(pU, pUk, lambda h: TT_[:, h, :], lambda h: ru[:, h * 128:(h + 1) * 128], [Tk, K("ru")], 128, 64)
                self.cp("act", u[:, 0:512], pU[0:64, 0:512], [pUk], [K("u")])
                self.cp("act", u[:, 512:1024], pU[0:64, 512:1024], [pUk], [K("u")])
                pW, pWk = self.pa.next()
                mm8(pW, pWk, lambda h: rw[:, h * 128:(h + 1) * 128], lambda h: TT_[:, h, :], [Tk, K("rw")], 64, 128)
                self.cp("act", wT[:], pW[:, :], [pWk], [K("wT")])
                pWS, pWSk = self.pd.next()
                mm8(pWS, pWSk, lambda h: wT[:, h * 64:(h + 1) * 64], lambda h: S32[:, h * 128:(h + 1) * 128],
                    [K("wT"), "S32"], 128, 64)
                self.tt("dve", vnew[:, 0:512], u[:, 0:512], pWS[0:64, 0:512], ALU.subtract, [K("u"), pWSk], [K("vnew")])
                self.tt("dve", vnew[:, 512:1024], u[:, 512:1024], pWS[0:64, 512:1024], ALU.subtract, [K("u"), pWSk], [K("vnew")])
                pO, pOk = self.pa.next()
                for h in range(8):
                    self.mm(pO[:, h * 64:(h + 1) * 64], S32[:, h * 128:(h + 1) * 128], qh[:, h, :], True, False,
                            ["S32", K("qh")], [pOk])
                    self.mm(pO[:, h * 64:(h + 1) * 64], vnew[:, h * 128:(h + 1) * 128], AT[:, h, :], False, True,
                            [K("vnew"), K("AT")], [pOk])
                self.cp("act", self.x32[:, :, n * 64:(n + 1) * 64], pO[:, :].rearrange("p (s c) -> p s c", c=64), [pOk],
                        self.x32k)
                pS, pSk = self.pd.next()
                mm8(pS, pSk, lambda h: khat[:, h * 128:(h + 1) * 128], lambda h: vnew[:, h * 128:(h + 1) * 128],
                    [K("khat"), K("vnew")], 128, 128)
                Sv = S32[:, :].rearrange("p (h e) -> p h e", e=128)
                self.tt("dve", Sv, Sv, gam.unsqueeze(2).broadcast_to([128, 8, 128]), ALU.mult, ["S32", "gam"], ["S32"])
                self.tt("dve", S32[:, 0:512], S32[:, 0:512], pS[:, 0:512], ALU.add, ["S32", pSk], ["S32"])
                self.tt("dve", S32[:, 512:1024], S32[:, 512:1024], pS[:, 512:1024], ALU.add, ["S32", pSk], ["S32"])
            self.epilogue(li, ti, 128, PV[("gnw", j)], yoff)


_CACHE = {}


def get_nc(TL, layers, dbg=()):
    key = (TL, tuple(layers), tuple(dbg))
    if key not in _CACHE:
        _CACHE[key] = Builder(TL, list(layers), dbg).build()
    return _CACHE[key]


LAYERS = [(0, 0), (1, 1), (2, 2), (0, 3)]


def core_inputs(inp, xs, layers):
    m = {"x": np.ascontiguousarray(xs, dtype=np.float32), "cst": make_consts(), "pv": make_pv(inp),
         "pr": make_pr(inp), "wgk2": np.ascontiguousarray(inp["gla_w_gk2"][0], dtype=np.float32)}
    for kind, li in layers:
        j = li // 3
        if kind == 0:
            wi, wo = inp["gdn_w_in"][j], inp["gdn_w_out"][j]
        elif kind == 1:
            wi, wo = inp["hgrn_w_in"][0], inp["hgrn_w_out"][0]
        else:
            wi, wo = inp["gla_w_in"][0], inp["gla_w_out"][0]
        m["win%d" % li] = np.ascontiguousarray(wi, dtype=np.float32)
        m["wout%d" % li] = np.ascontiguousarray(wo, dtype=np.float32)
        m["wf1_%d" % li] = np.ascontiguousarray(inp["ffn_w_in"][li], dtype=np.float32)
        m["wf2_%d" % li] = np.ascontiguousarray(inp["ffn_w_down"][li], dtype=np.float32)
    return m


def kernel(**inputs):
    inp = {k: np.asarray(v) for k, v in inputs.items()}
    x = inp["x"]
    B, T, _ = x.shape
    nc = get_nc(T, LAYERS)
    maps = [core_inputs(inp, x[c % B], LAYERS) for c in range(8)]
    res = run_bass_kernel_spmd(nc, maps, core_ids=list(range(8)))
    return np.stack([np.asarray(res.results[b]["out"], dtype=np.float32) for b in range(B)], axis=0)
```

```python
import numpy as np
from contextlib import ExitStack
import concourse.bass as bass
import concourse.mybir as mybir
from concourse.bass_utils import run_bass_kernel_spmd

F32 = mybir.dt.float32
BF16 = mybir.dt.bfloat16
AF = mybir.ActivationFunctionType
ALU = mybir.AluOpType
AX = mybir.AxisListType

D = 1024
DFF = 2816
TT = 512
C = 64
NCH = TT // C
DEPTH = 4
ALPHA = float((2 * DEPTH) ** 0.25)
LN_EPS, RMS_EPS, L2_EPS = 1e-5, 1e-6, 1e-6
GDN_IN, HGRN_IN, GLA_IN = 4112, 4096, 3088
SW = 1024

C_ID, C_ONE, C_MU, C_SL, C_SC, C_I8, C_N = 0, 128, 256, 768, 1280, 1792, 2304


def make_consts():
    c = np.zeros((128, C_N), np.float32)
    c[:, C_ID:C_ID + 128] = np.eye(128)
    c[:, C_ONE:C_ONE + 128] = 1.0
    i = np.arange(64)
    mu = (i[:, None] <= i[None, :]).astype(np.float32)
    sl = (i[:, None] > i[None, :]).astype(np.float32)
    c[:64, C_MU:C_MU + 512] = np.tile(mu, (1, 8))
    c[:64, C_SL:C_SL + 512] = np.tile(sl, (1, 8))
    c[:64, C_I8:C_I8 + 512] = np.tile(np.eye(64, dtype=np.float32), (1, 8))
    sc = np.ones((128, 512), np.float32)
    sc[:, ::64] = 0.0
    c[:, C_SC:C_SC + 512] = sc
    return c


def pv_layout():
    off = {}
    n = 0
    for i in range(DEPTH):
        for nm in ("g1", "b1", "g2", "b2"):
            off[(nm, i)] = n
            n += 8
    for j in range(2):
        off[("conv", j)] = n
        n += 96
        off[("gnw", j)] = n
        n += 1
    off["hnw"] = n
    n += 1
    off["lbl"] = n
    n += 32
    off["lnw"] = n
    n += 2
    off["bgk"] = n
    n += 4
    off["n"] = n
    return off


PV = pv_layout()


def fm(v):
    return np.ascontiguousarray(v.reshape(-1, 128).T)


def make_pv(inp):
    pv = np.zeros((128, PV["n"]), np.float32)
    for i in range(DEPTH):
        pv[:, PV[("g1", i)]:PV[("g1", i)] + 8] = fm(inp["mix_ln_g"][i])
        pv[:, PV[("b1", i)]:PV[("b1", i)] + 8] = fm(inp["mix_ln_b"][i])
        pv[:, PV[("g2", i)]:PV[("g2", i)] + 8] = fm(inp["ffn_ln_g"][i])
        pv[:, PV[("b2", i)]:PV[("b2", i)] + 8] = fm(inp["ffn_ln_b"][i])
    for j in range(2):
        cw = inp["gdn_conv_w"][j]
        pv[:, PV[("conv", j)]:PV[("conv", j)] + 96] = cw.T.reshape(24, 128, 4).transpose(1, 0, 2).reshape(128, 96)
        pv[:, PV[("gnw", j)]] = inp["gdn_norm_w"][j]
    pv[:, PV["hnw"]] = inp["hgrn_norm_w"][0]
    lb = inp["hgrn_lb_logits"]
    pv[:, PV["lbl"]:PV["lbl"] + 32] = lb.T.reshape(8, 128, 4).transpose(1, 0, 2).reshape(128, 32)
    pv[:, PV["lnw"]:PV["lnw"] + 2] = fm(inp["gla_norm_w"][0])
    pv[:, PV["bgk"]:PV["bgk"] + 4] = fm(inp["gla_b_gk"][0])
    return pv


def make_pr(inp):
    pr = np.zeros((128, 32), np.float32)
    for j in range(2):
        pr[:, j * 16:j * 16 + 8] = inp["gdn_a_log"][j][None, :]
        pr[:, j * 16 + 8:j * 16 + 16] = inp["gdn_dt_bias"][j][None, :]
    return pr


class Rot:
    def __init__(self, items):
        self.items = items
        self.i = 0

    def next(self):
        it = self.items[self.i % len(self.items)]
        self.i += 1
        return it


class Prog:
    ENGS = ("pe", "act", "dve", "pool", "sp")
    NSLOT = {"sp": 24, "pool": 8, "act": 4}

    def __init__(self, nc):
        self.nc = nc
        self.ops = []

    def op(self, eng, fn, r=(), w=(), dma=False):
        r = list(r)
        for k in list(r) + list(w):
            if isinstance(k, tuple) and k[0] in ("WA", "WB"):
                ek = ("epoch", k[0])
                if ek not in r and ek not in w:
                    r.append(ek)
        self.ops.append({"eng": eng, "fn": fn, "r": tuple(r), "w": tuple(w), "dma": dma})

    def dma(self, out, in_, r, w, q="sp"):
        self.op(q, lambda e: e.dma_start(out=out, in_=in_), r, w, dma=True)

    def emit(self, stack):
        nc = self.nc
        ops = self.ops
        last_w, readers = {}, {}
        eng_pos = {e: 0 for e in self.ENGS}
        for i, o in enumerate(ops):
            deps = set()
            for k in o["r"]:
                if k in last_w:
                    deps.add(last_w[k])
                if isinstance(k, str) and k[:2] in ("pa", "pd"):
                    for rd in readers.get(k, ()):
                        if ops[rd]["eng"] != o["eng"]:
                            deps.add(rd)
            for k in o["w"]:
                if k in last_w:
                    deps.add(last_w[k])
                deps.update(readers.get(k, ()))
            deps.discard(i)
            o["deps"] = deps
            for k in o["r"]:
                readers.setdefault(k, []).append(i)
            for k in o["w"]:
                last_w[k] = i
                readers[k] = []
            o["pos"] = eng_pos[o["eng"]]
            eng_pos[o["eng"]] += 1
            o["signal"] = False
            o["waits_eng"] = {}
            o["waits_dma"] = {}
        seen = {e: {p: -1 for p in self.ENGS} for e in self.ENGS}
        seen_dma = {e: {} for e in self.ENGS}
        slot_cnt, slot_last = {}, {}
        dma_n = {e: 0 for e in self.ENGS}
        for i, o in enumerate(ops):
            e = o["eng"]
            if o["dma"]:
                s = dma_n[e] % self.NSLOT[e]
                dma_n[e] += 1
                key = (e, s)
                if key in slot_last:
                    o["deps"].add(slot_last[key])
                slot_last[key] = i
                slot_cnt[key] = slot_cnt.get(key, 0) + 1
                o["slot"] = key
                o["slot_val"] = 16 * slot_cnt[key]
            for d in sorted(o["deps"]):
                od = ops[d]
                if od["dma"]:
                    if seen_dma[e].get(od["slot"], 0) < od["slot_val"]:
                        seen_dma[e][od["slot"]] = od["slot_val"]
                        o["waits_dma"][od["slot"]] = od["slot_val"]
                    continue
                pe = od["eng"]
                if pe == e:
                    if e in ("pe", "sp"):
                        continue
                    if o["pos"] - od["pos"] > 2:
                        continue
                if seen[e][pe] >= od["pos"]:
                    continue
                seen[e][pe] = od["pos"]
                od["signal"] = True
                cur = o["waits_eng"].get(pe)
                if cur is None or ops[cur]["pos"] < od["pos"]:
                    o["waits_eng"][pe] = d
        tick = {e: 0 for e in self.ENGS}
        for o in ops:
            if o["signal"]:
                tick[o["eng"]] += 1
                o["tick"] = tick[o["eng"]]
        esem = {e: stack.enter_context(nc.semaphore("s_" + e)) for e in self.ENGS}
        ssem = {k: stack.enter_context(nc.semaphore("d_%s%d" % k)) for k in slot_cnt}
        per_eng = {e: [o for o in ops if o["eng"] == e] for e in self.ENGS}

        def run(ename, eng):
            for o in per_eng[ename]:
                for pe, d in o["waits_eng"].items():
                    eng.wait_ge(esem[pe], ops[d]["tick"])
                for sk, sv in o["waits_dma"].items():
                    eng.wait_ge(ssem[sk], sv)
                inst = o["fn"](eng)
                if o["dma"]:
                    inst.then_inc(ssem[o["slot"]], 16)
                elif o["signal"]:
                    inst.then_inc(esem[ename], 1)
            if ename == "sp":
                for k, c in slot_cnt.items():
                    eng.wait_ge(ssem[k], 16 * c)
                for e2 in self.ENGS:
                    if tick[e2]:
                        eng.wait_ge(esem[e2], tick[e2])

        with nc.Block() as block:
            @block.tensor
            def _(eng):
                run("pe", eng)

            @block.scalar
            def _(eng):
                run("act", eng)

            @block.vector
            def _(eng):
                run("dve", eng)

            @block.gpsimd
            def _(eng):
                run("pool", eng)

            @block.sync
            def _(eng):
                run("sp", eng)


class Builder:
    def __init__(self, TL, layers, dbg=(), upto=9):
        self.upto = upto
        self.TL = TL
        self.NT = TL // TT
        self.layers = layers
        self.dbg = set(dbg)
        self.nc = bass.Bass("TRN2", target_bir_lowering=False)
        self.P = Prog(self.nc)
        self.uid = 0

    def mm(self, out, lhsT, rhs, start, stop, r, w):
        self.P.op("pe", lambda e: e.matmul(out, lhsT, rhs, start=start, stop=stop), r, w)

    def tr(self, out, in_, ident, r, w):
        self.P.op("pe", lambda e: e.transpose(out, in_, ident), r, w)

    def act(self, out, in_, func, r, w, bias=None, scale=None):
        kw = {}
        if bias is not None:
            kw["bias"] = bias
        if scale is not None:
            kw["scale"] = scale
        self.P.op("act", lambda e: e.activation(out=out, in_=in_, func=func, **kw), r, w)

    def tt(self, eng, out, in0, in1, op, r, w):
        self.P.op(eng, lambda e: e.tensor_tensor(out=out, in0=in0, in1=in1, op=op), r, w)

    def ts(self, eng, out, in0, s1, op0, r, w, s2=None, op1=None):
        if op1 is None:
            self.P.op(eng, lambda e: e.tensor_scalar(out=out, in0=in0, scalar1=s1, scalar2=None, op0=op0), r, w)
        else:
            self.P.op(eng, lambda e: e.tensor_scalar(out=out, in0=in0, scalar1=s1, scalar2=s2, op0=op0, op1=op1), r, w)

    def stt(self, eng, out, in0, scalar, in1, op0, op1, r, w):
        self.P.op(eng, lambda e: e.scalar_tensor_tensor(out=out, in0=in0, scalar=scalar, in1=in1, op0=op0, op1=op1), r, w)

    def run_pipe(self, gens, depth):
        active, gens, more = [], iter(gens), True
        while True:
            if more and len(active) < depth:
                try:
                    active.insert(0, next(gens))
                except StopIteration:
                    more = False
            if not active:
                break
            for g in list(active):
                try:
                    next(g)
                except StopIteration:
                    active.remove(g)

    def rsqrt(self, out, in_, scale, eps, r, wk):
        self.act(out, in_, AF.Ln, r, [wk], bias=eps, scale=scale)
        self.act(out, out, AF.Exp, [wk], [wk], scale=-0.5)

    def cp(self, eng, out, in_, r, w):
        if eng == "act":
            self.act(out, in_, AF.Copy, r, w)
        else:
            self.P.op(eng, lambda e: e.tensor_copy(out=out, in_=in_), r, w)

    def red(self, eng, out, in_, r, w):
        self.P.op(eng, lambda e: e.tensor_reduce(out=out, in_=in_, axis=AX.X, op=ALU.add), r, w)

    def scan(self, out, d0, d1, r, w):
        self.P.op("dve", lambda e: e.tensor_tensor_scan(out=out, data0=d0, data1=d1, initial=0.0,
                                                        op0=ALU.mult, op1=ALU.add), r, w)

    def dram(self, name, shape, dt):
        kind = "ExternalOutput" if name in self.dbg else "Internal"
        return self.nc.dram_tensor(name, shape, dt, kind=kind).ap()

    def build(self):
        nc, P, TL, NT = self.nc, self.P, self.TL, self.NT
        st = ExitStack()
        self.st = st
        self.in_names = []

        def ein(n, s):
            self.in_names.append(n)
            return nc.dram_tensor(n, s, F32, kind="ExternalInput").ap()
        self.x_in = ein("x", [TL, D])
        self.cst_d = ein("cst", [128, C_N])
        self.pv_d = ein("pv", [128, PV["n"]])
        self.pr_d = ein("pr", [128, 32])
        self.wgk2_d = ein("wgk2", [16, 512])
        self.w = {}
        for kind, li in self.layers:
            nin = (GDN_IN, HGRN_IN, GLA_IN)[kind]
            self.w[li] = (ein("win%d" % li, [D, nin]) if self.upto >= 1 else None,
                          ein("wout%d" % li, [D, D]) if self.upto >= 1 else None,
                          ein("wf1_%d" % li, [D, 2 * DFF]) if self.upto >= 3 else None,
                          ein("wf2_%d" % li, [DFF, D]) if self.upto >= 3 else None)
        self.out_d = nc.dram_tensor("out", [TL, D], F32, kind="ExternalOutput").ap()
        self.X32 = self.dram("X32", [D, TL], F32)
        self.Xb = self.dram("Xb", [D, TL], BF16)
        self.Hd = self.dram("Hd", [DFF, TL], BF16)
        self.Qd = self.dram("Qd", [D, TL], F32)
        self.Kd = self.dram("Kd", [D, TL], F32)
        self.Gd = self.dram("Gd", [D, TL], F32)
        self.Rd = self.dram("Rd", [D, TL], F32)
        self.Vt = self.dram("Vt", [TL, D], F32)
        self.Kt = self.dram("Kt", [TL, D], F32)
        self.BGd = self.dram("BGd", [TL, 16], F32)

        def sb(name, shape, dt=F32):
            return st.enter_context(nc.sbuf_tensor(name, shape, dt))

        def ps(name, shape, dt=F32):
            return st.enter_context(nc.psum_tensor(name, shape, dt))

        self.cst = sb("cst_s", [128, C_N])
        self.pv = sb("pv_s", [128, PV["n"]])
        self.pr = sb("pr_s", [128, 32])
        self.stg = Rot([(sb("stg%d" % i, [128, SW]), "stg%d" % i) for i in range(2)])
        self.xb = Rot([(sb("xb%d" % i, [128, 8, TT], BF16), "xb%d" % i) for i in range(2)])
        self.x32 = sb("x32", [128, 8, TT])
        self.t512 = Rot([(sb("t512_%d" % i, [128, TT]), "t512_%d" % i) for i in range(4)])
        self.b512 = Rot([(sb("b512_%d" % i, [128, TT], BF16), "b512_%d" % i) for i in range(2)])
        self.s1 = sb("ln_s1", [128, TT])
        self.s2 = sb("ln_s2", [128, TT])
        self.mean = sb("ln_mean", [128, TT])
        self.rstd = sb("ln_rstd", [128, TT])
        self.S32 = sb("S32", [128, 1024])
        self.sm = sb("small", [128, 216])
        self.carry = sb("carry", [128, 24, 4])
        self.dummy = sb("dummy_t", [128, 4])
        self.WA = sb("WA", [128, 23040])
        self.WB = sb("WB", [128, 11264])
        self.pa = Rot([(ps("pa%d" % i, [128, TT]), "pa%d" % i) for i in range(4)])
        self.pd = Rot([(ps("pd%d" % i, [128, 2 * TT]), "pd%d" % i) for i in range(2)])
        self.ident = self.cst[:, C_ID:C_ID + 128]
        self.ones = self.cst[:, C_ONE:C_ONE + 128]

        P.dma(self.cst[:], self.cst_d, [], ["cst"])
        P.dma(self.pv[:], self.pv_d, [], ["pv"])
        P.dma(self.pr[:], self.pr_d, [], ["pr"])
        P.op("pool", lambda e: e.memset(self.S32[:], 0.0), [], ["S32"])
        P.op("pool", lambda e: e.memset(self.carry[:], 0.0), [], ["carry"])

        self.phase0()
        nl = len(self.layers)
        for n, (kind, li) in enumerate(self.layers):
            if self.upto < 1:
                break
            self.wb_load(self.w[li][1], D, D, ("wO", li))
            [self.gdn_A, self.hgrn_A, self.gla_A][kind](li)
            if self.upto < 2:
                break
            if kind == 0:
                self.gdn_B(li)
            elif kind == 1:
                self.gla_B(li, 8, 128, PV["hnw"])
            else:
                self.gla_B(li, 4, 256, PV["lnw"])
            if self.upto < 3:
                break
            self.ffn1(li)
            if self.upto < 4:
                break
            self.ffn2(li, last=(n == nl - 1))
        P.emit(st)
        st.close()
        return nc

    def wa_view(self, kc_n, ncols):
        return self.WA[:, 0:kc_n * ncols // 2].bitcast(BF16).rearrange("p (k n) -> p k n", n=ncols)

    def wb_view(self, kc_n, ncols):
        return self.WB[:, 0:kc_n * ncols // 2].bitcast(BF16).rearrange("p (k n) -> p k n", n=ncols)

    def waf(self, off, shape):
        n = int(np.prod(shape[1:]))
        ap = self.WA[0:shape[0], off:off + n]
        if len(shape) == 3:
            ap = ap.rearrange("p (a b) -> p a b", b=shape[2])
        return ap

    def fence(self, arena):
        d = self.dummy
        self.P.op("pool", lambda e: e.memset(d[:, 0:1], 0.0), [], [("epoch", arena)])

    def load_w(self, dst, w_dram, K, N, arena, name):
        i = 0
        for kc in range(K // 128):
            for n0 in range(0, N, SW):
                nn = min(SW, N - n0)
                s, sk = self.stg.next()
                self.P.dma(s[:, :nn], w_dram[kc * 128:(kc + 1) * 128, n0:n0 + nn], [], [sk])
                self.cp(("pool", "act", "dve")[i % 3], dst[:, kc, n0:n0 + nn], s[:, :nn], [sk],
                        [(arena, name, kc, n0 // SW)])
                i += 1

    def wb_load(self, w_dram, K, N, key):
        self.fence("WB")
        self.load_w(self.wb_view(K // 128, N), w_dram, K, N, "WB", key[0])

    def tile_fm(self, dram, ti, r0=0, nrows=D):
        t0 = ti * TT
        return dram[r0:r0 + nrows, t0:t0 + TT].rearrange("(c p) t -> p c t", p=128)

    @property
    def x32k(self):
        return [("x32", fc) for fc in range(8)]

    def store_x(self, ti, xb, xbk):
        self.P.dma(self.tile_fm(self.X32, ti), self.x32[:], self.x32k, [("X32", ti)], q="sp")
        self.P.dma(self.tile_fm(self.Xb, ti), xb[:], [(xbk, fc) for fc in range(8)], [("Xb", ti)], q="sp")

    def load_xb(self, ti):
        xb, xbk = self.xb.next()
        self.P.dma(xb[:], self.tile_fm(self.Xb, ti), [("Xb", ti)], [(xbk, fc) for fc in range(8)])
        return xb, xbk

    def phase0(self):
        P = self.P
        self.fence("WA")
        xt = self.waf(0, [128, 4, D])
        for ti in range(self.NT):
            t0 = ti * TT
            P.dma(xt, self.x_in[t0:t0 + TT, :].rearrange("(b p) f -> p b f", p=128), [], [("WA", "xt")])
            xb, xbk = self.xb.next()
            for fc in range(8):
                pa, pk = self.pa.next()
                for b in range(4):
                    self.tr(pa[:, b * 128:(b + 1) * 128], xt[:, b, fc * 128:(fc + 1) * 128], self.ident,
                            [("WA", "xt"), "cst"], [pk])
                self.cp("act", self.x32[:, fc, :], pa[:], [pk], [("x32", fc)])
                self.cp("dve", xb[:, fc, :], pa[:], [pk], [(xbk, fc)])
            self.store_x(ti, xb, xbk)

    def ln_acc(self, fc, y_fc, yk):
        if fc == 0:
            self.cp("act", self.s1[:], y_fc, [yk], ["s1"])
            self.act(self.s2[:], y_fc, AF.Square, [yk], ["s2"])
        else:
            self.tt("dve", self.s1[:], self.s1[:], y_fc, ALU.add, [yk, "s1"], ["s1"])
            t, tk = self.t512.next()
            self.act(t[:], y_fc, AF.Square, [yk], [tk])
            self.tt("dve", self.s2[:], self.s2[:], t[:], ALU.add, [tk, "s2"], ["s2"])

    def ln_finish(self, y, yks, gcol, bcol, xb, xbk):
        pa, pk = self.pa.next()
        self.mm(pa[:], self.ones, self.s1[:], True, True, ["cst", "s1"], [pk])
        pb, pbk = self.pa.next()
        self.mm(pb[:], self.ones, self.s2[:], True, True, ["cst", "s2"], [pbk])
        self.ts("dve", self.mean[:], pa[:], 1.0 / D, ALU.mult, [pk], ["mean"])
        t, tk = self.t512.next()
        self.tt("dve", t[:], self.mean[:], self.mean[:], ALU.mult, ["mean"], [tk])
        self.stt("dve", t[:], pb[:], 1.0 / D, t[:], ALU.mult, ALU.subtract, [pbk, tk], [tk])
        self.rsqrt(self.rstd[:], t[:], 1.0, LN_EPS, [tk], "rstd")
        for fc in range(8):
            self.tt("dve", y[:, fc, :], y[:, fc, :], self.mean[:], ALU.subtract, [yks[fc], "mean"], [yks[fc]])
            self.tt("dve", y[:, fc, :], y[:, fc, :], self.rstd[:], ALU.mult, [yks[fc], "rstd"], [yks[fc]])
            self.act(self.x32[:, fc, :], y[:, fc, :], AF.Identity, [yks[fc], "pv"], [("x32", fc)],
                     bias=self.pv[:, bcol + fc:bcol + fc + 1], scale=self.pv[:, gcol + fc:gcol + fc + 1])
            self.cp("pool", xb[:, fc, :], self.x32[:, fc, :], [("x32", fc)], [(xbk, fc)])

    def proj_fm(self, wv, arena, name, col0, xb, xbk, M=128):
        pa, pk = self.pa.next()
        for kc in range(8):
            self.mm(pa[0:M, :], wv[:, kc, col0:col0 + M], xb[:, kc, :], kc == 0, kc == 7,
                    [(arena, name, kc, col0 // SW), (xbk, kc)], [pk])
        return pa, pk

    def proj_tok(self, wv, arena, name, col0, ncol, xb, xbk, b):
        pa, pk = self.pa.next()
        for kc in range(8):
            self.mm(pa[:, 0:ncol], xb[:, kc, b * 128:(b + 1) * 128], wv[:, kc, col0:col0 + ncol], kc == 0, kc == 7,
                    [(arena, name, kc, col0 // SW), (xbk, kc)], [pk])
        return pa, pk

    def out_fm(self, dram, dname, ti, fc, src, sk):
        t0 = ti * TT
        self.P.dma(dram[fc * 128:(fc + 1) * 128, t0:t0 + TT], src, [sk], [(dname, ti, fc)], q="sp")

    def v_tok(self, wv, name, col0, xb, xbk, ti):
        for b in range(4):
            for half in range(2):
                pa, pk = self.proj_tok(wv, "WA", name, col0 + half * 512, 512, xb, xbk, b)
                t, tk = self.t512.next()
                self.cp("act", t[:], pa[:], [pk], [tk])
                r0 = ti * TT + b * 128
                self.P.dma(self.Vt[r0:r0 + 128, half * 512:(half + 1) * 512], t[:], [tk], [("Vt", ti, b, half)], q="sp")

    def ffn1(self, li):
        P = self.P
        self.fence("WA")
        w1 = self.wa_view(8, 2 * DFF)
        self.load_w(w1, self.w[li][2], D, 2 * DFF, "WA", "wF1")
        self.fence("WB")
        self.load_w(self.wb_view(22, D), self.w[li][3], DFF, D, "WB", "wF2")
        for ti in range(self.NT):
            xb, xbk = self.load_xb(ti)
            for j in range(22):
                pg, pgk = self.proj_fm(w1, "WA", "wF1", j * 128, xb, xbk)
                pu, puk = self.proj_fm(w1, "WA", "wF1", DFF + j * 128, xb, xbk)
                t, tk = self.t512.next()
                self.act(t[:], pg[:], AF.Silu, [pgk], [tk])
                hb, hbk = self.b512.next()
                self.tt("dve", hb[:], t[:], pu[:], ALU.mult, [tk, puk], [hbk])
                self.out_fm(self.Hd, "Hd", ti, j, hb[:], hbk)

    def ffn2(self, li, last):
        P = self.P
        self.fence("WA")
        w2 = self.wb_view(22, D)
        hs = [self.WA[:, i * 5632:(i + 1) * 5632].bitcast(BF16).rearrange("p (j t) -> p j t", t=TT) for i in range(2)]
        ys = [self.waf(11264 + i * 4096, [128, 8, TT]) for i in range(2)]
        g2, b2 = PV[("g2", li)], PV[("b2", li)]
        for ti in range(self.NT):
            t0 = ti * TT
            h, hk = hs[ti % 2], ("WA", "h", ti % 2)
            P.dma(h, self.Hd[:, t0:t0 + TT].rearrange("(j p) t -> p j t", p=128),
                  [("Hd", ti, j) for j in range(22)], [hk])
            y = ys[ti % 2]
            yks = [("WA", "y", ti % 2, fc) for fc in range(8)]
            for fc in range(8):
                pa, pk = self.pa.next()
                for j in range(22):
                    self.mm(pa[:], w2[:, j, fc * 128:(fc + 1) * 128], h[:, j, :], j == 0, j == 21,
                            [("WB", "wF2", j, 0), hk], [pk])
                xr, xrk = self.t512.next()
                P.dma(xr[:], self.X32[fc * 128:(fc + 1) * 128, t0:t0 + TT], [("X32", ti)], [xrk])
                self.stt("dve", y[:, fc, :], xr[:], ALPHA, pa[:], ALU.mult, ALU.add, [xrk, pk], [yks[fc]])
                self.ln_acc(fc, y[:, fc, :], yks[fc])
            xb, xbk = self.xb.next()
            self.ln_finish(y, yks, g2, b2, xb, xbk)
            if last:
                for b in range(4):
                    for half in range(2):
                        pa, pk = self.pa.next()
                        for q in range(4):
                            self.tr(pa[:, q * 128:(q + 1) * 128], self.x32[:, half * 4 + q, b * 128:(b + 1) * 128],
                                    self.ident, [("x32", half * 4 + q), "cst"], [pk])
                        t, tk = self.t512.next()
                        self.cp("act", t[:], pa[:], [pk], [tk])
                        P.dma(self.out_d[t0 + b * 128:t0 + (b + 1) * 128, half * 512:(half + 1) * 512], t[:],
                              [tk], [("out", ti, b, half)], q="sp")
            else:
                self.store_x(ti, xb, xbk)

    def epilogue(self, li, ti, dv, nwcol, yoff):
        P = self.P
        import os
        if os.environ.get("K_SKIP_EPI"):
            xb, xbk = self.xb.next()
            self.store_x(ti, xb, xbk)
            return
        wO = self.wb_view(8, D)
        nfc = dv // 128
        ob, obk = self.xb.next()
        t0 = ti * TT
        for hd in range(8 // nfc):
            pa, pk = self.pa.next()
            for q in range(nfc):
                fc = hd * nfc + q
                t, tk = self.t512.next()
                self.act(t[:], self.x32[:, fc, :], AF.Square, [("x32", fc)], [tk])
                self.mm(pa[:], self.ones, t[:], q == 0, q == nfc - 1, ["cst", tk], [pk])
            rs, rsk = self.t512.next()
            self.rsqrt(rs[:], pa[:], 1.0 / dv, RMS_EPS, [pk], rsk)
            for q in range(nfc):
                fc = hd * nfc + q
                r, rk = self.t512.next()
                P.dma(r[:], self.Rd[fc * 128:(fc + 1) * 128, t0:t0 + TT], [("Rd", ti, fc)], [rk])
                t, tk = self.t512.next()
                self.tt("dve", t[:], self.x32[:, fc, :], rs[:], ALU.mult, [("x32", fc), rsk], [tk])
                self.stt("dve", ob[:, fc, :], t[:], self.pv[:, nwcol + q:nwcol + q + 1], r[:], ALU.mult, ALU.mult,
                         [tk, rk, "pv"], [(obk, fc)])
        y = self.waf(yoff, [128, 8, TT])
        yks = [("WA", "ye", fc) for fc in range(8)]
        for fo in range(8):
            pa, pk = self.pa.next()
            for kc in range(8):
                self.mm(pa[:], wO[:, kc, fo * 128:(fo + 1) * 128], ob[:, kc, :], kc == 0, kc == 7,
                        [("WB", "wO", kc, 0), (obk, kc)], [pk])
            xr, xrk = self.t512.next()
            P.dma(xr[:], self.X32[fo * 128:(fo + 1) * 128, t0:t0 + TT], [("X32", ti)], [xrk])
            self.stt("dve", y[:, fo, :], xr[:], ALPHA, pa[:], ALU.mult, ALU.add, [xrk, pk], [yks[fo]])
            self.ln_acc(fo, y[:, fo, :], yks[fo])
        xb, xbk = self.xb.next()
        self.ln_finish(y, yks, PV[("g1", li)], PV[("b1", li)], xb, xbk)
        self.store_x(ti, xb, xbk)

    def hgrn_A(self, li):
        P = self.P
        self.fence("WA")
        wA = self.wa_view(8, HGRN_IN)
        self.load_w(wA, self.w[li][0], D, HGRN_IN, "WA", "wA")
        sm = self.sm[:, 128:216]
        lbl = self.pv[:, PV["lbl"]:PV["lbl"] + 32]
        self.act(sm[:, 0:32], lbl, AF.Exp, ["pv"], ["sm"])
        ev = sm[:, 0:32].rearrange("p (c d) -> p c d", d=4)
        self.red("dve", sm[:, 32:40], ev, ["sm"], ["sm1"])
        self.red("dve", sm[:, 40:48], ev[:, :, 1:li + 1], ["sm"], ["sm2"])
        P.op("dve", lambda e: e.reciprocal(out=sm[:, 48:56], in_=sm[:, 32:40]), ["sm1"], ["sm3"])
        self.tt("dve", sm[:, 56:64], sm[:, 40:48], sm[:, 48:56], ALU.mult, ["sm2", "sm3"], ["lb"])
        self.ts("dve", sm[:, 64:72], sm[:, 56:64], -1.0, ALU.mult, ["lb"], ["oml"], s2=1.0, op1=ALU.add)
        for ti in range(self.NT):
            xb, xbk = self.load_xb(ti)
            for fc in range(8):
                pa, pk = self.proj_fm(wA, "WA", "wA", fc * 128, xb, xbk)
                t, tk = self.t512.next()
                self.act(t[:], pa[:], AF.Silu, [pk], [tk])
                self.out_fm(self.Qd, "Qd", ti, fc, t[:], tk)
            for fc in range(8):
                pa, pk = self.proj_fm(wA, "WA", "wA", 1024 + fc * 128, xb, xbk)
                t, tk = self.t512.next()
                self.act(t[:], pa[:], AF.Sigmoid, [pk], [tk])
                self.ts("dve", t[:], t[:], sm[:, 64 + fc:65 + fc], ALU.mult, [tk, "oml", "lb"], [tk],
                        s2=sm[:, 56 + fc:57 + fc], op1=ALU.add)
                g, gk = self.t512.next()
                self.act(g[:], t[:], AF.Ln, [tk], [gk])
                self.out_fm(self.Gd, "Gd", ti, fc, g[:], gk)
                k, kk = self.t512.next()
                self.ts("dve", k[:], t[:], -1.0, ALU.mult, [tk], [kk], s2=1.0, op1=ALU.add)
                self.out_fm(self.Kd, "Kd", ti, fc, k[:], kk)
            self.v_tok(wA, "wA", 2048, xb, xbk, ti)
            for fc in range(8):
                pa, pk = self.proj_fm(wA, "WA", "wA", 3072 + fc * 128, xb, xbk)
                t, tk = self.t512.next()
                self.act(t[:], pa[:], AF.Silu, [pk], [tk])
                self.out_fm(self.Rd, "Rd", ti, fc, t[:], tk)

    def gla_A(self, li):
        P = self.P
        self.fence("WA")
        wA = self.wa_view(8, GLA_IN)
        self.load_w(wA, self.w[li][0], D, GLA_IN, "WA", "wA")
        wg = self.waf(13000, [16, 512])
        P.dma(wg, self.wgk2_d, [], [("WA", "wgk2")])
        for ti in range(self.NT):
            xb, xbk = self.load_xb(ti)
            for fc in range(4):
                pa, pk = self.proj_fm(wA, "WA", "wA", fc * 128, xb, xbk)
                t, tk = self.t512.next()
                self.cp("act", t[:], pa[:], [pk], [tk])
                self.out_fm(self.Qd, "Qd", ti, fc, t[:], tk)
            for fc in range(4):
                pa, pk = self.proj_fm(wA, "WA", "wA", 512 + fc * 128, xb, xbk)
                t, tk = self.t512.next()
                self.cp("act", t[:], pa[:], [pk], [tk])
                self.out_fm(self.Kd, "Kd", ti, fc, t[:], tk)
            self.v_tok(wA, "wA", 1024, xb, xbk, ti)
            for fc in range(8):
                pa, pk = self.proj_fm(wA, "WA", "wA", 2048 + fc * 128, xb, xbk)
                t, tk = self.t512.next()
                self.act(t[:], pa[:], AF.Silu, [pk], [tk])
                self.out_fm(self.Rd, "Rd", ti, fc, t[:], tk)
            pa, pk = self.proj_fm(wA, "WA", "wA", 3072, xb, xbk, M=16)
            gk_, gkk = self.waf(13512, [16, 512]), ("WA", "gk")
            self.cp("act", gk_[0:16, :], pa[0:16, :], [pk], [gkk])
            for fc in range(4):
                pb, pbk = self.pa.next()
                self.mm(pb[:], wg[0:16, fc * 128:(fc + 1) * 128], gk_[0:16, :], True, True, [("WA", "wgk2"), gkk], [pbk])
                t, tk = self.t512.next()
                self.act(t[:], pb[:], AF.Sigmoid, [pbk, "pv"], [tk], bias=self.pv[:, PV["bgk"] + fc:PV["bgk"] + fc + 1])
                self.act(t[:], t[:], AF.Ln, [tk], [tk])
                g, gk2 = self.t512.next()
                self.ts("dve", g[:], t[:], 1.0 / 16.0, ALU.mult, [tk], [gk2])
                self.out_fm(self.Gd, "Gd", ti, fc, g[:], gk2)

    def gla_B(self, li, H, dv, nwcol):
        P = self.P
        self.fence("WA")
        S32 = self.S32
        P.op("pool", lambda e: e.memset(S32[:], 0.0), [], ["S32"])
        q32 = self.waf(0, [128, 4, TT])
        k32 = self.waf(2048, [128, 4, TT])
        g32 = self.waf(4096, [128, 4, TT])
        gm = self.waf(6144, [128, 4, TT])
        AT = self.waf(8192, [64, 4, TT])
        kT = self.WA[0:64, 10240:14336].rearrange("p (n h d) -> p n h d", h=4, d=128)
        Vc = [self.waf(14336 + i * 1024, [64, 1024]) for i in range(2)]
        yoff = 16384
        Gs = self.waf(20480, [128, 4, TT])
        sm = self.sm
        egm = sm[:, 0:32].rearrange("p (h n) -> p h n", n=8)
        egl = sm[:, 32:64].rearrange("p (h n) -> p h n", n=8)
        gam = sm[:, 64:96].rearrange("p (h n) -> p h n", n=8)
        nfc = dv // 128
        scm = self.cst[:, C_SC:C_SC + 512]
        mu8 = self.cst[0:64, C_MU:C_MU + 512]
        scale = 128 ** -0.5
        vi = 0
        import os
        lvl = int(os.environ.get("K_REC_LEVEL", "9"))
        for ti in range(self.NT):
            t0 = ti * TT
            for gi in range(H // 4 if lvl >= 1 else 0):
                h0 = gi * 4
                for (dst, dr, nm) in ((q32, self.Qd, "Qd"), (k32, self.Kd, "Kd"), (g32, self.Gd, "Gd")):
                    P.dma(dst, self.tile_fm(dr, ti, h0 * 128, 512), [(nm, ti, h0 + i) for i in range(4)],
                          [("WA", nm + "t", i) for i in range(4)])
                for hh in range(4 if lvl >= 2 else 0):
                    kq, kk_, kg, kgm = ("WA", "Qdt", hh), ("WA", "Kdt", hh), ("WA", "Gdt", hh), ("WA", "gm", hh)
                    if os.environ.get("K_NOSCAN"):
                        self.cp("dve", Gs[:, hh, :], g32[:, hh, :], [kg], [("WA", "Gs", hh)])
                    else:
                        self.scan(Gs[:, hh, :], scm, g32[:, hh, :], [kg, "cst"], [("WA", "Gs", hh)])
                    kg = ("WA", "Gs", hh)
                    Gv = Gs[:, hh, :].rearrange("p (n c) -> p n c", c=64)
                    gmv = gm[:, hh, :].rearrange("p (n c) -> p n c", c=64)
                    self.tt("dve", gmv, Gv, Gv[:, :, 31:32].broadcast_to([128, 8, 64]), ALU.subtract, [kg], [kgm])
                    e1, e1k = self.t512.next()
                    self.act(e1[:], gm[:, hh, :], AF.Exp, [kgm], [e1k])
                    e2, e2k = self.t512.next()
                    self.act(e2[:], gm[:, hh, :], AF.Exp, [kgm], [e2k], scale=-1.0)
                    self.stt("dve", q32[:, hh, :], q32[:, hh, :], scale, e1[:], ALU.mult, ALU.mult, [kq, e1k], [kq])
                    self.tt("dve", k32[:, hh, :], k32[:, hh, :], e2[:], ALU.mult, [kk_, e2k], [kk_])
                    self.act(egm[:, hh, :], Gv[:, :, 31], AF.Exp, [kg], [("egm", hh)])
                    self.act(egl[:, hh, :], gmv[:, :, 63], AF.Exp, [kgm], [("egl", hh)])
                    self.act(gam[:, hh, :], Gv[:, :, 63], AF.Exp, [kg], [("gam", hh)])
                    if lvl < 3:
                        continue
                    pa, pk = self.pa.next()
                    for n in range(8):
                        self.mm(pa[0:64, n * 64:(n + 1) * 64], k32[:, hh, n * 64:(n + 1) * 64],
                                q32[:, hh, n * 64:(n + 1) * 64], True, True, [kk_, kq], [pk])
                    self.tt("dve", AT[:, hh, :], pa[0:64, :], mu8, ALU.mult, [pk, "cst"], [("WA", "AT", hh)])
                    if lvl < 4:
                        continue
                    qv = q32[:, hh, :].rearrange("p (n c) -> p n c", c=64)
                    kv = k32[:, hh, :].rearrange("p (n c) -> p n c", c=64)
                    self.tt("dve", qv, qv, egm[:, hh, :].unsqueeze(2).broadcast_to([128, 8, 64]), ALU.mult,
                            [kq, ("egm", hh)], [kq])
                    self.tt("dve", kv, kv, egl[:, hh, :].unsqueeze(2).broadcast_to([128, 8, 64]), ALU.mult,
                            [kk_, ("egl", hh)], [kk_])
                    pd, pdk = self.pd.next()
                    for n in range(8):
                        self.tr(pd[0:64, n * 128:(n + 1) * 128], k32[:, hh, n * 64:(n + 1) * 64], self.ident,
                                [kk_, "cst"], [pdk])
                    for hf in range(2):
                        self.cp("act", kT[:, hf * 4:(hf + 1) * 4, hh, :],
                                pd[0:64, hf * 512:(hf + 1) * 512].rearrange("p (n d) -> p n d", d=128), [pdk],
                                [("WA", "kT", hh)])
                for n in range(8 if lvl >= 5 else 0):
                    V = Vc[vi % 2]
                    vk = ("WA", "Vc", vi % 2)
                    vi += 1
                    r0 = t0 + n * 64
                    P.dma(V, self.Vt[r0:r0 + 64, :], [("Vt", ti, n // 2, hf) for hf in range(2)], [vk])
                    po, pok = self.pa.next()
                    for hh in range(4):
                        h = h0 + hh
                        for eh in range(nfc):
                            slot = hh * nfc + eh
                            col = h * dv + eh * 128
                            self.mm(po[:, slot * 64:(slot + 1) * 64], S32[:, col:col + 128],
                                    q32[:, hh, n * 64:(n + 1) * 64], True, False, ["S32", ("WA", "Qdt", hh)], [pok])
                            self.mm(po[:, slot * 64:(slot + 1) * 64], V[:, col:col + 128],
                                    AT[:, hh, n * 64:(n + 1) * 64], False, True, [vk, ("WA", "AT", hh)], [pok])
                    fc0 = h0 * nfc
                    ns = 4 * nfc
                    self.cp("act", self.x32[:, fc0:fc0 + ns, n * 64:(n + 1) * 64],
                            po[:, 0:ns * 64].rearrange("p (s c) -> p s c", c=64), [pok],
                            [("x32", fc0 + s) for s in range(ns)])
                    if lvl < 6:
                        continue
                    pS, pSk = self.pd.next()
                    for hh in range(4):
                        h = h0 + hh
                        self.mm(pS[:, hh * dv:(hh + 1) * dv], kT[:, n, hh, :], V[:, h * dv:(h + 1) * dv], True, True,
                                [("WA", "kT", hh), vk], [pSk])
                    Sv = S32[:, h0 * dv:(h0 + 4) * dv].rearrange("p (h e) -> p h e", e=dv)
                    self.tt("dve", Sv, Sv, gam[:, :, n:n + 1].broadcast_to([128, 4, dv]), ALU.mult,
                            ["S32"] + [("gam", hh) for hh in range(4)], ["S32"])
                    for c0 in range(0, 4 * dv, 512):
                        self.tt("dve", S32[:, h0 * dv + c0:h0 * dv + c0 + 512], S32[:, h0 * dv + c0:h0 * dv + c0 + 512],
                                pS[:, c0:c0 + 512], ALU.add, ["S32", pSk], ["S32"])
            self.epilogue(li, ti, dv, nwcol, yoff)

    def gdn_A(self, li):
        P = self.P
        j = li // 3
        self.fence("WA")
        wA = self.wa_view(8, GDN_IN)
        self.load_w(wA, self.w[li][0], D, GDN_IN, "WA", "wA")
        pres = Rot([(self.waf(16448 + i * 520, [128, 516]), ("WA", "pre", i)) for i in range(3)])
        tl = Rot([(self.waf(18008 + i * 512, [128, 512]), ("WA", "tl", i)) for i in range(7)])
        tkR = Rot([(self.waf(18008 + i * 512, [128, 512]), ("WA", "tl", i)) for i in range(7, 9)])
        sqR = self.t512
        carry = self.carry
        P.op("pool", lambda e: e.memset(carry[:], 0.0), [], [("carry", fc) for fc in range(24)])
        sm = self.sm
        negA = sm[:, 208:216]
        self.act(negA, self.pr[:, j * 16:j * 16 + 8], AF.Exp, ["pr"], ["negA"])
        self.ts("dve", negA, negA, -1.0, ALU.mult, ["negA"], ["negA"])
        cw0 = PV[("conv", j)]
        for ti in range(self.NT):
            t0 = ti * TT
            xb, xbk = self.load_xb(ti)
            def fc_s1(fc, xb=xb, xbk=xbk):
                pa, pk = self.proj_fm(wA, "WA", "wA", fc * 128, xb, xbk)
                pre, prk = pres.next()
                self.cp("pool", pre[:, 0:3], carry[:, fc, 0:3], [("carry", fc)], [prk])
                self.cp("act", pre[:, 3:515], pa[:], [pk], [prk])
                self.cp("pool", carry[:, fc, 0:3], pre[:, 512:515], [prk], [("carry", fc)])
                acc, ak = tl.next()
                wc = lambda k: self.pv[:, cw0 + fc * 4 + k:cw0 + fc * 4 + k + 1]
                self.ts("dve", acc[:], pre[:, 3:515], wc(3), ALU.mult, [prk, "pv"], [ak])
                for k in (2, 1, 0):
                    self.stt("dve", acc[:], pre[:, k:k + 512], wc(k), acc[:], ALU.mult, ALU.add, [prk, ak, "pv"], [ak])
                return acc, ak

            def fc_gen(fcs, ti=ti, t0=t0):
                st = [fc_s1(fc) for fc in fcs]
                sqs = []
                for fc, (s, sk) in zip(fcs, st):
                    self.act(s[:], s[:], AF.Silu, [sk], [sk])
                    if fc < 16:
                        sq, sqk = sqR.next()
                        self.tt("dve", sq[:], s[:], s[:], ALU.mult, [sk], [sqk])
                        sqs.append((sq, sqk))
                yield
                if fcs[0] < 16:
                    pbs = []
                    for (sq, sqk) in sqs:
                        pb, pbk = self.pa.next()
                        self.mm(pb[:], self.ones, sq[:], True, True, ["cst", sqk], [pbk])
                        pbs.append((pb, pbk))
                    for fc, (s, sk), (sq, sqk), (pb, pbk) in zip(fcs, st, sqs, pbs):
                        self.rsqrt(sq[:], pb[:], 1.0, L2_EPS, [pbk, sqk], sqk)
                        if fc < 8:
                            self.stt("dve", s[:], s[:], 128 ** -0.5, sq[:], ALU.mult, ALU.mult, [sk, sqk], [sk])
                            self.out_fm(self.Qd, "Qd", ti, fc, s[:], sk)
                        else:
                            self.tt("dve", s[:], s[:], sq[:], ALU.mult, [sk, sqk], [sk])
                            self.out_fm(self.Kd, "Kd", ti, fc - 8, s[:], sk)
                    if fcs[0] < 8:
                        return
                    yield
                for fc, (s, sk) in zip(fcs, st):
                    dt_, dn, hcol = (self.Kt, "Kt", (fc - 8) * 128) if fc < 16 else (self.Vt, "Vt", (fc - 16) * 128)
                    pt, ptk = self.pa.next()
                    for b in range(4):
                        self.tr(pt[:, b * 128:(b + 1) * 128], s[:, b * 128:(b + 1) * 128], self.ident, [sk, "cst"], [ptk])
                    tk_, tkk = tkR.next()
                    self.cp("act", tk_[:], pt[:], [ptk], [tkk])
                    P.dma(dt_[t0:t0 + TT, hcol:hcol + 128].rearrange("(b p) d -> p b d", p=128),
                          tk_[:].rearrange("p (b d) -> p b d", d=128), [tkk], [(dn, ti, hcol // 128)], q="sp")

            self.run_pipe((fc_gen([fc, fc + 1]) for fc in range(0, 24, 2)), 3)
            for fc in range(8):
                pa, pk = self.proj_fm(wA, "WA", "wA", 3072 + fc * 128, xb, xbk)
                t, tk = tl.next()
                self.act(t[:], pa[:], AF.Silu, [pk], [tk])
                self.out_fm(self.Rd, "Rd", ti, fc, t[:], tk)
            for b in range(4):
                pa, pk = self.proj_tok(wA, "WA", "wA", 4096, 16, xb, xbk, b)
                bg, bgk = tl.next()
                self.act(bg[:, 0:8], pa[:, 0:8], AF.Sigmoid, [pk], [bgk])
                self.tt("dve", bg[:, 16:24], pa[:, 8:16], self.pr[:, j * 16 + 8:j * 16 + 16], ALU.add, [pk, "pr"], [bgk])
                self.act(bg[:, 16:24], bg[:, 16:24], AF.Exp, [bgk], [bgk])
                self.act(bg[:, 16:24], bg[:, 16:24], AF.Ln, [bgk], [bgk], bias=1.0)
                self.tt("dve", bg[:, 8:16], bg[:, 16:24], negA, ALU.mult, [bgk, "negA"], [bgk])
                r0 = t0 + b * 128
                P.dma(self.BGd[r0:r0 + 128, :], bg[:, 0:16], [bgk], [("BGd", ti, b)], q="sp")

    def gdn_B(self, li):
        P = self.P
        j = li // 3
        self.fence("WA")
        S32 = self.S32
        P.op("pool", lambda e: e.memset(S32[:], 0.0), [], ["S32"])
        W = lambda off, shape: self.waf(off, shape)
        qc = W(0, [128, 8, 64])
        kc = W(512, [128, 8, 64])
        ktok = W(1024, [64, 1024])
        vtok = W(2048, [64, 1024])
        bg = W(3072, [64, 16])
        o = 3136
        names512 = ["rhs1", "DL", "DU", "N", "NT", "Pa", "Pb", "PTa", "PTb", "Ta", "Tb", "AT"]
        t5 = {}
        for nm in names512:
            t5[nm] = W(o, [64, 8, 64])
            o += 512
        t10 = {}
        for nm in ["ru", "rw", "u", "vnew", "khat"]:
            t10[nm] = W(o, [64, 1024])
            o += 1024
        qh = W(o, [128, 8, 64])
        o += 512
        wT = W(o, [128, 512])
        o += 512
        eGr = W(o, [128, 512])
        o += 512
        qcs, kcs, bgs = [qc], [kc], [bg]
        qcs.append(W(o, [128, 8, 64]))
        o += 512
        kcs.append(W(o, [128, 8, 64]))
        o += 512
        bgs.append(W(o, [64, 16]))
        o += 64
        yoff = o
        assert yoff + 4096 <= 23040, yoff
        K = lambda nm: ("WA", nm)
        sg = self.sm
        Gcol, eGc, beg, egl, nbeta, tmp8 = (sg[0:64, 0:8], sg[0:64, 8:16], sg[0:64, 16:24], sg[0:64, 24:32],
                                            sg[0:64, 32:40], sg[0:64, 40:48])
        gam = sg[:, 48:56]
        cst = self.cst
        triU = cst[0:64, C_MU:C_MU + 64]
        mu8 = cst[0:64, C_MU:C_MU + 512].rearrange("p (h c) -> p h c", c=64)
        sl8 = cst[0:64, C_SL:C_SL + 512].rearrange("p (h c) -> p h c", c=64)
        i8 = cst[0:64, C_I8:C_I8 + 512].rearrange("p (h c) -> p h c", c=64)
        id64 = cst[0:64, C_ID:C_ID + 64]
        ones64 = cst[0:64, C_ONE:C_ONE + 128]
        v3 = lambda ap: ap.rearrange("p (h c) -> p h c", c=64)
        b8 = lambda ap, n: ap.unsqueeze(2).broadcast_to([64, 8, n])

        def mm8(ps_ap, pk, lhs_of, rhs_of, r, width, npart):
            for h in range(8):
                self.mm(ps_ap[0:npart, h * width:(h + 1) * width], lhs_of(h), rhs_of(h), True, True, r, [pk])

        for ti in range(self.NT):
            for n in range(NCH):
                t0c = ti * TT + n * C
                par = n % 2
                qc, kc, bg = qcs[par], kcs[par], bgs[par]
                kqc, kkc, kbg = K("qc%d" % par), K("kc%d" % par), K("bg%d" % par)
                P.dma(bg, self.BGd[t0c:t0c + C, :], [("BGd", ti, n // 2)], [kbg])
                P.dma(kc, self.Kd[:, t0c:t0c + C].rearrange("(h p) t -> p h t", p=128),
                      [("Kd", ti, f) for f in range(8)], [kkc])
                P.dma(qc, self.Qd[:, t0c:t0c + C].rearrange("(h p) t -> p h t", p=128),
                      [("Qd", ti, f) for f in range(8)], [kqc])
                P.dma(ktok, self.Kt[t0c:t0c + C, :], [("Kt", ti, f) for f in range(8)], [K("ktok")])
                P.dma(vtok, self.Vt[t0c:t0c + C, :], [("Vt", ti, f) for f in range(8)], [K("vtok")])
                beta, g = bg[:, 0:8], bg[:, 8:16]
                pA, pAk = self.pa.next()
                self.mm(pA[0:64, 0:8], triU, g, True, True, ["cst", kbg], [pAk])
                self.mm(pA[:, 8:16], ones64, g, True, True, ["cst", kbg], [pAk])
                self.cp("dve", Gcol, pA[0:64, 0:8], [pAk], ["Gcol"])
                self.tt("dve", tmp8, pA[0:64, 8:16], Gcol, ALU.subtract, [pAk, "Gcol"], ["tmp8"])
                self.act(gam, pA[:, 8:16], AF.Exp, [pAk], ["gam"])
                self.ts("dve", nbeta, beta, -1.0, ALU.mult, [kbg], ["nbeta"])
                self.tt("dve", t5["rhs1"], b8(g, 64), mu8, ALU.mult, [kbg, "cst"], [K("rhs1")])
                pT1, pT1k = self.pa.next()
                self.mm(pT1[:, :], ones64, t5["rhs1"].rearrange("p h c -> p (h c)"), True, True,
                        ["cst", K("rhs1")], [pT1k])
                DL, DU = t5["DL"], t5["DU"]
                self.tt("dve", DL, b8(Gcol, 64), v3(pT1[0:64, :]), ALU.subtract, ["Gcol", pT1k], [K("DL")])
                self.tt("dve", DU, v3(pT1[0:64, :]), b8(Gcol, 64), ALU.subtract, ["Gcol", pT1k], [K("DU")])
                self.act(eGr[:], pT1[:, :], AF.Exp, [pT1k], [K("eGr")])
                self.ts("dve", DL, DL, 0.0, ALU.min, [K("DL")], [K("DL")])
                self.act(DL, DL, AF.Exp, [K("DL")], [K("DL")])
                self.tt("dve", DL, DL, sl8, ALU.mult, [K("DL"), "cst"], [K("DL")])
                pKK, pKKk = self.pa.next()
                mm8(pKK, pKKk, lambda h: kc[:, h, :], lambda h: kc[:, h, :], [kkc], 64, 64)
                N, NT, AT = t5["N"], t5["NT"], t5["AT"]
                self.tt("dve", N, v3(pKK[0:64, :]), DL, ALU.mult, [pKKk, K("DL")], [K("N")])
                self.tt("dve", N, N, b8(nbeta, 64), ALU.mult, [K("N"), "nbeta"], [K("N")])
                pN, pNk = self.pa.next()
                for h in range(8):
                    self.tr(pN[0:64, h * 64:(h + 1) * 64], N[:, h, :], id64, [K("N"), "cst"], [pNk])
                self.cp("act", NT, v3(pN[0:64, :]), [pNk], [K("NT")])
                ru, rw, u, vnew, khat = t10["ru"], t10["rw"], t10["u"], t10["vnew"], t10["khat"]
                v4 = lambda ap: ap.rearrange("p (h d) -> p h d", d=128)

                def d_qk():
                    pQK, pQKk = self.pa.next()
                    mm8(pQK, pQKk, lambda h: kc[:, h, :], lambda h: qc[:, h, :], [kkc, kqc], 64, 64)
                    self.tt("dve", AT, v3(pQK[0:64, :]), DU, ALU.mult, [pQKk, K("DU")], [K("AT")])

                deferred = [
                    lambda: self.ts("dve", DU, DU, 0.0, ALU.min, [K("DU")], [K("DU")]),
                    lambda: self.act(DU, DU, AF.Exp, [K("DU")], [K("DU")]),
                    lambda: self.tt("dve", DU, DU, mu8, ALU.mult, [K("DU"), "cst"], [K("DU")]),
                    d_qk,
                    lambda: self.act(egl, tmp8, AF.Exp, ["tmp8"], ["egl"]),
                    lambda: self.act(eGc, Gcol, AF.Exp, ["Gcol"], ["eGc"]),
                    lambda: self.tt("dve", beg, beta, eGc, ALU.mult, [kbg, "eGc"], ["beg"]),
                    lambda: self.tt("dve", qh, qc, eGr[:].rearrange("p (h c) -> p h c", c=64), ALU.mult,
                                    [kqc, K("eGr")], [K("qh")]),
                    lambda: self.tt("dve", v4(ru), v4(vtok), b8(beta, 128), ALU.mult, [K("vtok"), kbg], [K("ru")]),
                    lambda: self.tt("dve", v4(rw), v4(ktok), b8(beg, 128), ALU.mult, [K("ktok"), "beg"], [K("rw")]),
                    lambda: self.tt("dve", v4(khat), v4(ktok), b8(egl, 128), ALU.mult, [K("ktok"), "egl"], [K("khat")]),
                ]
                Tc, Tk = t5["Ta"], K("Ta")
                self.tt("dve", Tc, NT, i8, ALU.add, [K("NT"), "cst"], [Tk])
                Pp, Ppk, PTp, PTpk = N, K("N"), NT, K("NT")
                for jj in range(1, 6):
                    Pn, Pnk = (t5["Pa"], K("Pa")) if jj % 2 else (t5["Pb"], K("Pb"))
                    PTn, PTnk = (t5["PTa"], K("PTa")) if jj % 2 else (t5["PTb"], K("PTb"))
                    Tn, Tnk = (t5["Tb"], K("Tb")) if jj % 2 else (t5["Ta"], K("Ta"))
                    p1, p1k = self.pa.next()
                    mm8(p1, p1k, lambda h: PTp[:, h, :], lambda h: Pp[:, h, :], [Ppk, PTpk], 64, 64)
                    self.cp("act", Pn, v3(p1[0:64, :]), [p1k], [Pnk])
                    if jj < 5:
                        p2, p2k = self.pa.next()
                        mm8(p2, p2k, lambda h: Pp[:, h, :], lambda h: PTp[:, h, :], [Ppk, PTpk], 64, 64)
                        self.cp("dve", PTn, v3(p2[0:64, :]), [p2k], [PTnk])
                    for _ in range(3):
                        if deferred:
                            deferred.pop(0)()
                    p3, p3k = self.pa.next()
                    mm8(p3, p3k, lambda h: Pn[:, h, :], lambda h: Tc[:, h, :], [Pnk, Tk], 64, 64)
                    self.tt("dve", Tn, v3(p3[0:64, :]), Tc, ALU.add, [p3k, Tk], [Tnk])
                    Pp, Ppk, PTp, PTpk, Tc, Tk = Pn, Pnk, PTn, PTnk, Tn, Tnk
                while deferred:
                    deferred.pop(0)()
                TT_ = Tc
                pU, pUk = self.pd.next()
                mm8(pU, pUk, lambda h: TT_[:, h, :], lambda h: ru[:, h * 128:(h + 1) * 128], [Tk, K("ru")], 128, 64)
                self.cp("act", u[:, 0:512], pU[0:64, 0:512], [pUk], [K("u")])
                self.cp("act", u[:, 512:1024], pU[0:64, 512:1024], [pUk], [K("u")])
                pW, pWk = self.pa.next()
                mm8(pW, pWk, lambda h: rw[:, h * 128:(h + 1) * 128], lambda h: TT_[:, h, :], [Tk, K("rw")], 64, 128)
                self.cp("act", wT[:], pW[:, :], [pWk], [K("wT")])
                pWS, pWSk = self.pd.next()
                mm8(pWS, pWSk, lambda h: wT[:, h * 64:(h + 1) * 64], lambda h: S32[:, h * 128:(h + 1) * 128],
                    [K("wT"), "S32"], 128, 64)
                self.tt("dve", vnew[:, 0:512], u[:, 0:512], pWS[0:64, 0:512], ALU.subtract, [K("u"), pWSk], [K("vnew")])
                self.tt("dve", vnew[:, 512:1024], u[:, 512:1024], pWS[0:64, 512:1024], ALU.subtract, [K("u"), pWSk], [K("vnew")])
                pO, pOk = self.pa.next()
                for h in range(8):
                    self.mm(pO[:, h * 64:(h + 1) * 64], S32[:, h * 128:(h + 1) * 128], qh[:, h, :], True, False,
                            ["S32", K("qh")], [pOk])
                    self.mm(pO[:, h * 64:(h + 1) * 64], vnew[:, h * 128:(h + 1) * 128], AT[:, h, :], False, True,
                            [K("vnew"), K("AT")], [pOk])
                self.cp("act", self.x32[:, :, n * 64:(n + 1) * 64], pO[:, :].rearrange("p (s c) -> p s c", c=64), [pOk],
                        self.x32k)
                pS, pSk = self.pd.next()
                mm8(pS, pSk, lambda h: khat[:, h * 128:(h + 1) * 128], lambda h: vnew[:, h * 128:(h + 1) * 128],
                    [K("khat"), K("vnew")], 128, 128)
                Sv = S32[:, :].rearrange("p (h e) -> p h e", e=128)
                self.tt("dve", Sv, Sv, gam.unsqueeze(2).broadcast_to([128, 8, 128]), ALU.mult, ["S32", "gam"], ["S32"])
                self.tt("dve", S32[:, 0:512], S32[:, 0:512], pS[:, 0:512], ALU.add, ["S32", pSk], ["S32"])
                self.tt("dve", S32[:, 512:1024], S32[:, 512:1024], pS[:, 512:1024], ALU.add, ["S32", pSk], ["S32"])
            self.epilogue(li, ti, 128, PV[("gnw", j)], yoff)


_CACHE = {}


def get_nc(TL, layers, dbg=()):
    key = (TL, tuple(layers), tuple(dbg))
    if key not in _CACHE:
        _CACHE[key] = Builder(TL, list(layers), dbg).build()
    return _CACHE[key]


LAYERS = [(0, 0), (1, 1), (2, 2), (0, 3)]


def core_inputs(inp, xs, layers):
    m = {"x": np.ascontiguousarray(xs, dtype=np.float32), "cst": make_consts(), "pv": make_pv(inp),
         "pr": make_pr(inp), "wgk2": np.ascontiguousarray(inp["gla_w_gk2"][0], dtype=np.float32)}
    for kind, li in layers:
        j = li // 3
        if kind == 0:
            wi, wo = inp["gdn_w_in"][j], inp["gdn_w_out"][j]
        elif kind == 1:
            wi, wo = inp["hgrn_w_in"][0], inp["hgrn_w_out"][0]
        else:
            wi, wo = inp["gla_w_in"][0], inp["gla_w_out"][0]
        m["win%d" % li] = np.ascontiguousarray(wi, dtype=np.float32)
        m["wout%d" % li] = np.ascontiguousarray(wo, dtype=np.float32)
        m["wf1_%d" % li] = np.ascontiguousarray(inp["ffn_w_in"][li], dtype=np.float32)
        m["wf2_%d" % li] = np.ascontiguousarray(inp["ffn_w_down"][li], dtype=np.float32)
    return m


def kernel(**inputs):
    inp = {k: np.asarray(v) for k, v in inputs.items()}
    x = inp["x"]
    B, T, _ = x.shape
    nc = get_nc(T, LAYERS)
    maps = [core_inputs(inp, x[c % B], LAYERS) for c in range(8)]
    res = run_bass_kernel_spmd(nc, maps, core_ids=list(range(8)))
    return np.stack([np.asarray(res.results[b]["out"], dtype=np.float32) for b in range(B)], axis=0)
```

```python
import numpy as np
from contextlib import ExitStack
import concourse.bass as bass
import concourse.mybir as mybir
from concourse.bass_utils import run_bass_kernel_spmd

F32 = mybir.dt.float32
BF16 = mybir.dt.bfloat16
AF = mybir.ActivationFunctionType
ALU = mybir.AluOpType
AX = mybir.AxisListType

D = 1024
DFF = 2816
TT = 512
C = 64
NCH = TT // C
DEPTH = 4
ALPHA = float((2 * DEPTH) ** 0.25)
LN_EPS, RMS_EPS, L2_EPS = 1e-5, 1e-6, 1e-6
GDN_IN, HGRN_IN, GLA_IN = 4112, 4096, 3088
SW = 1024

C_ID, C_ONE, C_MU, C_SL, C_SC, C_I8, C_N = 0, 128, 256, 768, 1280, 1792, 2304


def make_consts():
    c = np.zeros((128, C_N), np.float32)
    c[:, C_ID:C_ID + 128] = np.eye(128)
    c[:, C_ONE:C_ONE + 128] = 1.0
    i = np.arange(64)
    mu = (i[:, None] <= i[None, :]).astype(np.float32)
    sl = (i[:, None] > i[None, :]).astype(np.float32)
    c[:64, C_MU:C_MU + 512] = np.tile(mu, (1, 8))
    c[:64, C_SL:C_SL + 512] = np.tile(sl, (1, 8))
    c[:64, C_I8:C_I8 + 512] = np.tile(np.eye(64, dtype=np.float32), (1, 8))
    sc = np.ones((128, 512), np.float32)
    sc[:, ::64] = 0.0
    c[:, C_SC:C_SC + 512] = sc
    return c


def pv_layout():
    off = {}
    n = 0
    for i in range(DEPTH):
        for nm in ("g1", "b1", "g2", "b2"):
            off[(nm, i)] = n
            n += 8
    for j in range(2):
        off[("conv", j)] = n
        n += 96
        off[("gnw", j)] = n
        n += 1
    off["hnw"] = n
    n += 1
    off["lbl"] = n
    n += 32
    off["lnw"] = n
    n += 2
    off["bgk"] = n
    n += 4
    off["n"] = n
    return off


PV = pv_layout()


def fm(v):
    return np.ascontiguousarray(v.reshape(-1, 128).T)


def make_pv(inp):
    pv = np.zeros((128, PV["n"]), np.float32)
    for i in range(DEPTH):
        pv[:, PV[("g1", i)]:PV[("g1", i)] + 8] = fm(inp["mix_ln_g"][i])
        pv[:, PV[("b1", i)]:PV[("b1", i)] + 8] = fm(inp["mix_ln_b"][i])
        pv[:, PV[("g2", i)]:PV[("g2", i)] + 8] = fm(inp["ffn_ln_g"][i])
        pv[:, PV[("b2", i)]:PV[("b2", i)] + 8] = fm(inp["ffn_ln_b"][i])
    for j in range(2):
        cw = inp["gdn_conv_w"][j]
        pv[:, PV[("conv", j)]:PV[("conv", j)] + 96] = cw.T.reshape(24, 128, 4).transpose(1, 0, 2).reshape(128, 96)
        pv[:, PV[("gnw", j)]] = inp["gdn_norm_w"][j]
    pv[:, PV["hnw"]] = inp["hgrn_norm_w"][0]
    lb = inp["hgrn_lb_logits"]
    pv[:, PV["lbl"]:PV["lbl"] + 32] = lb.T.reshape(8, 128, 4).transpose(1, 0, 2).reshape(128, 32)
    pv[:, PV["lnw"]:PV["lnw"] + 2] = fm(inp["gla_norm_w"][0])
    pv[:, PV["bgk"]:PV["bgk"] + 4] = fm(inp["gla_b_gk"][0])
    return pv


def make_pr(inp):
    pr = np.zeros((128, 32), np.float32)
    for j in range(2):
        pr[:, j * 16:j * 16 + 8] = inp["gdn_a_log"][j][None, :]
        pr[:, j * 16 + 8:j * 16 + 16] = inp["gdn_dt_bias"][j][None, :]
    return pr


class Rot:
    def __init__(self, items):
        self.items = items
        self.i = 0

    def next(self):
        it = self.items[self.i % len(self.items)]
        self.i += 1
        return it


class Prog:
    ENGS = ("pe", "act", "dve", "pool", "sp")
    NSLOT = {"sp": 24, "pool": 8, "act": 4}

    def __init__(self, nc):
        self.nc = nc
        self.ops = []

    def op(self, eng, fn, r=(), w=(), dma=False):
        r = list(r)
        for k in list(r) + list(w):
            if isinstance(k, tuple) and k[0] in ("WA", "WB"):
                ek = ("epoch", k[0])
                if ek not in r and ek not in w:
                    r.append(ek)
        self.ops.append({"eng": eng, "fn": fn, "r": tuple(r), "w": tuple(w), "dma": dma})

    def dma(self, out, in_, r, w, q="sp"):
        self.op(q, lambda e: e.dma_start(out=out, in_=in_), r, w, dma=True)

    def emit(self, stack):
        nc = self.nc
        ops = self.ops
        last_w, readers = {}, {}
        eng_pos = {e: 0 for e in self.ENGS}
        for i, o in enumerate(ops):
            deps = set()
            for k in o["r"]:
                if k in last_w:
                    deps.add(last_w[k])
                if isinstance(k, str) and k[:2] in ("pa", "pd"):
                    for rd in readers.get(k, ()):
                        if ops[rd]["eng"] != o["eng"]:
                            deps.add(rd)
            for k in o["w"]:
                if k in last_w:
                    deps.add(last_w[k])
                deps.update(readers.get(k, ()))
            deps.discard(i)
            o["deps"] = deps
            for k in o["r"]:
                readers.setdefault(k, []).append(i)
            for k in o["w"]:
                last_w[k] = i
                readers[k] = []
            o["pos"] = eng_pos[o["eng"]]
            eng_pos[o["eng"]] += 1
            o["signal"] = False
            o["waits_eng"] = {}
            o["waits_dma"] = {}
        seen = {e: {p: -1 for p in self.ENGS} for e in self.ENGS}
        seen_dma = {e: {} for e in self.ENGS}
        slot_cnt, slot_last = {}, {}
        dma_n = {e: 0 for e in self.ENGS}
        for i, o in enumerate(ops):
            e = o["eng"]
            if o["dma"]:
                s = dma_n[e] % self.NSLOT[e]
                dma_n[e] += 1
                key = (e, s)
                if key in slot_last:
                    o["deps"].add(slot_last[key])
                slot_last[key] = i
                slot_cnt[key] = slot_cnt.get(key, 0) + 1
                o["slot"] = key
                o["slot_val"] = 16 * slot_cnt[key]
            for d in sorted(o["deps"]):
                od = ops[d]
                if od["dma"]:
                    if seen_dma[e].get(od["slot"], 0) < od["slot_val"]:
                        seen_dma[e][od["slot"]] = od["slot_val"]
                        o["waits_dma"][od["slot"]] = od["slot_val"]
                    continue
                pe = od["eng"]
                if pe == e:
                    if e in ("pe", "sp"):
                        continue
                    if o["pos"] - od["pos"] > 2:
                        continue
                if seen[e][pe] >= od["pos"]:
                    continue
                seen[e][pe] = od["pos"]
                od["signal"] = True
                cur = o["waits_eng"].get(pe)
                if cur is None or ops[cur]["pos"] < od["pos"]:
                    o["waits_eng"][pe] = d
        tick = {e: 0 for e in self.ENGS}
        for o in ops:
            if o["signal"]:
                tick[o["eng"]] += 1
                o["tick"] = tick[o["eng"]]
        esem = {e: stack.enter_context(nc.semaphore("s_" + e)) for e in self.ENGS}
        ssem = {k: stack.enter_context(nc.semaphore("d_%s%d" % k)) for k in slot_cnt}
        per_eng = {e: [o for o in ops if o["eng"] == e] for e in self.ENGS}

        def run(ename, eng):
            for o in per_eng[ename]:
                for pe, d in o["waits_eng"].items():
                    eng.wait_ge(esem[pe], ops[d]["tick"])
                for sk, sv in o["waits_dma"].items():
                    eng.wait_ge(ssem[sk], sv)
                inst = o["fn"](eng)
                if o["dma"]:
                    inst.then_inc(ssem[o["slot"]], 16)
                elif o["signal"]:
                    inst.then_inc(esem[ename], 1)
            if ename == "sp":
                for k, c in slot_cnt.items():
                    eng.wait_ge(ssem[k], 16 * c)
                for e2 in self.ENGS:
                    if tick[e2]:
                        eng.wait_ge(esem[e2], tick[e2])

        with nc.Block() as block:
            @block.tensor
            def _(eng):
                run("pe", eng)

            @block.scalar
            def _(eng):
                run("act", eng)

            @block.vector
            def _(eng):
                run("dve", eng)

            @block.gpsimd
            def _(eng):
                run("pool", eng)

            @block.sync
            def _(eng):
                run("sp", eng)


class Builder:
    def __init__(self, TL, layers, dbg=(), upto=9):
        self.upto = upto
        self.TL = TL
        self.NT = TL // TT
        self.layers = layers
        self.dbg = set(dbg)
        self.nc = bass.Bass("TRN2", target_bir_lowering=False)
        self.P = Prog(self.nc)
        self.uid = 0

    def mm(self, out, lhsT, rhs, start, stop, r, w):
        self.P.op("pe", lambda e: e.matmul(out, lhsT, rhs, start=start, stop=stop), r, w)

    def tr(self, out, in_, ident, r, w):
        self.P.op("pe", lambda e: e.transpose(out, in_, ident), r, w)

    def act(self, out, in_, func, r, w, bias=None, scale=None):
        kw = {}
        if bias is not None:
            kw["bias"] = bias
        if scale is not None:
            kw["scale"] = scale
        self.P.op("act", lambda e: e.activation(out=out, in_=in_, func=func, **kw), r, w)

    def tt(self, eng, out, in0, in1, op, r, w):
        self.P.op(eng, lambda e: e.tensor_tensor(out=out, in0=in0, in1=in1, op=op), r, w)

    def ts(self, eng, out, in0, s1, op0, r, w, s2=None, op1=None):
        if op1 is None:
            self.P.op(eng, lambda e: e.tensor_scalar(out=out, in0=in0, scalar1=s1, scalar2=None, op0=op0), r, w)
        else:
            self.P.op(eng, lambda e: e.tensor_scalar(out=out, in0=in0, scalar1=s1, scalar2=s2, op0=op0, op1=op1), r, w)

    def stt(self, eng, out, in0, scalar, in1, op0, op1, r, w):
        self.P.op(eng, lambda e: e.scalar_tensor_tensor(out=out, in0=in0, scalar=scalar, in1=in1, op0=op0, op1=op1), r, w)

    def run_pipe(self, gens, depth):
        active, gens, more = [], iter(gens), True
        while True:
            if more and len(active) < depth:
                try:
                    active.insert(0, next(gens))
                except StopIteration:
                    more = False
            if not active:
                break
            for g in list(active):
                try:
                    next(g)
                except StopIteration:
                    active.remove(g)

    def rsqrt(self, out, in_, scale, eps, r, wk):
        self.act(out, in_, AF.Ln, r, [wk], bias=eps, scale=scale)
        self.act(out, out, AF.Exp, [wk], [wk], scale=-0.5)

    def cp(self, eng, out, in_, r, w):
        if eng == "act":
            self.act(out, in_, AF.Copy, r, w)
        else:
            self.P.op(eng, lambda e: e.tensor_copy(out=out, in_=in_), r, w)

    def red(self, eng, out, in_, r, w):
        self.P.op(eng, lambda e: e.tensor_reduce(out=out, in_=in_, axis=AX.X, op=ALU.add), r, w)

    def scan(self, out, d0, d1, r, w):
        self.P.op("dve", lambda e: e.tensor_tensor_scan(out=out, data0=d0, data1=d1, initial=0.0,
                                                        op0=ALU.mult, op1=ALU.add), r, w)

    def dram(self, name, shape, dt):
        kind = "ExternalOutput" if name in self.dbg else "Internal"
        return self.nc.dram_tensor(name, shape, dt, kind=kind).ap()

    def build(self):
        nc, P, TL, NT = self.nc, self.P, self.TL, self.NT
        st = ExitStack()
        self.st = st
        self.in_names = []

        def ein(n, s):
            self.in_names.append(n)
            return nc.dram_tensor(n, s, F32, kind="ExternalInput").ap()
        self.x_in = ein("x", [TL, D])
        self.cst_d = ein("cst", [128, C_N])
        self.pv_d = ein("pv", [128, PV["n"]])
        self.pr_d = ein("pr", [128, 32])
        self.wgk2_d = ein("wgk2", [16, 512])
        self.w = {}
        for kind, li in self.layers:
            nin = (GDN_IN, HGRN_IN, GLA_IN)[kind]
            self.w[li] = (ein("win%d" % li, [D, nin]) if self.upto >= 1 else None,
                          ein("wout%d" % li, [D, D]) if self.upto >= 1 else None,
                          ein("wf1_%d" % li, [D, 2 * DFF]) if self.upto >= 3 else None,
                          ein("wf2_%d" % li, [DFF, D]) if self.upto >= 3 else None)
        self.out_d = nc.dram_tensor("out", [TL, D], F32, kind="ExternalOutput").ap()
        self.X32 = self.dram("X32", [D, TL], F32)
        self.Xb = self.dram("Xb", [D, TL], BF16)
        self.Hd = self.dram("Hd", [DFF, TL], BF16)
        self.Qd = self.dram("Qd", [D, TL], F32)
        self.Kd = self.dram("Kd", [D, TL], F32)
        self.Gd = self.dram("Gd", [D, TL], F32)
        self.Rd = self.dram("Rd", [D, TL], F32)
        self.Vt = self.dram("Vt", [TL, D], F32)
        self.Kt = self.dram("Kt", [TL, D], F32)
        self.BGd = self.dram("BGd", [TL, 16], F32)

        def sb(name, shape, dt=F32):
            return st.enter_context(nc.sbuf_tensor(name, shape, dt))

        def ps(name, shape, dt=F32):
            return st.enter_context(nc.psum_tensor(name, shape, dt))

        self.cst = sb("cst_s", [128, C_N])
        self.pv = sb("pv_s", [128, PV["n"]])
        self.pr = sb("pr_s", [128, 32])
        self.stg = Rot([(sb("stg%d" % i, [128, SW]), "stg%d" % i) for i in range(2)])
        self.xb = Rot([(sb("xb%d" % i, [128, 8, TT], BF16), "xb%d" % i) for i in range(2)])
        self.x32 = sb("x32", [128, 8, TT])
        self.t512 = Rot([(sb("t512_%d" % i, [128, TT]), "t512_%d" % i) for i in range(4)])
        self.b512 = Rot([(sb("b512_%d" % i, [128, TT], BF16), "b512_%d" % i) for i in range(2)])
        self.s1 = sb("ln_s1", [128, TT])
        self.s2 = sb("ln_s2", [128, TT])
        self.mean = sb("ln_mean", [128, TT])
        self.rstd = sb("ln_rstd", [128, TT])
        self.S32 = sb("S32", [128, 1024])
        self.sm = sb("small", [128, 216])
        self.carry = sb("carry", [128, 24, 4])
        self.dummy = sb("dummy_t", [128, 4])
        self.WA = sb("WA", [128, 23040])
        self.WB = sb("WB", [128, 11264])
        self.pa = Rot([(ps("pa%d" % i, [128, TT]), "pa%d" % i) for i in range(4)])
        self.pd = Rot([(ps("pd%d" % i, [128, 2 * TT]), "pd%d" % i) for i in range(2)])
        self.ident = self.cst[:, C_ID:C_ID + 128]
        self.ones = self.cst[:, C_ONE:C_ONE + 128]

        P.dma(self.cst[:], self.cst_d, [], ["cst"])
        P.dma(self.pv[:], self.pv_d, [], ["pv"])
        P.dma(self.pr[:], self.pr_d, [], ["pr"])
        P.op("pool", lambda e: e.memset(self.S32[:], 0.0), [], ["S32"])
        P.op("pool", lambda e: e.memset(self.carry[:], 0.0), [], ["carry"])

        self.phase0()
        nl = len(self.layers)
        for n, (kind, li) in enumerate(self.layers):
            if self.upto < 1:
                break
            self.wb_load(self.w[li][1], D, D, ("wO", li))
            [self.gdn_A, self.hgrn_A, self.gla_A][kind](li)
            if self.upto < 2:
                break
            if kind == 0:
                self.gdn_B(li)
            elif kind == 1:
                self.gla_B(li, 8, 128, PV["hnw"])
            else:
                self.gla_B(li, 4, 256, PV["lnw"])
            if self.upto < 3:
                break
            self.ffn1(li)
            if self.upto < 4:
                break
            self.ffn2(li, last=(n == nl - 1))
        P.emit(st)
        st.close()
        return nc

    def wa_view(self, kc_n, ncols):
        return self.WA[:, 0:kc_n * ncols // 2].bitcast(BF16).rearrange("p (k n) -> p k n", n=ncols)

    def wb_view(self, kc_n, ncols):
        return self.WB[:, 0:kc_n * ncols // 2].bitcast(BF16).rearrange("p (k n) -> p k n", n=ncols)

    def waf(self, off, shape):
        n = int(np.prod(shape[1:]))
        ap = self.WA[0:shape[0], off:off + n]
        if len(shape) == 3:
            ap = ap.rearrange("p (a b) -> p a b", b=shape[2])
        return ap

    def fence(self, arena):
        d = self.dummy
        self.P.op("pool", lambda e: e.memset(d[:, 0:1], 0.0), [], [("epoch", arena)])

    def load_w(self, dst, w_dram, K, N, arena, name):
        i = 0
        for kc in range(K // 128):
            for n0 in range(0, N, SW):
                nn = min(SW, N - n0)
                s, sk = self.stg.next()
                self.P.dma(s[:, :nn], w_dram[kc * 128:(kc + 1) * 128, n0:n0 + nn], [], [sk])
                self.cp(("pool", "act", "dve")[i % 3], dst[:, kc, n0:n0 + nn], s[:, :nn], [sk],
                        [(arena, name, kc, n0 // SW)])
                i += 1

    def wb_load(self, w_dram, K, N, key):
        self.fence("WB")
        self.load_w(self.wb_view(K // 128, N), w_dram, K, N, "WB", key[0])

    def tile_fm(self, dram, ti, r0=0, nrows=D):
        t0 = ti * TT
        return dram[r0:r0 + nrows, t0:t0 + TT].rearrange("(c p) t -> p c t", p=128)

    @property
    def x32k(self):
        return [("x32", fc) for fc in range(8)]

    def store_x(self, ti, xb, xbk):
        self.P.dma(self.tile_fm(self.X32, ti), self.x32[:], self.x32k, [("X32", ti)], q="sp")
        self.P.dma(self.tile_fm(self.Xb, ti), xb[:], [(xbk, fc) for fc in range(8)], [("Xb", ti)], q="sp")

    def load_xb(self, ti):
        xb, xbk = self.xb.next()
        self.P.dma(xb[:], self.tile_fm(self.Xb, ti), [("Xb", ti)], [(xbk, fc) for fc in range(8)])
        return xb, xbk

    def phase0(self):
        P = self.P
        self.fence("WA")
        xt = self.waf(0, [128, 4, D])
        for ti in range(self.NT):
            t0 = ti * TT
            P.dma(xt, self.x_in[t0:t0 + TT, :].rearrange("(b p) f -> p b f", p=128), [], [("WA", "xt")])
            xb, xbk = self.xb.next()
            for fc in range(8):
                pa, pk = self.pa.next()
                for b in range(4):
                    self.tr(pa[:, b * 128:(b + 1) * 128], xt[:, b, fc * 128:(fc + 1) * 128], self.ident,
                            [("WA", "xt"), "cst"], [pk])
                self.cp("act", self.x32[:, fc, :], pa[:], [pk], [("x32", fc)])
                self.cp("dve", xb[:, fc, :], pa[:], [pk], [(xbk, fc)])
            self.store_x(ti, xb, xbk)

    def ln_acc(self, fc, y_fc, yk):
        if fc == 0:
            self.cp("act", self.s1[:], y_fc, [yk], ["s1"])
            self.act(self.s2[:], y_fc, AF.Square, [yk], ["s2"])
        else:
            self.tt("dve", self.s1[:], self.s1[:], y_fc, ALU.add, [yk, "s1"], ["s1"])
            t, tk = self.t512.next()
            self.act(t[:], y_fc, AF.Square, [yk], [tk])
            self.tt("dve", self.s2[:], self.s2[:], t[:], ALU.add, [tk, "s2"], ["s2"])

    def ln_finish(self, y, yks, gcol, bcol, xb, xbk):
        pa, pk = self.pa.next()
        self.mm(pa[:], self.ones, self.s1[:], True, True, ["cst", "s1"], [pk])
        pb, pbk = self.pa.next()
        self.mm(pb[:], self.ones, self.s2[:], True, True, ["cst", "s2"], [pbk])
        self.ts("dve", self.mean[:], pa[:], 1.0 / D, ALU.mult, [pk], ["mean"])
        t, tk = self.t512.next()
        self.tt("dve", t[:], self.mean[:], self.mean[:], ALU.mult, ["mean"], [tk])
        self.stt("dve", t[:], pb[:], 1.0 / D, t[:], ALU.mult, ALU.subtract, [pbk, tk], [tk])
        self.rsqrt(self.rstd[:], t[:], 1.0, LN_EPS, [tk], "rstd")
        for fc in range(8):
            self.tt("dve", y[:, fc, :], y[:, fc, :], self.mean[:], ALU.subtract, [yks[fc], "mean"], [yks[fc]])
            self.tt("dve", y[:, fc, :], y[:, fc, :], self.rstd[:], ALU.mult, [yks[fc], "rstd"], [yks[fc]])
            self.act(self.x32[:, fc, :], y[:, fc, :], AF.Identity, [yks[fc], "pv"], [("x32", fc)],
                     bias=self.pv[:, bcol + fc:bcol + fc + 1], scale=self.pv[:, gcol + fc:gcol + fc + 1])
            self.cp("pool", xb[:, fc, :], self.x32[:, fc, :], [("x32", fc)], [(xbk, fc)])

    def proj_fm(self, wv, arena, name, col0, xb, xbk, M=128):
        pa, pk = self.pa.next()
        for kc in range(8):
            self.mm(pa[0:M, :], wv[:, kc, col0:col0 + M], xb[:, kc, :], kc == 0, kc == 7,
                    [(arena, name, kc, col0 // SW), (xbk, kc)], [pk])
        return pa, pk

    def proj_tok(self, wv, arena, name, col0, ncol, xb, xbk, b):
        pa, pk = self.pa.next()
        for kc in range(8):
            self.mm(pa[:, 0:ncol], xb[:, kc, b * 128:(b + 1) * 128], wv[:, kc, col0:col0 + ncol], kc == 0, kc == 7,
                    [(arena, name, kc, col0 // SW), (xbk, kc)], [pk])
        return pa, pk

    def out_fm(self, dram, dname, ti, fc, src, sk):
        t0 = ti * TT
        self.P.dma(dram[fc * 128:(fc + 1) * 128, t0:t0 + TT], src, [sk], [(dname, ti, fc)], q="sp")

    def v_tok(self, wv, name, col0, xb, xbk, ti):
        for b in range(4):
            for half in range(2):
                pa, pk = self.proj_tok(wv, "WA", name, col0 + half * 512, 512, xb, xbk, b)
                t, tk = self.t512.next()
                self.cp("act", t[:], pa[:], [pk], [tk])
                r0 = ti * TT + b * 128
                self.P.dma(self.Vt[r0:r0 + 128, half * 512:(half + 1) * 512], t[:], [tk], [("Vt", ti, b, half)], q="sp")

    def ffn1(self, li):
        P = self.P
        self.fence("WA")
        w1 = self.wa_view(8, 2 * DFF)
        self.load_w(w1, self.w[li][2], D, 2 * DFF, "WA", "wF1")
        self.fence("WB")
        self.load_w(self.wb_view(22, D), self.w[li][3], DFF, D, "WB", "wF2")
        for ti in range(self.NT):
            xb, xbk = self.load_xb(ti)
            for j in range(22):
                pg, pgk = self.proj_fm(w1, "WA", "wF1", j * 128, xb, xbk)
                pu, puk = self.proj_fm(w1, "WA", "wF1", DFF + j * 128, xb, xbk)
                t, tk = self.t512.next()
                self.act(t[:], pg[:], AF.Silu, [pgk], [tk])
                hb, hbk = self.b512.next()
                self.tt("dve", hb[:], t[:], pu[:], ALU.mult, [tk, puk], [hbk])
                self.out_fm(self.Hd, "Hd", ti, j, hb[:], hbk)

    def ffn2(self, li, last):
        P = self.P
        self.fence("WA")
        w2 = self.wb_view(22, D)
        hs = [self.WA[:, i * 5632:(i + 1) * 5632].bitcast(BF16).rearrange("p (j t) -> p j t", t=TT) for i in range(2)]
        ys = [self.waf(11264 + i * 4096, [128, 8, TT]) for i in range(2)]
        g2, b2 = PV[("g2", li)], PV[("b2", li)]
        for ti in range(self.NT):
            t0 = ti * TT
            h, hk = hs[ti % 2], ("WA", "h", ti % 2)
            P.dma(h, self.Hd[:, t0:t0 + TT].rearrange("(j p) t -> p j t", p=128),
                  [("Hd", ti, j) for j in range(22)], [hk])
            y = ys[ti % 2]
            yks = [("WA", "y", ti % 2, fc) for fc in range(8)]
            for fc in range(8):
                pa, pk = self.pa.next()
                for j in range(22):
                    self.mm(pa[:], w2[:, j, fc * 128:(fc + 1) * 128], h[:, j, :], j == 0, j == 21,
                            [("WB", "wF2", j, 0), hk], [pk])
                xr, xrk = self.t512.next()
                P.dma(xr[:], self.X32[fc * 128:(fc + 1) * 128, t0:t0 + TT], [("X32", ti)], [xrk])
                self.stt("dve", y[:, fc, :], xr[:], ALPHA, pa[:], ALU.mult, ALU.add, [xrk, pk], [yks[fc]])
                self.ln_acc(fc, y[:, fc, :], yks[fc])
            xb, xbk = self.xb.next()
            self.ln_finish(y, yks, g2, b2, xb, xbk)
            if last:
                for b in range(4):
                    for half in range(2):
                        pa, pk = self.pa.next()
                        for q in range(4):
                            self.tr(pa[:, q * 128:(q + 1) * 128], self.x32[:, half * 4 + q, b * 128:(b + 1) * 128],
                                    self.ident, [("x32", half * 4 + q), "cst"], [pk])
                        t, tk = self.t512.next()
                        self.cp("act", t[:], pa[:], [pk], [tk])
                        P.dma(self.out_d[t0 + b * 128:t0 + (b + 1) * 128, half * 512:(half + 1) * 512], t[:],
                              [tk], [("out", ti, b, half)], q="sp")
            else:
                self.store_x(ti, xb, xbk)

    def epilogue(self, li, ti, dv, nwcol, yoff):
        P = self.P
        import os
        if os.environ.get("K_SKIP_EPI"):
            xb, xbk = self.xb.next()
            self.store_x(ti, xb, xbk)
            return
        wO = self.wb_view(8, D)
        nfc = dv // 128
        ob, obk = self.xb.next()
        t0 = ti * TT
        for hd in range(8 // nfc):
            pa, pk = self.pa.next()
            for q in range(nfc):
                fc = hd * nfc + q
                t, tk = self.t512.next()
                self.act(t[:], self.x32[:, fc, :], AF.Square, [("x32", fc)], [tk])
                self.mm(pa[:], self.ones, t[:], q == 0, q == nfc - 1, ["cst", tk], [pk])
            rs, rsk = self.t512.next()
            self.rsqrt(rs[:], pa[:], 1.0 / dv, RMS_EPS, [pk], rsk)
            for q in range(nfc):
                fc = hd * nfc + q
                r, rk = self.t512.next()
                P.dma(r[:], self.Rd[fc * 128:(fc + 1) * 128, t0:t0 + TT], [("Rd", ti, fc)], [rk])
                t, tk = self.t512.next()
                self.tt("dve", t[:], self.x32[:, fc, :], rs[:], ALU.mult, [("x32", fc), rsk], [tk])
                self.stt("dve", ob[:, fc, :], t[:], self.pv[:, nwcol + q:nwcol + q + 1], r[:], ALU.mult, ALU.mult,
                         [tk, rk, "pv"], [(obk, fc)])
        y = self.waf(yoff, [128, 8, TT])
        yks = [("WA", "ye", fc) for fc in range(8)]
        for fo in range(8):
            pa, pk = self.pa.next()
            for kc in range(8):
                self.mm(pa[:], wO[:, kc, fo * 128:(fo + 1) * 128], ob[:, kc, :], kc == 0, kc == 7,
                        [("WB", "wO", kc, 0), (obk, kc)], [pk])
            xr, xrk = self.t512.next()
            P.dma(xr[:], self.X32[fo * 128:(fo + 1) * 128, t0:t0 + TT], [("X32", ti)], [xrk])
            self.stt("dve", y[:, fo, :], xr[:], ALPHA, pa[:], ALU.mult, ALU.add, [xrk, pk], [yks[fo]])
            self.ln_acc(fo, y[:, fo, :], yks[fo])
        xb, xbk = self.xb.next()
        self.ln_finish(y, yks, PV[("g1", li)], PV[("b1", li)], xb, xbk)
        self.store_x(ti, xb, xbk)

    def hgrn_A(self, li):
        P = self.P
        self.fence("WA")
        wA = self.wa_view(8, HGRN_IN)
        self.load_w(wA, self.w[li][0], D, HGRN_IN, "WA", "wA")
        sm = self.sm[:, 128:216]
        lbl = self.pv[:, PV["lbl"]:PV["lbl"] + 32]
        self.act(sm[:, 0:32], lbl, AF.Exp, ["pv"], ["sm"])
        ev = sm[:, 0:32].rearrange("p (c d) -> p c d", d=4)
        self.red("dve", sm[:, 32:40], ev, ["sm"], ["sm1"])
        self.red("dve", sm[:, 40:48], ev[:, :, 1:li + 1], ["sm"], ["sm2"])
        P.op("dve", lambda e: e.reciprocal(out=sm[:, 48:56], in_=sm[:, 32:40]), ["sm1"], ["sm3"])
        self.tt("dve", sm[:, 56:64], sm[:, 40:48], sm[:, 48:56], ALU.mult, ["sm2", "sm3"], ["lb"])
        self.ts("dve", sm[:, 64:72], sm[:, 56:64], -1.0, ALU.mult, ["lb"], ["oml"], s2=1.0, op1=ALU.add)
        for ti in range(self.NT):
            xb, xbk = self.load_xb(ti)
            for fc in range(8):
                pa, pk = self.proj_fm(wA, "WA", "wA", fc * 128, xb, xbk)
                t, tk = self.t512.next()
                self.act(t[:], pa[:], AF.Silu, [pk], [tk])
                self.out_fm(self.Qd, "Qd", ti, fc, t[:], tk)
            for fc in range(8):
                pa, pk = self.proj_fm(wA, "WA", "wA", 1024 + fc * 128, xb, xbk)
                t, tk = self.t512.next()
                self.act(t[:], pa[:], AF.Sigmoid, [pk], [tk])
                self.ts("dve", t[:], t[:], sm[:, 64 + fc:65 + fc], ALU.mult, [tk, "oml", "lb"], [tk],
                        s2=sm[:, 56 + fc:57 + fc], op1=ALU.add)
                g, gk = self.t512.next()
                self.act(g[:], t[:], AF.Ln, [tk], [gk])
                self.out_fm(self.Gd, "Gd", ti, fc, g[:], gk)
                k, kk = self.t512.next()
                self.ts("dve", k[:], t[:], -1.0, ALU.mult, [tk], [kk], s2=1.0, op1=ALU.add)
                self.out_fm(self.Kd, "Kd", ti, fc, k[:], kk)
            self.v_tok(wA, "wA", 2048, xb, xbk, ti)
            for fc in range(8):
                pa, pk = self.proj_fm(wA, "WA", "wA", 3072 + fc * 128, xb, xbk)
                t, tk = self.t512.next()
                self.act(t[:], pa[:], AF.Silu, [pk], [tk])
                self.out_fm(self.Rd, "Rd", ti, fc, t[:], tk)

    def gla_A(self, li):
        P = self.P
        self.fence("WA")
        wA = self.wa_view(8, GLA_IN)
        self.load_w(wA, self.w[li][0], D, GLA_IN, "WA", "wA")
        wg = self.waf(13000, [16, 512])
        P.dma(wg, self.wgk2_d, [], [("WA", "wgk2")])
        for ti in range(self.NT):
            xb, xbk = self.load_xb(ti)
            for fc in range(4):
                pa, pk = self.proj_fm(wA, "WA", "wA", fc * 128, xb, xbk)
                t, tk = self.t512.next()
                self.cp("act", t[:], pa[:], [pk], [tk])
                self.out_fm(self.Qd, "Qd", ti, fc, t[:], tk)
            for fc in range(4):
                pa, pk = self.proj_fm(wA, "WA", "wA", 512 + fc * 128, xb, xbk)
                t, tk = self.t512.next()
                self.cp("act", t[:], pa[:], [pk], [tk])
                self.out_fm(self.Kd, "Kd", ti, fc, t[:], tk)
            self.v_tok(wA, "wA", 1024, xb, xbk, ti)
            for fc in range(8):
                pa, pk = self.proj_fm(wA, "WA", "wA", 2048 + fc * 128, xb, xbk)
                t, tk = self.t512.next()
                self.act(t[:], pa[:], AF.Silu, [pk], [tk])
                self.out_fm(self.Rd, "Rd", ti, fc, t[:], tk)
            pa, pk = self.proj_fm(wA, "WA", "wA", 3072, xb, xbk, M=16)
            gk_, gkk = self.waf(13512, [16, 512]), ("WA", "gk")
            self.cp("act", gk_[0:16, :], pa[0:16, :], [pk], [gkk])
            for fc in range(4):
                pb, pbk = self.pa.next()
                self.mm(pb[:], wg[0:16, fc * 128:(fc + 1) * 128], gk_[0:16, :], True, True, [("WA", "wgk2"), gkk], [pbk])
                t, tk = self.t512.next()
                self.act(t[:], pb[:], AF.Sigmoid, [pbk, "pv"], [tk], bias=self.pv[:, PV["bgk"] + fc:PV["bgk"] + fc + 1])
                self.act(t[:], t[:], AF.Ln, [tk], [tk])
                g, gk2 = self.t512.next()
                self.ts("dve", g[:], t[:], 1.0 / 16.0, ALU.mult, [tk], [gk2])
                self.out_fm(self.Gd, "Gd", ti, fc, g[:], gk2)

    def gla_B(self, li, H, dv, nwcol):
        P = self.P
        self.fence("WA")
        S32 = self.S32
        P.op("pool", lambda e: e.memset(S32[:], 0.0), [], ["S32"])
        q32 = self.waf(0, [128, 4, TT])
        k32 = self.waf(2048, [128, 4, TT])
        g32 = self.waf(4096, [128, 4, TT])
        gm = self.waf(6144, [128, 4, TT])
        AT = self.waf(8192, [64, 4, TT])
        kT = self.WA[0:64, 10240:14336].rearrange("p (n h d) -> p n h d", h=4, d=128)
        Vc = [self.waf(14336 + i * 1024, [64, 1024]) for i in range(2)]
        yoff = 16384
        Gs = self.waf(20480, [128, 4, TT])
        sm = self.sm
        egm = sm[:, 0:32].rearrange("p (h n) -> p h n", n=8)
        egl = sm[:, 32:64].rearrange("p (h n) -> p h n", n=8)
        gam = sm[:, 64:96].rearrange("p (h n) -> p h n", n=8)
        nfc = dv // 128
        scm = self.cst[:, C_SC:C_SC + 512]
        mu8 = self.cst[0:64, C_MU:C_MU + 512]
        scale = 128 ** -0.5
        vi = 0
        import os
        lvl = int(os.environ.get("K_REC_LEVEL", "9"))
        for ti in range(self.NT):
            t0 = ti * TT
            for gi in range(H // 4 if lvl >= 1 else 0):
                h0 = gi * 4
                for (dst, dr, nm) in ((q32, self.Qd, "Qd"), (k32, self.Kd, "Kd"), (g32, self.Gd, "Gd")):
                    P.dma(dst, self.tile_fm(dr, ti, h0 * 128, 512), [(nm, ti, h0 + i) for i in range(4)],
                          [("WA", nm + "t", i) for i in range(4)])
                def head_gen(hh):
                    kq, kk_, kg, kgm = ("WA", "Qdt", hh), ("WA", "Kdt", hh), ("WA", "Gdt", hh), ("WA", "gm", hh)
                    self.scan(Gs[:, hh, :], scm, g32[:, hh, :], [kg, "cst"], [("WA", "Gs", hh)])
                    kg = ("WA", "Gs", hh)
                    Gv = Gs[:, hh, :].rearrange("p (n c) -> p n c", c=64)
                    gmv = gm[:, hh, :].rearrange("p (n c) -> p n c", c=64)
                    self.tt("dve", gmv, Gv, Gv[:, :, 31:32].broadcast_to([128, 8, 64]), ALU.subtract, [kg], [kgm])
                    e1, e1k = self.t512.next()
                    self.act(e1[:], gm[:, hh, :], AF.Exp, [kgm], [e1k])
                    e2, e2k = self.t512.next()
                    self.act(e2[:], gm[:, hh, :], AF.Exp, [kgm], [e2k], scale=-1.0)
                    self.act(egm[:, hh, :], Gv[:, :, 31], AF.Exp, [kg], [("egm", hh)])
                    self.act(egl[:, hh, :], gmv[:, :, 63], AF.Exp, [kgm], [("egl", hh)])
                    self.act(gam[:, hh, :], Gv[:, :, 63], AF.Exp, [kg], [("gam", hh)])
                    yield
                    self.stt("dve", q32[:, hh, :], q32[:, hh, :], scale, e1[:], ALU.mult, ALU.mult, [kq, e1k], [kq])
                    self.tt("dve", k32[:, hh, :], k32[:, hh, :], e2[:], ALU.mult, [kk_, e2k], [kk_])
                    pa, pk = self.pa.next()
                    for n in range(8):
                        self.mm(pa[0:64, n * 64:(n + 1) * 64], k32[:, hh, n * 64:(n + 1) * 64],
                                q32[:, hh, n * 64:(n + 1) * 64], True, True, [kk_, kq], [pk])
                    yield
                    self.tt("dve", AT[:, hh, :], pa[0:64, :], mu8, ALU.mult, [pk, "cst"], [("WA", "AT", hh)])
                    qv = q32[:, hh, :].rearrange("p (n c) -> p n c", c=64)
                    kv = k32[:, hh, :].rearrange("p (n c) -> p n c", c=64)
                    self.tt("dve", qv, qv, egm[:, hh, :].unsqueeze(2).broadcast_to([128, 8, 64]), ALU.mult,
                            [kq, ("egm", hh)], [kq])
                    self.tt("dve", kv, kv, egl[:, hh, :].unsqueeze(2).broadcast_to([128, 8, 64]), ALU.mult,
                            [kk_, ("egl", hh)], [kk_])
                    pd, pdk = self.pd.next()
                    for n in range(8):
                        self.tr(pd[0:64, n * 128:(n + 1) * 128], k32[:, hh, n * 64:(n + 1) * 64], self.ident,
                                [kk_, "cst"], [pdk])
                    for hf in range(2):
                        self.cp("act", kT[:, hf * 4:(hf + 1) * 4, hh, :],
                                pd[0:64, hf * 512:(hf + 1) * 512].rearrange("p (n d) -> p n d", d=128), [pdk],
                                [("WA", "kT", hh)])

                self.run_pipe((head_gen(hh) for hh in range(4)), 3)
                for n in range(8 if lvl >= 5 else 0):
                    V = Vc[vi % 2]
                    vk = ("WA", "Vc", vi % 2)
                    vi += 1
                    r0 = t0 + n * 64
                    P.dma(V, self.Vt[r0:r0 + 64, :], [("Vt", ti, n // 2, hf) for hf in range(2)], [vk])
                    po, pok = self.pa.next()
                    for hh in range(4):
                        h = h0 + hh
                        for eh in range(nfc):
                            slot = hh * nfc + eh
                            col = h * dv + eh * 128
                            self.mm(po[:, slot * 64:(slot + 1) * 64], S32[:, col:col + 128],
                                    q32[:, hh, n * 64:(n + 1) * 64], True, False, ["S32", ("WA", "Qdt", hh)], [pok])
                            self.mm(po[:, slot * 64:(slot + 1) * 64], V[:, col:col + 128],
                                    AT[:, hh, n * 64:(n + 1) * 64], False, True, [vk, ("WA", "AT", hh)], [pok])
                    fc0 = h0 * nfc
                    ns = 4 * nfc
                    self.cp("act", self.x32[:, fc0:fc0 + ns, n * 64:(n + 1) * 64],
                            po[:, 0:ns * 64].rearrange("p (s c) -> p s c", c=64), [pok],
                            [("x32", fc0 + s) for s in range(ns)])
                    if lvl < 6:
                        continue
                    pS, pSk = self.pd.next()
                    for hh in range(4):
                        h = h0 + hh
                        self.mm(pS[:, hh * dv:(hh + 1) * dv], kT[:, n, hh, :], V[:, h * dv:(h + 1) * dv], True, True,
                                [("WA", "kT", hh), vk], [pSk])
                    Sv = S32[:, h0 * dv:(h0 + 4) * dv].rearrange("p (h e) -> p h e", e=dv)
                    self.tt("dve", Sv, Sv, gam[:, :, n:n + 1].broadcast_to([128, 4, dv]), ALU.mult,
                            ["S32"] + [("gam", hh) for hh in range(4)], ["S32"])
                    for c0 in range(0, 4 * dv, 512):
                        self.tt("dve", S32[:, h0 * dv + c0:h0 * dv + c0 + 512], S32[:, h0 * dv + c0:h0 * dv + c0 + 512],
                                pS[:, c0:c0 + 512], ALU.add, ["S32", pSk], ["S32"])
            self.epilogue(li, ti, dv, nwcol, yoff)

    def gdn_A(self, li):
        P = self.P
        j = li // 3
        self.fence("WA")
        wA = self.wa_view(8, GDN_IN)
        self.load_w(wA, self.w[li][0], D, GDN_IN, "WA", "wA")
        pres = Rot([(self.waf(16448 + i * 520, [128, 516]), ("WA", "pre", i)) for i in range(3)])
        tl = Rot([(self.waf(18008 + i * 512, [128, 512]), ("WA", "tl", i)) for i in range(7)])
        tkR = Rot([(self.waf(18008 + i * 512, [128, 512]), ("WA", "tl", i)) for i in range(7, 9)])
        sqR = self.t512
        carry = self.carry
        P.op("pool", lambda e: e.memset(carry[:], 0.0), [], [("carry", fc) for fc in range(24)])
        sm = self.sm
        negA = sm[:, 208:216]
        self.act(negA, self.pr[:, j * 16:j * 16 + 8], AF.Exp, ["pr"], ["negA"])
        self.ts("dve", negA, negA, -1.0, ALU.mult, ["negA"], ["negA"])
        cw0 = PV[("conv", j)]
        for ti in range(self.NT):
            t0 = ti * TT
            xb, xbk = self.load_xb(ti)
            def fc_s1(fc, xb=xb, xbk=xbk):
                pa, pk = self.proj_fm(wA, "WA", "wA", fc * 128, xb, xbk)
                pre, prk = pres.next()
                self.cp("pool", pre[:, 0:3], carry[:, fc, 0:3], [("carry", fc)], [prk])
                self.cp("act", pre[:, 3:515], pa[:], [pk], [prk])
                self.cp("pool", carry[:, fc, 0:3], pre[:, 512:515], [prk], [("carry", fc)])
                acc, ak = tl.next()
                wc = lambda k: self.pv[:, cw0 + fc * 4 + k:cw0 + fc * 4 + k + 1]
                self.ts("dve", acc[:], pre[:, 3:515], wc(3), ALU.mult, [prk, "pv"], [ak])
                for k in (2, 1, 0):
                    self.stt("dve", acc[:], pre[:, k:k + 512], wc(k), acc[:], ALU.mult, ALU.add, [prk, ak, "pv"], [ak])
                return acc, ak

            def fc_gen(fcs, ti=ti, t0=t0):
                st = [fc_s1(fc) for fc in fcs]
                sqs = []
                for fc, (s, sk) in zip(fcs, st):
                    self.act(s[:], s[:], AF.Silu, [sk], [sk])
                    if fc < 16:
                        sq, sqk = sqR.next()
                        self.tt("dve", sq[:], s[:], s[:], ALU.mult, [sk], [sqk])
                        sqs.append((sq, sqk))
                yield
                if fcs[0] < 16:
                    pbs = []
                    for (sq, sqk) in sqs:
                        pb, pbk = self.pa.next()
                        self.mm(pb[:], self.ones, sq[:], True, True, ["cst", sqk], [pbk])
                        pbs.append((pb, pbk))
                    for fc, (s, sk), (sq, sqk), (pb, pbk) in zip(fcs, st, sqs, pbs):
                        self.rsqrt(sq[:], pb[:], 1.0, L2_EPS, [pbk, sqk], sqk)
                        if fc < 8:
                            self.stt("dve", s[:], s[:], 128 ** -0.5, sq[:], ALU.mult, ALU.mult, [sk, sqk], [sk])
                            self.out_fm(self.Qd, "Qd", ti, fc, s[:], sk)
                        else:
                            self.tt("dve", s[:], s[:], sq[:], ALU.mult, [sk, sqk], [sk])
                            self.out_fm(self.Kd, "Kd", ti, fc - 8, s[:], sk)
                    if fcs[0] < 8:
                        return
                    yield
                for fc, (s, sk) in zip(fcs, st):
                    dt_, dn, hcol = (self.Kt, "Kt", (fc - 8) * 128) if fc < 16 else (self.Vt, "Vt", (fc - 16) * 128)
                    pt, ptk = self.pa.next()
                    for b in range(4):
                        self.tr(pt[:, b * 128:(b + 1) * 128], s[:, b * 128:(b + 1) * 128], self.ident, [sk, "cst"], [ptk])
                    tk_, tkk = tkR.next()
                    self.cp("act", tk_[:], pt[:], [ptk], [tkk])
                    P.dma(dt_[t0:t0 + TT, hcol:hcol + 128].rearrange("(b p) d -> p b d", p=128),
                          tk_[:].rearrange("p (b d) -> p b d", d=128), [tkk], [(dn, ti, hcol // 128)], q="sp")

            self.run_pipe((fc_gen([fc, fc + 1]) for fc in range(0, 24, 2)), 3)
            for fc in range(8):
                pa, pk = self.proj_fm(wA, "WA", "wA", 3072 + fc * 128, xb, xbk)
                t, tk = tl.next()
                self.act(t[:], pa[:], AF.Silu, [pk], [tk])
                self.out_fm(self.Rd, "Rd", ti, fc, t[:], tk)
            for b in range(4):
                pa, pk = self.proj_tok(wA, "WA", "wA", 4096, 16, xb, xbk, b)
                bg, bgk = tl.next()
                self.act(bg[:, 0:8], pa[:, 0:8], AF.Sigmoid, [pk], [bgk])
                self.tt("dve", bg[:, 16:24], pa[:, 8:16], self.pr[:, j * 16 + 8:j * 16 + 16], ALU.add, [pk, "pr"], [bgk])
                self.act(bg[:, 16:24], bg[:, 16:24], AF.Exp, [bgk], [bgk])
                self.act(bg[:, 16:24], bg[:, 16:24], AF.Ln, [bgk], [bgk], bias=1.0)
                self.tt("dve", bg[:, 8:16], bg[:, 16:24], negA, ALU.mult, [bgk, "negA"], [bgk])
                r0 = t0 + b * 128
                P.dma(self.BGd[r0:r0 + 128, :], bg[:, 0:16], [bgk], [("BGd", ti, b)], q="sp")

    def gdn_B(self, li):
        P = self.P
        j = li // 3
        self.fence("WA")
        S32 = self.S32
        P.op("pool", lambda e: e.memset(S32[:], 0.0), [], ["S32"])
        W = lambda off, shape: self.waf(off, shape)
        qc = W(0, [128, 8, 64])
        kc = W(512, [128, 8, 64])
        ktok = W(1024, [64, 1024])
        vtok = W(2048, [64, 1024])
        bg = W(3072, [64, 16])
        o = 3136
        names512 = ["rhs1", "DL", "DU", "N", "NT", "Pa", "Pb", "PTa", "PTb", "Ta", "Tb", "AT"]
        t5 = {}
        for nm in names512:
            t5[nm] = W(o, [64, 8, 64])
            o += 512
        t10 = {}
        for nm in ["ru", "rw", "u", "vnew", "khat"]:
            t10[nm] = W(o, [64, 1024])
            o += 1024
        qh = W(o, [128, 8, 64])
        o += 512
        wT = W(o, [128, 512])
        o += 512
        eGr = W(o, [128, 512])
        o += 512
        qcs, kcs, bgs = [qc], [kc], [bg]
        qcs.append(W(o, [128, 8, 64]))
        o += 512
        kcs.append(W(o, [128, 8, 64]))
        o += 512
        bgs.append(W(o, [64, 16]))
        o += 64
        yoff = o
        assert yoff + 4096 <= 23040, yoff
        K = lambda nm: ("WA", nm)
        sg = self.sm
        Gcol, eGc, beg, egl, nbeta, tmp8 = (sg[0:64, 0:8], sg[0:64, 8:16], sg[0:64, 16:24], sg[0:64, 24:32],
                                            sg[0:64, 32:40], sg[0:64, 40:48])
        gam = sg[:, 48:56]
        cst = self.cst
        triU = cst[0:64, C_MU:C_MU + 64]
        mu8 = cst[0:64, C_MU:C_MU + 512].rearrange("p (h c) -> p h c", c=64)
        sl8 = cst[0:64, C_SL:C_SL + 512].rearrange("p (h c) -> p h c", c=64)
        i8 = cst[0:64, C_I8:C_I8 + 512].rearrange("p (h c) -> p h c", c=64)
        id64 = cst[0:64, C_ID:C_ID + 64]
        ones64 = cst[0:64, C_ONE:C_ONE + 128]
        v3 = lambda ap: ap.rearrange("p (h c) -> p h c", c=64)
        b8 = lambda ap, n: ap.unsqueeze(2).broadcast_to([64, 8, n])

        def mm8(ps_ap, pk, lhs_of, rhs_of, r, width, npart):
            for h in range(8):
                self.mm(ps_ap[0:npart, h * width:(h + 1) * width], lhs_of(h), rhs_of(h), True, True, r, [pk])

        for ti in range(self.NT):
            for n in range(NCH):
                t0c = ti * TT + n * C
                par = n % 2
                qc, kc, bg = qcs[par], kcs[par], bgs[par]
                kqc, kkc, kbg = K("qc%d" % par), K("kc%d" % par), K("bg%d" % par)
                P.dma(bg, self.BGd[t0c:t0c + C, :], [("BGd", ti, n // 2)], [kbg])
                P.dma(kc, self.Kd[:, t0c:t0c + C].rearrange("(h p) t -> p h t", p=128),
                      [("Kd", ti, f) for f in range(8)], [kkc])
                P.dma(qc, self.Qd[:, t0c:t0c + C].rearrange("(h p) t -> p h t", p=128),
                      [("Qd", ti, f) for f in range(8)], [kqc])
                P.dma(ktok, self.Kt[t0c:t0c + C, :], [("Kt", ti, f) for f in range(8)], [K("ktok")])
                P.dma(vtok, self.Vt[t0c:t0c + C, :], [("Vt", ti, f) for f in range(8)], [K("vtok")])
                beta, g = bg[:, 0:8], bg[:, 8:16]
                pA, pAk = self.pa.next()
                self.mm(pA[0:64, 0:8], triU, g, True, True, ["cst", kbg], [pAk])
                self.mm(pA[:, 8:16], ones64, g, True, True, ["cst", kbg], [pAk])
                self.cp("dve", Gcol, pA[0:64, 0:8], [pAk], ["Gcol"])
                self.tt("dve", tmp8, pA[0:64, 8:16], Gcol, ALU.subtract, [pAk, "Gcol"], ["tmp8"])
                self.act(gam, pA[:, 8:16], AF.Exp, [pAk], ["gam"])
                self.ts("dve", nbeta, beta, -1.0, ALU.mult, [kbg], ["nbeta"])
                self.tt("dve", t5["rhs1"], b8(g, 64), mu8, ALU.mult, [kbg, "cst"], [K("rhs1")])
                pT1, pT1k = self.pa.next()
                self.mm(pT1[:, :], ones64, t5["rhs1"].rearrange("p h c -> p (h c)"), True, True,
                        ["cst", K("rhs1")], [pT1k])
                DL, DU = t5["DL"], t5["DU"]
                self.tt("dve", DL, b8(Gcol, 64), v3(pT1[0:64, :]), ALU.subtract, ["Gcol", pT1k], [K("DL")])
                self.tt("dve", DU, v3(pT1[0:64, :]), b8(Gcol, 64), ALU.subtract, ["Gcol", pT1k], [K("DU")])
                self.act(eGr[:], pT1[:, :], AF.Exp, [pT1k], [K("eGr")])
                self.ts("dve", DL, DL, 0.0, ALU.min, [K("DL")], [K("DL")])
                self.act(DL, DL, AF.Exp, [K("DL")], [K("DL")])
                self.tt("dve", DL, DL, sl8, ALU.mult, [K("DL"), "cst"], [K("DL")])
                pKK, pKKk = self.pa.next()
                mm8(pKK, pKKk, lambda h: kc[:, h, :], lambda h: kc[:, h, :], [kkc], 64, 64)
                N, NT, AT = t5["N"], t5["NT"], t5["AT"]
                self.tt("dve", N, v3(pKK[0:64, :]), DL, ALU.mult, [pKKk, K("DL")], [K("N")])
                self.tt("dve", N, N, b8(nbeta, 64), ALU.mult, [K("N"), "nbeta"], [K("N")])
                pN, pNk = self.pa.next()
                for h in range(8):
                    self.tr(pN[0:64, h * 64:(h + 1) * 64], N[:, h, :], id64, [K("N"), "cst"], [pNk])
                self.cp("act", NT, v3(pN[0:64, :]), [pNk], [K("NT")])
                ru, rw, u, vnew, khat = t10["ru"], t10["rw"], t10["u"], t10["vnew"], t10["khat"]
                v4 = lambda ap: ap.rearrange("p (h d) -> p h d", d=128)

                def d_qk():
                    pQK, pQKk = self.pa.next()
                    mm8(pQK, pQKk, lambda h: kc[:, h, :], lambda h: qc[:, h, :], [kkc, kqc], 64, 64)
                    self.tt("dve", AT, v3(pQK[0:64, :]), DU, ALU.mult, [pQKk, K("DU")], [K("AT")])

                deferred = [
                    lambda: self.ts("dve", DU, DU, 0.0, ALU.min, [K("DU")], [K("DU")]),
                    lambda: self.act(DU, DU, AF.Exp, [K("DU")], [K("DU")]),
                    lambda: self.tt("dve", DU, DU, mu8, ALU.mult, [K("DU"), "cst"], [K("DU")]),
                    d_qk,
                    lambda: self.act(egl, tmp8, AF.Exp, ["tmp8"], ["egl"]),
                    lambda: self.act(eGc, Gcol, AF.Exp, ["Gcol"], ["eGc"]),
                    lambda: self.tt("dve", beg, beta, eGc, ALU.mult, [kbg, "eGc"], ["beg"]),
                    lambda: self.tt("dve", qh, qc, eGr[:].rearrange("p (h c) -> p h c", c=64), ALU.mult,
                                    [kqc, K("eGr")], [K("qh")]),
                    lambda: self.tt("dve", v4(ru), v4(vtok), b8(beta, 128), ALU.mult, [K("vtok"), kbg], [K("ru")]),
                    lambda: self.tt("dve", v4(rw), v4(ktok), b8(beg, 128), ALU.mult, [K("ktok"), "beg"], [K("rw")]),
                    lambda: self.tt("dve", v4(khat), v4(ktok), b8(egl, 128), ALU.mult, [K("ktok"), "egl"], [K("khat")]),
                ]
                Tc, Tk = t5["Ta"], K("Ta")
                self.tt("dve", Tc, NT, i8, ALU.add, [K("NT"), "cst"], [Tk])
                Pp, Ppk, PTp, PTpk = N, K("N"), NT, K("NT")
                for jj in range(1, 6):
                    Pn, Pnk = (t5["Pa"], K("Pa")) if jj % 2 else (t5["Pb"], K("Pb"))
                    PTn, PTnk = (t5["PTa"], K("PTa")) if jj % 2 else (t5["PTb"], K("PTb"))
                    Tn, Tnk = (t5["Tb"], K("Tb")) if jj % 2 else (t5["Ta"], K("Ta"))
                    p1, p1k = self.pa.next()
                    mm8(p1, p1k, lambda h: PTp[:, h, :], lambda h: Pp[:, h, :], [Ppk, PTpk], 64, 64)
                    self.cp("act", Pn, v3(p1[0:64, :]), [p1k], [Pnk])
                    if jj < 5:
                        p2, p2k = self.pa.next()
                        mm8(p2, p2k, lambda h: Pp[:, h, :], lambda h: PTp[:, h, :], [Ppk, PTpk], 64, 64)
                        self.cp("dve", PTn, v3(p2[0:64, :]), [p2k], [PTnk])
                    for _ in range(3):
                        if deferred:
                            deferred.pop(0)()
                    p3, p3k = self.pa.next()
                    mm8(p3, p3k, lambda h: Pn[:, h, :], lambda h: Tc[:, h, :], [Pnk, Tk], 64, 64)
                    self.tt("dve", Tn, v3(p3[0:64, :]), Tc, ALU.add, [p3k, Tk], [Tnk])
                    Pp, Ppk, PTp, PTpk, Tc, Tk = Pn, Pnk, PTn, PTnk, Tn, Tnk
                while deferred:
                    deferred.pop(0)()
                TT_ = Tc
                pU, pUk = self.pd.next()
                mm8(pU, pUk, lambda h: TT_[:, h, :], lambda h: ru[:, h * 128:(h + 1) * 128], [Tk, K("ru")], 128, 64)
                self.cp("act", u[:, 0:512], pU[0:64, 0:512], [pUk], [K("u")])
                self.cp("act", u[:, 512:1024], pU[0:64, 512:1024], [pUk], [K("u")])
                pW, pWk = self.pa.next()
                mm8(pW, pWk, lambda h: rw[:, h * 128:(h + 1) * 128], lambda h: TT_[:, h, :], [Tk, K("rw")], 64, 128)
                self.cp("act", wT[:], pW[:, :], [pWk], [K("wT")])
                pWS, pWSk = self.pd.next()
                mm8(pWS, pWSk, lambda h: wT[:, h * 64:(h + 1) * 64], lambda h: S32[:, h * 128:(h + 1) * 128],
                    [K("wT"), "S32"], 128, 64)
                self.tt("dve", vnew[:, 0:512], u[:, 0:512], pWS[0:64, 0:512], ALU.subtract, [K("u"), pWSk], [K("vnew")])
                self.tt("dve", vnew[:, 512:1024], u[:, 512:1024], pWS[0:64, 512:1024], ALU.subtract, [K("u"), pWSk], [K("vnew")])
                pO, pOk = self.pa.next()
                for h in range(8):
                    self.mm(pO[:, h * 64:(h + 1) * 64], S32[:, h * 128:(h + 1) * 128], qh[:, h, :], True, False,
                            ["S32", K("qh")], [pOk])
                    self.mm(pO[:, h * 64:(h + 1) * 64], vnew[:, h * 128:(h + 1) * 128], AT[:, h, :], False, True,
                            [K("vnew"), K("AT")], [pOk])
                self.cp("act", self.x32[:, :, n * 64:(n + 1) * 64], pO[:, :].rearrange("p (s c) -> p s c", c=64), [pOk],
                        self.x32k)
                pS, pSk = self.pd.next()
                mm8(pS, pSk, lambda h: khat[:, h * 128:(h + 1) * 128], lambda h: vnew[:, h * 128:(h + 1) * 128],
                    [K("khat"), K("vnew")], 128, 128)
                Sv = S32[:, :].rearrange("p (h e) -> p h e", e=128)
                self.tt("dve", Sv, Sv, gam.unsqueeze(2).broadcast_to([128, 8, 128]), ALU.mult, ["S32", "gam"], ["S32"])
                self.tt("dve", S32[:, 0:512], S32[:, 0:512], pS[:, 0:512], ALU.add, ["S32", pSk], ["S32"])
                self.tt("dve", S32[:, 512:1024], S32[:, 512:1024], pS[:, 512:1024], ALU.add, ["S32", pSk], ["S32"])
            self.epilogue(li, ti, 128, PV[("gnw", j)], yoff)


_CACHE = {}


def get_nc(TL, layers, dbg=()):
    key = (TL, tuple(layers), tuple(dbg))
    if key not in _CACHE:
        _CACHE[key] = Builder(TL, list(layers), dbg).build()
    return _CACHE[key]


LAYERS = [(0, 0), (1, 1), (2, 2), (0, 3)]


def core_inputs(inp, xs, layers):
    m = {"x": np.ascontiguousarray(xs, dtype=np.float32), "cst": make_consts(), "pv": make_pv(inp),
         "pr": make_pr(inp), "wgk2": np.ascontiguousarray(inp["gla_w_gk2"][0], dtype=np.float32)}
    for kind, li in layers:
        j = li // 3
        if kind == 0:
            wi, wo = inp["gdn_w_in"][j], inp["gdn_w_out"][j]
        elif kind == 1:
            wi, wo = inp["hgrn_w_in"][0], inp["hgrn_w_out"][0]
        else:
            wi, wo = inp["gla_w_in"][0], inp["gla_w_out"][0]
        m["win%d" % li] = np.ascontiguousarray(wi, dtype=np.float32)
        m["wout%d" % li] = np.ascontiguousarray(wo, dtype=np.float32)
        m["wf1_%d" % li] = np.ascontiguousarray(inp["ffn_w_in"][li], dtype=np.float32)
        m["wf2_%d" % li] = np.ascontiguousarray(inp["ffn_w_down"][li], dtype=np.float32)
    return m


def kernel(**inputs):
    inp = {k: np.asarray(v) for k, v in inputs.items()}
    x = inp["x"]
    B, T, _ = x.shape
    nc = get_nc(T, LAYERS)
    maps = [core_inputs(inp, x[c % B], LAYERS) for c in range(8)]
    res = run_bass_kernel_spmd(nc, maps, core_ids=list(range(8)))
    return np.stack([np.asarray(res.results[b]["out"], dtype=np.float32) for b in range(B)], axis=0)
```

```python
import numpy as np
from contextlib import ExitStack
import concourse.bass as bass
import concourse.mybir as mybir
from concourse.bass_utils import run_bass_kernel_spmd

F32 = mybir.dt.float32
BF16 = mybir.dt.bfloat16
AF = mybir.ActivationFunctionType
ALU = mybir.AluOpType
AX = mybir.AxisListType

D = 1024
DFF = 2816
TT = 512
C = 64
NCH = TT // C
DEPTH = 4
ALPHA = float((2 * DEPTH) ** 0.25)
LN_EPS, RMS_EPS, L2_EPS = 1e-5, 1e-6, 1e-6
GDN_IN, HGRN_IN, GLA_IN = 4112, 4096, 3088
SW = 1024

C_ID, C_ONE, C_MU, C_SL, C_SC, C_I8, C_N = 0, 128, 256, 768, 1280, 1792, 2304


def make_consts():
    c = np.zeros((128, C_N), np.float32)
    c[:, C_ID:C_ID + 128] = np.eye(128)
    c[:, C_ONE:C_ONE + 128] = 1.0
    i = np.arange(64)
    mu = (i[:, None] <= i[None, :]).astype(np.float32)
    sl = (i[:, None] > i[None, :]).astype(np.float32)
    c[:64, C_MU:C_MU + 512] = np.tile(mu, (1, 8))
    c[:64, C_SL:C_SL + 512] = np.tile(sl, (1, 8))
    c[:64, C_I8:C_I8 + 512] = np.tile(np.eye(64, dtype=np.float32), (1, 8))
    sc = np.ones((128, 512), np.float32)
    sc[:, ::64] = 0.0
    c[:, C_SC:C_SC + 512] = sc
    return c


def pv_layout():
    off = {}
    n = 0
    for i in range(DEPTH):
        for nm in ("g1", "b1", "g2", "b2"):
            off[(nm, i)] = n
            n += 8
    for j in range(2):
        off[("conv", j)] = n
        n += 96
        off[("gnw", j)] = n
        n += 1
    off["hnw"] = n
    n += 1
    off["lbl"] = n
    n += 32
    off["lnw"] = n
    n += 2
    off["bgk"] = n
    n += 4
    off["n"] = n
    return off


PV = pv_layout()


def fm(v):
    return np.ascontiguousarray(v.reshape(-1, 128).T)


def make_pv(inp):
    pv = np.zeros((128, PV["n"]), np.float32)
    for i in range(DEPTH):
        pv[:, PV[("g1", i)]:PV[("g1", i)] + 8] = fm(inp["mix_ln_g"][i])
        pv[:, PV[("b1", i)]:PV[("b1", i)] + 8] = fm(inp["mix_ln_b"][i])
        pv[:, PV[("g2", i)]:PV[("g2", i)] + 8] = fm(inp["ffn_ln_g"][i])
        pv[:, PV[("b2", i)]:PV[("b2", i)] + 8] = fm(inp["ffn_ln_b"][i])
    for j in range(2):
        cw = inp["gdn_conv_w"][j]
        pv[:, PV[("conv", j)]:PV[("conv", j)] + 96] = cw.T.reshape(24, 128, 4).transpose(1, 0, 2).reshape(128, 96)
        pv[:, PV[("gnw", j)]] = inp["gdn_norm_w"][j]
    pv[:, PV["hnw"]] = inp["hgrn_norm_w"][0]
    lb = inp["hgrn_lb_logits"]
    pv[:, PV["lbl"]:PV["lbl"] + 32] = lb.T.reshape(8, 128, 4).transpose(1, 0, 2).reshape(128, 32)
    pv[:, PV["lnw"]:PV["lnw"] + 2] = fm(inp["gla_norm_w"][0])
    pv[:, PV["bgk"]:PV["bgk"] + 4] = fm(inp["gla_b_gk"][0])
    return pv


def make_pr(inp):
    pr = np.zeros((128, 32), np.float32)
    for j in range(2):
        pr[:, j * 16:j * 16 + 8] = inp["gdn_a_log"][j][None, :]
        pr[:, j * 16 + 8:j * 16 + 16] = inp["gdn_dt_bias"][j][None, :]
    return pr


class Rot:
    def __init__(self, items):
        self.items = items
        self.i = 0

    def next(self):
        it = self.items[self.i % len(self.items)]
        self.i += 1
        return it


class Prog:
    ENGS = ("pe", "act", "dve", "pool", "sp")
    NSLOT = {"sp": 24, "pool": 8, "act": 4}

    def __init__(self, nc):
        self.nc = nc
        self.ops = []

    def op(self, eng, fn, r=(), w=(), dma=False):
        r = list(r)
        for k in list(r) + list(w):
            if isinstance(k, tuple) and k[0] in ("WA", "WB"):
                ek = ("epoch", k[0])
                if ek not in r and ek not in w:
                    r.append(ek)
        self.ops.append({"eng": eng, "fn": fn, "r": tuple(r), "w": tuple(w), "dma": dma})

    def dma(self, out, in_, r, w, q="sp"):
        self.op(q, lambda e: e.dma_start(out=out, in_=in_), r, w, dma=True)

    def emit(self, stack):
        nc = self.nc
        ops = self.ops
        last_w, readers = {}, {}
        eng_pos = {e: 0 for e in self.ENGS}
        for i, o in enumerate(ops):
            deps = set()
            for k in o["r"]:
                if k in last_w:
                    deps.add(last_w[k])
                if isinstance(k, str) and k[:2] in ("pa", "pd"):
                    for rd in readers.get(k, ()):
                        if ops[rd]["eng"] != o["eng"]:
                            deps.add(rd)
            for k in o["w"]:
                if k in last_w:
                    deps.add(last_w[k])
                deps.update(readers.get(k, ()))
            deps.discard(i)
            o["deps"] = deps
            for k in o["r"]:
                readers.setdefault(k, []).append(i)
            for k in o["w"]:
                last_w[k] = i
                readers[k] = []
            o["pos"] = eng_pos[o["eng"]]
            eng_pos[o["eng"]] += 1
            o["signal"] = False
            o["waits_eng"] = {}
            o["waits_dma"] = {}
        seen = {e: {p: -1 for p in self.ENGS} for e in self.ENGS}
        seen_dma = {e: {} for e in self.ENGS}
        slot_cnt, slot_last = {}, {}
        dma_n = {e: 0 for e in self.ENGS}
        for i, o in enumerate(ops):
            e = o["eng"]
            if o["dma"]:
                s = dma_n[e] % self.NSLOT[e]
                dma_n[e] += 1
                key = (e, s)
                if key in slot_last:
                    o["deps"].add(slot_last[key])
                slot_last[key] = i
                slot_cnt[key] = slot_cnt.get(key, 0) + 1
                o["slot"] = key
                o["slot_val"] = 16 * slot_cnt[key]
            for d in sorted(o["deps"]):
                od = ops[d]
                if od["dma"]:
                    if seen_dma[e].get(od["slot"], 0) < od["slot_val"]:
                        seen_dma[e][od["slot"]] = od["slot_val"]
                        o["waits_dma"][od["slot"]] = od["slot_val"]
                    continue
                pe = od["eng"]
                if pe == e:
                    if e in ("pe", "sp"):
                        continue
                    if o["pos"] - od["pos"] > 2:
                        continue
                if seen[e][pe] >= od["pos"]:
                    continue
                seen[e][pe] = od["pos"]
                od["signal"] = True
                cur = o["waits_eng"].get(pe)
                if cur is None or ops[cur]["pos"] < od["pos"]:
                    o["waits_eng"][pe] = d
        tick = {e: 0 for e in self.ENGS}
        for o in ops:
            if o["signal"]:
                tick[o["eng"]] += 1
                o["tick"] = tick[o["eng"]]
        esem = {e: stack.enter_context(nc.semaphore("s_" + e)) for e in self.ENGS}
        ssem = {k: stack.enter_context(nc.semaphore("d_%s%d" % k)) for k in slot_cnt}
        per_eng = {e: [o for o in ops if o["eng"] == e] for e in self.ENGS}

        def run(ename, eng):
            for o in per_eng[ename]:
                for pe, d in o["waits_eng"].items():
                    eng.wait_ge(esem[pe], ops[d]["tick"])
                for sk, sv in o["waits_dma"].items():
                    eng.wait_ge(ssem[sk], sv)
                inst = o["fn"](eng)
                if o["dma"]:
                    inst.then_inc(ssem[o["slot"]], 16)
                elif o["signal"]:
                    inst.then_inc(esem[ename], 1)
            if ename == "sp":
                for k, c in slot_cnt.items():
                    eng.wait_ge(ssem[k], 16 * c)
                for e2 in self.ENGS:
                    if tick[e2]:
                        eng.wait_ge(esem[e2], tick[e2])

        with nc.Block() as block:
            @block.tensor
            def _(eng):
                run("pe", eng)

            @block.scalar
            def _(eng):
                run("act", eng)

            @block.vector
            def _(eng):
                run("dve", eng)

            @block.gpsimd
            def _(eng):
                run("pool", eng)

            @block.sync
            def _(eng):
                run("sp", eng)


class Builder:
    def __init__(self, TL, layers, dbg=(), upto=9):
        self.upto = upto
        self.TL = TL
        self.NT = TL // TT
        self.layers = layers
        self.dbg = set(dbg)
        self.nc = bass.Bass("TRN2", target_bir_lowering=False)
        self.P = Prog(self.nc)
        self.uid = 0

    def mm(self, out, lhsT, rhs, start, stop, r, w):
        self.P.op("pe", lambda e: e.matmul(out, lhsT, rhs, start=start, stop=stop), r, w)

    def tr(self, out, in_, ident, r, w):
        self.P.op("pe", lambda e: e.transpose(out, in_, ident), r, w)

    def act(self, out, in_, func, r, w, bias=None, scale=None):
        kw = {}
        if bias is not None:
            kw["bias"] = bias
        if scale is not None:
            kw["scale"] = scale
        self.P.op("act", lambda e: e.activation(out=out, in_=in_, func=func, **kw), r, w)

    def tt(self, eng, out, in0, in1, op, r, w):
        self.P.op(eng, lambda e: e.tensor_tensor(out=out, in0=in0, in1=in1, op=op), r, w)

    def ts(self, eng, out, in0, s1, op0, r, w, s2=None, op1=None):
        if op1 is None:
            self.P.op(eng, lambda e: e.tensor_scalar(out=out, in0=in0, scalar1=s1, scalar2=None, op0=op0), r, w)
        else:
            self.P.op(eng, lambda e: e.tensor_scalar(out=out, in0=in0, scalar1=s1, scalar2=s2, op0=op0, op1=op1), r, w)

    def stt(self, eng, out, in0, scalar, in1, op0, op1, r, w):
        self.P.op(eng, lambda e: e.scalar_tensor_tensor(out=out, in0=in0, scalar=scalar, in1=in1, op0=op0, op1=op1), r, w)

    def run_pipe(self, gens, depth):
        active, gens, more = [], iter(gens), True
        while True:
            if more and len(active) < depth:
                try:
                    active.insert(0, next(gens))
                except StopIteration:
                    more = False
            if not active:
                break
            for g in list(active):
                try:
                    next(g)
                except StopIteration:
                    active.remove(g)

    def rsqrt(self, out, in_, scale, eps, r, wk):
        self.act(out, in_, AF.Ln, r, [wk], bias=eps, scale=scale)
        self.act(out, out, AF.Exp, [wk], [wk], scale=-0.5)

    def cp(self, eng, out, in_, r, w):
        if eng == "act":
            self.act(out, in_, AF.Copy, r, w)
        else:
            self.P.op(eng, lambda e: e.tensor_copy(out=out, in_=in_), r, w)

    def red(self, eng, out, in_, r, w):
        self.P.op(eng, lambda e: e.tensor_reduce(out=out, in_=in_, axis=AX.X, op=ALU.add), r, w)

    def scan(self, out, d0, d1, r, w):
        self.P.op("dve", lambda e: e.tensor_tensor_scan(out=out, data0=d0, data1=d1, initial=0.0,
                                                        op0=ALU.mult, op1=ALU.add), r, w)

    def dram(self, name, shape, dt):
        kind = "ExternalOutput" if name in self.dbg else "Internal"
        return self.nc.dram_tensor(name, shape, dt, kind=kind).ap()

    def build(self):
        nc, P, TL, NT = self.nc, self.P, self.TL, self.NT
        st = ExitStack()
        self.st = st
        self.in_names = []

        def ein(n, s):
            self.in_names.append(n)
            return nc.dram_tensor(n, s, F32, kind="ExternalInput").ap()
        self.x_in = ein("x", [TL, D])
        self.cst_d = ein("cst", [128, C_N])
        self.pv_d = ein("pv", [128, PV["n"]])
        self.pr_d = ein("pr", [128, 32])
        self.wgk2_d = ein("wgk2", [16, 512])
        self.w = {}
        for kind, li in self.layers:
            nin = (GDN_IN, HGRN_IN, GLA_IN)[kind]
            self.w[li] = (ein("win%d" % li, [D, nin]) if self.upto >= 1 else None,
                          ein("wout%d" % li, [D, D]) if self.upto >= 1 else None,
                          ein("wf1_%d" % li, [D, 2 * DFF]) if self.upto >= 3 else None,
                          ein("wf2_%d" % li, [DFF, D]) if self.upto >= 3 else None)
        self.out_d = nc.dram_tensor("out", [TL, D], F32, kind="ExternalOutput").ap()
        self.X32 = self.dram("X32", [D, TL], F32)
        self.Xb = self.dram("Xb", [D, TL], BF16)
        self.Hd = self.dram("Hd", [DFF, TL], BF16)
        self.Qd = self.dram("Qd", [D, TL], F32)
        self.Kd = self.dram("Kd", [D, TL], F32)
        self.Gd = self.dram("Gd", [D, TL], F32)
        self.Rd = self.dram("Rd", [D, TL], F32)
        self.Vt = self.dram("Vt", [TL, D], F32)
        self.Kt = self.dram("Kt", [TL, D], F32)
        self.BGd = self.dram("BGd", [TL, 16], F32)

        def sb(name, shape, dt=F32):
            return st.enter_context(nc.sbuf_tensor(name, shape, dt))

        def ps(name, shape, dt=F32):
            return st.enter_context(nc.psum_tensor(name, shape, dt))

        self.cst = sb("cst_s", [128, C_N])
        self.pv = sb("pv_s", [128, PV["n"]])
        self.pr = sb("pr_s", [128, 32])
        self.stg = Rot([(sb("stg%d" % i, [128, SW]), "stg%d" % i) for i in range(2)])
        self.xb = Rot([(sb("xb%d" % i, [128, 8, TT], BF16), "xb%d" % i) for i in range(2)])
        self.x32 = sb("x32", [128, 8, TT])
        self.t512 = Rot([(sb("t512_%d" % i, [128, TT]), "t512_%d" % i) for i in range(4)])
        self.b512 = Rot([(sb("b512_%d" % i, [128, TT], BF16), "b512_%d" % i) for i in range(2)])
        self.s1 = sb("ln_s1", [128, TT])
        self.s2 = sb("ln_s2", [128, TT])
        self.mean = sb("ln_mean", [128, TT])
        self.rstd = sb("ln_rstd", [128, TT])
        self.S32 = sb("S32", [128, 1024])
        self.sm = sb("small", [128, 216])
        self.carry = sb("carry", [128, 24, 4])
        self.dummy = sb("dummy_t", [128, 4])
        self.WA = sb("WA", [128, 23040])
        self.WB = sb("WB", [128, 11264])
        self.pa = Rot([(ps("pa%d" % i, [128, TT]), "pa%d" % i) for i in range(4)])
        self.pd = Rot([(ps("pd%d" % i, [128, 2 * TT]), "pd%d" % i) for i in range(2)])
        self.ident = self.cst[:, C_ID:C_ID + 128]
        self.ones = self.cst[:, C_ONE:C_ONE + 128]

        P.dma(self.cst[:], self.cst_d, [], ["cst"])
        P.dma(self.pv[:], self.pv_d, [], ["pv"])
        P.dma(self.pr[:], self.pr_d, [], ["pr"])
        P.op("pool", lambda e: e.memset(self.S32[:], 0.0), [], ["S32"])
        P.op("pool", lambda e: e.memset(self.carry[:], 0.0), [], ["carry"])

        self.phase0()
        nl = len(self.layers)
        for n, (kind, li) in enumerate(self.layers):
            if self.upto < 1:
                break
            self.wb_load(self.w[li][1], D, D, ("wO", li))
            [self.gdn_A, self.hgrn_A, self.gla_A][kind](li)
            if self.upto < 2:
                break
            if kind == 0:
                self.gdn_B(li)
            elif kind == 1:
                self.gla_B(li, 8, 128, PV["hnw"])
            else:
                self.gla_B(li, 4, 256, PV["lnw"])
            if self.upto < 3:
                break
            self.ffn1(li)
            if self.upto < 4:
                break
            self.ffn2(li, last=(n == nl - 1))
        P.emit(st)
        st.close()
        return nc

    def wa_view(self, kc_n, ncols):
        return self.WA[:, 0:kc_n * ncols // 2].bitcast(BF16).rearrange("p (k n) -> p k n", n=ncols)

    def wb_view(self, kc_n, ncols):
        return self.WB[:, 0:kc_n * ncols // 2].bitcast(BF16).rearrange("p (k n) -> p k n", n=ncols)

    def waf(self, off, shape):
        n = int(np.prod(shape[1:]))
        ap = self.WA[0:shape[0], off:off + n]
        if len(shape) == 3:
            ap = ap.rearrange("p (a b) -> p a b", b=shape[2])
        return ap

    def fence(self, arena):
        d = self.dummy
        self.P.op("pool", lambda e: e.memset(d[:, 0:1], 0.0), [], [("epoch", arena)])

    def load_w(self, dst, w_dram, K, N, arena, name):
        i = 0
        for kc in range(K // 128):
            for n0 in range(0, N, SW):
                nn = min(SW, N - n0)
                s, sk = self.stg.next()
                self.P.dma(s[:, :nn], w_dram[kc * 128:(kc + 1) * 128, n0:n0 + nn], [], [sk])
                self.cp(("pool", "act", "dve")[i % 3], dst[:, kc, n0:n0 + nn], s[:, :nn], [sk],
                        [(arena, name, kc, n0 // SW)])
                i += 1

    def wb_load(self, w_dram, K, N, key):
        self.fence("WB")
        self.load_w(self.wb_view(K // 128, N), w_dram, K, N, "WB", key[0])

    def tile_fm(self, dram, ti, r0=0, nrows=D):
        t0 = ti * TT
        return dram[r0:r0 + nrows, t0:t0 + TT].rearrange("(c p) t -> p c t", p=128)

    @property
    def x32k(self):
        return [("x32", fc) for fc in range(8)]

    def store_x(self, ti, xb, xbk):
        self.P.dma(self.tile_fm(self.X32, ti), self.x32[:], self.x32k, [("X32", ti)], q="sp")
        self.P.dma(self.tile_fm(self.Xb, ti), xb[:], [(xbk, fc) for fc in range(8)], [("Xb", ti)], q="sp")

    def load_xb(self, ti):
        xb, xbk = self.xb.next()
        self.P.dma(xb[:], self.tile_fm(self.Xb, ti), [("Xb", ti)], [(xbk, fc) for fc in range(8)])
        return xb, xbk

    def phase0(self):
        P = self.P
        self.fence("WA")
        xt = self.waf(0, [128, 4, D])
        for ti in range(self.NT):
            t0 = ti * TT
            P.dma(xt, self.x_in[t0:t0 + TT, :].rearrange("(b p) f -> p b f", p=128), [], [("WA", "xt")])
            xb, xbk = self.xb.next()
            for fc in range(8):
                pa, pk = self.pa.next()
                for b in range(4):
                    self.tr(pa[:, b * 128:(b + 1) * 128], xt[:, b, fc * 128:(fc + 1) * 128], self.ident,
                            [("WA", "xt"), "cst"], [pk])
                self.cp("act", self.x32[:, fc, :], pa[:], [pk], [("x32", fc)])
                self.cp("dve", xb[:, fc, :], pa[:], [pk], [(xbk, fc)])
            self.store_x(ti, xb, xbk)

    def ln_acc(self, fc, y_fc, yk):
        if fc == 0:
            self.cp("act", self.s1[:], y_fc, [yk], ["s1"])
            self.act(self.s2[:], y_fc, AF.Square, [yk], ["s2"])
        else:
            self.tt("dve", self.s1[:], self.s1[:], y_fc, ALU.add, [yk, "s1"], ["s1"])
            t, tk = self.t512.next()
            self.act(t[:], y_fc, AF.Square, [yk], [tk])
            self.tt("dve", self.s2[:], self.s2[:], t[:], ALU.add, [tk, "s2"], ["s2"])

    def ln_finish(self, y, yks, gcol, bcol, xb, xbk):
        pa, pk = self.pa.next()
        self.mm(pa[:], self.ones, self.s1[:], True, True, ["cst", "s1"], [pk])
        pb, pbk = self.pa.next()
        self.mm(pb[:], self.ones, self.s2[:], True, True, ["cst", "s2"], [pbk])
        self.ts("dve", self.mean[:], pa[:], 1.0 / D, ALU.mult, [pk], ["mean"])
        t, tk = self.t512.next()
        self.tt("dve", t[:], self.mean[:], self.mean[:], ALU.mult, ["mean"], [tk])
        self.stt("dve", t[:], pb[:], 1.0 / D, t[:], ALU.mult, ALU.subtract, [pbk, tk], [tk])
        self.rsqrt(self.rstd[:], t[:], 1.0, LN_EPS, [tk], "rstd")
        for fc in range(8):
            self.tt("dve", y[:, fc, :], y[:, fc, :], self.mean[:], ALU.subtract, [yks[fc], "mean"], [yks[fc]])
            self.tt("dve", y[:, fc, :], y[:, fc, :], self.rstd[:], ALU.mult, [yks[fc], "rstd"], [yks[fc]])
            self.act(self.x32[:, fc, :], y[:, fc, :], AF.Identity, [yks[fc], "pv"], [("x32", fc)],
                     bias=self.pv[:, bcol + fc:bcol + fc + 1], scale=self.pv[:, gcol + fc:gcol + fc + 1])
            self.cp("pool", xb[:, fc, :], self.x32[:, fc, :], [("x32", fc)], [(xbk, fc)])

    def proj_fm(self, wv, arena, name, col0, xb, xbk, M=128):
        pa, pk = self.pa.next()
        for kc in range(8):
            self.mm(pa[0:M, :], wv[:, kc, col0:col0 + M], xb[:, kc, :], kc == 0, kc == 7,
                    [(arena, name, kc, col0 // SW), (xbk, kc)], [pk])
        return pa, pk

    def proj_tok(self, wv, arena, name, col0, ncol, xb, xbk, b):
        pa, pk = self.pa.next()
        for kc in range(8):
            self.mm(pa[:, 0:ncol], xb[:, kc, b * 128:(b + 1) * 128], wv[:, kc, col0:col0 + ncol], kc == 0, kc == 7,
                    [(arena, name, kc, col0 // SW), (xbk, kc)], [pk])
        return pa, pk

    def out_fm(self, dram, dname, ti, fc, src, sk):
        t0 = ti * TT
        self.P.dma(dram[fc * 128:(fc + 1) * 128, t0:t0 + TT], src, [sk], [(dname, ti, fc)], q="sp")

    def v_tok(self, wv, name, col0, xb, xbk, ti):
        for b in range(4):
            for half in range(2):
                pa, pk = self.proj_tok(wv, "WA", name, col0 + half * 512, 512, xb, xbk, b)
                t, tk = self.t512.next()
                self.cp("act", t[:], pa[:], [pk], [tk])
                r0 = ti * TT + b * 128
                self.P.dma(self.Vt[r0:r0 + 128, half * 512:(half + 1) * 512], t[:], [tk], [("Vt", ti, b, half)], q="sp")

    def ffn1(self, li):
        P = self.P
        self.fence("WA")
        w1 = self.wa_view(8, 2 * DFF)
        self.load_w(w1, self.w[li][2], D, 2 * DFF, "WA", "wF1")
        self.fence("WB")
        self.load_w(self.wb_view(22, D), self.w[li][3], DFF, D, "WB", "wF2")
        for ti in range(self.NT):
            xb, xbk = self.load_xb(ti)
            for j in range(22):
                pg, pgk = self.proj_fm(w1, "WA", "wF1", j * 128, xb, xbk)
                pu, puk = self.proj_fm(w1, "WA", "wF1", DFF + j * 128, xb, xbk)
                t, tk = self.t512.next()
                self.act(t[:], pg[:], AF.Silu, [pgk], [tk])
                hb, hbk = self.b512.next()
                self.tt("dve", hb[:], t[:], pu[:], ALU.mult, [tk, puk], [hbk])
                self.out_fm(self.Hd, "Hd", ti, j, hb[:], hbk)

    def ffn2(self, li, last):
        P = self.P
        self.fence("WA")
        w2 = self.wb_view(22, D)
        hs = [self.WA[:, i * 5632:(i + 1) * 5632].bitcast(BF16).rearrange("p (j t) -> p j t", t=TT) for i in range(2)]
        ys = [self.waf(11264 + i * 4096, [128, 8, TT]) for i in range(2)]
        g2, b2 = PV[("g2", li)], PV[("b2", li)]
        for ti in range(self.NT):
            t0 = ti * TT
            h, hk = hs[ti % 2], ("WA", "h", ti % 2)
            P.dma(h, self.Hd[:, t0:t0 + TT].rearrange("(j p) t -> p j t", p=128),
                  [("Hd", ti, j) for j in range(22)], [hk])
            y = ys[ti % 2]
            yks = [("WA", "y", ti % 2, fc) for fc in range(8)]
            for fc in range(8):
                pa, pk = self.pa.next()
                for j in range(22):
                    self.mm(pa[:], w2[:, j, fc * 128:(fc + 1) * 128], h[:, j, :], j == 0, j == 21,
                            [("WB", "wF2", j, 0), hk], [pk])
                xr, xrk = self.t512.next()
                P.dma(xr[:], self.X32[fc * 128:(fc + 1) * 128, t0:t0 + TT], [("X32", ti)], [xrk])
                self.stt("dve", y[:, fc, :], xr[:], ALPHA, pa[:], ALU.mult, ALU.add, [xrk, pk], [yks[fc]])
                self.ln_acc(fc, y[:, fc, :], yks[fc])
            xb, xbk = self.xb.next()
            self.ln_finish(y, yks, g2, b2, xb, xbk)
            if last:
                for b in range(4):
                    for half in range(2):
                        pa, pk = self.pa.next()
                        for q in range(4):
                            self.tr(pa[:, q * 128:(q + 1) * 128], self.x32[:, half * 4 + q, b * 128:(b + 1) * 128],
                                    self.ident, [("x32", half * 4 + q), "cst"], [pk])
                        t, tk = self.t512.next()
                        self.cp("act", t[:], pa[:], [pk], [tk])
                        P.dma(self.out_d[t0 + b * 128:t0 + (b + 1) * 128, half * 512:(half + 1) * 512], t[:],
                              [tk], [("out", ti, b, half)], q="sp")
            else:
                self.store_x(ti, xb, xbk)

    def epilogue(self, li, ti, dv, nwcol, yoff):
        P = self.P
        import os
        if os.environ.get("K_SKIP_EPI"):
            xb, xbk = self.xb.next()
            self.store_x(ti, xb, xbk)
            return
        wO = self.wb_view(8, D)
        nfc = dv // 128
        ob, obk = self.xb.next()
        t0 = ti * TT
        def head_gen(hd):
            pa, pk = self.pa.next()
            for q in range(nfc):
                fc = hd * nfc + q
                t, tk = self.t512.next()
                self.act(t[:], self.x32[:, fc, :], AF.Square, [("x32", fc)], [tk])
                self.mm(pa[:], self.ones, t[:], q == 0, q == nfc - 1, ["cst", tk], [pk])
            yield
            rs, rsk = self.t512.next()
            self.rsqrt(rs[:], pa[:], 1.0 / dv, RMS_EPS, [pk], rsk)
            for q in range(nfc):
                fc = hd * nfc + q
                r, rk = self.t512.next()
                P.dma(r[:], self.Rd[fc * 128:(fc + 1) * 128, t0:t0 + TT], [("Rd", ti, fc)], [rk])
                t, tk = self.t512.next()
                self.tt("dve", t[:], self.x32[:, fc, :], rs[:], ALU.mult, [("x32", fc), rsk], [tk])
                self.stt("dve", ob[:, fc, :], t[:], self.pv[:, nwcol + q:nwcol + q + 1], r[:], ALU.mult, ALU.mult,
                         [tk, rk, "pv"], [(obk, fc)])

        self.run_pipe((head_gen(hd) for hd in range(8 // nfc)), 2)
        y = self.waf(yoff, [128, 8, TT])
        yks = [("WA", "ye", fc) for fc in range(8)]
        for fo in range(8):
            pa, pk = self.pa.next()
            for kc in range(8):
                self.mm(pa[:], wO[:, kc, fo * 128:(fo + 1) * 128], ob[:, kc, :], kc == 0, kc == 7,
                        [("WB", "wO", kc, 0), (obk, kc)], [pk])
            xr, xrk = self.t512.next()
            P.dma(xr[:], self.X32[fo * 128:(fo + 1) * 128, t0:t0 + TT], [("X32", ti)], [xrk])
            self.stt("dve", y[:, fo, :], xr[:], ALPHA, pa[:], ALU.mult, ALU.add, [xrk, pk], [yks[fo]])
            self.ln_acc(fo, y[:, fo, :], yks[fo])
        xb, xbk = self.xb.next()
        self.ln_finish(y, yks, PV[("g1", li)], PV[("b1", li)], xb, xbk)
        self.store_x(ti, xb, xbk)

    def hgrn_A(self, li):
        P = self.P
        self.fence("WA")
        wA = self.wa_view(8, HGRN_IN)
        self.load_w(wA, self.w[li][0], D, HGRN_IN, "WA", "wA")
        sm = self.sm[:, 128:216]
        lbl = self.pv[:, PV["lbl"]:PV["lbl"] + 32]
        self.act(sm[:, 0:32], lbl, AF.Exp, ["pv"], ["sm"])
        ev = sm[:, 0:32].rearrange("p (c d) -> p c d", d=4)
        self.red("dve", sm[:, 32:40], ev, ["sm"], ["sm1"])
        self.red("dve", sm[:, 40:48], ev[:, :, 1:li + 1], ["sm"], ["sm2"])
        P.op("dve", lambda e: e.reciprocal(out=sm[:, 48:56], in_=sm[:, 32:40]), ["sm1"], ["sm3"])
        self.tt("dve", sm[:, 56:64], sm[:, 40:48], sm[:, 48:56], ALU.mult, ["sm2", "sm3"], ["lb"])
        self.ts("dve", sm[:, 64:72], sm[:, 56:64], -1.0, ALU.mult, ["lb"], ["oml"], s2=1.0, op1=ALU.add)
        for ti in range(self.NT):
            xb, xbk = self.load_xb(ti)
            for fc in range(8):
                pa, pk = self.proj_fm(wA, "WA", "wA", fc * 128, xb, xbk)
                t, tk = self.t512.next()
                self.act(t[:], pa[:], AF.Silu, [pk], [tk])
                self.out_fm(self.Qd, "Qd", ti, fc, t[:], tk)
            for fc in range(8):
                pa, pk = self.proj_fm(wA, "WA", "wA", 1024 + fc * 128, xb, xbk)
                t, tk = self.t512.next()
                self.act(t[:], pa[:], AF.Sigmoid, [pk], [tk])
                self.ts("dve", t[:], t[:], sm[:, 64 + fc:65 + fc], ALU.mult, [tk, "oml", "lb"], [tk],
                        s2=sm[:, 56 + fc:57 + fc], op1=ALU.add)
                g, gk = self.t512.next()
                self.act(g[:], t[:], AF.Ln, [tk], [gk])
                self.out_fm(self.Gd, "Gd", ti, fc, g[:], gk)
                k, kk = self.t512.next()
                self.ts("dve", k[:], t[:], -1.0, ALU.mult, [tk], [kk], s2=1.0, op1=ALU.add)
                self.out_fm(self.Kd, "Kd", ti, fc, k[:], kk)
            self.v_tok(wA, "wA", 2048, xb, xbk, ti)
            for fc in range(8):
                pa, pk = self.proj_fm(wA, "WA", "wA", 3072 + fc * 128, xb, xbk)
                t, tk = self.t512.next()
                self.act(t[:], pa[:], AF.Silu, [pk], [tk])
                self.out_fm(self.Rd, "Rd", ti, fc, t[:], tk)

    def gla_A(self, li):
        P = self.P
        self.fence("WA")
        wA = self.wa_view(8, GLA_IN)
        self.load_w(wA, self.w[li][0], D, GLA_IN, "WA", "wA")
        wg = self.waf(13000, [16, 512])
        P.dma(wg, self.wgk2_d, [], [("WA", "wgk2")])
        for ti in range(self.NT):
            xb, xbk = self.load_xb(ti)
            for fc in range(4):
                pa, pk = self.proj_fm(wA, "WA", "wA", fc * 128, xb, xbk)
                t, tk = self.t512.next()
                self.cp("act", t[:], pa[:], [pk], [tk])
                self.out_fm(self.Qd, "Qd", ti, fc, t[:], tk)
            for fc in range(4):
                pa, pk = self.proj_fm(wA, "WA", "wA", 512 + fc * 128, xb, xbk)
                t, tk = self.t512.next()
                self.cp("act", t[:], pa[:], [pk], [tk])
                self.out_fm(self.Kd, "Kd", ti, fc, t[:], tk)
            self.v_tok(wA, "wA", 1024, xb, xbk, ti)
            for fc in range(8):
                pa, pk = self.proj_fm(wA, "WA", "wA", 2048 + fc * 128, xb, xbk)
                t, tk = self.t512.next()
                self.act(t[:], pa[:], AF.Silu, [pk], [tk])
                self.out_fm(self.Rd, "Rd", ti, fc, t[:], tk)
            pa, pk = self.proj_fm(wA, "WA", "wA", 3072, xb, xbk, M=16)
            gk_, gkk = self.waf(13512, [16, 512]), ("WA", "gk")
            self.cp("act", gk_[0:16, :], pa[0:16, :], [pk], [gkk])
            for fc in range(4):
                pb, pbk = self.pa.next()
                self.mm(pb[:], wg[0:16, fc * 128:(fc + 1) * 128], gk_[0:16, :], True, True, [("WA", "wgk2"), gkk], [pbk])
                t, tk = self.t512.next()
                self.act(t[:], pb[:], AF.Sigmoid, [pbk, "pv"], [tk], bias=self.pv[:, PV["bgk"] + fc:PV["bgk"] + fc + 1])
                self.act(t[:], t[:], AF.Ln, [tk], [tk])
                g, gk2 = self.t512.next()
                self.ts("dve", g[:], t[:], 1.0 / 16.0, ALU.mult, [tk], [gk2])
                self.out_fm(self.Gd, "Gd", ti, fc, g[:], gk2)

    def gla_B(self, li, H, dv, nwcol):
        P = self.P
        self.fence("WA")
        S32 = self.S32
        P.op("pool", lambda e: e.memset(S32[:], 0.0), [], ["S32"])
        q32 = self.waf(0, [128, 4, TT])
        k32 = self.waf(2048, [128, 4, TT])
        g32 = self.waf(4096, [128, 4, TT])
        gm = self.waf(6144, [128, 4, TT])
        AT = self.waf(8192, [64, 4, TT])
        kT = self.WA[0:64, 10240:14336].rearrange("p (n h d) -> p n h d", h=4, d=128)
        Vc = [self.waf(14336 + i * 1024, [64, 1024]) for i in range(2)]
        yoff = 16384
        Gs = self.waf(20480, [128, 4, TT])
        sm = self.sm
        egm = sm[:, 0:32].rearrange("p (h n) -> p h n", n=8)
        egl = sm[:, 32:64].rearrange("p (h n) -> p h n", n=8)
        gam = sm[:, 64:96].rearrange("p (h n) -> p h n", n=8)
        nfc = dv // 128
        scm = self.cst[:, C_SC:C_SC + 512]
        mu8 = self.cst[0:64, C_MU:C_MU + 512]
        scale = 128 ** -0.5
        vi = 0
        import os
        lvl = int(os.environ.get("K_REC_LEVEL", "9"))
        for ti in range(self.NT):
            t0 = ti * TT
            for gi in range(H // 4 if lvl >= 1 else 0):
                h0 = gi * 4
                for (dst, dr, nm) in ((q32, self.Qd, "Qd"), (k32, self.Kd, "Kd"), (g32, self.Gd, "Gd")):
                    P.dma(dst, self.tile_fm(dr, ti, h0 * 128, 512), [(nm, ti, h0 + i) for i in range(4)],
                          [("WA", nm + "t", i) for i in range(4)])
                def head_gen(hh):
                    kq, kk_, kg, kgm = ("WA", "Qdt", hh), ("WA", "Kdt", hh), ("WA", "Gdt", hh), ("WA", "gm", hh)
                    self.scan(Gs[:, hh, :], scm, g32[:, hh, :], [kg, "cst"], [("WA", "Gs", hh)])
                    kg = ("WA", "Gs", hh)
                    Gv = Gs[:, hh, :].rearrange("p (n c) -> p n c", c=64)
                    gmv = gm[:, hh, :].rearrange("p (n c) -> p n c", c=64)
                    self.tt("dve", gmv, Gv, Gv[:, :, 31:32].broadcast_to([128, 8, 64]), ALU.subtract, [kg], [kgm])
                    e1, e1k = self.t512.next()
                    self.act(e1[:], gm[:, hh, :], AF.Exp, [kgm], [e1k])
                    e2, e2k = self.t512.next()
                    self.act(e2[:], gm[:, hh, :], AF.Exp, [kgm], [e2k], scale=-1.0)
                    self.act(egm[:, hh, :], Gv[:, :, 31], AF.Exp, [kg], [("egm", hh)])
                    self.act(egl[:, hh, :], gmv[:, :, 63], AF.Exp, [kgm], [("egl", hh)])
                    self.act(gam[:, hh, :], Gv[:, :, 63], AF.Exp, [kg], [("gam", hh)])
                    yield
                    self.stt("dve", q32[:, hh, :], q32[:, hh, :], scale, e1[:], ALU.mult, ALU.mult, [kq, e1k], [kq])
                    self.tt("dve", k32[:, hh, :], k32[:, hh, :], e2[:], ALU.mult, [kk_, e2k], [kk_])
                    pa, pk = self.pa.next()
                    for n in range(8):
                        self.mm(pa[0:64, n * 64:(n + 1) * 64], k32[:, hh, n * 64:(n + 1) * 64],
                                q32[:, hh, n * 64:(n + 1) * 64], True, True, [kk_, kq], [pk])
                    yield
                    self.tt("dve", AT[:, hh, :], pa[0:64, :], mu8, ALU.mult, [pk, "cst"], [("WA", "AT", hh)])
                    qv = q32[:, hh, :].rearrange("p (n c) -> p n c", c=64)
                    kv = k32[:, hh, :].rearrange("p (n c) -> p n c", c=64)
                    self.tt("dve", qv, qv, egm[:, hh, :].unsqueeze(2).broadcast_to([128, 8, 64]), ALU.mult,
                            [kq, ("egm", hh)], [kq])
                    self.tt("dve", kv, kv, egl[:, hh, :].unsqueeze(2).broadcast_to([128, 8, 64]), ALU.mult,
                            [kk_, ("egl", hh)], [kk_])
                    pd, pdk = self.pd.next()
                    for n in range(8):
                        self.tr(pd[0:64, n * 128:(n + 1) * 128], k32[:, hh, n * 64:(n + 1) * 64], self.ident,
                                [kk_, "cst"], [pdk])
                    for hf in range(2):
                        self.cp("act", kT[:, hf * 4:(hf + 1) * 4, hh, :],
                                pd[0:64, hf * 512:(hf + 1) * 512].rearrange("p (n d) -> p n d", d=128), [pdk],
                                [("WA", "kT", hh)])

                self.run_pipe((head_gen(hh) for hh in range(4)), 3)
                for n in range(8 if lvl >= 5 else 0):
                    V = Vc[vi % 2]
                    vk = ("WA", "Vc", vi % 2)
                    vi += 1
                    r0 = t0 + n * 64
                    P.dma(V, self.Vt[r0:r0 + 64, :], [("Vt", ti, n // 2, hf) for hf in range(2)], [vk])
                    po, pok = self.pa.next()
                    for hh in range(4):
                        h = h0 + hh
                        for eh in range(nfc):
                            slot = hh * nfc + eh
                            col = h * dv + eh * 128
                            self.mm(po[:, slot * 64:(slot + 1) * 64], S32[:, col:col + 128],
                                    q32[:, hh, n * 64:(n + 1) * 64], True, False, ["S32", ("WA", "Qdt", hh)], [pok])
                            self.mm(po[:, slot * 64:(slot + 1) * 64], V[:, col:col + 128],
                                    AT[:, hh, n * 64:(n + 1) * 64], False, True, [vk, ("WA", "AT", hh)], [pok])
                    fc0 = h0 * nfc
                    ns = 4 * nfc
                    self.cp("act", self.x32[:, fc0:fc0 + ns, n * 64:(n + 1) * 64],
                            po[:, 0:ns * 64].rearrange("p (s c) -> p s c", c=64), [pok],
                            [("x32", fc0 + s) for s in range(ns)])
                    if lvl < 6:
                        continue
                    pS, pSk = self.pd.next()
                    for hh in range(4):
                        h = h0 + hh
                        self.mm(pS[:, hh * dv:(hh + 1) * dv], kT[:, n, hh, :], V[:, h * dv:(h + 1) * dv], True, True,
                                [("WA", "kT", hh), vk], [pSk])
                    Sv = S32[:, h0 * dv:(h0 + 4) * dv].rearrange("p (h e) -> p h e", e=dv)
                    self.tt("dve", Sv, Sv, gam[:, :, n:n + 1].broadcast_to([128, 4, dv]), ALU.mult,
                            ["S32"] + [("gam", hh) for hh in range(4)], ["S32"])
                    for c0 in range(0, 4 * dv, 512):
                        self.tt("dve", S32[:, h0 * dv + c0:h0 * dv + c0 + 512], S32[:, h0 * dv + c0:h0 * dv + c0 + 512],
                                pS[:, c0:c0 + 512], ALU.add, ["S32", pSk], ["S32"])
            self.epilogue(li, ti, dv, nwcol, yoff)

    def gdn_A(self, li):
        P = self.P
        j = li // 3
        self.fence("WA")
        wA = self.wa_view(8, GDN_IN)
        self.load_w(wA, self.w[li][0], D, GDN_IN, "WA", "wA")
        pres = Rot([(self.waf(16448 + i * 520, [128, 516]), ("WA", "pre", i)) for i in range(3)])
        tl = Rot([(self.waf(18008 + i * 512, [128, 512]), ("WA", "tl", i)) for i in range(7)])
        tkR = Rot([(self.waf(18008 + i * 512, [128, 512]), ("WA", "tl", i)) for i in range(7, 9)])
        sqR = self.t512
        carry = self.carry
        P.op("pool", lambda e: e.memset(carry[:], 0.0), [], [("carry", fc) for fc in range(24)])
        sm = self.sm
        negA = sm[:, 208:216]
        self.act(negA, self.pr[:, j * 16:j * 16 + 8], AF.Exp, ["pr"], ["negA"])
        self.ts("dve", negA, negA, -1.0, ALU.mult, ["negA"], ["negA"])
        cw0 = PV[("conv", j)]
        for ti in range(self.NT):
            t0 = ti * TT
            xb, xbk = self.load_xb(ti)
            def fc_s1(fc, xb=xb, xbk=xbk):
                pa, pk = self.proj_fm(wA, "WA", "wA", fc * 128, xb, xbk)
                pre, prk = pres.next()
                self.cp("pool", pre[:, 0:3], carry[:, fc, 0:3], [("carry", fc)], [prk])
                self.cp("act", pre[:, 3:515], pa[:], [pk], [prk])
                self.cp("pool", carry[:, fc, 0:3], pre[:, 512:515], [prk], [("carry", fc)])
                acc, ak = tl.next()
                wc = lambda k: self.pv[:, cw0 + fc * 4 + k:cw0 + fc * 4 + k + 1]
                self.ts("dve", acc[:], pre[:, 3:515], wc(3), ALU.mult, [prk, "pv"], [ak])
                for k in (2, 1, 0):
                    self.stt("dve", acc[:], pre[:, k:k + 512], wc(k), acc[:], ALU.mult, ALU.add, [prk, ak, "pv"], [ak])
                return acc, ak

            def fc_gen(fcs, ti=ti, t0=t0):
                st = [fc_s1(fc) for fc in fcs]
                sqs = []
                for fc, (s, sk) in zip(fcs, st):
                    self.act(s[:], s[:], AF.Silu, [sk], [sk])
                    if fc < 16:
                        sq, sqk = sqR.next()
                        self.tt("dve", sq[:], s[:], s[:], ALU.mult, [sk], [sqk])
                        sqs.append((sq, sqk))
                yield
                if fcs[0] < 16:
                    pbs = []
                    for (sq, sqk) in sqs:
                        pb, pbk = self.pa.next()
                        self.mm(pb[:], self.ones, sq[:], True, True, ["cst", sqk], [pbk])
                        pbs.append((pb, pbk))
                    for fc, (s, sk), (sq, sqk), (pb, pbk) in zip(fcs, st, sqs, pbs):
                        self.rsqrt(sq[:], pb[:], 1.0, L2_EPS, [pbk, sqk], sqk)
                        if fc < 8:
                            self.stt("dve", s[:], s[:], 128 ** -0.5, sq[:], ALU.mult, ALU.mult, [sk, sqk], [sk])
                            self.out_fm(self.Qd, "Qd", ti, fc, s[:], sk)
                        else:
                            self.tt("dve", s[:], s[:], sq[:], ALU.mult, [sk, sqk], [sk])
                            self.out_fm(self.Kd, "Kd", ti, fc - 8, s[:], sk)
                    if fcs[0] < 8:
                        return
                    yield
                for fc, (s, sk) in zip(fcs, st):
                    dt_, dn, hcol = (self.Kt, "Kt", (fc - 8) * 128) if fc < 16 else (self.Vt, "Vt", (fc - 16) * 128)
                    pt, ptk = self.pa.next()
                    for b in range(4):
                        self.tr(pt[:, b * 128:(b + 1) * 128], s[:, b * 128:(b + 1) * 128], self.ident, [sk, "cst"], [ptk])
                    tk_, tkk = tkR.next()
                    self.cp("act", tk_[:], pt[:], [ptk], [tkk])
                    P.dma(dt_[t0:t0 + TT, hcol:hcol + 128].rearrange("(b p) d -> p b d", p=128),
                          tk_[:].rearrange("p (b d) -> p b d", d=128), [tkk], [(dn, ti, hcol // 128)], q="sp")

            self.run_pipe((fc_gen([fc, fc + 1]) for fc in range(0, 24, 2)), 3)
            for fc in range(8):
                pa, pk = self.proj_fm(wA, "WA", "wA", 3072 + fc * 128, xb, xbk)
                t, tk = tl.next()
                self.act(t[:], pa[:], AF.Silu, [pk], [tk])
                self.out_fm(self.Rd, "Rd", ti, fc, t[:], tk)
            for b in range(4):
                pa, pk = self.proj_tok(wA, "WA", "wA", 4096, 16, xb, xbk, b)
                bg, bgk = tl.next()
                self.act(bg[:, 0:8], pa[:, 0:8], AF.Sigmoid, [pk], [bgk])
                self.tt("dve", bg[:, 16:24], pa[:, 8:16], self.pr[:, j * 16 + 8:j * 16 + 16], ALU.add, [pk, "pr"], [bgk])
                self.act(bg[:, 16:24], bg[:, 16:24], AF.Exp, [bgk], [bgk])
                self.act(bg[:, 16:24], bg[:, 16:24], AF.Ln, [bgk], [bgk], bias=1.0)
                self.tt("dve", bg[:, 8:16], bg[:, 16:24], negA, ALU.mult, [bgk, "negA"], [bgk])
                r0 = t0 + b * 128
                P.dma(self.BGd[r0:r0 + 128, :], bg[:, 0:16], [bgk], [("BGd", ti, b)], q="sp")

    def gdn_B(self, li):
        P = self.P
        j = li // 3
        self.fence("WA")
        S32 = self.S32
        P.op("pool", lambda e: e.memset(S32[:], 0.0), [], ["S32"])
        W = lambda off, shape: self.waf(off, shape)
        qc = W(0, [128, 8, 64])
        kc = W(512, [128, 8, 64])
        ktok = W(1024, [64, 1024])
        vtok = W(2048, [64, 1024])
        bg = W(3072, [64, 16])
        o = 3136
        names512 = ["rhs1", "DL", "DU", "N", "NT", "Pa", "Pb", "PTa", "PTb", "Ta", "Tb", "AT"]
        t5 = {}
        for nm in names512:
            t5[nm] = W(o, [64, 8, 64])
            o += 512
        t10 = {}
        for nm in ["ru", "rw", "u", "vnew", "khat"]:
            t10[nm] = W(o, [64, 1024])
            o += 1024
        qh = W(o, [128, 8, 64])
        o += 512
        wT = W(o, [128, 512])
        o += 512
        eGr = W(o, [128, 512])
        o += 512
        qcs, kcs, bgs = [qc], [kc], [bg]
        qcs.append(W(o, [128, 8, 64]))
        o += 512
        kcs.append(W(o, [128, 8, 64]))
        o += 512
        bgs.append(W(o, [64, 16]))
        o += 64
        yoff = o
        assert yoff + 4096 <= 23040, yoff
        K = lambda nm: ("WA", nm)
        sg = self.sm
        Gcol, eGc, beg, egl, nbeta, tmp8 = (sg[0:64, 0:8], sg[0:64, 8:16], sg[0:64, 16:24], sg[0:64, 24:32],
                                            sg[0:64, 32:40], sg[0:64, 40:48])
        gam = sg[:, 48:56]
        cst = self.cst
        triU = cst[0:64, C_MU:C_MU + 64]
        mu8 = cst[0:64, C_MU:C_MU + 512].rearrange("p (h c) -> p h c", c=64)
        sl8 = cst[0:64, C_SL:C_SL + 512].rearrange("p (h c) -> p h c", c=64)
        i8 = cst[0:64, C_I8:C_I8 + 512].rearrange("p (h c) -> p h c", c=64)
        id64 = cst[0:64, C_ID:C_ID + 64]
        ones64 = cst[0:64, C_ONE:C_ONE + 128]
        v3 = lambda ap: ap.rearrange("p (h c) -> p h c", c=64)
        b8 = lambda ap, n: ap.unsqueeze(2).broadcast_to([64, 8, n])

        def mm8(ps_ap, pk, lhs_of, rhs_of, r, width, npart):
            for h in range(8):
                self.mm(ps_ap[0:npart, h * width:(h + 1) * width], lhs_of(h), rhs_of(h), True, True, r, [pk])

        for ti in range(self.NT):
            for n in range(NCH):
                t0c = ti * TT + n * C
                par = n % 2
                qc, kc, bg = qcs[par], kcs[par], bgs[par]
                kqc, kkc, kbg = K("qc%d" % par), K("kc%d" % par), K("bg%d" % par)
                P.dma(bg, self.BGd[t0c:t0c + C, :], [("BGd", ti, n // 2)], [kbg])
                P.dma(kc, self.Kd[:, t0c:t0c + C].rearrange("(h p) t -> p h t", p=128),
                      [("Kd", ti, f) for f in range(8)], [kkc])
                P.dma(qc, self.Qd[:, t0c:t0c + C].rearrange("(h p) t -> p h t", p=128),
                      [("Qd", ti, f) for f in range(8)], [kqc])
                P.dma(ktok, self.Kt[t0c:t0c + C, :], [("Kt", ti, f) for f in range(8)], [K("ktok")])
                P.dma(vtok, self.Vt[t0c:t0c + C, :], [("Vt", ti, f) for f in range(8)], [K("vtok")])
                beta, g = bg[:, 0:8], bg[:, 8:16]
                pA, pAk = self.pa.next()
                self.mm(pA[0:64, 0:8], triU, g, True, True, ["cst", kbg], [pAk])
                self.mm(pA[:, 8:16], ones64, g, True, True, ["cst", kbg], [pAk])
                self.cp("dve", Gcol, pA[0:64, 0:8], [pAk], ["Gcol"])
                self.tt("dve", tmp8, pA[0:64, 8:16], Gcol, ALU.subtract, [pAk, "Gcol"], ["tmp8"])
                self.act(gam, pA[:, 8:16], AF.Exp, [pAk], ["gam"])
                self.ts("dve", nbeta, beta, -1.0, ALU.mult, [kbg], ["nbeta"])
                self.tt("dve", t5["rhs1"], b8(g, 64), mu8, ALU.mult, [kbg, "cst"], [K("rhs1")])
                pT1, pT1k = self.pa.next()
                self.mm(pT1[:, :], ones64, t5["rhs1"].rearrange("p h c -> p (h c)"), True, True,
                        ["cst", K("rhs1")], [pT1k])
                DL, DU = t5["DL"], t5["DU"]
                self.tt("dve", DL, b8(Gcol, 64), v3(pT1[0:64, :]), ALU.subtract, ["Gcol", pT1k], [K("DL")])
                self.tt("dve", DU, v3(pT1[0:64, :]), b8(Gcol, 64), ALU.subtract, ["Gcol", pT1k], [K("DU")])
                self.act(eGr[:], pT1[:, :], AF.Exp, [pT1k], [K("eGr")])
                self.ts("dve", DL, DL, 0.0, ALU.min, [K("DL")], [K("DL")])
                self.act(DL, DL, AF.Exp, [K("DL")], [K("DL")])
                self.tt("dve", DL, DL, sl8, ALU.mult, [K("DL"), "cst"], [K("DL")])
                pKK, pKKk = self.pa.next()
                mm8(pKK, pKKk, lambda h: kc[:, h, :], lambda h: kc[:, h, :], [kkc], 64, 64)
                N, NT, AT = t5["N"], t5["NT"], t5["AT"]
                self.tt("dve", N, v3(pKK[0:64, :]), DL, ALU.mult, [pKKk, K("DL")], [K("N")])
                self.tt("dve", N, N, b8(nbeta, 64), ALU.mult, [K("N"), "nbeta"], [K("N")])
                pN, pNk = self.pa.next()
                for h in range(8):
                    self.tr(pN[0:64, h * 64:(h + 1) * 64], N[:, h, :], id64, [K("N"), "cst"], [pNk])
                self.cp("act", NT, v3(pN[0:64, :]), [pNk], [K("NT")])
                ru, rw, u, vnew, khat = t10["ru"], t10["rw"], t10["u"], t10["vnew"], t10["khat"]
                v4 = lambda ap: ap.rearrange("p (h d) -> p h d", d=128)

                def d_qk():
                    pQK, pQKk = self.pa.next()
                    mm8(pQK, pQKk, lambda h: kc[:, h, :], lambda h: qc[:, h, :], [kkc, kqc], 64, 64)
                    self.tt("dve", AT, v3(pQK[0:64, :]), DU, ALU.mult, [pQKk, K("DU")], [K("AT")])

                deferred = [
                    lambda: self.ts("dve", DU, DU, 0.0, ALU.min, [K("DU")], [K("DU")]),
                    lambda: self.act(DU, DU, AF.Exp, [K("DU")], [K("DU")]),
                    lambda: self.tt("dve", DU, DU, mu8, ALU.mult, [K("DU"), "cst"], [K("DU")]),
                    d_qk,
                    lambda: self.act(egl, tmp8, AF.Exp, ["tmp8"], ["egl"]),
                    lambda: self.act(eGc, Gcol, AF.Exp, ["Gcol"], ["eGc"]),
                    lambda: self.tt("dve", beg, beta, eGc, ALU.mult, [kbg, "eGc"], ["beg"]),
                    lambda: self.tt("dve", qh, qc, eGr[:].rearrange("p (h c) -> p h c", c=64), ALU.mult,
                                    [kqc, K("eGr")], [K("qh")]),
                    lambda: self.tt("dve", v4(ru), v4(vtok), b8(beta, 128), ALU.mult, [K("vtok"), kbg], [K("ru")]),
                    lambda: self.tt("dve", v4(rw), v4(ktok), b8(beg, 128), ALU.mult, [K("ktok"), "beg"], [K("rw")]),
                    lambda: self.tt("dve", v4(khat), v4(ktok), b8(egl, 128), ALU.mult, [K("ktok"), "egl"], [K("khat")]),
                ]
                Tc, Tk = t5["Ta"], K("Ta")
                self.tt("dve", Tc, NT, i8, ALU.add, [K("NT"), "cst"], [Tk])
                Pp, Ppk, PTp, PTpk = N, K("N"), NT, K("NT")
                for jj in range(1, 6):
                    Pn, Pnk = (t5["Pa"], K("Pa")) if jj % 2 else (t5["Pb"], K("Pb"))
                    PTn, PTnk = (t5["PTa"], K("PTa")) if jj % 2 else (t5["PTb"], K("PTb"))
                    Tn, Tnk = (t5["Tb"], K("Tb")) if jj % 2 else (t5["Ta"], K("Ta"))
                    p1, p1k = self.pa.next()
                    mm8(p1, p1k, lambda h: PTp[:, h, :], lambda h: Pp[:, h, :], [Ppk, PTpk], 64, 64)
                    self.cp("act", Pn, v3(p1[0:64, :]), [p1k], [Pnk])
                    if jj < 5:
                        p2, p2k = self.pa.next()
                        mm8(p2, p2k, lambda h: Pp[:, h, :], lambda h: PTp[:, h, :], [Ppk, PTpk], 64, 64)
                        self.cp("dve", PTn, v3(p2[0:64, :]), [p2k], [PTnk])
                    for _ in range(3):
                        if deferred:
                            deferred.pop(0)()
                    p3, p3k = self.pa.next()
                    mm8(p3, p3k, lambda h: Pn[:, h, :], lambda h: Tc[:, h, :], [Pnk, Tk], 64, 64)
                    self.tt("dve", Tn, v3(p3[0:64, :]), Tc, ALU.add, [p3k, Tk], [Tnk])
                    Pp, Ppk, PTp, PTpk, Tc, Tk = Pn, Pnk, PTn, PTnk, Tn, Tnk
                while deferred:
                    deferred.pop(0)()
                TT_ = Tc
                pU, pUk = self.pd.next()
                mm8(pU, pUk, lambda h: TT_[:, h, :], lambda h: ru[:, h * 128:(h + 1) * 128], [Tk, K("ru")], 128, 64)
                self.cp("act", u[:, 0:512], pU[0:64, 0:512], [pUk], [K("u")])
                self.cp("act", u[:, 512:1024], pU[0:64, 512:1024], [pUk], [K("u")])
                pW, pWk = self.pa.next()
                mm8(pW, pWk, lambda h: rw[:, h * 128:(h + 1) * 128], lambda h: TT_[:, h, :], [Tk, K("rw")], 64, 128)
                self.cp("act", wT[:], pW[:, :], [pWk], [K("wT")])
                pWS, pWSk = self.pd.next()
                mm8(pWS, pWSk, lambda h: wT[:, h * 64:(h + 1) * 64], lambda h: S32[:, h * 128:(h + 1) * 128],
                    [K("wT"), "S32"], 128, 64)
                self.tt("dve", vnew[:, 0:512], u[:, 0:512], pWS[0:64, 0:512], ALU.subtract, [K("u"), pWSk], [K("vnew")])
                self.tt("dve", vnew[:, 512:1024], u[:, 512:1024], pWS[0:64, 512:1024], ALU.subtract, [K("u"), pWSk], [K("vnew")])
                pO, pOk = self.pa.next()
                for h in range(8):
                    self.mm(pO[:, h * 64:(h + 1) * 64], S32[:, h * 128:(h + 1) * 128], qh[:, h, :], True, False,
                            ["S32", K("qh")], [pOk])
                    self.mm(pO[:, h * 64:(h + 1) * 64], vnew[:, h * 128:(h + 1) * 128], AT[:, h, :], False, True,
                            [K("vnew"), K("AT")], [pOk])
                self.cp("act", self.x32[:, :, n * 64:(n + 1) * 64], pO[:, :].rearrange("p (s c) -> p s c", c=64), [pOk],
                        self.x32k)
                pS, pSk = self.pd.next()
                mm8(pS, pSk, lambda h: khat[:, h * 128:(h + 1) * 128], lambda h: vnew[:, h * 128:(h + 1) * 128],
                    [K("khat"), K("vnew")], 128, 128)
                Sv = S32[:, :].rearrange("p (h e) -> p h e", e=128)
                self.tt("dve", Sv, Sv, gam.unsqueeze(2).broadcast_to([128, 8, 128]), ALU.mult, ["S32", "gam"], ["S32"])
                self.tt("dve", S32[:, 0:512], S32[:, 0:512], pS[:, 0:512], ALU.add, ["S32", pSk], ["S32"])
                self.tt("dve", S32[:, 512:1024], S32[:, 512:1024], pS[:, 512:1024], ALU.add, ["S32", pSk], ["S32"])
            self.epilogue(li, ti, 128, PV[("gnw", j)], yoff)


_CACHE = {}


def get_nc(TL, layers, dbg=()):
    key = (TL, tuple(layers), tuple(dbg))
    if key not in _CACHE:
        _CACHE[key] = Builder(TL, list(layers), dbg).build()
    return _CACHE[key]


LAYERS = [(0, 0), (1, 1), (2, 2), (0, 3)]


def core_inputs(inp, xs, layers):
    m = {"x": np.ascontiguousarray(xs, dtype=np.float32), "cst": make_consts(), "pv": make_pv(inp),
         "pr": make_pr(inp), "wgk2": np.ascontiguousarray(inp["gla_w_gk2"][0], dtype=np.float32)}
    for kind, li in layers:
        j = li // 3
        if kind == 0:
            wi, wo = inp["gdn_w_in"][j], inp["gdn_w_out"][j]
        elif kind == 1:
            wi, wo = inp["hgrn_w_in"][0], inp["hgrn_w_out"][0]
        else:
            wi, wo = inp["gla_w_in"][0], inp["gla_w_out"][0]
        m["win%d" % li] = np.ascontiguousarray(wi, dtype=np.float32)
        m["wout%d" % li] = np.ascontiguousarray(wo, dtype=np.float32)
        m["wf1_%d" % li] = np.ascontiguousarray(inp["ffn_w_in"][li], dtype=np.float32)
        m["wf2_%d" % li] = np.ascontiguousarray(inp["ffn_w_down"][li], dtype=np.float32)
    return m


def kernel(**inputs):
    inp = {k: np.asarray(v) for k, v in inputs.items()}
    x = inp["x"]
    B, T, _ = x.shape
    nc = get_nc(T, LAYERS)
    maps = [core_inputs(inp, x[c % B], LAYERS) for c in range(8)]
    res = run_bass_kernel_spmd(nc, maps, core_ids=list(range(8)))
    return np.stack([np.asarray(res.results[b]["out"], dtype=np.float32) for b in range(B)], axis=0)
```

```python
import numpy as np
from contextlib import ExitStack
import concourse.bass as bass
import concourse.mybir as mybir
from concourse.bass_utils import run_bass_kernel_spmd

F32 = mybir.dt.float32
BF16 = mybir.dt.bfloat16
AF = mybir.ActivationFunctionType
ALU = mybir.AluOpType
AX = mybir.AxisListType

D = 1024
DFF = 2816
TT = 512
C = 64
NCH = TT // C
DEPTH = 4
ALPHA = float((2 * DEPTH) ** 0.25)
LN_EPS, RMS_EPS, L2_EPS = 1e-5, 1e-6, 1e-6
GDN_IN, HGRN_IN, GLA_IN = 4112, 4096, 3088
SW = 1024

C_ID, C_ONE, C_MU, C_SL, C_SC, C_I8, C_N = 0, 128, 256, 768, 1280, 1792, 2304


def make_consts():
    c = np.zeros((128, C_N), np.float32)
    c[:, C_ID:C_ID + 128] = np.eye(128)
    c[:, C_ONE:C_ONE + 128] = 1.0
    i = np.arange(64)
    mu = (i[:, None] <= i[None, :]).astype(np.float32)
    sl = (i[:, None] > i[None, :]).astype(np.float32)
    c[:64, C_MU:C_MU + 512] = np.tile(mu, (1, 8))
    c[:64, C_SL:C_SL + 512] = np.tile(sl, (1, 8))
    c[:64, C_I8:C_I8 + 512] = np.tile(np.eye(64, dtype=np.float32), (1, 8))
    sc = np.ones((128, 512), np.float32)
    sc[:, ::64] = 0.0
    c[:, C_SC:C_SC + 512] = sc
    return c


def pv_layout():
    off = {}
    n = 0
    for i in range(DEPTH):
        for nm in ("g1", "b1", "g2", "b2"):
            off[(nm, i)] = n
            n += 8
    for j in range(2):
        off[("conv", j)] = n
        n += 96
        off[("gnw", j)] = n
        n += 1
    off["hnw"] = n
    n += 1
    off["lbl"] = n
    n += 32
    off["lnw"] = n
    n += 2
    off["bgk"] = n
    n += 4
    off["n"] = n
    return off


PV = pv_layout()


def fm(v):
    return np.ascontiguousarray(v.reshape(-1, 128).T)


def make_pv(inp):
    pv = np.zeros((128, PV["n"]), np.float32)
    for i in range(DEPTH):
        pv[:, PV[("g1", i)]:PV[("g1", i)] + 8] = fm(inp["mix_ln_g"][i])
        pv[:, PV[("b1", i)]:PV[("b1", i)] + 8] = fm(inp["mix_ln_b"][i])
        pv[:, PV[("g2", i)]:PV[("g2", i)] + 8] = fm(inp["ffn_ln_g"][i])
        pv[:, PV[("b2", i)]:PV[("b2", i)] + 8] = fm(inp["ffn_ln_b"][i])
    for j in range(2):
        cw = inp["gdn_conv_w"][j]
        pv[:, PV[("conv", j)]:PV[("conv", j)] + 96] = cw.T.reshape(24, 128, 4).transpose(1, 0, 2).reshape(128, 96)
        pv[:, PV[("gnw", j)]] = inp["gdn_norm_w"][j]
    pv[:, PV["hnw"]] = inp["hgrn_norm_w"][0]
    lb = inp["hgrn_lb_logits"]
    pv[:, PV["lbl"]:PV["lbl"] + 32] = lb.T.reshape(8, 128, 4).transpose(1, 0, 2).reshape(128, 32)
    pv[:, PV["lnw"]:PV["lnw"] + 2] = fm(inp["gla_norm_w"][0])
    pv[:, PV["bgk"]:PV["bgk"] + 4] = fm(inp["gla_b_gk"][0])
    return pv


def make_pr(inp):
    pr = np.zeros((128, 32), np.float32)
    for j in range(2):
        pr[:, j * 16:j * 16 + 8] = inp["gdn_a_log"][j][None, :]
        pr[:, j * 16 + 8:j * 16 + 16] = inp["gdn_dt_bias"][j][None, :]
    return pr


class Rot:
    def __init__(self, items):
        self.items = items
        self.i = 0

    def next(self):
        it = self.items[self.i % len(self.items)]
        self.i += 1
        return it


class Prog:
    ENGS = ("pe", "act", "dve", "pool", "sp")
    NSLOT = {"sp": 24, "pool": 8, "act": 4}

    def __init__(self, nc):
        self.nc = nc
        self.ops = []

    def op(self, eng, fn, r=(), w=(), dma=False):
        r = list(r)
        for k in list(r) + list(w):
            if isinstance(k, tuple) and k[0] in ("WA", "WB"):
                ek = ("epoch", k[0])
                if ek not in r and ek not in w:
                    r.append(ek)
        self.ops.append({"eng": eng, "fn": fn, "r": tuple(r), "w": tuple(w), "dma": dma})

    def dma(self, out, in_, r, w, q="sp"):
        self.op(q, lambda e: e.dma_start(out=out, in_=in_), r, w, dma=True)

    def emit(self, stack):
        nc = self.nc
        ops = self.ops
        last_w, readers = {}, {}
        eng_pos = {e: 0 for e in self.ENGS}
        for i, o in enumerate(ops):
            deps = set()
            for k in o["r"]:
                if k in last_w:
                    deps.add(last_w[k])
                if isinstance(k, str) and k[:2] in ("pa", "pd"):
                    for rd in readers.get(k, ()):
                        if ops[rd]["eng"] != o["eng"]:
                            deps.add(rd)
            for k in o["w"]:
                if k in last_w:
                    deps.add(last_w[k])
                deps.update(readers.get(k, ()))
            deps.discard(i)
            o["deps"] = deps
            for k in o["r"]:
                readers.setdefault(k, []).append(i)
            for k in o["w"]:
                last_w[k] = i
                readers[k] = []
            o["pos"] = eng_pos[o["eng"]]
            eng_pos[o["eng"]] += 1
            o["signal"] = False
            o["waits_eng"] = {}
            o["waits_dma"] = {}
        seen = {e: {p: -1 for p in self.ENGS} for e in self.ENGS}
        seen_dma = {e: {} for e in self.ENGS}
        slot_cnt, slot_last = {}, {}
        dma_n = {e: 0 for e in self.ENGS}
        for i, o in enumerate(ops):
            e = o["eng"]
            if o["dma"]:
                s = dma_n[e] % self.NSLOT[e]
                dma_n[e] += 1
                key = (e, s)
                if key in slot_last:
                    o["deps"].add(slot_last[key])
                slot_last[key] = i
                slot_cnt[key] = slot_cnt.get(key, 0) + 1
                o["slot"] = key
                o["slot_val"] = 16 * slot_cnt[key]
            for d in sorted(o["deps"]):
                od = ops[d]
                if od["dma"]:
                    if seen_dma[e].get(od["slot"], 0) < od["slot_val"]:
                        seen_dma[e][od["slot"]] = od["slot_val"]
                        o["waits_dma"][od["slot"]] = od["slot_val"]
                    continue
                pe = od["eng"]
                if pe == e:
                    if e in ("pe", "sp"):
                        continue
                    if o["pos"] - od["pos"] > 2:
                        continue
                if seen[e][pe] >= od["pos"]:
                    continue
                seen[e][pe] = od["pos"]
                od["signal"] = True
                cur = o["waits_eng"].get(pe)
                if cur is None or ops[cur]["pos"] < od["pos"]:
                    o["waits_eng"][pe] = d
        tick = {e: 0 for e in self.ENGS}
        for o in ops:
            if o["signal"]:
                tick[o["eng"]] += 1
                o["tick"] = tick[o["eng"]]
        esem = {e: stack.enter_context(nc.semaphore("s_" + e)) for e in self.ENGS}
        ssem = {k: stack.enter_context(nc.semaphore("d_%s%d" % k)) for k in slot_cnt}
        per_eng = {e: [o for o in ops if o["eng"] == e] for e in self.ENGS}

        def run(ename, eng):
            for o in per_eng[ename]:
                for pe, d in o["waits_eng"].items():
                    eng.wait_ge(esem[pe], ops[d]["tick"])
                for sk, sv in o["waits_dma"].items():
                    eng.wait_ge(ssem[sk], sv)
                inst = o["fn"](eng)
                if o["dma"]:
                    inst.then_inc(ssem[o["slot"]], 16)
                elif o["signal"]:
                    inst.then_inc(esem[ename], 1)
            if ename == "sp":
                for k, c in slot_cnt.items():
                    eng.wait_ge(ssem[k], 16 * c)
                for e2 in self.ENGS:
                    if tick[e2]:
                        eng.wait_ge(esem[e2], tick[e2])

        with nc.Block() as block:
            @block.tensor
            def _(eng):
                run("pe", eng)

            @block.scalar
            def _(eng):
                run("act", eng)

            @block.vector
            def _(eng):
                run("dve", eng)

            @block.gpsimd
            def _(eng):
                run("pool", eng)

            @block.sync
            def _(eng):
                run("sp", eng)


class Builder:
    def __init__(self, TL, layers, dbg=(), upto=9):
        self.upto = upto
        self.TL = TL
        self.NT = TL // TT
        self.layers = layers
        self.dbg = set(dbg)
        self.nc = bass.Bass("TRN2", target_bir_lowering=False)
        self.P = Prog(self.nc)
        self.uid = 0

    def mm(self, out, lhsT, rhs, start, stop, r, w):
        self.P.op("pe", lambda e: e.matmul(out, lhsT, rhs, start=start, stop=stop), r, w)

    def tr(self, out, in_, ident, r, w):
        self.P.op("pe", lambda e: e.transpose(out, in_, ident), r, w)

    def act(self, out, in_, func, r, w, bias=None, scale=None):
        kw = {}
        if bias is not None:
            kw["bias"] = bias
        if scale is not None:
            kw["scale"] = scale
        self.P.op("act", lambda e: e.activation(out=out, in_=in_, func=func, **kw), r, w)

    def tt(self, eng, out, in0, in1, op, r, w):
        self.P.op(eng, lambda e: e.tensor_tensor(out=out, in0=in0, in1=in1, op=op), r, w)

    def ts(self, eng, out, in0, s1, op0, r, w, s2=None, op1=None):
        if op1 is None:
            self.P.op(eng, lambda e: e.tensor_scalar(out=out, in0=in0, scalar1=s1, scalar2=None, op0=op0), r, w)
        else:
            self.P.op(eng, lambda e: e.tensor_scalar(out=out, in0=in0, scalar1=s1, scalar2=s2, op0=op0, op1=op1), r, w)

    def stt(self, eng, out, in0, scalar, in1, op0, op1, r, w):
        self.P.op(eng, lambda e: e.scalar_tensor_tensor(out=out, in0=in0, scalar=scalar, in1=in1, op0=op0, op1=op1), r, w)

    def run_pipe(self, gens, depth):
        active, gens, more = [], iter(gens), True
        while True:
            if more and len(active) < depth:
                try:
                    active.insert(0, next(gens))
                except StopIteration:
                    more = False
            if not active:
                break
            for g in list(active):
                try:
                    next(g)
                except StopIteration:
                    active.remove(g)

    def rsqrt(self, out, in_, scale, eps, r, wk):
        self.act(out, in_, AF.Ln, r, [wk], bias=eps, scale=scale)
        self.act(out, out, AF.Exp, [wk], [wk], scale=-0.5)

    def cp(self, eng, out, in_, r, w):
        if eng == "act":
            self.act(out, in_, AF.Copy, r, w)
        else:
            self.P.op(eng, lambda e: e.tensor_copy(out=out, in_=in_), r, w)

    def red(self, eng, out, in_, r, w):
        self.P.op(eng, lambda e: e.tensor_reduce(out=out, in_=in_, axis=AX.X, op=ALU.add), r, w)

    def scan(self, out, d0, d1, r, w):
        self.P.op("dve", lambda e: e.tensor_tensor_scan(out=out, data0=d0, data1=d1, initial=0.0,
                                                        op0=ALU.mult, op1=ALU.add), r, w)

    def dram(self, name, shape, dt):
        kind = "ExternalOutput" if name in self.dbg else "Internal"
        return self.nc.dram_tensor(name, shape, dt, kind=kind).ap()

    def build(self):
        nc, P, TL, NT = self.nc, self.P, self.TL, self.NT
        st = ExitStack()
        self.st = st
        self.in_names = []

        def ein(n, s):
            self.in_names.append(n)
            return nc.dram_tensor(n, s, F32, kind="ExternalInput").ap()
        self.x_in = ein("x", [TL, D])
        self.cst_d = ein("cst", [128, C_N])
        self.pv_d = ein("pv", [128, PV["n"]])
        self.pr_d = ein("pr", [128, 32])
        self.wgk2_d = ein("wgk2", [16, 512])
        self.w = {}
        for kind, li in self.layers:
            nin = (GDN_IN, HGRN_IN, GLA_IN)[kind]
            self.w[li] = (ein("win%d" % li, [D, nin]) if self.upto >= 1 else None,
                          ein("wout%d" % li, [D, D]) if self.upto >= 1 else None,
                          ein("wf1_%d" % li, [D, 2 * DFF]) if self.upto >= 3 else None,
                          ein("wf2_%d" % li, [DFF, D]) if self.upto >= 3 else None)
        self.out_d = nc.dram_tensor("out", [TL, D], F32, kind="ExternalOutput").ap()
        self.X32 = self.dram("X32", [D, TL], F32)
        self.Xb = self.dram("Xb", [D, TL], BF16)
        self.Hd = self.dram("Hd", [DFF, TL], BF16)
        self.Qd = self.dram("Qd", [D, TL], F32)
        self.Kd = self.dram("Kd", [D, TL], F32)
        self.Gd = self.dram("Gd", [D, TL], F32)
        self.Rd = self.dram("Rd", [D, TL], F32)
        self.Vt = self.dram("Vt", [TL, D], F32)
        self.Kt = self.dram("Kt", [TL, D], F32)
        self.BGd = self.dram("BGd", [TL, 16], F32)

        def sb(name, shape, dt=F32):
            return st.enter_context(nc.sbuf_tensor(name, shape, dt))

        def ps(name, shape, dt=F32):
            return st.enter_context(nc.psum_tensor(name, shape, dt))

        self.cst = sb("cst_s", [128, C_N])
        self.pv = sb("pv_s", [128, PV["n"]])
        self.pr = sb("pr_s", [128, 32])
        self.stg = Rot([(sb("stg%d" % i, [128, SW]), "stg%d" % i) for i in range(2)])
        self.xb = Rot([(sb("xb%d" % i, [128, 8, TT], BF16), "xb%d" % i) for i in range(2)])
        self.x32 = sb("x32", [128, 8, TT])
        self.t512 = Rot([(sb("t512_%d" % i, [128, TT]), "t512_%d" % i) for i in range(4)])
        self.b512 = Rot([(sb("b512_%d" % i, [128, TT], BF16), "b512_%d" % i) for i in range(2)])
        self.s1 = sb("ln_s1", [128, TT])
        self.s2 = sb("ln_s2", [128, TT])
        self.mean = sb("ln_mean", [128, TT])
        self.rstd = sb("ln_rstd", [128, TT])
        self.S32 = sb("S32", [128, 1024])
        self.sm = sb("small", [128, 216])
        self.carry = sb("carry", [128, 24, 4])
        self.dummy = sb("dummy_t", [128, 4])
        self.WA = sb("WA", [128, 23040])
        self.WB = sb("WB", [128, 11264])
        self.pa = Rot([(ps("pa%d" % i, [128, TT]), "pa%d" % i) for i in range(4)])
        self.pd = Rot([(ps("pd%d" % i, [128, 2 * TT]), "pd%d" % i) for i in range(2)])
        self.ident = self.cst[:, C_ID:C_ID + 128]
        self.ones = self.cst[:, C_ONE:C_ONE + 128]

        P.dma(self.cst[:], self.cst_d, [], ["cst"])
        P.dma(self.pv[:], self.pv_d, [], ["pv"])
        P.dma(self.pr[:], self.pr_d, [], ["pr"])
        P.op("pool", lambda e: e.memset(self.S32[:], 0.0), [], ["S32"])
        P.op("pool", lambda e: e.memset(self.carry[:], 0.0), [], ["carry"])

        self.phase0()
        nl = len(self.layers)
        for n, (kind, li) in enumerate(self.layers):
            if self.upto < 1:
                break
            self.wb_load(self.w[li][1], D, D, ("wO", li))
            [self.gdn_A, self.hgrn_A, self.gla_A][kind](li)
            if self.upto < 2:
                break
            if kind == 0:
                self.gdn_B(li)
            elif kind == 1:
                self.gla_B(li, 8, 128, PV["hnw"])
            else:
                self.gla_B(li, 4, 256, PV["lnw"])
            if self.upto < 3:
                break
            self.ffn1(li)
            if self.upto < 4:
                break
            self.ffn2(li, last=(n == nl - 1))
        P.emit(st)
        st.close()
        return nc

    def wa_view(self, kc_n, ncols):
        return self.WA[:, 0:kc_n * ncols // 2].bitcast(BF16).rearrange("p (k n) -> p k n", n=ncols)

    def wb_view(self, kc_n, ncols):
        return self.WB[:, 0:kc_n * ncols // 2].bitcast(BF16).rearrange("p (k n) -> p k n", n=ncols)

    def waf(self, off, shape):
        n = int(np.prod(shape[1:]))
        ap = self.WA[0:shape[0], off:off + n]
        if len(shape) == 3:
            ap = ap.rearrange("p (a b) -> p a b", b=shape[2])
        return ap

    def fence(self, arena):
        d = self.dummy
        self.P.op("pool", lambda e: e.memset(d[:, 0:1], 0.0), [], [("epoch", arena)])

    def load_w(self, dst, w_dram, K, N, arena, name):
        i = 0
        for kc in range(K // 128):
            for n0 in range(0, N, SW):
                nn = min(SW, N - n0)
                s, sk = self.stg.next()
                self.P.dma(s[:, :nn], w_dram[kc * 128:(kc + 1) * 128, n0:n0 + nn], [], [sk])
                self.cp(("pool", "act", "dve")[i % 3], dst[:, kc, n0:n0 + nn], s[:, :nn], [sk],
                        [(arena, name, kc, n0 // SW)])
                i += 1

    def wb_load(self, w_dram, K, N, key):
        self.fence("WB")
        self.load_w(self.wb_view(K // 128, N), w_dram, K, N, "WB", key[0])

    def tile_fm(self, dram, ti, r0=0, nrows=D):
        t0 = ti * TT
        return dram[r0:r0 + nrows, t0:t0 + TT].rearrange("(c p) t -> p c t", p=128)

    @property
    def x32k(self):
        return [("x32", fc) for fc in range(8)]

    def store_x(self, ti, xb, xbk):
        self.P.dma(self.tile_fm(self.X32, ti), self.x32[:], self.x32k, [("X32", ti)], q="sp")
        self.P.dma(self.tile_fm(self.Xb, ti), xb[:], [(xbk, fc) for fc in range(8)], [("Xb", ti)], q="sp")

    def load_xb(self, ti):
        xb, xbk = self.xb.next()
        self.P.dma(xb[:], self.tile_fm(self.Xb, ti), [("Xb", ti)], [(xbk, fc) for fc in range(8)])
        return xb, xbk

    def phase0(self):
        P = self.P
        self.fence("WA")
        xt = self.waf(0, [128, 4, D])
        for ti in range(self.NT):
            t0 = ti * TT
            P.dma(xt, self.x_in[t0:t0 + TT, :].rearrange("(b p) f -> p b f", p=128), [], [("WA", "xt")])
            xb, xbk = self.xb.next()
            for fc in range(8):
                pa, pk = self.pa.next()
                for b in range(4):
                    self.tr(pa[:, b * 128:(b + 1) * 128], xt[:, b, fc * 128:(fc + 1) * 128], self.ident,
                            [("WA", "xt"), "cst"], [pk])
                self.cp("act", self.x32[:, fc, :], pa[:], [pk], [("x32", fc)])
                self.cp("dve", xb[:, fc, :], pa[:], [pk], [(xbk, fc)])
            self.store_x(ti, xb, xbk)

    def ln_acc(self, fc, y_fc, yk):
        if fc == 0:
            self.cp("act", self.s1[:], y_fc, [yk], ["s1"])
            self.act(self.s2[:], y_fc, AF.Square, [yk], ["s2"])
        else:
            self.tt("dve", self.s1[:], self.s1[:], y_fc, ALU.add, [yk, "s1"], ["s1"])
            t, tk = self.t512.next()
            self.act(t[:], y_fc, AF.Square, [yk], [tk])
            self.tt("dve", self.s2[:], self.s2[:], t[:], ALU.add, [tk, "s2"], ["s2"])

    def ln_finish(self, y, yks, gcol, bcol, xb, xbk):
        pa, pk = self.pa.next()
        self.mm(pa[:], self.ones, self.s1[:], True, True, ["cst", "s1"], [pk])
        pb, pbk = self.pa.next()
        self.mm(pb[:], self.ones, self.s2[:], True, True, ["cst", "s2"], [pbk])
        self.ts("dve", self.mean[:], pa[:], 1.0 / D, ALU.mult, [pk], ["mean"])
        t, tk = self.t512.next()
        self.tt("dve", t[:], self.mean[:], self.mean[:], ALU.mult, ["mean"], [tk])
        self.stt("dve", t[:], pb[:], 1.0 / D, t[:], ALU.mult, ALU.subtract, [pbk, tk], [tk])
        self.rsqrt(self.rstd[:], t[:], 1.0, LN_EPS, [tk], "rstd")
        for fc in range(8):
            self.tt("dve", y[:, fc, :], y[:, fc, :], self.mean[:], ALU.subtract, [yks[fc], "mean"], [yks[fc]])
            self.tt("dve", y[:, fc, :], y[:, fc, :], self.rstd[:], ALU.mult, [yks[fc], "rstd"], [yks[fc]])
            self.act(self.x32[:, fc, :], y[:, fc, :], AF.Identity, [yks[fc], "pv"], [("x32", fc)],
                     bias=self.pv[:, bcol + fc:bcol + fc + 1], scale=self.pv[:, gcol + fc:gcol + fc + 1])
            self.act(xb[:, fc, :], y[:, fc, :], AF.Identity, [yks[fc], "pv"], [(xbk, fc)],
                     bias=self.pv[:, bcol + fc:bcol + fc + 1], scale=self.pv[:, gcol + fc:gcol + fc + 1])

    def proj_fm(self, wv, arena, name, col0, xb, xbk, M=128):
        pa, pk = self.pa.next()
        for kc in range(8):
            self.mm(pa[0:M, :], wv[:, kc, col0:col0 + M], xb[:, kc, :], kc == 0, kc == 7,
                    [(arena, name, kc, col0 // SW), (xbk, kc)], [pk])
        return pa, pk

    def proj_tok(self, wv, arena, name, col0, ncol, xb, xbk, b):
        pa, pk = self.pa.next()
        for kc in range(8):
            self.mm(pa[:, 0:ncol], xb[:, kc, b * 128:(b + 1) * 128], wv[:, kc, col0:col0 + ncol], kc == 0, kc == 7,
                    [(arena, name, kc, col0 // SW), (xbk, kc)], [pk])
        return pa, pk

    def out_fm(self, dram, dname, ti, fc, src, sk):
        t0 = ti * TT
        self.P.dma(dram[fc * 128:(fc + 1) * 128, t0:t0 + TT], src, [sk], [(dname, ti, fc)], q="sp")

    def v_tok(self, wv, name, col0, xb, xbk, ti):
        for b in range(4):
            for half in range(2):
                pa, pk = self.proj_tok(wv, "WA", name, col0 + half * 512, 512, xb, xbk, b)
                t, tk = self.t512.next()
                self.cp("act", t[:], pa[:], [pk], [tk])
                r0 = ti * TT + b * 128
                self.P.dma(self.Vt[r0:r0 + 128, half * 512:(half + 1) * 512], t[:], [tk], [("Vt", ti, b, half)], q="sp")

    def ffn1(self, li):
        P = self.P
        self.fence("WA")
        w1 = self.wa_view(8, 2 * DFF)
        self.load_w(w1, self.w[li][2], D, 2 * DFF, "WA", "wF1")
        self.fence("WB")
        self.load_w(self.wb_view(22, D), self.w[li][3], DFF, D, "WB", "wF2")
        for ti in range(self.NT):
            xb, xbk = self.load_xb(ti)
            for j in range(22):
                pg, pgk = self.proj_fm(w1, "WA", "wF1", j * 128, xb, xbk)
                pu, puk = self.proj_fm(w1, "WA", "wF1", DFF + j * 128, xb, xbk)
                t, tk = self.t512.next()
                self.act(t[:], pg[:], AF.Silu, [pgk], [tk])
                hb, hbk = self.b512.next()
                self.tt("dve", hb[:], t[:], pu[:], ALU.mult, [tk, puk], [hbk])
                self.out_fm(self.Hd, "Hd", ti, j, hb[:], hbk)

    def ffn2(self, li, last):
        P = self.P
        self.fence("WA")
        w2 = self.wb_view(22, D)
        hs = [self.WA[:, i * 5632:(i + 1) * 5632].bitcast(BF16).rearrange("p (j t) -> p j t", t=TT) for i in range(2)]
        ys = [self.waf(11264 + i * 4096, [128, 8, TT]) for i in range(2)]
        g2, b2 = PV[("g2", li)], PV[("b2", li)]
        for ti in range(self.NT):
            t0 = ti * TT
            h, hk = hs[ti % 2], ("WA", "h", ti % 2)
            P.dma(h, self.Hd[:, t0:t0 + TT].rearrange("(j p) t -> p j t", p=128),
                  [("Hd", ti, j) for j in range(22)], [hk])
            y = ys[ti % 2]
            yks = [("WA", "y", ti % 2, fc) for fc in range(8)]
            for fc in range(8):
                pa, pk = self.pa.next()
                for j in range(22):
                    self.mm(pa[:], w2[:, j, fc * 128:(fc + 1) * 128], h[:, j, :], j == 0, j == 21,
                            [("WB", "wF2", j, 0), hk], [pk])
                xr, xrk = self.t512.next()
                P.dma(xr[:], self.X32[fc * 128:(fc + 1) * 128, t0:t0 + TT], [("X32", ti)], [xrk])
                self.stt("dve", y[:, fc, :], xr[:], ALPHA, pa[:], ALU.mult, ALU.add, [xrk, pk], [yks[fc]])
                self.ln_acc(fc, y[:, fc, :], yks[fc])
            xb, xbk = self.xb.next()
            self.ln_finish(y, yks, g2, b2, xb, xbk)
            if last:
                for b in range(4):
                    for half in range(2):
                        pa, pk = self.pa.next()
                        for q in range(4):
                            self.tr(pa[:, q * 128:(q + 1) * 128], self.x32[:, half * 4 + q, b * 128:(b + 1) * 128],
                                    self.ident, [("x32", half * 4 + q), "cst"], [pk])
                        t, tk = self.t512.next()
                        self.cp("act", t[:], pa[:], [pk], [tk])
                        P.dma(self.out_d[t0 + b * 128:t0 + (b + 1) * 128, half * 512:(half + 1) * 512], t[:],
                              [tk], [("out", ti, b, half)], q="sp")
            else:
                self.store_x(ti, xb, xbk)

    def epilogue(self, li, ti, dv, nwcol, yoff):
        P = self.P
        import os
        if os.environ.get("K_SKIP_EPI"):
            xb, xbk = self.xb.next()
            self.store_x(ti, xb, xbk)
            return
        wO = self.wb_view(8, D)
        nfc = dv // 128
        ob, obk = self.xb.next()
        t0 = ti * TT
        def head_gen(hd):
            pa, pk = self.pa.next()
            for q in range(nfc):
                fc = hd * nfc + q
                t, tk = self.t512.next()
                self.act(t[:], self.x32[:, fc, :], AF.Square, [("x32", fc)], [tk])
                self.mm(pa[:], self.ones, t[:], q == 0, q == nfc - 1, ["cst", tk], [pk])
            yield
            rs, rsk = self.t512.next()
            self.rsqrt(rs[:], pa[:], 1.0 / dv, RMS_EPS, [pk], rsk)
            for q in range(nfc):
                fc = hd * nfc + q
                r, rk = self.t512.next()
                P.dma(r[:], self.Rd[fc * 128:(fc + 1) * 128, t0:t0 + TT], [("Rd", ti, fc)], [rk])
                t, tk = self.t512.next()
                self.tt("dve", t[:], self.x32[:, fc, :], rs[:], ALU.mult, [("x32", fc), rsk], [tk])
                self.stt("dve", ob[:, fc, :], t[:], self.pv[:, nwcol + q:nwcol + q + 1], r[:], ALU.mult, ALU.mult,
                         [tk, rk, "pv"], [(obk, fc)])

        self.run_pipe((head_gen(hd) for hd in range(8 // nfc)), 2)
        y = self.waf(yoff, [128, 8, TT])
        yks = [("WA", "ye", fc) for fc in range(8)]
        for fo in range(8):
            pa, pk = self.pa.next()
            for kc in range(8):
                self.mm(pa[:], wO[:, kc, fo * 128:(fo + 1) * 128], ob[:, kc, :], kc == 0, kc == 7,
                        [("WB", "wO", kc, 0), (obk, kc)], [pk])
            xr, xrk = self.t512.next()
            P.dma(xr[:], self.X32[fo * 128:(fo + 1) * 128, t0:t0 + TT], [("X32", ti)], [xrk])
            self.stt("dve", y[:, fo, :], xr[:], ALPHA, pa[:], ALU.mult, ALU.add, [xrk, pk], [yks[fo]])
            self.ln_acc(fo, y[:, fo, :], yks[fo])
        xb, xbk = self.xb.next()
        self.ln_finish(y, yks, PV[("g1", li)], PV[("b1", li)], xb, xbk)
        self.store_x(ti, xb, xbk)

    def hgrn_A(self, li):
        P = self.P
        self.fence("WA")
        wA = self.wa_view(8, HGRN_IN)
        self.load_w(wA, self.w[li][0], D, HGRN_IN, "WA", "wA")
        sm = self.sm[:, 128:216]
        lbl = self.pv[:, PV["lbl"]:PV["lbl"] + 32]
        self.act(sm[:, 0:32], lbl, AF.Exp, ["pv"], ["sm"])
        ev = sm[:, 0:32].rearrange("p (c d) -> p c d", d=4)
        self.red("dve", sm[:, 32:40], ev, ["sm"], ["sm1"])
        self.red("dve", sm[:, 40:48], ev[:, :, 1:li + 1], ["sm"], ["sm2"])
        P.op("dve", lambda e: e.reciprocal(out=sm[:, 48:56], in_=sm[:, 32:40]), ["sm1"], ["sm3"])
        self.tt("dve", sm[:, 56:64], sm[:, 40:48], sm[:, 48:56], ALU.mult, ["sm2", "sm3"], ["lb"])
        self.ts("dve", sm[:, 64:72], sm[:, 56:64], -1.0, ALU.mult, ["lb"], ["oml"], s2=1.0, op1=ALU.add)
        for ti in range(self.NT):
            xb, xbk = self.load_xb(ti)
            for fc in range(8):
                pa, pk = self.proj_fm(wA, "WA", "wA", fc * 128, xb, xbk)
                t, tk = self.t512.next()
                self.act(t[:], pa[:], AF.Silu, [pk], [tk])
                self.out_fm(self.Qd, "Qd", ti, fc, t[:], tk)
            for fc in range(8):
                pa, pk = self.proj_fm(wA, "WA", "wA", 1024 + fc * 128, xb, xbk)
                t, tk = self.t512.next()
                self.act(t[:], pa[:], AF.Sigmoid, [pk], [tk])
                self.ts("dve", t[:], t[:], sm[:, 64 + fc:65 + fc], ALU.mult, [tk, "oml", "lb"], [tk],
                        s2=sm[:, 56 + fc:57 + fc], op1=ALU.add)
                g, gk = self.t512.next()
                self.act(g[:], t[:], AF.Ln, [tk], [gk])
                self.out_fm(self.Gd, "Gd", ti, fc, g[:], gk)
                k, kk = self.t512.next()
                self.ts("dve", k[:], t[:], -1.0, ALU.mult, [tk], [kk], s2=1.0, op1=ALU.add)
                self.out_fm(self.Kd, "Kd", ti, fc, k[:], kk)
            self.v_tok(wA, "wA", 2048, xb, xbk, ti)
            for fc in range(8):
                pa, pk = self.proj_fm(wA, "WA", "wA", 3072 + fc * 128, xb, xbk)
                t, tk = self.t512.next()
                self.act(t[:], pa[:], AF.Silu, [pk], [tk])
                self.out_fm(self.Rd, "Rd", ti, fc, t[:], tk)

    def gla_A(self, li):
        P = self.P
        self.fence("WA")
        wA = self.wa_view(8, GLA_IN)
        self.load_w(wA, self.w[li][0], D, GLA_IN, "WA", "wA")
        wg = self.waf(13000, [16, 512])
        P.dma(wg, self.wgk2_d, [], [("WA", "wgk2")])
        for ti in range(self.NT):
            xb, xbk = self.load_xb(ti)
            for fc in range(4):
                pa, pk = self.proj_fm(wA, "WA", "wA", fc * 128, xb, xbk)
                t, tk = self.t512.next()
                self.cp("act", t[:], pa[:], [pk], [tk])
                self.out_fm(self.Qd, "Qd", ti, fc, t[:], tk)
            for fc in range(4):
                pa, pk = self.proj_fm(wA, "WA", "wA", 512 + fc * 128, xb, xbk)
                t, tk = self.t512.next()
                self.cp("act", t[:], pa[:], [pk], [tk])
                self.out_fm(self.Kd, "Kd", ti, fc, t[:], tk)
            self.v_tok(wA, "wA", 1024, xb, xbk, ti)
            for fc in range(8):
                pa, pk = self.proj_fm(wA, "WA", "wA", 2048 + fc * 128, xb, xbk)
                t, tk = self.t512.next()
                self.act(t[:], pa[:], AF.Silu, [pk], [tk])
                self.out_fm(self.Rd, "Rd", ti, fc, t[:], tk)
            pa, pk = self.proj_fm(wA, "WA", "wA", 3072, xb, xbk, M=16)
            gk_, gkk = self.waf(13512, [16, 512]), ("WA", "gk")
            self.cp("act", gk_[0:16, :], pa[0:16, :], [pk], [gkk])
            for fc in range(4):
                pb, pbk = self.pa.next()
                self.mm(pb[:], wg[0:16, fc * 128:(fc + 1) * 128], gk_[0:16, :], True, True, [("WA", "wgk2"), gkk], [pbk])
                t, tk = self.t512.next()
                self.act(t[:], pb[:], AF.Sigmoid, [pbk, "pv"], [tk], bias=self.pv[:, PV["bgk"] + fc:PV["bgk"] + fc + 1])
                self.act(t[:], t[:], AF.Ln, [tk], [tk])
                g, gk2 = self.t512.next()
                self.ts("dve", g[:], t[:], 1.0 / 16.0, ALU.mult, [tk], [gk2])
                self.out_fm(self.Gd, "Gd", ti, fc, g[:], gk2)

    def gla_B(self, li, H, dv, nwcol):
        P = self.P
        self.fence("WA")
        S32 = self.S32
        P.op("pool", lambda e: e.memset(S32[:], 0.0), [], ["S32"])
        q32 = self.waf(0, [128, 4, TT])
        k32 = self.waf(2048, [128, 4, TT])
        g32 = self.waf(4096, [128, 4, TT])
        gm = self.waf(6144, [128, 4, TT])
        AT = self.waf(8192, [64, 4, TT])
        kT = self.WA[0:64, 10240:14336].rearrange("p (n h d) -> p n h d", h=4, d=128)
        Vc = [self.waf(14336 + i * 1024, [64, 1024]) for i in range(2)]
        yoff = 16384
        Gs = self.waf(20480, [128, 4, TT])
        sm = self.sm
        egm = sm[:, 0:32].rearrange("p (h n) -> p h n", n=8)
        egl = sm[:, 32:64].rearrange("p (h n) -> p h n", n=8)
        gam = sm[:, 64:96].rearrange("p (h n) -> p h n", n=8)
        nfc = dv // 128
        scm = self.cst[:, C_SC:C_SC + 512]
        mu8 = self.cst[0:64, C_MU:C_MU + 512]
        scale = 128 ** -0.5
        vi = 0
        import os
        lvl = int(os.environ.get("K_REC_LEVEL", "9"))
        for ti in range(self.NT):
            t0 = ti * TT
            for gi in range(H // 4 if lvl >= 1 else 0):
                h0 = gi * 4
                for (dst, dr, nm) in ((q32, self.Qd, "Qd"), (k32, self.Kd, "Kd"), (g32, self.Gd, "Gd")):
                    P.dma(dst, self.tile_fm(dr, ti, h0 * 128, 512), [(nm, ti, h0 + i) for i in range(4)],
                          [("WA", nm + "t", i) for i in range(4)])
                def head_gen(hh):
                    kq, kk_, kg, kgm = ("WA", "Qdt", hh), ("WA", "Kdt", hh), ("WA", "Gdt", hh), ("WA", "gm", hh)
                    self.scan(Gs[:, hh, :], scm, g32[:, hh, :], [kg, "cst"], [("WA", "Gs", hh)])
                    kg = ("WA", "Gs", hh)
                    Gv = Gs[:, hh, :].rearrange("p (n c) -> p n c", c=64)
                    gmv = gm[:, hh, :].rearrange("p (n c) -> p n c", c=64)
                    self.tt("dve", gmv, Gv, Gv[:, :, 31:32].broadcast_to([128, 8, 64]), ALU.subtract, [kg], [kgm])
                    e1, e1k = self.t512.next()
                    self.act(e1[:], gm[:, hh, :], AF.Exp, [kgm], [e1k])
                    e2, e2k = self.t512.next()
                    self.act(e2[:], gm[:, hh, :], AF.Exp, [kgm], [e2k], scale=-1.0)
                    self.act(egm[:, hh, :], Gv[:, :, 31], AF.Exp, [kg], [("egm", hh)])
                    self.act(egl[:, hh, :], gmv[:, :, 63], AF.Exp, [kgm], [("egl", hh)])
                    self.act(gam[:, hh, :], Gv[:, :, 63], AF.Exp, [kg], [("gam", hh)])
                    yield
                    self.stt("dve", q32[:, hh, :], q32[:, hh, :], scale, e1[:], ALU.mult, ALU.mult, [kq, e1k], [kq])
                    self.tt("dve", k32[:, hh, :], k32[:, hh, :], e2[:], ALU.mult, [kk_, e2k], [kk_])
                    pa, pk = self.pa.next()
                    for n in range(8):
                        self.mm(pa[0:64, n * 64:(n + 1) * 64], k32[:, hh, n * 64:(n + 1) * 64],
                                q32[:, hh, n * 64:(n + 1) * 64], True, True, [kk_, kq], [pk])
                    yield
                    self.tt("dve", AT[:, hh, :], pa[0:64, :], mu8, ALU.mult, [pk, "cst"], [("WA", "AT", hh)])
                    qv = q32[:, hh, :].rearrange("p (n c) -> p n c", c=64)
                    kv = k32[:, hh, :].rearrange("p (n c) -> p n c", c=64)
                    self.tt("dve", qv, qv, egm[:, hh, :].unsqueeze(2).broadcast_to([128, 8, 64]), ALU.mult,
                            [kq, ("egm", hh)], [kq])
                    self.tt("dve", kv, kv, egl[:, hh, :].unsqueeze(2).broadcast_to([128, 8, 64]), ALU.mult,
                            [kk_, ("egl", hh)], [kk_])
                    pd, pdk = self.pd.next()
                    for n in range(8):
                        self.tr(pd[0:64, n * 128:(n + 1) * 128], k32[:, hh, n * 64:(n + 1) * 64], self.ident,
                                [kk_, "cst"], [pdk])
                    for hf in range(2):
                        self.cp("act", kT[:, hf * 4:(hf + 1) * 4, hh, :],
                                pd[0:64, hf * 512:(hf + 1) * 512].rearrange("p (n d) -> p n d", d=128), [pdk],
                                [("WA", "kT", hh)])

                self.run_pipe((head_gen(hh) for hh in range(4)), 3)
                for n in range(8 if lvl >= 5 else 0):
                    V = Vc[vi % 2]
                    vk = ("WA", "Vc", vi % 2)
                    vi += 1
                    r0 = t0 + n * 64
                    P.dma(V, self.Vt[r0:r0 + 64, :], [("Vt", ti, n // 2, hf) for hf in range(2)], [vk])
                    po, pok = self.pa.next()
                    for hh in range(4):
                        h = h0 + hh
                        for eh in range(nfc):
                            slot = hh * nfc + eh
                            col = h * dv + eh * 128
                            self.mm(po[:, slot * 64:(slot + 1) * 64], S32[:, col:col + 128],
                                    q32[:, hh, n * 64:(n + 1) * 64], True, False, ["S32", ("WA", "Qdt", hh)], [pok])
                            self.mm(po[:, slot * 64:(slot + 1) * 64], V[:, col:col + 128],
                                    AT[:, hh, n * 64:(n + 1) * 64], False, True, [vk, ("WA", "AT", hh)], [pok])
                    fc0 = h0 * nfc
                    ns = 4 * nfc
                    self.cp("act", self.x32[:, fc0:fc0 + ns, n * 64:(n + 1) * 64],
                            po[:, 0:ns * 64].rearrange("p (s c) -> p s c", c=64), [pok],
                            [("x32", fc0 + s) for s in range(ns)])
                    if lvl < 6:
                        continue
                    pS, pSk = self.pd.next()
                    for hh in range(4):
                        h = h0 + hh
                        self.mm(pS[:, hh * dv:(hh + 1) * dv], kT[:, n, hh, :], V[:, h * dv:(h + 1) * dv], True, True,
                                [("WA", "kT", hh), vk], [pSk])
                    Sv = S32[:, h0 * dv:(h0 + 4) * dv].rearrange("p (h e) -> p h e", e=dv)
                    self.tt("dve", Sv, Sv, gam[:, :, n:n + 1].broadcast_to([128, 4, dv]), ALU.mult,
                            ["S32"] + [("gam", hh) for hh in range(4)], ["S32"])
                    for c0 in range(0, 4 * dv, 512):
                        self.tt("dve", S32[:, h0 * dv + c0:h0 * dv + c0 + 512], S32[:, h0 * dv + c0:h0 * dv + c0 + 512],
                                pS[:, c0:c0 + 512], ALU.add, ["S32", pSk], ["S32"])
            self.epilogue(li, ti, dv, nwcol, yoff)

    def gdn_A(self, li):
        P = self.P
        j = li // 3
        self.fence("WA")
        wA = self.wa_view(8, GDN_IN)
        self.load_w(wA, self.w[li][0], D, GDN_IN, "WA", "wA")
        pres = Rot([(self.waf(16448 + i * 520, [128, 516]), ("WA", "pre", i)) for i in range(3)])
        tl = Rot([(self.waf(18008 + i * 512, [128, 512]), ("WA", "tl", i)) for i in range(7)])
        tkR = Rot([(self.waf(18008 + i * 512, [128, 512]), ("WA", "tl", i)) for i in range(7, 9)])
        sqR = self.t512
        carry = self.carry
        P.op("pool", lambda e: e.memset(carry[:], 0.0), [], [("carry", fc) for fc in range(24)])
        sm = self.sm
        negA = sm[:, 208:216]
        self.act(negA, self.pr[:, j * 16:j * 16 + 8], AF.Exp, ["pr"], ["negA"])
        self.ts("dve", negA, negA, -1.0, ALU.mult, ["negA"], ["negA"])
        cw0 = PV[("conv", j)]
        for ti in range(self.NT):
            t0 = ti * TT
            xb, xbk = self.load_xb(ti)
            def fc_s1(fc, xb=xb, xbk=xbk):
                pa, pk = self.proj_fm(wA, "WA", "wA", fc * 128, xb, xbk)
                pre, prk = pres.next()
                self.cp("pool", pre[:, 0:3], carry[:, fc, 0:3], [("carry", fc)], [prk])
                self.cp("act", pre[:, 3:515], pa[:], [pk], [prk])
                self.cp("pool", carry[:, fc, 0:3], pre[:, 512:515], [prk], [("carry", fc)])
                acc, ak = tl.next()
                wc = lambda k: self.pv[:, cw0 + fc * 4 + k:cw0 + fc * 4 + k + 1]
                self.ts("dve", acc[:], pre[:, 3:515], wc(3), ALU.mult, [prk, "pv"], [ak])
                for k in (2, 1, 0):
                    self.stt("dve", acc[:], pre[:, k:k + 512], wc(k), acc[:], ALU.mult, ALU.add, [prk, ak, "pv"], [ak])
                return acc, ak

            def fc_gen(fcs, ti=ti, t0=t0):
                st = [fc_s1(fc) for fc in fcs]
                sqs = []
                for fc, (s, sk) in zip(fcs, st):
                    self.act(s[:], s[:], AF.Silu, [sk], [sk])
                    if fc < 16:
                        sq, sqk = sqR.next()
                        self.tt("dve", sq[:], s[:], s[:], ALU.mult, [sk], [sqk])
                        sqs.append((sq, sqk))
                yield
                if fcs[0] < 16:
                    pbs = []
                    for (sq, sqk) in sqs:
                        pb, pbk = self.pa.next()
                        self.mm(pb[:], self.ones, sq[:], True, True, ["cst", sqk], [pbk])
                        pbs.append((pb, pbk))
                    for fc, (s, sk), (sq, sqk), (pb, pbk) in zip(fcs, st, sqs, pbs):
                        self.rsqrt(sq[:], pb[:], 1.0, L2_EPS, [pbk, sqk], sqk)
                        if fc < 8:
                            self.stt("dve", s[:], s[:], 128 ** -0.5, sq[:], ALU.mult, ALU.mult, [sk, sqk], [sk])
                            self.out_fm(self.Qd, "Qd", ti, fc, s[:], sk)
                        else:
                            self.tt("dve", s[:], s[:], sq[:], ALU.mult, [sk, sqk], [sk])
                            self.out_fm(self.Kd, "Kd", ti, fc - 8, s[:], sk)
                    if fcs[0] < 8:
                        return
                    yield
                for fc, (s, sk) in zip(fcs, st):
                    dt_, dn, hcol = (self.Kt, "Kt", (fc - 8) * 128) if fc < 16 else (self.Vt, "Vt", (fc - 16) * 128)
                    pt, ptk = self.pa.next()
                    for b in range(4):
                        self.tr(pt[:, b * 128:(b + 1) * 128], s[:, b * 128:(b + 1) * 128], self.ident, [sk, "cst"], [ptk])
                    tk_, tkk = tkR.next()
                    self.cp("act", tk_[:], pt[:], [ptk], [tkk])
                    P.dma(dt_[t0:t0 + TT, hcol:hcol + 128].rearrange("(b p) d -> p b d", p=128),
                          tk_[:].rearrange("p (b d) -> p b d", d=128), [tkk], [(dn, ti, hcol // 128)], q="sp")

            self.run_pipe((fc_gen([fc, fc + 1]) for fc in range(0, 24, 2)), 3)
            for fc in range(8):
                pa, pk = self.proj_fm(wA, "WA", "wA", 3072 + fc * 128, xb, xbk)
                t, tk = tl.next()
                self.act(t[:], pa[:], AF.Silu, [pk], [tk])
                self.out_fm(self.Rd, "Rd", ti, fc, t[:], tk)
            for b in range(4):
                pa, pk = self.proj_tok(wA, "WA", "wA", 4096, 16, xb, xbk, b)
                bg, bgk = tl.next()
                self.act(bg[:, 0:8], pa[:, 0:8], AF.Sigmoid, [pk], [bgk])
                self.tt("dve", bg[:, 16:24], pa[:, 8:16], self.pr[:, j * 16 + 8:j * 16 + 16], ALU.add, [pk, "pr"], [bgk])
                self.act(bg[:, 16:24], bg[:, 16:24], AF.Exp, [bgk], [bgk])
                self.act(bg[:, 16:24], bg[:, 16:24], AF.Ln, [bgk], [bgk], bias=1.0)
                self.tt("dve", bg[:, 8:16], bg[:, 16:24], negA, ALU.mult, [bgk, "negA"], [bgk])
                r0 = t0 + b * 128
                P.dma(self.BGd[r0:r0 + 128, :], bg[:, 0:16], [bgk], [("BGd", ti, b)], q="sp")

    def gdn_B(self, li):
        P = self.P
        j = li // 3
        self.fence("WA")
        S32 = self.S32
        P.op("pool", lambda e: e.memset(S32[:], 0.0), [], ["S32"])
        W = lambda off, shape: self.waf(off, shape)
        qc = W(0, [128, 8, 64])
        kc = W(512, [128, 8, 64])
        ktok = W(1024, [64, 1024])
        vtok = W(2048, [64, 1024])
        bg = W(3072, [64, 16])
        o = 3136
        names512 = ["rhs1", "DL", "DU", "N", "NT", "Pa", "Pb", "PTa", "PTb", "Ta", "Tb", "AT"]
        t5 = {}
        for nm in names512:
            t5[nm] = W(o, [64, 8, 64])
            o += 512
        t10 = {}
        for nm in ["ru", "rw", "u", "vnew", "khat"]:
            t10[nm] = W(o, [64, 1024])
            o += 1024
        qh = W(o, [128, 8, 64])
        o += 512
        wT = W(o, [128, 512])
        o += 512
        eGr = W(o, [128, 512])
        o += 512
        qcs, kcs, bgs = [qc], [kc], [bg]
        qcs.append(W(o, [128, 8, 64]))
        o += 512
        kcs.append(W(o, [128, 8, 64]))
        o += 512
        bgs.append(W(o, [64, 16]))
        o += 64
        yoff = o
        assert yoff + 4096 <= 23040, yoff
        K = lambda nm: ("WA", nm)
        sg = self.sm
        Gcol, eGc, beg, egl, nbeta, tmp8 = (sg[0:64, 0:8], sg[0:64, 8:16], sg[0:64, 16:24], sg[0:64, 24:32],
                                            sg[0:64, 32:40], sg[0:64, 40:48])
        gam = sg[:, 48:56]
        cst = self.cst
        triU = cst[0:64, C_MU:C_MU + 64]
        mu8 = cst[0:64, C_MU:C_MU + 512].rearrange("p (h c) -> p h c", c=64)
        sl8 = cst[0:64, C_SL:C_SL + 512].rearrange("p (h c) -> p h c", c=64)
        i8 = cst[0:64, C_I8:C_I8 + 512].rearrange("p (h c) -> p h c", c=64)
        id64 = cst[0:64, C_ID:C_ID + 64]
        ones64 = cst[0:64, C_ONE:C_ONE + 128]
        v3 = lambda ap: ap.rearrange("p (h c) -> p h c", c=64)
        b8 = lambda ap, n: ap.unsqueeze(2).broadcast_to([64, 8, n])

        def mm8(ps_ap, pk, lhs_of, rhs_of, r, width, npart):
            for h in range(8):
                self.mm(ps_ap[0:npart, h * width:(h + 1) * width], lhs_of(h), rhs_of(h), True, True, r, [pk])

        for ti in range(self.NT):
            for n in range(NCH):
                t0c = ti * TT + n * C
                par = n % 2
                qc, kc, bg = qcs[par], kcs[par], bgs[par]
                kqc, kkc, kbg = K("qc%d" % par), K("kc%d" % par), K("bg%d" % par)
                P.dma(bg, self.BGd[t0c:t0c + C, :], [("BGd", ti, n // 2)], [kbg])
                P.dma(kc, self.Kd[:, t0c:t0c + C].rearrange("(h p) t -> p h t", p=128),
                      [("Kd", ti, f) for f in range(8)], [kkc])
                P.dma(qc, self.Qd[:, t0c:t0c + C].rearrange("(h p) t -> p h t", p=128),
                      [("Qd", ti, f) for f in range(8)], [kqc])
                P.dma(ktok, self.Kt[t0c:t0c + C, :], [("Kt", ti, f) for f in range(8)], [K("ktok")])
                P.dma(vtok, self.Vt[t0c:t0c + C, :], [("Vt", ti, f) for f in range(8)], [K("vtok")])
                beta, g = bg[:, 0:8], bg[:, 8:16]
                pA, pAk = self.pa.next()
                self.mm(pA[0:64, 0:8], triU, g, True, True, ["cst", kbg], [pAk])
                self.mm(pA[:, 8:16], ones64, g, True, True, ["cst", kbg], [pAk])
                self.cp("dve", Gcol, pA[0:64, 0:8], [pAk], ["Gcol"])
                self.tt("dve", tmp8, pA[0:64, 8:16], Gcol, ALU.subtract, [pAk, "Gcol"], ["tmp8"])
                self.act(gam, pA[:, 8:16], AF.Exp, [pAk], ["gam"])
                self.ts("dve", nbeta, beta, -1.0, ALU.mult, [kbg], ["nbeta"])
                self.tt("dve", t5["rhs1"], b8(g, 64), mu8, ALU.mult, [kbg, "cst"], [K("rhs1")])
                pT1, pT1k = self.pa.next()
                self.mm(pT1[:, :], ones64, t5["rhs1"].rearrange("p h c -> p (h c)"), True, True,
                        ["cst", K("rhs1")], [pT1k])
                DL, DU = t5["DL"], t5["DU"]
                self.tt("dve", DL, b8(Gcol, 64), v3(pT1[0:64, :]), ALU.subtract, ["Gcol", pT1k], [K("DL")])
                self.tt("dve", DU, v3(pT1[0:64, :]), b8(Gcol, 64), ALU.subtract, ["Gcol", pT1k], [K("DU")])
                self.act(eGr[:], pT1[:, :], AF.Exp, [pT1k], [K("eGr")])
                self.ts("dve", DL, DL, 0.0, ALU.min, [K("DL")], [K("DL")])
                self.act(DL, DL, AF.Exp, [K("DL")], [K("DL")])
                self.tt("dve", DL, DL, sl8, ALU.mult, [K("DL"), "cst"], [K("DL")])
                pKK, pKKk = self.pa.next()
                mm8(pKK, pKKk, lambda h: kc[:, h, :], lambda h: kc[:, h, :], [kkc], 64, 64)
                N, NT, AT = t5["N"], t5["NT"], t5["AT"]
                self.tt("dve", N, v3(pKK[0:64, :]), DL, ALU.mult, [pKKk, K("DL")], [K("N")])
                self.tt("dve", N, N, b8(nbeta, 64), ALU.mult, [K("N"), "nbeta"], [K("N")])
                pN, pNk = self.pa.next()
                for h in range(8):
                    self.tr(pN[0:64, h * 64:(h + 1) * 64], N[:, h, :], id64, [K("N"), "cst"], [pNk])
                self.cp("act", NT, v3(pN[0:64, :]), [pNk], [K("NT")])
                ru, rw, u, vnew, khat = t10["ru"], t10["rw"], t10["u"], t10["vnew"], t10["khat"]
                v4 = lambda ap: ap.rearrange("p (h d) -> p h d", d=128)

                def d_qk():
                    pQK, pQKk = self.pa.next()
                    mm8(pQK, pQKk, lambda h: kc[:, h, :], lambda h: qc[:, h, :], [kkc, kqc], 64, 64)
                    self.tt("dve", AT, v3(pQK[0:64, :]), DU, ALU.mult, [pQKk, K("DU")], [K("AT")])

                deferred = [
                    lambda: self.ts("dve", DU, DU, 0.0, ALU.min, [K("DU")], [K("DU")]),
                    lambda: self.act(DU, DU, AF.Exp, [K("DU")], [K("DU")]),
                    lambda: self.tt("dve", DU, DU, mu8, ALU.mult, [K("DU"), "cst"], [K("DU")]),
                    d_qk,
                    lambda: self.act(egl, tmp8, AF.Exp, ["tmp8"], ["egl"]),
                    lambda: self.act(eGc, Gcol, AF.Exp, ["Gcol"], ["eGc"]),
                    lambda: self.tt("dve", beg, beta, eGc, ALU.mult, [kbg, "eGc"], ["beg"]),
                    lambda: self.tt("dve", qh, qc, eGr[:].rearrange("p (h c) -> p h c", c=64), ALU.mult,
                                    [kqc, K("eGr")], [K("qh")]),
                    lambda: self.tt("dve", v4(ru), v4(vtok), b8(beta, 128), ALU.mult, [K("vtok"), kbg], [K("ru")]),
                    lambda: self.tt("dve", v4(rw), v4(ktok), b8(beg, 128), ALU.mult, [K("ktok"), "beg"], [K("rw")]),
                    lambda: self.tt("dve", v4(khat), v4(ktok), b8(egl, 128), ALU.mult, [K("ktok"), "egl"], [K("khat")]),
                ]
                Tc, Tk = t5["Ta"], K("Ta")
                self.tt("dve", Tc, NT, i8, ALU.add, [K("NT"), "cst"], [Tk])
                Pp, Ppk, PTp, PTpk = N, K("N"), NT, K("NT")
                for jj in range(1, 6):
                    Pn, Pnk = (t5["Pa"], K("Pa")) if jj % 2 else (t5["Pb"], K("Pb"))
                    PTn, PTnk = (t5["PTa"], K("PTa")) if jj % 2 else (t5["PTb"], K("PTb"))
                    Tn, Tnk = (t5["Tb"], K("Tb")) if jj % 2 else (t5["Ta"], K("Ta"))
                    p1, p1k = self.pa.next()
                    mm8(p1, p1k, lambda h: PTp[:, h, :], lambda h: Pp[:, h, :], [Ppk, PTpk], 64, 64)
                    self.cp("act", Pn, v3(p1[0:64, :]), [p1k], [Pnk])
                    if jj < 5:
                        p2, p2k = self.pa.next()
                        mm8(p2, p2k, lambda h: Pp[:, h, :], lambda h: PTp[:, h, :], [Ppk, PTpk], 64, 64)
                        self.cp("dve", PTn, v3(p2[0:64, :]), [p2k], [PTnk])
                    for _ in range(3):
                        if deferred:
                            deferred.pop(0)()
                    p3, p3k = self.pa.next()
                    mm8(p3, p3k, lambda h: Pn[:, h, :], lambda h: Tc[:, h, :], [Pnk, Tk], 64, 64)
                    self.tt("dve", Tn, v3(p3[0:64, :]), Tc, ALU.add, [p3k, Tk], [Tnk])
                    Pp, Ppk, PTp, PTpk, Tc, Tk = Pn, Pnk, PTn, PTnk, Tn, Tnk
                while deferred:
                    deferred.pop(0)()
                TT_ = Tc
                pU, pUk = self.pd.next()
                mm8(pU, pUk, lambda h: TT_[:, h, :], lambda h: ru[:, h * 128:(h + 1) * 128], [Tk, K("ru")], 128, 64)
                self.cp("act", u[:, 0:512], pU[0:64, 0:512], [pUk], [K("u")])
                self.cp("act", u[:, 512:1024], pU[0:64, 512:1024], [pUk], [K("u")])
                pW, pWk = self.pa.next()
                mm8(pW, pWk, lambda h: rw[:, h * 128:(h + 1) * 128], lambda h: TT_[:, h, :], [Tk, K("rw")], 64, 128)
                self.cp("act", wT[:], pW[:, :], [pWk], [K("wT")])
                pWS, pWSk = self.pd.next()
                mm8(pWS, pWSk, lambda h: wT[:, h * 64:(h + 1) * 64], lambda h: S32[:, h * 128:(h + 1) * 128],
                    [K("wT"), "S32"], 128, 64)
                self.tt("dve", vnew[:, 0:512], u[:, 0:512], pWS[0:64, 0:512], ALU.subtract, [K("u"), pWSk], [K("vnew")])
                self.tt("dve", vnew[:, 512:1024], u[:, 512:1024], pWS[0:64, 512:1024], ALU.subtract, [K("u"), pWSk], [K("vnew")])
                pO, pOk = self.pa.next()
                for h in range(8):
                    self.mm(pO[:, h * 64:(h + 1) * 64], S32[:, h * 128:(h + 1) * 128], qh[:, h, :], True, False,
                            ["S32", K("qh")], [pOk])
                    self.mm(pO[:, h * 64:(h + 1) * 64], vnew[:, h * 128:(h + 1) * 128], AT[:, h, :], False, True,
                            [K("vnew"), K("AT")], [pOk])
                self.cp("act", self.x32[:, :, n * 64:(n + 1) * 64], pO[:, :].rearrange("p (s c) -> p s c", c=64), [pOk],
                        self.x32k)
                pS, pSk = self.pd.next()
                mm8(pS, pSk, lambda h: khat[:, h * 128:(h + 1) * 128], lambda h: vnew[:, h * 128:(h + 1) * 128],
                    [K("khat"), K("vnew")], 128, 128)
                Sv = S32[:, :].rearrange("p (h e) -> p h e", e=128)
                self.tt("dve", Sv, Sv, gam.unsqueeze(2).broadcast_to([128, 8, 128]), ALU.mult, ["S32", "gam"], ["S32"])
                self.tt("dve", S32[:, 0:512], S32[:, 0:512], pS[:, 0:512], ALU.add, ["S32", pSk], ["S32"])
                self.tt("dve", S32[:, 512:1024], S32[:, 512:1024], pS[:, 512:1024], ALU.add, ["S32", pSk], ["S32"])
            self.epilogue(li, ti, 128, PV[("gnw", j)], yoff)


_CACHE = {}


def get_nc(TL, layers, dbg=()):
    key = (TL, tuple(layers), tuple(dbg))
    if key not in _CACHE:
        _CACHE[key] = Builder(TL, list(layers), dbg).build()
    return _CACHE[key]


LAYERS = [(0, 0), (1, 1), (2, 2), (0, 3)]


def core_inputs(inp, xs, layers):
    m = {"x": np.ascontiguousarray(xs, dtype=np.float32), "cst": make_consts(), "pv": make_pv(inp),
         "pr": make_pr(inp), "wgk2": np.ascontiguousarray(inp["gla_w_gk2"][0], dtype=np.float32)}
    for kind, li in layers:
        j = li // 3
        if kind == 0:
            wi, wo = inp["gdn_w_in"][j], inp["gdn_w_out"][j]
        elif kind == 1:
            wi, wo = inp["hgrn_w_in"][0], inp["hgrn_w_out"][0]
        else:
            wi, wo = inp["gla_w_in"][0], inp["gla_w_out"][0]
        m["win%d" % li] = np.ascontiguousarray(wi, dtype=np.float32)
        m["wout%d" % li] = np.ascontiguousarray(wo, dtype=np.float32)
        m["wf1_%d" % li] = np.ascontiguousarray(inp["ffn_w_in"][li], dtype=np.float32)
        m["wf2_%d" % li] = np.ascontiguousarray(inp["ffn_w_down"][li], dtype=np.float32)
    return m


def kernel(**inputs):
    inp = {k: np.asarray(v) for k, v in inputs.items()}
    x = inp["x"]
    B, T, _ = x.shape
    nc = get_nc(T, LAYERS)
    maps = [core_inputs(inp, x[c % B], LAYERS) for c in range(8)]
    res = run_bass_kernel_spmd(nc, maps, core_ids=list(range(8)))
    return np.stack([np.asarray(res.results[b]["out"], dtype=np.float32) for b in range(B)], axis=0)
```

```python
import numpy as np
from contextlib import ExitStack
import concourse.bass as bass
import concourse.mybir as mybir
from concourse.bass_utils import run_bass_kernel_spmd

F32 = mybir.dt.float32
BF16 = mybir.dt.bfloat16
AF = mybir.ActivationFunctionType
ALU = mybir.AluOpType
AX = mybir.AxisListType

D = 1024
DFF = 2816
TT = 512
C = 64
NCH = TT // C
DEPTH = 4
ALPHA = float((2 * DEPTH) ** 0.25)
LN_EPS, RMS_EPS, L2_EPS = 1e-5, 1e-6, 1e-6
GDN_IN, HGRN_IN, GLA_IN = 4112, 4096, 3088
SW = 1024

C_ID, C_ONE, C_MU, C_SL, C_SC, C_I8, C_N = 0, 128, 256, 768, 1280, 1792, 2304


def make_consts():
    c = np.zeros((128, C_N), np.float32)
    c[:, C_ID:C_ID + 128] = np.eye(128)
    c[:, C_ONE:C_ONE + 128] = 1.0
    i = np.arange(64)
    mu = (i[:, None] <= i[None, :]).astype(np.float32)
    sl = (i[:, None] > i[None, :]).astype(np.float32)
    c[:64, C_MU:C_MU + 512] = np.tile(mu, (1, 8))
    c[:64, C_SL:C_SL + 512] = np.tile(sl, (1, 8))
    c[:64, C_I8:C_I8 + 512] = np.tile(np.eye(64, dtype=np.float32), (1, 8))
    sc = np.ones((128, 512), np.float32)
    sc[:, ::64] = 0.0
    c[:, C_SC:C_SC + 512] = sc
    return c


def pv_layout():
    off = {}
    n = 0
    for i in range(DEPTH):
        for nm in ("g1", "b1", "g2", "b2"):
            off[(nm, i)] = n
            n += 8
    for j in range(2):
        off[("conv", j)] = n
        n += 96
        off[("gnw", j)] = n
        n += 1
    off["hnw"] = n
    n += 1
    off["lbl"] = n
    n += 32
    off["lnw"] = n
    n += 2
    off["bgk"] = n
    n += 4
    off["n"] = n
    return off


PV = pv_layout()


def fm(v):
    return np.ascontiguousarray(v.reshape(-1, 128).T)


def make_pv(inp):
    pv = np.zeros((128, PV["n"]), np.float32)
    for i in range(DEPTH):
        pv[:, PV[("g1", i)]:PV[("g1", i)] + 8] = fm(inp["mix_ln_g"][i])
        pv[:, PV[("b1", i)]:PV[("b1", i)] + 8] = fm(inp["mix_ln_b"][i])
        pv[:, PV[("g2", i)]:PV[("g2", i)] + 8] = fm(inp["ffn_ln_g"][i])
        pv[:, PV[("b2", i)]:PV[("b2", i)] + 8] = fm(inp["ffn_ln_b"][i])
    for j in range(2):
        cw = inp["gdn_conv_w"][j]
        pv[:, PV[("conv", j)]:PV[("conv", j)] + 96] = cw.T.reshape(24, 128, 4).transpose(1, 0, 2).reshape(128, 96)
        pv[:, PV[("gnw", j)]] = inp["gdn_norm_w"][j]
    pv[:, PV["hnw"]] = inp["hgrn_norm_w"][0]
    lb = inp["hgrn_lb_logits"]
    pv[:, PV["lbl"]:PV["lbl"] + 32] = lb.T.reshape(8, 128, 4).transpose(1, 0, 2).reshape(128, 32)
    pv[:, PV["lnw"]:PV["lnw"] + 2] = fm(inp["gla_norm_w"][0])
    pv[:, PV["bgk"]:PV["bgk"] + 4] = fm(inp["gla_b_gk"][0])
    return pv


def make_pr(inp):
    pr = np.zeros((128, 32), np.float32)
    for j in range(2):
        pr[:, j * 16:j * 16 + 8] = inp["gdn_a_log"][j][None, :]
        pr[:, j * 16 + 8:j * 16 + 16] = inp["gdn_dt_bias"][j][None, :]
    return pr


class Rot:
    def __init__(self, items):
        self.items = items
        self.i = 0

    def next(self):
        it = self.items[self.i % len(self.items)]
        self.i += 1
        return it


class Prog:
    ENGS = ("pe", "act", "dve", "pool", "sp")
    NSLOT = {"sp": 24, "pool": 8, "act": 4}

    def __init__(self, nc):
        self.nc = nc
        self.ops = []

    def op(self, eng, fn, r=(), w=(), dma=False):
        r = list(r)
        for k in list(r) + list(w):
            if isinstance(k, tuple) and k[0] in ("WA", "WB"):
                ek = ("epoch", k[0])
                if ek not in r and ek not in w:
                    r.append(ek)
        self.ops.append({"eng": eng, "fn": fn, "r": tuple(r), "w": tuple(w), "dma": dma})

    def dma(self, out, in_, r, w, q="sp"):
        self.op(q, lambda e: e.dma_start(out=out, in_=in_), r, w, dma=True)

    def emit(self, stack):
        nc = self.nc
        ops = self.ops
        last_w, readers = {}, {}
        eng_pos = {e: 0 for e in self.ENGS}
        for i, o in enumerate(ops):
            deps = set()
            for k in o["r"]:
                if k in last_w:
                    deps.add(last_w[k])
                if isinstance(k, str) and k[:2] in ("pa", "pd"):
                    for rd in readers.get(k, ()):
                        if ops[rd]["eng"] != o["eng"]:
                            deps.add(rd)
            for k in o["w"]:
                if k in last_w:
                    deps.add(last_w[k])
                deps.update(readers.get(k, ()))
            deps.discard(i)
            o["deps"] = deps
            for k in o["r"]:
                readers.setdefault(k, []).append(i)
            for k in o["w"]:
                last_w[k] = i
                readers[k] = []
            o["pos"] = eng_pos[o["eng"]]
            eng_pos[o["eng"]] += 1
            o["signal"] = False
            o["waits_eng"] = {}
            o["waits_dma"] = {}
        seen = {e: {p: -1 for p in self.ENGS} for e in self.ENGS}
        seen_dma = {e: {} for e in self.ENGS}
        slot_cnt, slot_last = {}, {}
        dma_n = {e: 0 for e in self.ENGS}
        for i, o in enumerate(ops):
            e = o["eng"]
            if o["dma"]:
                s = dma_n[e] % self.NSLOT[e]
                dma_n[e] += 1
                key = (e, s)
                if key in slot_last:
                    o["deps"].add(slot_last[key])
                slot_last[key] = i
                slot_cnt[key] = slot_cnt.get(key, 0) + 1
                o["slot"] = key
                o["slot_val"] = 16 * slot_cnt[key]
            for d in sorted(o["deps"]):
                od = ops[d]
                if od["dma"]:
                    if seen_dma[e].get(od["slot"], 0) < od["slot_val"]:
                        seen_dma[e][od["slot"]] = od["slot_val"]
                        o["waits_dma"][od["slot"]] = od["slot_val"]
                    continue
                pe = od["eng"]
                if pe == e:
                    if e in ("pe", "sp"):
                        continue
                    if o["pos"] - od["pos"] > 2:
                        continue
                if seen[e][pe] >= od["pos"]:
                    continue
                seen[e][pe] = od["pos"]
                od["signal"] = True
                cur = o["waits_eng"].get(pe)
                if cur is None or ops[cur]["pos"] < od["pos"]:
                    o["waits_eng"][pe] = d
        tick = {e: 0 for e in self.ENGS}
        for o in ops:
            if o["signal"]:
                tick[o["eng"]] += 1
                o["tick"] = tick[o["eng"]]
        esem = {e: stack.enter_context(nc.semaphore("s_" + e)) for e in self.ENGS}
        ssem = {k: stack.enter_context(nc.semaphore("d_%s%d" % k)) for k in slot_cnt}
        per_eng = {e: [o for o in ops if o["eng"] == e] for e in self.ENGS}

        def run(ename, eng):
            for o in per_eng[ename]:
                for pe, d in o["waits_eng"].items():
                    eng.wait_ge(esem[pe], ops[d]["tick"])
                for sk, sv in o["waits_dma"].items():
                    eng.wait_ge(ssem[sk], sv)
                inst = o["fn"](eng)
                if o["dma"]:
                    inst.then_inc(ssem[o["slot"]], 16)
                elif o["signal"]:
                    inst.then_inc(esem[ename], 1)
            if ename == "sp":
                for k, c in slot_cnt.items():
                    eng.wait_ge(ssem[k], 16 * c)
                for e2 in self.ENGS:
                    if tick[e2]:
                        eng.wait_ge(esem[e2], tick[e2])

        with nc.Block() as block:
            @block.tensor
            def _(eng):
                run("pe", eng)

            @block.scalar
            def _(eng):
                run("act", eng)

            @block.vector
            def _(eng):
                run("dve", eng)

            @block.gpsimd
            def _(eng):
                run("pool", eng)

            @block.sync
            def _(eng):
                run("sp", eng)


class Builder:
    def __init__(self, TL, layers, dbg=(), upto=9):
        self.upto = upto
        self.TL = TL
        self.NT = TL // TT
        self.layers = layers
        self.dbg = set(dbg)
        self.nc = bass.Bass("TRN2", target_bir_lowering=False)
        self.P = Prog(self.nc)
        self.uid = 0

    def mm(self, out, lhsT, rhs, start, stop, r, w):
        self.P.op("pe", lambda e: e.matmul(out, lhsT, rhs, start=start, stop=stop), r, w)

    def tr(self, out, in_, ident, r, w):
        self.P.op("pe", lambda e: e.transpose(out, in_, ident), r, w)

    def act(self, out, in_, func, r, w, bias=None, scale=None):
        kw = {}
        if bias is not None:
            kw["bias"] = bias
        if scale is not None:
            kw["scale"] = scale
        self.P.op("act", lambda e: e.activation(out=out, in_=in_, func=func, **kw), r, w)

    def tt(self, eng, out, in0, in1, op, r, w):
        self.P.op(eng, lambda e: e.tensor_tensor(out=out, in0=in0, in1=in1, op=op), r, w)

    def ts(self, eng, out, in0, s1, op0, r, w, s2=None, op1=None):
        if op1 is None:
            self.P.op(eng, lambda e: e.tensor_scalar(out=out, in0=in0, scalar1=s1, scalar2=None, op0=op0), r, w)
        else:
            self.P.op(eng, lambda e: e.tensor_scalar(out=out, in0=in0, scalar1=s1, scalar2=s2, op0=op0, op1=op1), r, w)

    def stt(self, eng, out, in0, scalar, in1, op0, op1, r, w):
        self.P.op(eng, lambda e: e.scalar_tensor_tensor(out=out, in0=in0, scalar=scalar, in1=in1, op0=op0, op1=op1), r, w)

    def run_pipe(self, gens, depth):
        active, gens, more = [], iter(gens), True
        while True:
            if more and len(active) < depth:
                try:
                    active.insert(0, next(gens))
                except StopIteration:
                    more = False
            if not active:
                break
            for g in list(active):
                try:
                    next(g)
                except StopIteration:
                    active.remove(g)

    def rsqrt(self, out, in_, scale, eps, r, wk):
        self.act(out, in_, AF.Ln, r, [wk], bias=eps, scale=scale)
        self.act(out, out, AF.Exp, [wk], [wk], scale=-0.5)

    def cp(self, eng, out, in_, r, w):
        if eng == "act":
            self.act(out, in_, AF.Copy, r, w)
        else:
            self.P.op(eng, lambda e: e.tensor_copy(out=out, in_=in_), r, w)

    def red(self, eng, out, in_, r, w):
        self.P.op(eng, lambda e: e.tensor_reduce(out=out, in_=in_, axis=AX.X, op=ALU.add), r, w)

    def scan(self, out, d0, d1, r, w):
        self.P.op("dve", lambda e: e.tensor_tensor_scan(out=out, data0=d0, data1=d1, initial=0.0,
                                                        op0=ALU.mult, op1=ALU.add), r, w)

    def dram(self, name, shape, dt):
        kind = "ExternalOutput" if name in self.dbg else "Internal"
        return self.nc.dram_tensor(name, shape, dt, kind=kind).ap()

    def build(self):
        nc, P, TL, NT = self.nc, self.P, self.TL, self.NT
        st = ExitStack()
        self.st = st
        self.in_names = []

        def ein(n, s):
            self.in_names.append(n)
            return nc.dram_tensor(n, s, F32, kind="ExternalInput").ap()
        self.x_in = ein("x", [TL, D])
        self.cst_d = ein("cst", [128, C_N])
        self.pv_d = ein("pv", [128, PV["n"]])
        self.pr_d = ein("pr", [128, 32])
        self.wgk2_d = ein("wgk2", [16, 512])
        self.w = {}
        for kind, li in self.layers:
            nin = (GDN_IN, HGRN_IN, GLA_IN)[kind]
            self.w[li] = (ein("win%d" % li, [D, nin]) if self.upto >= 1 else None,
                          ein("wout%d" % li, [D, D]) if self.upto >= 1 else None,
                          ein("wf1_%d" % li, [D, 2 * DFF]) if self.upto >= 3 else None,
                          ein("wf2_%d" % li, [DFF, D]) if self.upto >= 3 else None)
        self.out_d = nc.dram_tensor("out", [TL, D], F32, kind="ExternalOutput").ap()
        self.X32 = self.dram("X32", [D, TL], F32)
        self.Xb = self.dram("Xb", [D, TL], BF16)
        self.Hd = self.dram("Hd", [DFF, TL], BF16)
        self.Qd = self.dram("Qd", [D, TL], F32)
        self.Kd = self.dram("Kd", [D, TL], F32)
        self.Gd = self.dram("Gd", [D, TL], F32)
        self.Rd = self.dram("Rd", [D, TL], F32)
        self.Vt = self.dram("Vt", [TL, D], F32)
        self.Kt = self.dram("Kt", [TL, D], F32)
        self.BGd = self.dram("BGd", [TL, 16], F32)

        def sb(name, shape, dt=F32):
            return st.enter_context(nc.sbuf_tensor(name, shape, dt))

        def ps(name, shape, dt=F32):
            return st.enter_context(nc.psum_tensor(name, shape, dt))

        self.cst = sb("cst_s", [128, C_N])
        self.pv = sb("pv_s", [128, PV["n"]])
        self.pr = sb("pr_s", [128, 32])
        self.stg = Rot([(sb("stg%d" % i, [128, SW]), "stg%d" % i) for i in range(2)])
        self.xb = Rot([(sb("xb%d" % i, [128, 8, TT], BF16), "xb%d" % i) for i in range(2)])
        self.x32 = sb("x32", [128, 8, TT])
        self.t512 = Rot([(sb("t512_%d" % i, [128, TT]), "t512_%d" % i) for i in range(4)])
        self.b512 = Rot([(sb("b512_%d" % i, [128, TT], BF16), "b512_%d" % i) for i in range(2)])
        self.s1 = sb("ln_s1", [128, TT])
        self.s2 = sb("ln_s2", [128, TT])
        self.mean = sb("ln_mean", [128, TT])
        self.rstd = sb("ln_rstd", [128, TT])
        self.S32 = sb("S32", [128, 1024])
        self.sm = sb("small", [128, 216])
        self.carry = sb("carry", [128, 24, 4])
        self.dummy = sb("dummy_t", [128, 4])
        self.WA = sb("WA", [128, 23040])
        self.WB = sb("WB", [128, 11264])
        self.pa = Rot([(ps("pa%d" % i, [128, TT]), "pa%d" % i) for i in range(4)])
        self.pd = Rot([(ps("pd%d" % i, [128, 2 * TT]), "pd%d" % i) for i in range(2)])
        self.ident = self.cst[:, C_ID:C_ID + 128]
        self.ones = self.cst[:, C_ONE:C_ONE + 128]

        P.dma(self.cst[:], self.cst_d, [], ["cst"])
        P.dma(self.pv[:], self.pv_d, [], ["pv"])
        P.dma(self.pr[:], self.pr_d, [], ["pr"])
        P.op("pool", lambda e: e.memset(self.S32[:], 0.0), [], ["S32"])
        P.op("pool", lambda e: e.memset(self.carry[:], 0.0), [], ["carry"])

        self.phase0()
        nl = len(self.layers)
        for n, (kind, li) in enumerate(self.layers):
            if self.upto < 1:
                break
            self.wb_load(self.w[li][1], D, D, ("wO", li))
            [self.gdn_A, self.hgrn_A, self.gla_A][kind](li)
            if self.upto < 2:
                break
            if kind == 0:
                self.gdn_B(li)
            elif kind == 1:
                self.gla_B(li, 8, 128, PV["hnw"])
            else:
                self.gla_B(li, 4, 256, PV["lnw"])
            if self.upto < 3:
                break
            self.ffn1(li)
            if self.upto < 4:
                break
            self.ffn2(li, last=(n == nl - 1))
        P.emit(st)
        st.close()
        return nc

    def wa_view(self, kc_n, ncols):
        return self.WA[:, 0:kc_n * ncols // 2].bitcast(BF16).rearrange("p (k n) -> p k n", n=ncols)

    def wb_view(self, kc_n, ncols):
        return self.WB[:, 0:kc_n * ncols // 2].bitcast(BF16).rearrange("p (k n) -> p k n", n=ncols)

    def waf(self, off, shape):
        n = int(np.prod(shape[1:]))
        ap = self.WA[0:shape[0], off:off + n]
        if len(shape) == 3:
            ap = ap.rearrange("p (a b) -> p a b", b=shape[2])
        return ap

    def fence(self, arena):
        d = self.dummy
        self.P.op("pool", lambda e: e.memset(d[:, 0:1], 0.0), [], [("epoch", arena)])

    def load_w(self, dst, w_dram, K, N, arena, name):
        i = 0
        for kc in range(K // 128):
            for n0 in range(0, N, SW):
                nn = min(SW, N - n0)
                s, sk = self.stg.next()
                self.P.dma(s[:, :nn], w_dram[kc * 128:(kc + 1) * 128, n0:n0 + nn], [], [sk])
                self.cp(("act", "dve")[i % 2], dst[:, kc, n0:n0 + nn], s[:, :nn], [sk],
                        [(arena, name, kc, n0 // SW)])
                i += 1

    def wb_load(self, w_dram, K, N, key):
        self.fence("WB")
        self.load_w(self.wb_view(K // 128, N), w_dram, K, N, "WB", key[0])

    def tile_fm(self, dram, ti, r0=0, nrows=D):
        t0 = ti * TT
        return dram[r0:r0 + nrows, t0:t0 + TT].rearrange("(c p) t -> p c t", p=128)

    @property
    def x32k(self):
        return [("x32", fc) for fc in range(8)]

    def store_x(self, ti, xb, xbk):
        self.P.dma(self.tile_fm(self.X32, ti), self.x32[:], self.x32k, [("X32", ti)], q="sp")
        self.P.dma(self.tile_fm(self.Xb, ti), xb[:], [(xbk, fc) for fc in range(8)], [("Xb", ti)], q="sp")

    def load_xb(self, ti):
        xb, xbk = self.xb.next()
        self.P.dma(xb[:], self.tile_fm(self.Xb, ti), [("Xb", ti)], [(xbk, fc) for fc in range(8)])
        return xb, xbk

    def phase0(self):
        P = self.P
        self.fence("WA")
        xt = self.waf(0, [128, 4, D])
        for ti in range(self.NT):
            t0 = ti * TT
            P.dma(xt, self.x_in[t0:t0 + TT, :].rearrange("(b p) f -> p b f", p=128), [], [("WA", "xt")])
            xb, xbk = self.xb.next()
            for fc in range(8):
                pa, pk = self.pa.next()
                for b in range(4):
                    self.tr(pa[:, b * 128:(b + 1) * 128], xt[:, b, fc * 128:(fc + 1) * 128], self.ident,
                            [("WA", "xt"), "cst"], [pk])
                self.cp("act", self.x32[:, fc, :], pa[:], [pk], [("x32", fc)])
                self.cp("dve", xb[:, fc, :], pa[:], [pk], [(xbk, fc)])
            self.store_x(ti, xb, xbk)

    def ln_acc(self, fc, y_fc, yk):
        if fc == 0:
            self.cp("act", self.s1[:], y_fc, [yk], ["s1"])
            self.act(self.s2[:], y_fc, AF.Square, [yk], ["s2"])
        else:
            self.tt("dve", self.s1[:], self.s1[:], y_fc, ALU.add, [yk, "s1"], ["s1"])
            t, tk = self.t512.next()
            self.act(t[:], y_fc, AF.Square, [yk], [tk])
            self.tt("dve", self.s2[:], self.s2[:], t[:], ALU.add, [tk, "s2"], ["s2"])

    def ln_finish(self, y, yks, gcol, bcol, xb, xbk):
        pa, pk = self.pa.next()
        self.mm(pa[:], self.ones, self.s1[:], True, True, ["cst", "s1"], [pk])
        pb, pbk = self.pa.next()
        self.mm(pb[:], self.ones, self.s2[:], True, True, ["cst", "s2"], [pbk])
        self.ts("dve", self.mean[:], pa[:], 1.0 / D, ALU.mult, [pk], ["mean"])
        t, tk = self.t512.next()
        self.tt("dve", t[:], self.mean[:], self.mean[:], ALU.mult, ["mean"], [tk])
        self.stt("dve", t[:], pb[:], 1.0 / D, t[:], ALU.mult, ALU.subtract, [pbk, tk], [tk])
        self.rsqrt(self.rstd[:], t[:], 1.0, LN_EPS, [tk], "rstd")
        for fc in range(8):
            self.tt("dve", y[:, fc, :], y[:, fc, :], self.mean[:], ALU.subtract, [yks[fc], "mean"], [yks[fc]])
            self.tt("dve", y[:, fc, :], y[:, fc, :], self.rstd[:], ALU.mult, [yks[fc], "rstd"], [yks[fc]])
            self.act(self.x32[:, fc, :], y[:, fc, :], AF.Identity, [yks[fc], "pv"], [("x32", fc)],
                     bias=self.pv[:, bcol + fc:bcol + fc + 1], scale=self.pv[:, gcol + fc:gcol + fc + 1])
            self.act(xb[:, fc, :], y[:, fc, :], AF.Identity, [yks[fc], "pv"], [(xbk, fc)],
                     bias=self.pv[:, bcol + fc:bcol + fc + 1], scale=self.pv[:, gcol + fc:gcol + fc + 1])

    def proj_fm(self, wv, arena, name, col0, xb, xbk, M=128):
        pa, pk = self.pa.next()
        for kc in range(8):
            self.mm(pa[0:M, :], wv[:, kc, col0:col0 + M], xb[:, kc, :], kc == 0, kc == 7,
                    [(arena, name, kc, col0 // SW), (xbk, kc)], [pk])
        return pa, pk

    def proj_tok(self, wv, arena, name, col0, ncol, xb, xbk, b):
        pa, pk = self.pa.next()
        for kc in range(8):
            self.mm(pa[:, 0:ncol], xb[:, kc, b * 128:(b + 1) * 128], wv[:, kc, col0:col0 + ncol], kc == 0, kc == 7,
                    [(arena, name, kc, col0 // SW), (xbk, kc)], [pk])
        return pa, pk

    def out_fm(self, dram, dname, ti, fc, src, sk):
        t0 = ti * TT
        self.P.dma(dram[fc * 128:(fc + 1) * 128, t0:t0 + TT], src, [sk], [(dname, ti, fc)], q="sp")

    def v_tok(self, wv, name, col0, xb, xbk, ti):
        for b in range(4):
            for half in range(2):
                pa, pk = self.proj_tok(wv, "WA", name, col0 + half * 512, 512, xb, xbk, b)
                t, tk = self.t512.next()
                self.cp("act", t[:], pa[:], [pk], [tk])
                r0 = ti * TT + b * 128
                self.P.dma(self.Vt[r0:r0 + 128, half * 512:(half + 1) * 512], t[:], [tk], [("Vt", ti, b, half)], q="sp")

    def ffn1(self, li):
        P = self.P
        self.fence("WA")
        w1 = self.wa_view(8, 2 * DFF)
        self.load_w(w1, self.w[li][2], D, 2 * DFF, "WA", "wF1")
        self.fence("WB")
        self.load_w(self.wb_view(22, D), self.w[li][3], DFF, D, "WB", "wF2")
        for ti in range(self.NT):
            xb, xbk = self.load_xb(ti)
            for j in range(22):
                pg, pgk = self.proj_fm(w1, "WA", "wF1", j * 128, xb, xbk)
                pu, puk = self.proj_fm(w1, "WA", "wF1", DFF + j * 128, xb, xbk)
                t, tk = self.t512.next()
                self.act(t[:], pg[:], AF.Silu, [pgk], [tk])
                hb, hbk = self.b512.next()
                self.tt("dve", hb[:], t[:], pu[:], ALU.mult, [tk, puk], [hbk])
                self.out_fm(self.Hd, "Hd", ti, j, hb[:], hbk)

    def ffn2(self, li, last):
        P = self.P
        self.fence("WA")
        w2 = self.wb_view(22, D)
        hs = [self.WA[:, i * 5632:(i + 1) * 5632].bitcast(BF16).rearrange("p (j t) -> p j t", t=TT) for i in range(2)]
        ys = [self.waf(11264 + i * 4096, [128, 8, TT]) for i in range(2)]
        g2, b2 = PV[("g2", li)], PV[("b2", li)]
        for ti in range(self.NT):
            t0 = ti * TT
            h, hk = hs[ti % 2], ("WA", "h", ti % 2)
            P.dma(h, self.Hd[:, t0:t0 + TT].rearrange("(j p) t -> p j t", p=128),
                  [("Hd", ti, j) for j in range(22)], [hk])
            y = ys[ti % 2]
            yks = [("WA", "y", ti % 2, fc) for fc in range(8)]
            for fc in range(8):
                pa, pk = self.pa.next()
                for j in range(22):
                    self.mm(pa[:], w2[:, j, fc * 128:(fc + 1) * 128], h[:, j, :], j == 0, j == 21,
                            [("WB", "wF2", j, 0), hk], [pk])
                xr, xrk = self.t512.next()
                P.dma(xr[:], self.X32[fc * 128:(fc + 1) * 128, t0:t0 + TT], [("X32", ti)], [xrk])
                self.stt("dve", y[:, fc, :], xr[:], ALPHA, pa[:], ALU.mult, ALU.add, [xrk, pk], [yks[fc]])
                self.ln_acc(fc, y[:, fc, :], yks[fc])
            xb, xbk = self.xb.next()
            self.ln_finish(y, yks, g2, b2, xb, xbk)
            if last:
                for b in range(4):
                    for half in range(2):
                        pa, pk = self.pa.next()
                        for q in range(4):
                            self.tr(pa[:, q * 128:(q + 1) * 128], self.x32[:, half * 4 + q, b * 128:(b + 1) * 128],
                                    self.ident, [("x32", half * 4 + q), "cst"], [pk])
                        t, tk = self.t512.next()
                        self.cp("act", t[:], pa[:], [pk], [tk])
                        P.dma(self.out_d[t0 + b * 128:t0 + (b + 1) * 128, half * 512:(half + 1) * 512], t[:],
                              [tk], [("out", ti, b, half)], q="sp")
            else:
                self.store_x(ti, xb, xbk)

    def epilogue(self, li, ti, dv, nwcol, yoff):
        P = self.P
        import os
        if os.environ.get("K_SKIP_EPI"):
            xb, xbk = self.xb.next()
            self.store_x(ti, xb, xbk)
            return
        wO = self.wb_view(8, D)
        nfc = dv // 128
        ob, obk = self.xb.next()
        t0 = ti * TT
        def head_gen(hd):
            pa, pk = self.pa.next()
            for q in range(nfc):
                fc = hd * nfc + q
                t, tk = self.t512.next()
                self.act(t[:], self.x32[:, fc, :], AF.Square, [("x32", fc)], [tk])
                self.mm(pa[:], self.ones, t[:], q == 0, q == nfc - 1, ["cst", tk], [pk])
            yield
            rs, rsk = self.t512.next()
            self.rsqrt(rs[:], pa[:], 1.0 / dv, RMS_EPS, [pk], rsk)
            for q in range(nfc):
                fc = hd * nfc + q
                r, rk = self.t512.next()
                P.dma(r[:], self.Rd[fc * 128:(fc + 1) * 128, t0:t0 + TT], [("Rd", ti, fc)], [rk])
                t, tk = self.t512.next()
                self.tt("dve", t[:], self.x32[:, fc, :], rs[:], ALU.mult, [("x32", fc), rsk], [tk])
                self.stt("dve", ob[:, fc, :], t[:], self.pv[:, nwcol + q:nwcol + q + 1], r[:], ALU.mult, ALU.mult,
                         [tk, rk, "pv"], [(obk, fc)])

        self.run_pipe((head_gen(hd) for hd in range(8 // nfc)), 2)
        y = self.waf(yoff, [128, 8, TT])
        yks = [("WA", "ye", fc) for fc in range(8)]
        for fo in range(8):
            pa, pk = self.pa.next()
            for kc in range(8):
                self.mm(pa[:], wO[:, kc, fo * 128:(fo + 1) * 128], ob[:, kc, :], kc == 0, kc == 7,
                        [("WB", "wO", kc, 0), (obk, kc)], [pk])
            xr, xrk = self.t512.next()
            P.dma(xr[:], self.X32[fo * 128:(fo + 1) * 128, t0:t0 + TT], [("X32", ti)], [xrk])
            self.stt("dve", y[:, fo, :], xr[:], ALPHA, pa[:], ALU.mult, ALU.add, [xrk, pk], [yks[fo]])
            self.ln_acc(fo, y[:, fo, :], yks[fo])
        xb, xbk = self.xb.next()
        self.ln_finish(y, yks, PV[("g1", li)], PV[("b1", li)], xb, xbk)
        self.store_x(ti, xb, xbk)

    def hgrn_A(self, li):
        P = self.P
        self.fence("WA")
        wA = self.wa_view(8, HGRN_IN)
        self.load_w(wA, self.w[li][0], D, HGRN_IN, "WA", "wA")
        sm = self.sm[:, 128:216]
        lbl = self.pv[:, PV["lbl"]:PV["lbl"] + 32]
        self.act(sm[:, 0:32], lbl, AF.Exp, ["pv"], ["sm"])
        ev = sm[:, 0:32].rearrange("p (c d) -> p c d", d=4)
        self.red("dve", sm[:, 32:40], ev, ["sm"], ["sm1"])
        self.red("dve", sm[:, 40:48], ev[:, :, 1:li + 1], ["sm"], ["sm2"])
        P.op("dve", lambda e: e.reciprocal(out=sm[:, 48:56], in_=sm[:, 32:40]), ["sm1"], ["sm3"])
        self.tt("dve", sm[:, 56:64], sm[:, 40:48], sm[:, 48:56], ALU.mult, ["sm2", "sm3"], ["lb"])
        self.ts("dve", sm[:, 64:72], sm[:, 56:64], -1.0, ALU.mult, ["lb"], ["oml"], s2=1.0, op1=ALU.add)
        for ti in range(self.NT):
            xb, xbk = self.load_xb(ti)
            for fc in range(8):
                pa, pk = self.proj_fm(wA, "WA", "wA", fc * 128, xb, xbk)
                t, tk = self.t512.next()
                self.act(t[:], pa[:], AF.Silu, [pk], [tk])
                self.out_fm(self.Qd, "Qd", ti, fc, t[:], tk)
            for fc in range(8):
                pa, pk = self.proj_fm(wA, "WA", "wA", 1024 + fc * 128, xb, xbk)
                t, tk = self.t512.next()
                self.act(t[:], pa[:], AF.Sigmoid, [pk], [tk])
                self.ts("dve", t[:], t[:], sm[:, 64 + fc:65 + fc], ALU.mult, [tk, "oml", "lb"], [tk],
                        s2=sm[:, 56 + fc:57 + fc], op1=ALU.add)
                g, gk = self.t512.next()
                self.act(g[:], t[:], AF.Ln, [tk], [gk])
                self.out_fm(self.Gd, "Gd", ti, fc, g[:], gk)
                k, kk = self.t512.next()
                self.ts("dve", k[:], t[:], -1.0, ALU.mult, [tk], [kk], s2=1.0, op1=ALU.add)
                self.out_fm(self.Kd, "Kd", ti, fc, k[:], kk)
            self.v_tok(wA, "wA", 2048, xb, xbk, ti)
            for fc in range(8):
                pa, pk = self.proj_fm(wA, "WA", "wA", 3072 + fc * 128, xb, xbk)
                t, tk = self.t512.next()
                self.act(t[:], pa[:], AF.Silu, [pk], [tk])
                self.out_fm(self.Rd, "Rd", ti, fc, t[:], tk)

    def gla_A(self, li):
        P = self.P
        self.fence("WA")
        wA = self.wa_view(8, GLA_IN)
        self.load_w(wA, self.w[li][0], D, GLA_IN, "WA", "wA")
        wg = self.waf(13000, [16, 512])
        P.dma(wg, self.wgk2_d, [], [("WA", "wgk2")])
        for ti in range(self.NT):
            xb, xbk = self.load_xb(ti)
            for fc in range(4):
                pa, pk = self.proj_fm(wA, "WA", "wA", fc * 128, xb, xbk)
                t, tk = self.t512.next()
                self.cp("act", t[:], pa[:], [pk], [tk])
                self.out_fm(self.Qd, "Qd", ti, fc, t[:], tk)
            for fc in range(4):
                pa, pk = self.proj_fm(wA, "WA", "wA", 512 + fc * 128, xb, xbk)
                t, tk = self.t512.next()
                self.cp("act", t[:], pa[:], [pk], [tk])
                self.out_fm(self.Kd, "Kd", ti, fc, t[:], tk)
            self.v_tok(wA, "wA", 1024, xb, xbk, ti)
            for fc in range(8):
                pa, pk = self.proj_fm(wA, "WA", "wA", 2048 + fc * 128, xb, xbk)
                t, tk = self.t512.next()
                self.act(t[:], pa[:], AF.Silu, [pk], [tk])
                self.out_fm(self.Rd, "Rd", ti, fc, t[:], tk)
            pa, pk = self.proj_fm(wA, "WA", "wA", 3072, xb, xbk, M=16)
            gk_, gkk = self.waf(13512, [16, 512]), ("WA", "gk")
            self.cp("act", gk_[0:16, :], pa[0:16, :], [pk], [gkk])
            for fc in range(4):
                pb, pbk = self.pa.next()
                self.mm(pb[:], wg[0:16, fc * 128:(fc + 1) * 128], gk_[0:16, :], True, True, [("WA", "wgk2"), gkk], [pbk])
                t, tk = self.t512.next()
                self.act(t[:], pb[:], AF.Sigmoid, [pbk, "pv"], [tk], bias=self.pv[:, PV["bgk"] + fc:PV["bgk"] + fc + 1])
                self.act(t[:], t[:], AF.Ln, [tk], [tk])
                g, gk2 = self.t512.next()
                self.ts("dve", g[:], t[:], 1.0 / 16.0, ALU.mult, [tk], [gk2])
                self.out_fm(self.Gd, "Gd", ti, fc, g[:], gk2)

    def gla_B(self, li, H, dv, nwcol):
        P = self.P
        self.fence("WA")
        S32 = self.S32
        P.op("pool", lambda e: e.memset(S32[:], 0.0), [], ["S32"])
        q32 = self.waf(0, [128, 4, TT])
        k32 = self.waf(2048, [128, 4, TT])
        g32 = self.waf(4096, [128, 4, TT])
        gm = self.waf(6144, [128, 4, TT])
        AT = self.waf(8192, [64, 4, TT])
        kT = self.WA[0:64, 10240:14336].rearrange("p (n h d) -> p n h d", h=4, d=128)
        Vc = [self.waf(14336 + i * 1024, [64, 1024]) for i in range(2)]
        yoff = 16384
        Gs = self.waf(20480, [128, 4, TT])
        sm = self.sm
        egm = sm[:, 0:32].rearrange("p (h n) -> p h n", n=8)
        egl = sm[:, 32:64].rearrange("p (h n) -> p h n", n=8)
        gam = sm[:, 64:96].rearrange("p (h n) -> p h n", n=8)
        nfc = dv // 128
        scm = self.cst[:, C_SC:C_SC + 512]
        mu8 = self.cst[0:64, C_MU:C_MU + 512]
        scale = 128 ** -0.5
        vi = 0
        import os
        lvl = int(os.environ.get("K_REC_LEVEL", "9"))
        for ti in range(self.NT):
            t0 = ti * TT
            for gi in range(H // 4 if lvl >= 1 else 0):
                h0 = gi * 4
                for (dst, dr, nm) in ((q32, self.Qd, "Qd"), (k32, self.Kd, "Kd"), (g32, self.Gd, "Gd")):
                    P.dma(dst, self.tile_fm(dr, ti, h0 * 128, 512), [(nm, ti, h0 + i) for i in range(4)],
                          [("WA", nm + "t", i) for i in range(4)])
                def head_gen(hh):
                    kq, kk_, kg, kgm = ("WA", "Qdt", hh), ("WA", "Kdt", hh), ("WA", "Gdt", hh), ("WA", "gm", hh)
                    self.scan(Gs[:, hh, :], scm, g32[:, hh, :], [kg, "cst"], [("WA", "Gs", hh)])
                    kg = ("WA", "Gs", hh)
                    Gv = Gs[:, hh, :].rearrange("p (n c) -> p n c", c=64)
                    gmv = gm[:, hh, :].rearrange("p (n c) -> p n c", c=64)
                    self.tt("dve", gmv, Gv, Gv[:, :, 31:32].broadcast_to([128, 8, 64]), ALU.subtract, [kg], [kgm])
                    e1, e1k = self.t512.next()
                    self.act(e1[:], gm[:, hh, :], AF.Exp, [kgm], [e1k])
                    e2, e2k = self.t512.next()
                    self.act(e2[:], gm[:, hh, :], AF.Exp, [kgm], [e2k], scale=-1.0)
                    self.act(egm[:, hh, :], Gv[:, :, 31], AF.Exp, [kg], [("egm", hh)])
                    self.act(egl[:, hh, :], gmv[:, :, 63], AF.Exp, [kgm], [("egl", hh)])
                    self.act(gam[:, hh, :], Gv[:, :, 63], AF.Exp, [kg], [("gam", hh)])
                    yield
                    self.stt("dve", q32[:, hh, :], q32[:, hh, :], scale, e1[:], ALU.mult, ALU.mult, [kq, e1k], [kq])
                    self.tt("dve", k32[:, hh, :], k32[:, hh, :], e2[:], ALU.mult, [kk_, e2k], [kk_])
                    pa, pk = self.pa.next()
                    for n in range(8):
                        self.mm(pa[0:64, n * 64:(n + 1) * 64], k32[:, hh, n * 64:(n + 1) * 64],
                                q32[:, hh, n * 64:(n + 1) * 64], True, True, [kk_, kq], [pk])
                    yield
                    self.tt("dve", AT[:, hh, :], pa[0:64, :], mu8, ALU.mult, [pk, "cst"], [("WA", "AT", hh)])
                    qv = q32[:, hh, :].rearrange("p (n c) -> p n c", c=64)
                    kv = k32[:, hh, :].rearrange("p (n c) -> p n c", c=64)
                    self.tt("dve", qv, qv, egm[:, hh, :].unsqueeze(2).broadcast_to([128, 8, 64]), ALU.mult,
                            [kq, ("egm", hh)], [kq])
                    self.tt("dve", kv, kv, egl[:, hh, :].unsqueeze(2).broadcast_to([128, 8, 64]), ALU.mult,
                            [kk_, ("egl", hh)], [kk_])
                    pd, pdk = self.pd.next()
                    for n in range(8):
                        self.tr(pd[0:64, n * 128:(n + 1) * 128], k32[:, hh, n * 64:(n + 1) * 64], self.ident,
                                [kk_, "cst"], [pdk])
                    for hf in range(2):
                        self.cp("act", kT[:, hf * 4:(hf + 1) * 4, hh, :],
                                pd[0:64, hf * 512:(hf + 1) * 512].rearrange("p (n d) -> p n d", d=128), [pdk],
                                [("WA", "kT", hh)])

                self.run_pipe((head_gen(hh) for hh in range(4)), 3)
                for n in range(8 if lvl >= 5 else 0):
                    V = Vc[vi % 2]
                    vk = ("WA", "Vc", vi % 2)
                    vi += 1
                    r0 = t0 + n * 64
                    P.dma(V, self.Vt[r0:r0 + 64, :], [("Vt", ti, n // 2, hf) for hf in range(2)], [vk])
                    po, pok = self.pa.next()
                    for hh in range(4):
                        h = h0 + hh
                        for eh in range(nfc):
                            slot = hh * nfc + eh
                            col = h * dv + eh * 128
                            self.mm(po[:, slot * 64:(slot + 1) * 64], S32[:, col:col + 128],
                                    q32[:, hh, n * 64:(n + 1) * 64], True, False, ["S32", ("WA", "Qdt", hh)], [pok])
                            self.mm(po[:, slot * 64:(slot + 1) * 64], V[:, col:col + 128],
                                    AT[:, hh, n * 64:(n + 1) * 64], False, True, [vk, ("WA", "AT", hh)], [pok])
                    fc0 = h0 * nfc
                    ns = 4 * nfc
                    self.cp("act", self.x32[:, fc0:fc0 + ns, n * 64:(n + 1) * 64],
                            po[:, 0:ns * 64].rearrange("p (s c) -> p s c", c=64), [pok],
                            [("x32", fc0 + s) for s in range(ns)])
                    if lvl < 6:
                        continue
                    pS, pSk = self.pd.next()
                    for hh in range(4):
                        h = h0 + hh
                        self.mm(pS[:, hh * dv:(hh + 1) * dv], kT[:, n, hh, :], V[:, h * dv:(h + 1) * dv], True, True,
                                [("WA", "kT", hh), vk], [pSk])
                    Sv = S32[:, h0 * dv:(h0 + 4) * dv].rearrange("p (h e) -> p h e", e=dv)
                    self.tt("dve", Sv, Sv, gam[:, :, n:n + 1].broadcast_to([128, 4, dv]), ALU.mult,
                            ["S32"] + [("gam", hh) for hh in range(4)], ["S32"])
                    for c0 in range(0, 4 * dv, 512):
                        self.tt("dve", S32[:, h0 * dv + c0:h0 * dv + c0 + 512], S32[:, h0 * dv + c0:h0 * dv + c0 + 512],
                                pS[:, c0:c0 + 512], ALU.add, ["S32", pSk], ["S32"])
            self.epilogue(li, ti, dv, nwcol, yoff)

    def gdn_A(self, li):
        P = self.P
        j = li // 3
        self.fence("WA")
        wA = self.wa_view(8, GDN_IN)
        self.load_w(wA, self.w[li][0], D, GDN_IN, "WA", "wA")
        pres = Rot([(self.waf(16448 + i * 520, [128, 516]), ("WA", "pre", i)) for i in range(3)])
        tl = Rot([(self.waf(18008 + i * 512, [128, 512]), ("WA", "tl", i)) for i in range(7)])
        tkR = Rot([(self.waf(18008 + i * 512, [128, 512]), ("WA", "tl", i)) for i in range(7, 9)])
        sqR = self.t512
        carry = self.carry
        P.op("pool", lambda e: e.memset(carry[:], 0.0), [], [("carry", fc) for fc in range(24)])
        sm = self.sm
        negA = sm[:, 208:216]
        self.act(negA, self.pr[:, j * 16:j * 16 + 8], AF.Exp, ["pr"], ["negA"])
        self.ts("dve", negA, negA, -1.0, ALU.mult, ["negA"], ["negA"])
        cw0 = PV[("conv", j)]
        for ti in range(self.NT):
            t0 = ti * TT
            xb, xbk = self.load_xb(ti)
            def fc_s1(fc, xb=xb, xbk=xbk):
                pa, pk = self.proj_fm(wA, "WA", "wA", fc * 128, xb, xbk)
                pre, prk = pres.next()
                self.cp("pool", pre[:, 0:3], carry[:, fc, 0:3], [("carry", fc)], [prk])
                self.cp("act", pre[:, 3:515], pa[:], [pk], [prk])
                self.cp("pool", carry[:, fc, 0:3], pre[:, 512:515], [prk], [("carry", fc)])
                acc, ak = tl.next()
                wc = lambda k: self.pv[:, cw0 + fc * 4 + k:cw0 + fc * 4 + k + 1]
                self.ts("dve", acc[:], pre[:, 3:515], wc(3), ALU.mult, [prk, "pv"], [ak])
                for k in (2, 1, 0):
                    self.stt("dve", acc[:], pre[:, k:k + 512], wc(k), acc[:], ALU.mult, ALU.add, [prk, ak, "pv"], [ak])
                return acc, ak

            def fc_gen(fcs, ti=ti, t0=t0):
                st = [fc_s1(fc) for fc in fcs]
                sqs = []
                for fc, (s, sk) in zip(fcs, st):
                    self.act(s[:], s[:], AF.Silu, [sk], [sk])
                    if fc < 16:
                        sq, sqk = sqR.next()
                        self.tt("dve", sq[:], s[:], s[:], ALU.mult, [sk], [sqk])
                        sqs.append((sq, sqk))
                yield
                if fcs[0] < 16:
                    pbs = []
                    for (sq, sqk) in sqs:
                        pb, pbk = self.pa.next()
                        self.mm(pb[:], self.ones, sq[:], True, True, ["cst", sqk], [pbk])
                        pbs.append((pb, pbk))
                    for fc, (s, sk), (sq, sqk), (pb, pbk) in zip(fcs, st, sqs, pbs):
                        self.rsqrt(sq[:], pb[:], 1.0, L2_EPS, [pbk, sqk], sqk)
                        if fc < 8:
                            self.stt("dve", s[:], s[:], 128 ** -0.5, sq[:], ALU.mult, ALU.mult, [sk, sqk], [sk])
                            self.out_fm(self.Qd, "Qd", ti, fc, s[:], sk)
                        else:
                            self.tt("dve", s[:], s[:], sq[:], ALU.mult, [sk, sqk], [sk])
                            self.out_fm(self.Kd, "Kd", ti, fc - 8, s[:], sk)
                    if fcs[0] < 8:
                        return
                    yield
                for fc, (s, sk) in zip(fcs, st):
                    dt_, dn, hcol = (self.Kt, "Kt", (fc - 8) * 128) if fc < 16 else (self.Vt, "Vt", (fc - 16) * 128)
                    pt, ptk = self.pa.next()
                    for b in range(4):
                        self.tr(pt[:, b * 128:(b + 1) * 128], s[:, b * 128:(b + 1) * 128], self.ident, [sk, "cst"], [ptk])
                    tk_, tkk = tkR.next()
                    self.cp("act", tk_[:], pt[:], [ptk], [tkk])
                    P.dma(dt_[t0:t0 + TT, hcol:hcol + 128].rearrange("(b p) d -> p b d", p=128),
                          tk_[:].rearrange("p (b d) -> p b d", d=128), [tkk], [(dn, ti, hcol // 128)], q="sp")

            self.run_pipe((fc_gen([fc, fc + 1]) for fc in range(0, 24, 2)), 3)
            for fc in range(8):
                pa, pk = self.proj_fm(wA, "WA", "wA", 3072 + fc * 128, xb, xbk)
                t, tk = tl.next()
                self.act(t[:], pa[:], AF.Silu, [pk], [tk])
                self.out_fm(self.Rd, "Rd", ti, fc, t[:], tk)
            for b in range(4):
                pa, pk = self.proj_tok(wA, "WA", "wA", 4096, 16, xb, xbk, b)
                bg, bgk = tl.next()
                self.act(bg[:, 0:8], pa[:, 0:8], AF.Sigmoid, [pk], [bgk])
                self.tt("dve", bg[:, 16:24], pa[:, 8:16], self.pr[:, j * 16 + 8:j * 16 + 16], ALU.add, [pk, "pr"], [bgk])
                self.act(bg[:, 16:24], bg[:, 16:24], AF.Exp, [bgk], [bgk])
                self.act(bg[:, 16:24], bg[:, 16:24], AF.Ln, [bgk], [bgk], bias=1.0)
                self.tt("dve", bg[:, 8:16], bg[:, 16:24], negA, ALU.mult, [bgk, "negA"], [bgk])
                r0 = t0 + b * 128
                P.dma(self.BGd[r0:r0 + 128, :], bg[:, 0:16], [bgk], [("BGd", ti, b)], q="sp")

    def gdn_B(self, li):
        P = self.P
        j = li // 3
        self.fence("WA")
        S32 = self.S32
        P.op("pool", lambda e: e.memset(S32[:], 0.0), [], ["S32"])
        W = lambda off, shape: self.waf(off, shape)
        qc = W(0, [128, 8, 64])
        kc = W(512, [128, 8, 64])
        ktok = W(1024, [64, 1024])
        vtok = W(2048, [64, 1024])
        bg = W(3072, [64, 16])
        o = 3136
        names512 = ["rhs1", "DL", "DU", "N", "NT", "Pa", "Pb", "PTa", "PTb", "Ta", "Tb", "AT"]
        t5 = {}
        for nm in names512:
            t5[nm] = W(o, [64, 8, 64])
            o += 512
        t10 = {}
        for nm in ["ru", "rw", "u", "vnew", "khat"]:
            t10[nm] = W(o, [64, 1024])
            o += 1024
        qh = W(o, [128, 8, 64])
        o += 512
        wT = W(o, [128, 512])
        o += 512
        eGr = W(o, [128, 512])
        o += 512
        qcs, kcs, bgs = [qc], [kc], [bg]
        qcs.append(W(o, [128, 8, 64]))
        o += 512
        kcs.append(W(o, [128, 8, 64]))
        o += 512
        bgs.append(W(o, [64, 16]))
        o += 64
        yoff = o
        assert yoff + 4096 <= 23040, yoff
        K = lambda nm: ("WA", nm)
        sg = self.sm
        Gcol, eGc, beg, egl, nbeta, tmp8 = (sg[0:64, 0:8], sg[0:64, 8:16], sg[0:64, 16:24], sg[0:64, 24:32],
                                            sg[0:64, 32:40], sg[0:64, 40:48])
        gam = sg[:, 48:56]
        cst = self.cst
        triU = cst[0:64, C_MU:C_MU + 64]
        mu8 = cst[0:64, C_MU:C_MU + 512].rearrange("p (h c) -> p h c", c=64)
        sl8 = cst[0:64, C_SL:C_SL + 512].rearrange("p (h c) -> p h c", c=64)
        i8 = cst[0:64, C_I8:C_I8 + 512].rearrange("p (h c) -> p h c", c=64)
        id64 = cst[0:64, C_ID:C_ID + 64]
        ones64 = cst[0:64, C_ONE:C_ONE + 128]
        v3 = lambda ap: ap.rearrange("p (h c) -> p h c", c=64)
        b8 = lambda ap, n: ap.unsqueeze(2).broadcast_to([64, 8, n])

        def mm8(ps_ap, pk, lhs_of, rhs_of, r, width, npart):
            for h in range(8):
                self.mm(ps_ap[0:npart, h * width:(h + 1) * width], lhs_of(h), rhs_of(h), True, True, r, [pk])

        for ti in range(self.NT):
            for n in range(NCH):
                t0c = ti * TT + n * C
                par = n % 2
                qc, kc, bg = qcs[par], kcs[par], bgs[par]
                kqc, kkc, kbg = K("qc%d" % par), K("kc%d" % par), K("bg%d" % par)
                P.dma(bg, self.BGd[t0c:t0c + C, :], [("BGd", ti, n // 2)], [kbg])
                P.dma(kc, self.Kd[:, t0c:t0c + C].rearrange("(h p) t -> p h t", p=128),
                      [("Kd", ti, f) for f in range(8)], [kkc])
                P.dma(qc, self.Qd[:, t0c:t0c + C].rearrange("(h p) t -> p h t", p=128),
                      [("Qd", ti, f) for f in range(8)], [kqc])
                P.dma(ktok, self.Kt[t0c:t0c + C, :], [("Kt", ti, f) for f in range(8)], [K("ktok")])
                P.dma(vtok, self.Vt[t0c:t0c + C, :], [("Vt", ti, f) for f in range(8)], [K("vtok")])
                beta, g = bg[:, 0:8], bg[:, 8:16]
                pA, pAk = self.pa.next()
                self.mm(pA[0:64, 0:8], triU, g, True, True, ["cst", kbg], [pAk])
                self.mm(pA[:, 8:16], ones64, g, True, True, ["cst", kbg], [pAk])
                self.cp("dve", Gcol, pA[0:64, 0:8], [pAk], ["Gcol"])
                self.tt("dve", tmp8, pA[0:64, 8:16], Gcol, ALU.subtract, [pAk, "Gcol"], ["tmp8"])
                self.act(gam, pA[:, 8:16], AF.Exp, [pAk], ["gam"])
                self.ts("dve", nbeta, beta, -1.0, ALU.mult, [kbg], ["nbeta"])
                self.tt("dve", t5["rhs1"], b8(g, 64), mu8, ALU.mult, [kbg, "cst"], [K("rhs1")])
                pT1, pT1k = self.pa.next()
                self.mm(pT1[:, :], ones64, t5["rhs1"].rearrange("p h c -> p (h c)"), True, True,
                        ["cst", K("rhs1")], [pT1k])
                DL, DU = t5["DL"], t5["DU"]
                self.tt("dve", DL, b8(Gcol, 64), v3(pT1[0:64, :]), ALU.subtract, ["Gcol", pT1k], [K("DL")])
                self.tt("dve", DU, v3(pT1[0:64, :]), b8(Gcol, 64), ALU.subtract, ["Gcol", pT1k], [K("DU")])
                self.act(eGr[:], pT1[:, :], AF.Exp, [pT1k], [K("eGr")])
                self.ts("dve", DL, DL, 0.0, ALU.min, [K("DL")], [K("DL")])
                self.act(DL, DL, AF.Exp, [K("DL")], [K("DL")])
                self.tt("dve", DL, DL, sl8, ALU.mult, [K("DL"), "cst"], [K("DL")])
                pKK, pKKk = self.pa.next()
                mm8(pKK, pKKk, lambda h: kc[:, h, :], lambda h: kc[:, h, :], [kkc], 64, 64)
                N, NT, AT = t5["N"], t5["NT"], t5["AT"]
                self.tt("dve", N, v3(pKK[0:64, :]), DL, ALU.mult, [pKKk, K("DL")], [K("N")])
                self.tt("dve", N, N, b8(nbeta, 64), ALU.mult, [K("N"), "nbeta"], [K("N")])
                pN, pNk = self.pa.next()
                for h in range(8):
                    self.tr(pN[0:64, h * 64:(h + 1) * 64], N[:, h, :], id64, [K("N"), "cst"], [pNk])
                self.cp("act", NT, v3(pN[0:64, :]), [pNk], [K("NT")])
                ru, rw, u, vnew, khat = t10["ru"], t10["rw"], t10["u"], t10["vnew"], t10["khat"]
                v4 = lambda ap: ap.rearrange("p (h d) -> p h d", d=128)

                def d_qk():
                    pQK, pQKk = self.pa.next()
                    mm8(pQK, pQKk, lambda h: kc[:, h, :], lambda h: qc[:, h, :], [kkc, kqc], 64, 64)
                    self.tt("dve", AT, v3(pQK[0:64, :]), DU, ALU.mult, [pQKk, K("DU")], [K("AT")])

                deferred = [
                    lambda: self.ts("dve", DU, DU, 0.0, ALU.min, [K("DU")], [K("DU")]),
                    lambda: self.act(DU, DU, AF.Exp, [K("DU")], [K("DU")]),
                    lambda: self.tt("dve", DU, DU, mu8, ALU.mult, [K("DU"), "cst"], [K("DU")]),
                    d_qk,
                    lambda: self.act(egl, tmp8, AF.Exp, ["tmp8"], ["egl"]),
                    lambda: self.act(eGc, Gcol, AF.Exp, ["Gcol"], ["eGc"]),
                    lambda: self.tt("dve", beg, beta, eGc, ALU.mult, [kbg, "eGc"], ["beg"]),
                    lambda: self.tt("dve", qh, qc, eGr[:].rearrange("p (h c) -> p h c", c=64), ALU.mult,
                                    [kqc, K("eGr")], [K("qh")]),
                    lambda: self.tt("dve", v4(ru), v4(vtok), b8(beta, 128), ALU.mult, [K("vtok"), kbg], [K("ru")]),
                    lambda: self.tt("dve", v4(rw), v4(ktok), b8(beg, 128), ALU.mult, [K("ktok"), "beg"], [K("rw")]),
                    lambda: self.tt("dve", v4(khat), v4(ktok), b8(egl, 128), ALU.mult, [K("ktok"), "egl"], [K("khat")]),
                ]
                Tc, Tk = t5["Ta"], K("Ta")
                self.tt("dve", Tc, NT, i8, ALU.add, [K("NT"), "cst"], [Tk])
                Pp, Ppk, PTp, PTpk = N, K("N"), NT, K("NT")
                for jj in range(1, 6):
                    Pn, Pnk = (t5["Pa"], K("Pa")) if jj % 2 else (t5["Pb"], K("Pb"))
                    PTn, PTnk = (t5["PTa"], K("PTa")) if jj % 2 else (t5["PTb"], K("PTb"))
                    Tn, Tnk = (t5["Tb"], K("Tb")) if jj % 2 else (t5["Ta"], K("Ta"))
                    p1, p1k = self.pa.next()
                    mm8(p1, p1k, lambda h: PTp[:, h, :], lambda h: Pp[:, h, :], [Ppk, PTpk], 64, 64)
                    self.cp("act", Pn, v3(p1[0:64, :]), [p1k], [Pnk])
                    if jj < 5:
                        p2, p2k = self.pa.next()
                        mm8(p2, p2k, lambda h: Pp[:, h, :], lambda h: PTp[:, h, :], [Ppk, PTpk], 64, 64)
                        self.cp("dve", PTn, v3(p2[0:64, :]), [p2k], [PTnk])
                    for _ in range(3):
                        if deferred:
                            deferred.pop(0)()
                    p3, p3k = self.pa.next()
                    mm8(p3, p3k, lambda h: Pn[:, h, :], lambda h: Tc[:, h, :], [Pnk, Tk], 64, 64)
                    self.tt("dve", Tn, v3(p3[0:64, :]), Tc, ALU.add, [p3k, Tk], [Tnk])
                    Pp, Ppk, PTp, PTpk, Tc, Tk = Pn, Pnk, PTn, PTnk, Tn, Tnk
                while deferred:
                    deferred.pop(0)()
                TT_ = Tc
                pU, pUk = self.pd.next()
                mm8(pU, pUk, lambda h: TT_[:, h, :], lambda h: ru[:, h * 128:(h + 1) * 128], [Tk, K("ru")], 128, 64)
                self.cp("act", u[:, 0:512], pU[0:64, 0:512], [pUk], [K("u")])
                self.cp("act", u[:, 512:1024], pU[0:64, 512:1024], [pUk], [K("u")])
                pW, pWk = self.pa.next()
                mm8(pW, pWk, lambda h: rw[:, h * 128:(h + 1) * 128], lambda h: TT_[:, h, :], [Tk, K("rw")], 64, 128)
                self.cp("act", wT[:], pW[:, :], [pWk], [K("wT")])
                pWS, pWSk = self.pd.next()
                mm8(pWS, pWSk, lambda h: wT[:, h * 64:(h + 1) * 64], lambda h: S32[:, h * 128:(h + 1) * 128],
                    [K("wT"), "S32"], 128, 64)
                self.tt("dve", vnew[:, 0:512], u[:, 0:512], pWS[0:64, 0:512], ALU.subtract, [K("u"), pWSk], [K("vnew")])
                self.tt("dve", vnew[:, 512:1024], u[:, 512:1024], pWS[0:64, 512:1024], ALU.subtract, [K("u"), pWSk], [K("vnew")])
                pO, pOk = self.pa.next()
                for h in range(8):
                    self.mm(pO[:, h * 64:(h + 1) * 64], S32[:, h * 128:(h + 1) * 128], qh[:, h, :], True, False,
                            ["S32", K("qh")], [pOk])
                    self.mm(pO[:, h * 64:(h + 1) * 64], vnew[:, h * 128:(h + 1) * 128], AT[:, h, :], False, True,
                            [K("vnew"), K("AT")], [pOk])
                self.cp("act", self.x32[:, :, n * 64:(n + 1) * 64], pO[:, :].rearrange("p (s c) -> p s c", c=64), [pOk],
                        self.x32k)
                pS, pSk = self.pd.next()
                mm8(pS, pSk, lambda h: khat[:, h * 128:(h + 1) * 128], lambda h: vnew[:, h * 128:(h + 1) * 128],
                    [K("khat"), K("vnew")], 128, 128)
                Sv = S32[:, :].rearrange("p (h e) -> p h e", e=128)
                self.tt("dve", Sv, Sv, gam.unsqueeze(2).broadcast_to([128, 8, 128]), ALU.mult, ["S32", "gam"], ["S32"])
                self.tt("dve", S32[:, 0:512], S32[:, 0:512], pS[:, 0:512], ALU.add, ["S32", pSk], ["S32"])
                self.tt("dve", S32[:, 512:1024], S32[:, 512:1024], pS[:, 512:1024], ALU.add, ["S32", pSk], ["S32"])
            self.epilogue(li, ti, 128, PV[("gnw", j)], yoff)


_CACHE = {}


def get_nc(TL, layers, dbg=()):
    key = (TL, tuple(layers), tuple(dbg))
    if key not in _CACHE:
        _CACHE[key] = Builder(TL, list(layers), dbg).build()
    return _CACHE[key]


LAYERS = [(0, 0), (1, 1), (2, 2), (0, 3)]


def core_inputs(inp, xs, layers):
    m = {"x": np.ascontiguousarray(xs, dtype=np.float32), "cst": make_consts(), "pv": make_pv(inp),
         "pr": make_pr(inp), "wgk2": np.ascontiguousarray(inp["gla_w_gk2"][0], dtype=np.float32)}
    for kind, li in layers:
        j = li // 3
        if kind == 0:
            wi, wo = inp["gdn_w_in"][j], inp["gdn_w_out"][j]
        elif kind == 1:
            wi, wo = inp["hgrn_w_in"][0], inp["hgrn_w_out"][0]
        else:
            wi, wo = inp["gla_w_in"][0], inp["gla_w_out"][0]
        m["win%d" % li] = np.ascontiguousarray(wi, dtype=np.float32)
        m["wout%d" % li] = np.ascontiguousarray(wo, dtype=np.float32)
        m["wf1_%d" % li] = np.ascontiguousarray(inp["ffn_w_in"][li], dtype=np.float32)
        m["wf2_%d" % li] = np.ascontiguousarray(inp["ffn_w_down"][li], dtype=np.float32)
    return m


def kernel(**inputs):
    inp = {k: np.asarray(v) for k, v in inputs.items()}
    x = inp["x"]
    B, T, _ = x.shape
    nc = get_nc(T, LAYERS)
    maps = [core_inputs(inp, x[c % B], LAYERS) for c in range(8)]
    res = run_bass_kernel_spmd(nc, maps, core_ids=list(range(8)))
    return np.stack([np.asarray(res.results[b]["out"], dtype=np.float32) for b in range(B)], axis=0)
```
